# Optimizing a Trainium2 kernel written in Bass

```python
import math
import jax, jax.numpy as jnp
from jax import lax
import numpy as np

D_MODEL = 1024
BATCH = 2
SEQ = 8192
DEPTH = 4

GRID_W = 64
CHUNK = 128
EPS = 1e-6
MIX_WIDTH = D_MODEL
RET_HEADS = 8
RET_DH = (MIX_WIDTH // 2) // RET_HEADS
RET_WIDTH = RET_HEADS * RET_DH
ROPE_BASE = 10000.0
NA_HEADS = 8
NA_DH = (MIX_WIDTH // 2) // NA_HEADS
NA_WIDTH = NA_HEADS * NA_DH
NA_WIN_R = 8
NA_WIN_C = 16
NA_QBLK = 16
AB_IN_DIM = 4 * RET_WIDTH + 3 * NA_WIDTH
SSD_INNER = 2 * D_MODEL
SSD_HEADDIM = 64
SSD_HEADS = SSD_INNER // SSD_HEADDIM
SSD_GROUPS = 4
SSD_HPG = SSD_HEADS // SSD_GROUPS
SSD_STATE = 128
SSD_CONV = 5
SSD_XBC = SSD_INNER + 2 * SSD_GROUPS * SSD_STATE
SSD_IN_DIM = SSD_INNER + SSD_XBC + 2 * SSD_HEADS
FFN_DIM = 2816
FFN_CONV = 3
N_EVEN = (DEPTH + 1) // 2
N_ODD = DEPTH // 2

kernel_name = 'hybrid_retention_natten_ssd_encoder'


def rms_norm(x, g):
    xf = x.astype(jnp.float32)
    y = xf * lax.rsqrt(jnp.mean(xf * xf, axis=-1, keepdims=True) + EPS)
    return (y * g.astype(jnp.float32)).astype(x.dtype)


def depthwise_conv_centered(x, w, b):
    width, ch = w.shape
    pad = width // 2
    y = lax.conv_general_dilated(x, w[:, None, :].astype(x.dtype), window_strides=(1,),
                                 padding=[(pad, pad)], dimension_numbers=('NWC', 'WIO', 'NWC'),
                                 feature_group_count=ch)
    return y + b.astype(x.dtype)


def rotary(x, pos):
    half = x.shape[-1] // 2
    inv = 1.0 / (ROPE_BASE ** (jnp.arange(half, dtype=jnp.float32) / half))
    ang = pos.astype(jnp.float32)[:, None] * inv[None, :]
    cos = jnp.cos(ang)[None, :, None, :]
    sin = jnp.sin(ang)[None, :, None, :]
    xf = x.astype(jnp.float32)
    x1, x2 = xf[..., :half], xf[..., half:]
    return jnp.concatenate([x1 * cos - x2 * sin, x1 * sin + x2 * cos], axis=-1).astype(x.dtype)


def chunked_scan(q, k, v, log_a, include_diag):
    f32 = jnp.float32
    bsz, s, g, n = q.shape
    hg, p = v.shape[3], v.shape[4]
    n_chunks = s // CHUNK

    def to_chunks(t):
        return jnp.moveaxis(t.astype(f32).reshape(bsz, n_chunks, CHUNK, *t.shape[2:]), 1, 0)

    qc, kc, vc, ac = to_chunks(q), to_chunks(k), to_chunks(v), to_chunks(log_a)
    idx = jnp.arange(CHUNK)
    mask = (idx[:, None] >= idx[None, :]) if include_diag else (idx[:, None] > idx[None, :])

    def step(h, inp):
        qq, kk, vv, aa = inp
        cs = jnp.cumsum(aa, axis=1)
        seg = cs[:, :, None] - cs[:, None, :]
        decay = jnp.exp(jnp.where(mask[None, :, :, None, None], seg, -jnp.inf))
        qk = jnp.einsum('bjgn,blgn->bjlg', qq, kk)
        y_intra = jnp.einsum('bjlgh,blghp->bjghp', qk[..., None] * decay, vv)
        y_inter = jnp.einsum('bjgn,bghnp->bjghp', qq, h) * jnp.exp(cs)[..., None]
        tail = jnp.exp(cs[:, -1:] - cs)
        h_new = h * jnp.exp(cs[:, -1])[..., None, None] + jnp.einsum('blgn,blghp->bghnp', kk, vv * tail[..., None])
        return h_new, y_intra + y_inter

    h0 = jnp.zeros((bsz, g, hg, n, p), f32)
    _, ys = lax.scan(step, h0, (qc, kc, vc, ac))
    return jnp.moveaxis(ys, 0, 1).reshape(bsz, s, g, hg, p)


def bidir_scan(q, k, v_f, v_b, a_f, a_b):
    y_f = chunked_scan(q, k, v_f, a_f, True)
    flip = lambda t: jnp.flip(t, axis=1)
    y_b = flip(chunked_scan(flip(q), flip(k), flip(v_b), flip(a_b), False))
    return y_f + y_b


def neighborhood_attention(q, k, v, rpb):
    f32 = jnp.float32
    bsz, s, h, dh = q.shape
    rows = s // GRID_W
    win_r = min(NA_WIN_R, rows)
    n_cb = GRID_W // NA_QBLK
    span = NA_QBLK + NA_WIN_C
    qb_all = (q.astype(f32) * dh ** -0.5).reshape(bsz, rows, n_cb, NA_QBLK, h, dh)
    qb_all = jnp.moveaxis(qb_all, 2, 0)
    k = k.reshape(bsz, rows, GRID_W, h, dh)
    v = v.reshape(bsz, rows, GRID_W, h, dh)
    r = jnp.arange(rows)
    key_rows = jnp.clip(r - win_r // 2, 0, rows - win_r)[:, None] + jnp.arange(win_r)[None, :]
    dr = key_rows - r[:, None] + (NA_WIN_R - 1)
    c0s = jnp.arange(n_cb) * NA_QBLK

    def block(args):
        qb, c0 = args
        qcols = c0 + jnp.arange(NA_QBLK)
        kcols = jnp.clip(c0 - NA_WIN_C // 2, 0, GRID_W - span) + jnp.arange(span)
        cstart = jnp.clip(qcols - NA_WIN_C // 2, 0, GRID_W - NA_WIN_C)
        valid = (kcols[None, :] >= cstart[:, None]) & (kcols[None, :] < cstart[:, None] + NA_WIN_C)
        dc = jnp.clip(kcols[None, :] - qcols[:, None], -(NA_WIN_C - 1), NA_WIN_C - 1) + (NA_WIN_C - 1)
        kb = k[:, key_rows[:, :, None], kcols[None, None, :]].astype(f32)
        vb = v[:, key_rows[:, :, None], kcols[None, None, :]].astype(f32)
        bias = rpb[:, dr[:, None, :, None], dc[None, :, None, :]].astype(f32)
        sc = jnp.einsum('brqhd,brwchd->bhrqwc', qb, kb) + bias[None]
        sc = jnp.where(valid[:, None, :], sc, -jnp.inf)
        pr = jax.nn.softmax(sc.reshape(*sc.shape[:4], -1), axis=-1).reshape(sc.shape)
        return jnp.einsum('bhrqwc,brwchd->brqhd', pr, vb)

    out = lax.map(block, (qb_all, c0s))
    return jnp.moveaxis(out, 0, 2).reshape(bsz, s, h * dh)


def retention_na_mixer(hn, w_in, ret_decay_logit, ret_gn_g, na_rpb, w_out):
    f32 = jnp.float32
    bsz, s, _ = hn.shape
    proj = hn @ w_in
    R, N = RET_WIDTH, NA_WIDTH
    rq, rk, rv, rg, nq, nk, nv = jnp.split(proj, [R, 2 * R, 3 * R, 4 * R, 4 * R + N, 4 * R + 2 * N], axis=-1)
    pos = jnp.arange(s)
    rshape = (bsz, s, RET_HEADS, RET_DH)
    rq = rotary(rq.reshape(rshape), pos)
    rk = rotary(rk.reshape(rshape), pos) * (RET_DH ** -0.5)
    rv = rv.reshape(bsz, s, RET_HEADS, 1, RET_DH)
    log_gamma = -jax.nn.softplus(-ret_decay_logit.astype(f32))
    a_f = jnp.broadcast_to(log_gamma[0][None, None, :, None], (bsz, s, RET_HEADS, 1))
    a_b = jnp.broadcast_to(log_gamma[1][None, None, :, None], (bsz, s, RET_HEADS, 1))
    y = bidir_scan(rq, rk, rv, rv, a_f, a_b).reshape(rshape)
    mu = jnp.mean(y, axis=-1, keepdims=True)
    var = jnp.mean(jnp.square(y - mu), axis=-1, keepdims=True)
    y = ((y - mu) * lax.rsqrt(var + EPS)).reshape(bsz, s, RET_WIDTH) * ret_gn_g.astype(f32)
    ret_out = (jax.nn.silu(rg.astype(f32)) * y).astype(hn.dtype)
    nshape = (bsz, s, NA_HEADS, NA_DH)
    na_out = neighborhood_attention(nq.reshape(nshape), nk.reshape(nshape), nv.reshape(nshape), na_rpb).astype(hn.dtype)
    return jnp.concatenate([ret_out, na_out], axis=-1) @ w_out


def ssd_mixer(hn, w_in, conv_w, conv_b, dt_bias, a_log, d_skip, norm_g, w_out):
    f32 = jnp.float32
    bsz, s, _ = hn.shape
    proj = hn @ w_in
    z, xbc, dt_raw = jnp.split(proj, [SSD_INNER, SSD_INNER + SSD_XBC], axis=-1)
    xbc = jax.nn.silu(depthwise_conv_centered(xbc, conv_w, conv_b))
    xs, bm, cm = jnp.split(xbc, [SSD_INNER, SSD_INNER + SSD_GROUPS * SSD_STATE], axis=-1)
    dt = jax.nn.softplus(dt_raw.astype(f32).reshape(bsz, s, 2, SSD_HEADS) + dt_bias.astype(f32))
    A = -jnp.exp(a_log.astype(f32))
    log_a = dt * A
    grp = (bsz, s, SSD_GROUPS, SSD_HPG)
    xh = xs.astype(f32).reshape(bsz, s, SSD_GROUPS, SSD_HPG, SSD_HEADDIM)
    v_f = xh * dt[:, :, 0].reshape(grp)[..., None]
    v_b = xh * dt[:, :, 1].reshape(grp)[..., None]
    qc = cm.reshape(bsz, s, SSD_GROUPS, SSD_STATE)
    kb = bm.reshape(bsz, s, SSD_GROUPS, SSD_STATE)
    y = bidir_scan(qc, kb, v_f, v_b, log_a[:, :, 0].reshape(grp), log_a[:, :, 1].reshape(grp))
    y = y + xh * d_skip.astype(f32).reshape(SSD_GROUPS, SSD_HPG)[..., None]
    y = y.reshape(bsz, s, SSD_INNER) * jax.nn.silu(z.astype(f32))
    yg = y.reshape(bsz, s, SSD_GROUPS, SSD_INNER // SSD_GROUPS)
    yg = yg * lax.rsqrt(jnp.mean(yg * yg, axis=-1, keepdims=True) + EPS)
    y = yg.reshape(bsz, s, SSD_INNER) * norm_g.astype(f32)
    return y.astype(hn.dtype) @ w_out


def conv_geglu_ffn(hn, w_up, conv_w, conv_b, w_down):
    u = depthwise_conv_centered(hn @ w_up, conv_w, conv_b)
    gate, val = jnp.split(u, 2, axis=-1)
    return (jax.nn.gelu(gate, approximate=True) * val) @ w_down


def setup_inputs(seed: int = 0) -> dict:
    key = jax.random.key(seed)
    ks = jax.random.split(key, 24)
    f32 = jnp.float32

    def nrm(k, shape, scale):
        return jax.random.normal(k, shape, f32) * scale

    def gain(k, shape):
        return 1.0 + 0.05 * jax.random.normal(k, shape, f32)

    x = nrm(ks[0], (BATCH, SEQ, D_MODEL), 1.0)
    norm_mix_pre = gain(ks[1], (DEPTH, D_MODEL))
    norm_mix_post = gain(ks[2], (DEPTH, D_MODEL))
    norm_ffn_pre = gain(ks[3], (DEPTH, D_MODEL))
    norm_ffn_post = gain(ks[4], (DEPTH, D_MODEL))
    ab_w_in = nrm(ks[5], (N_EVEN, D_MODEL, AB_IN_DIM), D_MODEL ** -0.5)
    gamma0 = 1.0 - 2.0 ** (-5.0 - jnp.arange(RET_HEADS, dtype=f32))
    ab_ret_decay_logit = (jnp.log(gamma0) - jnp.log1p(-gamma0))[None, None, :] + nrm(ks[6], (N_EVEN, 2, RET_HEADS), 0.05)
    ab_ret_gn_g = gain(ks[7], (N_EVEN, RET_WIDTH))
    ab_na_rpb = nrm(ks[8], (N_EVEN, NA_HEADS, 2 * NA_WIN_R - 1, 2 * NA_WIN_C - 1), 0.1)
    ab_w_out = nrm(ks[9], (N_EVEN, RET_WIDTH + NA_WIDTH, D_MODEL), (RET_WIDTH + NA_WIDTH) ** -0.5)
    c_w_in = nrm(ks[10], (N_ODD, D_MODEL, SSD_IN_DIM), D_MODEL ** -0.5)
    c_conv_w = nrm(ks[11], (N_ODD, SSD_CONV, SSD_XBC), SSD_CONV ** -0.5)
    c_conv_b = nrm(ks[12], (N_ODD, SSD_XBC), 0.02)
    dt0 = jnp.exp(jax.random.uniform(ks[13], (N_ODD, 2, SSD_HEADS), f32, math.log(1e-3), math.log(1e-1)))
    c_dt_bias = dt0 + jnp.log(-jnp.expm1(-dt0))
    c_a_log = jnp.log(jax.random.uniform(ks[14], (N_ODD, 2, SSD_HEADS), f32, 1.0, 16.0))
    c_d_skip = 1.0 + 0.1 * jax.random.normal(ks[15], (N_ODD, SSD_HEADS), f32)
    c_norm_g = gain(ks[16], (N_ODD, SSD_INNER))
    c_w_out = nrm(ks[17], (N_ODD, SSD_INNER, D_MODEL), SSD_INNER ** -0.5)
    ffn_w_up = nrm(ks[18], (DEPTH, D_MODEL, 2 * FFN_DIM), D_MODEL ** -0.5)
    ffn_conv_w = nrm(ks[19], (DEPTH, FFN_CONV, 2 * FFN_DIM), FFN_CONV ** -0.5)
    ffn_conv_b = nrm(ks[20], (DEPTH, 2 * FFN_DIM), 0.02)
    ffn_w_down = nrm(ks[21], (DEPTH, FFN_DIM, D_MODEL), FFN_DIM ** -0.5)
    return {'x': x, 'norm_mix_pre': norm_mix_pre, 'norm_mix_post': norm_mix_post,
            'norm_ffn_pre': norm_ffn_pre, 'norm_ffn_post': norm_ffn_post,
            'ab_w_in': ab_w_in, 'ab_ret_decay_logit': ab_ret_decay_logit, 'ab_ret_gn_g': ab_ret_gn_g,
            'ab_na_rpb': ab_na_rpb, 'ab_w_out': ab_w_out,
            'c_w_in': c_w_in, 'c_conv_w': c_conv_w, 'c_conv_b': c_conv_b, 'c_dt_bias': c_dt_bias,
            'c_a_log': c_a_log, 'c_d_skip': c_d_skip, 'c_norm_g': c_norm_g, 'c_w_out': c_w_out,
            'ffn_w_up': ffn_w_up, 'ffn_conv_w': ffn_conv_w, 'ffn_conv_b': ffn_conv_b, 'ffn_w_down': ffn_w_down}


def reference(x, norm_mix_pre, norm_mix_post, norm_ffn_pre, norm_ffn_post,
              ab_w_in, ab_ret_decay_logit, ab_ret_gn_g, ab_na_rpb, ab_w_out,
              c_w_in, c_conv_w, c_conv_b, c_dt_bias, c_a_log, c_d_skip, c_norm_g, c_w_out,
              ffn_w_up, ffn_conv_w, ffn_conv_b, ffn_w_down):
    for layer in range(DEPTH):
        i = layer // 2
        hn = rms_norm(x, norm_mix_pre[layer])
        if layer % 2 == 0:
            m = retention_na_mixer(hn, ab_w_in[i], ab_ret_decay_logit[i], ab_ret_gn_g[i], ab_na_rpb[i], ab_w_out[i])
        else:
            m = ssd_mixer(hn, c_w_in[i], c_conv_w[i], c_conv_b[i], c_dt_bias[i], c_a_log[i],
                          c_d_skip[i], c_norm_g[i], c_w_out[i])
        x = x + rms_norm(m, norm_mix_post[layer])
        f = conv_geglu_ffn(rms_norm(x, norm_ffn_pre[layer]), ffn_w_up[layer], ffn_conv_w[layer],
                           ffn_conv_b[layer], ffn_w_down[layer])
        x = x + rms_norm(f, norm_ffn_post[layer])
    return x
```

```python
from contextlib import ExitStack
import numpy as np
import concourse.bass as bass
import concourse.mybir as mybir
from concourse.bass_utils import run_bass_kernel_spmd

F32 = mybir.dt.float32
BF16 = mybir.dt.bfloat16
ALU = mybir.AluOpType
AF = mybir.ActivationFunctionType
AX = mybir.AxisListType

EPOCH = 8192
SAME_ENGINE_SYNC = True
OWN_DIST = 10 ** 9


class Buf:
    __slots__ = ("name", "w", "r")

    def __init__(self, name):
        self.name = name
        self.w = None
        self.r = []


class Op:
    __slots__ = ("eng", "emit", "waits", "inc")


class KB:
    ENGS = ("sp", "act", "pool", "pe", "dve")

    def __init__(self, nc, n_dma_sems=16):
        self.nc = nc
        self.gstack = ExitStack()
        self.stack = self.gstack
        self.stream = {e: [] for e in self.ENGS}
        self.count = {e: 0 for e in self.ENGS}
        self.csems = {e: [] for e in self.ENGS}
        self.waited = {e: {} for e in self.ENGS}
        self.dma_pool = {}
        self.dma_next = {}
        self.n_dma_sems = n_dma_sems
        self.semobjs = {}
        self.nbuf = 0
        self.nalloc = 0

    def sem(self, name):
        s = self.gstack.enter_context(self.nc.semaphore(name))
        self.semobjs[name] = s
        return name

    def sb(self, name, shape, dt=F32):
        self.nalloc += 1
        return self.stack.enter_context(self.nc.sbuf_tensor(f"{name}_{self.nalloc}", list(shape), dt))

    def ps(self, name, shape, dt=F32):
        self.nalloc += 1
        return self.stack.enter_context(self.nc.psum_tensor(f"{name}_{self.nalloc}", list(shape), dt))

    def buf(self, name=None):
        self.nbuf += 1
        return Buf(name or f"b{self.nbuf}")

    def bufs(self, n):
        return [self.buf() for _ in range(n)]

    def _deps(self, reads, writes):
        ids = []
        for b in reads:
            if b.w is not None:
                ids.append(b.w)
        for b in writes:
            if b.w is not None:
                ids.append(b.w)
            ids.extend(b.r)
        return ids

    def _finish(self, eng, emit, reads, writes, cid3, ids):
        cid = cid3[:2]
        need = {}
        for t in ids:
            if t[1] > need.get(t[0], 0):
                need[t[0]] = t[1]
        waits = []
        wd = self.waited[eng]
        own = self.csems[eng]
        cur = self.count[eng] if cid3[2] == 1 else None
        for s, v in need.items():
            if s in own:
                if not SAME_ENGINE_SYNC:
                    continue
                if cur is not None and cur - (own.index(s) * EPOCH + v) >= OWN_DIST:
                    continue
            if wd.get(s, 0) >= v:
                continue
            wd[s] = v
            waits.append((s, v))
        op = Op()
        op.eng, op.emit, op.waits, op.inc = eng, emit, waits, cid3
        self.stream[eng].append(op)
        for b in reads:
            b.r.append(cid)
        for b in writes:
            b.w = cid
            b.r = []
        return cid

    def op(self, eng, emit, reads=(), writes=()):
        n = self.count[eng]
        self.count[eng] = n + 1
        k = n // EPOCH
        while len(self.csems[eng]) <= k:
            self.csems[eng].append(self.sem(f"c_{eng}_{len(self.csems[eng])}"))
        cid3 = (self.csems[eng][k], (n % EPOCH) + 1, 1)
        return self._finish(eng, emit, reads, writes, cid3, self._deps(reads, writes))

    def dma(self, eng, out, in_, reads=(), writes=(), **kw):
        if eng not in self.dma_pool:
            self.dma_pool[eng] = [[self.sem(f"d_{eng}_{i}"), 0] for i in range(self.n_dma_sems)]
            self.dma_next[eng] = 0
        i = self.dma_next[eng]
        self.dma_next[eng] = (i + 1) % self.n_dma_sems
        slot = self.dma_pool[eng][i]
        prev = slot[1]
        slot[1] = prev + 16
        cid3 = (slot[0], slot[1], 16)

        def emit(e, out=out, in_=in_, kw=kw):
            return e.dma_start(out=out, in_=in_, **kw)

        ids = self._deps(reads, writes)
        if prev > 0:
            ids = ids + [(slot[0], prev)]
        return self._finish(eng, emit, reads, writes, cid3, ids)

    def coll(self, emit, reads=(), writes=()):
        eng = "pool"
        if eng not in self.dma_pool:
            self.dma_pool[eng] = [[self.sem(f"d_{eng}_{i}"), 0] for i in range(self.n_dma_sems)]
            self.dma_next[eng] = 0
        i = self.dma_next[eng]
        self.dma_next[eng] = (i + 1) % self.n_dma_sems
        slot = self.dma_pool[eng][i]
        prev = slot[1]
        slot[1] = prev + 16
        cid3 = (slot[0], slot[1], 16)
        ids = self._deps(reads, writes)
        if prev > 0:
            ids = ids + [(slot[0], prev)]
        return self._finish(eng, emit, reads, writes, cid3, ids)

    def all_ids(self):
        ids = []
        for e in self.ENGS:
            n = self.count[e]
            if n > 0:
                ids.append((self.csems[e][(n - 1) // EPOCH], ((n - 1) % EPOCH) + 1))
        for e in self.dma_pool:
            for (s, v) in self.dma_pool[e]:
                if v > 0:
                    ids.append((s, v))
        return ids

    def barrier(self):
        ids = self.all_ids()
        for e in self.ENGS:
            waits = []
            wd = self.waited[e]
            for (s, v) in ids:
                if wd.get(s, 0) >= v:
                    continue
                wd[s] = v
                waits.append((s, v))
            op = Op()
            op.eng, op.emit, op.waits, op.inc = e, None, waits, None
            self.stream[e].append(op)

    def emit_all(self):
        nc = self.nc
        so = self.semobjs
        with nc.Block() as block:
            decos = {"sp": block.sync, "act": block.scalar, "pool": block.gpsimd,
                     "pe": block.tensor, "dve": block.vector}
            for name in self.ENGS:
                ops = self.stream[name]
                if not ops:
                    continue

                def f(eng, ops=ops):
                    for op in ops:
                        for (s, v) in op.waits:
                            eng.wait_ge(so[s], v)
                        if op.emit is None:
                            continue
                        ins = op.emit(eng)
                        if op.inc is not None:
                            ins.then_inc(so[op.inc[0]], op.inc[2])
                decos[name](f)
        self.stream = {e: [] for e in self.ENGS}

    def stage(self):
        return _Stage(self)


class _Stage:
    def __init__(self, kb):
        self.kb = kb

    def __enter__(self):
        self.st = ExitStack()
        self.kb.stack = self.st
        return self

    def __exit__(self, *a):
        self.kb.barrier()
        self.kb.emit_all()
        self.kb.stack = self.kb.gstack
        self.st.close()
        return False


D = 1024
FF = 2816
NCH = FF // 128
EPS = 1e-6
L_TOTAL = 4


class Prog:
    def __init__(self, T, layers, do_mix=True, do_ffn=True):
        self.T = T
        self.layers = layers
        nc = bass.Bass("TRN2", target_bir_lowering=False)
        self.nc = nc
        kb = KB(nc)
        self.kb = kb
        L = L_TOTAL

        def inp(name, shape):
            return nc.dram_tensor(name, list(shape), F32, kind="ExternalInput").ap()

        self.x = inp("x", [T, D])
        self.gains = inp("gains", [L * 4, 128, D])
        self.ffn_w_up = inp("ffn_w_up", [L, D, 2 * FF])
        self.ffn_w_down = inp("ffn_w_down", [L, FF, D])
        self.fcw = inp("fcw", [L, 128, 2 * NCH * 3])
        self.fcb = inp("fcb", [L, 128, 2 * NCH])
        self.c_w_in = inp("c_w_in", [2, D, 5184])
        self.c_w_out = inp("c_w_out", [2, 2048, D])
        self.scw = inp("scw", [2, 128, 24 * 5])
        self.scb = inp("scb", [2, 128, 24])
        self.sdtb = inp("sdtb", [2, 128, 64])
        self.salog = inp("salog", [2, 128, 64])
        self.sdsk = inp("sdsk", [2, 128, 32])
        self.sng = inp("sng", [2, 128, 2048])
        self.cmats = inp("cmats", [5, 128, 128])
        self.ab_w_in = inp("ab_w_in", [2, D, 3584])
        self.ab_w_out = inp("ab_w_out", [2, D, D])
        self.rlogit = inp("rlogit", [2, 128, 16])
        self.ridx = inp("ridx", [128, 4])
        self.rdj = inp("rdj", [2, 128, 128])
        self.rgn = inp("rgn", [2, 128, 512])
        self.rope = inp("rope", [T, 64])
        self.natab = inp("natab", [2, 8, 128, 2048])
        self.out = nc.dram_tensor("out", [T, D], F32, kind="ExternalOutput").ap()
        self.YN = nc.dram_tensor("YN", [T, 2048], BF16).ap()
        self.HB = nc.dram_tensor("HB", [T // 128, 128, 2048], BF16).ap()
        self.RHB = nc.dram_tensor("RHB", [T // 128, 64, 512], BF16).ap()
        self.XTK = nc.dram_tensor("XTK", [T, 2048], BF16).ap()
        self.BTK = nc.dram_tensor("BTK", [T, 512], BF16).ap()
        self.BTF = nc.dram_tensor("BTF", [4, 128, T], BF16).ap()
        self.CTF = nc.dram_tensor("CTF", [4, 128, T], BF16).ap()
        self.DTL = nc.dram_tensor("DTL", [T, 128], F32).ap()
        self.ZS = nc.dram_tensor("ZS", [T, 2048], BF16).ap()
        self.NQT = nc.dram_tensor("NQT", [4, 128, T], BF16).ap()
        self.NKT = nc.dram_tensor("NKT", [4, 128, T], BF16).ap()
        self.NV = nc.dram_tensor("NV", [T, 512], BF16).ap()
        self.Xa = nc.dram_tensor("Xa", [T, D], F32).ap()
        self.Xb = nc.dram_tensor("Xb", [T, D], F32).ap()

        self.ident = kb.sb("ident", [128, 128], BF16)
        self.bident = kb.buf()
        with kb.stage():
            one = kb.sb("one", [128, 128])
            idf = kb.sb("idf", [128, 128])
            b1, b2 = kb.buf(), kb.buf()
            kb.op("pool", lambda e: e.memset(one[:], 1.0), writes=[b1])
            kb.op("pool", lambda e: e.affine_select(idf[:], one[:], [[-1, 128]], ALU.is_equal, 0.0,
                                                    base=0, channel_multiplier=1), reads=[b1], writes=[b2])
            kb.op("pool", lambda e: e.tensor_copy(self.ident[:], idf[:]), reads=[b2], writes=[self.bident])

        cur = self.x
        for li, l in enumerate(layers):
            last = li == len(layers) - 1
            if do_mix:
                i = l // 2
                mdst = self.out if (last and not do_ffn) else self.Xa
                if l % 2 == 0:
                    import os
                    EVS = os.environ.get("EV_STAGES", "1234")
                    if "1" in EVS:
                        with kb.stage():
                            self.even_p1(i, l, cur)
                    if "2" in EVS:
                        with kb.stage():
                            self.na_stage(i)
                    if "3" in EVS:
                        with kb.stage():
                            self.even_p3(i, l, cur)
                    if "4" in EVS:
                        with kb.stage():
                            self.outproj_stage(l, self.ab_w_out[i], 1024, self.YN, cur, mdst)
                else:
                    import os
                    SDS = os.environ.get("SSD_STAGES", "123")
                    if "1" in SDS:
                        with kb.stage():
                            self.ssd_p1(i, l, cur)
                    if "2" in SDS:
                        with kb.stage():
                            self.ssd_p2a(i, l, cur)
                    if "3" in SDS:
                        with kb.stage():
                            self.outproj_stage(l, self.c_w_out[i], 2048, self.YN, cur, mdst)
                cur = mdst
            if do_ffn:
                dst = self.out if last else self.Xb
                with kb.stage():
                    self.ffn_stage(l, cur, dst)
                cur = dst
        kb.gstack.close()

    def rms_stats(self, src, bsrc, ss, bss, junk, bjunk, nparts=128, width=D):
        kb = self.kb
        kb.op("dve", lambda e: e.memset(ss[0:nparts, 0:1], 0.0), writes=[bss])
        kb.op("act", lambda e: e.activation(junk[0:nparts, 0:width], src, AF.Square,
                                            accum_out=ss[0:nparts, 0:1]),
              reads=[bsrc, bss], writes=[bjunk, bss])
        kb.op("act", lambda e: e.activation(ss[0:nparts, 0:1], ss[0:nparts, 0:1], AF.Sqrt,
                                            bias=EPS, scale=1.0 / width), reads=[bss], writes=[bss])
        kb.op("dve", lambda e: e.reciprocal(ss[0:nparts, 0:1], ss[0:nparts, 0:1]), reads=[bss], writes=[bss])

    def make_fe(self, BT, nh):
        kb = self.kb
        fe = {}
        fe["BT"], fe["nh"] = BT, nh
        fe["xts"] = [kb.sb("xt", [128, D]) for _ in range(2)]
        fe["bxt"] = kb.bufs(2)
        fe["hns"] = [kb.sb("hn", [128, D], BF16) for _ in range(2)]
        fe["bhn"] = kb.bufs(2)
        fe["junk"] = kb.sb("junk", [128, D])
        fe["bjunk"] = kb.buf()
        fe["sss"] = [kb.sb("ss", [128, 1]) for _ in range(2)]
        fe["bss"] = kb.bufs(2)
        nhh = max(nh, 1)
        fe["xh"] = kb.sb("xh", [2 * nhh, D]) if nh > 0 else None
        fe["bxh"] = kb.buf()
        fe["hnh"] = kb.sb("hnh", [2 * nhh, D], BF16) if nh > 0 else None
        fe["bhnh"] = kb.buf()
        fe["ssh"] = kb.sb("ssh", [2 * nhh, 1]) if nh > 0 else None
        fe["bssh"] = kb.buf()
        fe["pT"] = kb.ps("pT", [128, D], BF16)
        fe["bpT"] = kb.buf()
        if nh > 0:
            fe["pTh"] = kb.ps("pTh", [128, D], BF16)
            fe["bpTh"] = kb.buf()
        else:
            fe["pTh"], fe["bpTh"] = fe["pT"], fe["bpT"]
        fe["ti"] = 0
        return fe

    def run_fe(self, fe, Xin, t0, gpre, bconst, hnT, bh):
        kb = self.kb
        T = self.T
        BT, nh = fe["BT"], fe["nh"]
        ident, bident = self.ident, self.bident
        pT, bpT, pTh, bpTh = fe["pT"], fe["bpT"], fe["pTh"], fe["bpTh"]
        junk, bjunk = fe["junk"], fe["bjunk"]
        for j in range(BT // 128):
            i = fe["ti"] % 2
            fe["ti"] += 1
            xt, bx = fe["xts"][i], fe["bxt"][i]
            hn, bn = fe["hns"][i], fe["bhn"][i]
            ss, bs = fe["sss"][i], fe["bss"][i]
            r0 = t0 + j * 128
            kb.dma("sp", xt[:], Xin[r0:r0 + 128, :], writes=[bx])
            self.rms_stats(xt[:], bx, ss, bs, junk, bjunk)
            kb.op("dve", lambda e, hn=hn, xt=xt, ss=ss: e.scalar_tensor_tensor(
                hn[:], xt[:], ss[:, 0:1], gpre[:], ALU.mult, ALU.mult),
                reads=[bx, bs, bconst], writes=[bn])

            def tr(e, hn=hn):
                for k in range(8):
                    ins = e.transpose(pT[:, k * 128:(k + 1) * 128], hn[:, k * 128:(k + 1) * 128], ident[:])
                return ins
            kb.op("pe", tr, reads=[bn, bident], writes=[bpT])
            kb.op("act", lambda e, j=j: e.activation(
                hnT[:, :, j * 128:(j + 1) * 128], pT[:].rearrange("p (k t) -> p k t", k=8), AF.Copy),
                reads=[bpT], writes=[bh])
        if nh == 0:
            return
        xh, bxh, hnh, bhnh, ssh, bssh = fe["xh"], fe["bxh"], fe["hnh"], fe["bhnh"], fe["ssh"], fe["bssh"]
        kb.op("pool", lambda e: e.memset(xh[:], 0.0), writes=[bxh])
        if t0 - nh >= 0:
            kb.dma("sp", xh[0:nh, :], Xin[t0 - nh:t0, :], writes=[bxh])
        if t0 + BT + nh <= T:
            kb.dma("sp", xh[nh:2 * nh, :], Xin[t0 + BT:t0 + BT + nh, :], writes=[bxh])
        self.rms_stats(xh[:], bxh, ssh, bssh, junk, bjunk, nparts=2 * nh)
        kb.op("dve", lambda e: e.scalar_tensor_tensor(
            hnh[:], xh[:], ssh[:, 0:1], gpre[0:2 * nh, :], ALU.mult, ALU.mult),
            reads=[bxh, bssh, bconst], writes=[bhnh])
        w = 2 * nh

        def trh(e):
            for k in range(8):
                ins = e.transpose(pTh[:, k * w:(k + 1) * w], hnh[:, k * 128:(k + 1) * 128], ident[0:w, 0:w])
            return ins
        kb.op("pe", trh, reads=[bhnh, bident], writes=[bpTh])
        kb.op("act", lambda e: e.activation(
            hnT[:, :, BT:BT + w], pTh[:, 0:8 * w].rearrange("p (k t) -> p k t", k=8), AF.Copy),
            reads=[bpTh], writes=[bh])

    def ssd_consts(self, i, l):
        kb = self.kb
        c = {}
        c["b"] = kb.buf()
        b = c["b"]
        c["gpre"] = kb.sb("gpre", [128, D])
        kb.dma("sp", c["gpre"][:], self.gains[l * 4 + 0], writes=[b])
        c["scw"] = kb.sb("scw", [128, 24, 5])
        c["scb"] = kb.sb("scb", [128, 24])
        kb.dma("sp", c["scw"][:], self.scw[i].rearrange("p (c w) -> p c w", w=5), writes=[b])
        kb.dma("sp", c["scb"][:], self.scb[i], writes=[b])
        c["dtb"] = kb.sb("dtb", [128, 64])
        kb.dma("sp", c["dtb"][:], self.sdtb[i], writes=[b])
        c["A"] = kb.sb("A", [128, 64])
        kb.dma("sp", c["A"][:], self.salog[i], writes=[b])
        kb.op("act", lambda e: e.activation(c["A"][:], c["A"][:], AF.Exp), reads=[b], writes=[b])
        kb.op("dve", lambda e: e.tensor_scalar(c["A"][:], c["A"][:], -1.0, None, ALU.mult), reads=[b], writes=[b])
        c["cm"] = kb.sb("cm", [128, 5, 128])
        kb.dma("sp", c["cm"][:], self.cmats.rearrange("m p j -> p m j"), writes=[b])
        return c

    def ssd_load_win(self, i, cols_list):
        kb = self.kb
        W = kb.sb("swin", [128, 8, 5184], BF16)
        bw = kb.buf()
        for k in range(8):
            for (a, bnd) in cols_list:
                kb.dma("pool", W[:, k, a:bnd], self.c_w_in[i, k * 128:(k + 1) * 128, a:bnd], writes=[bw])
        return W, bw

    def ssd_block_front(self, fe, cst, W, bw, Xin, t0, hnT, bh, chunks, ue_s, acc_s, ps, outs):
        kb = self.kb
        BT = fe["BT"]
        NT = BT // 128
        scw, scb, bc = cst["scw"], cst["scb"], cst["b"]
        self.run_fe(fe, Xin, t0, cst["gpre"], bc, hnT, bh)
        for n0 in range(0, len(chunks), 2):
            ctx = []
            for n in range(n0, min(n0 + 2, len(chunks))):
                cc = chunks[n]
                par = n % 2
                pA, bpA = ps["pA"][par], ps["bpA"][par]
                pH, bpH = ps["pHh"][par], ps["bpHh"][par]
                ue, bue = ue_s[par]
                acc, bacc = acc_s[par]
                col = 2048 + cc * 128

                def mm(e, col=col, pA=pA):
                    for k in range(8):
                        ins = e.matmul(pA[:, 0:BT], W[:, k, col:col + 128], hnT[:, k, 0:BT], start=(k == 0), stop=(k == 7))
                    return ins
                kb.op("pe", mm, reads=[bw, bh], writes=[bpA])

                def mmh(e, col=col, pH=pH):
                    for k in range(8):
                        ins = e.matmul(pH[:, 0:4], W[:, k, col:col + 128], hnT[:, k, BT:BT + 4], start=(k == 0), stop=(k == 7))
                    return ins
                kb.op("pe", mmh, reads=[bw, bh], writes=[bpH])
                kb.op("act", lambda e, ue=ue, pA=pA: e.activation(ue[:, 2:2 + BT], pA[:, 0:BT], AF.Copy),
                      reads=[bpA], writes=[bue])
                kb.op("act", lambda e, ue=ue, pH=pH: e.activation(ue[:, 0:2], pH[:, 0:2], AF.Copy), reads=[bpH], writes=[bue])
                kb.op("act", lambda e, ue=ue, pH=pH: e.activation(ue[:, BT + 2:BT + 4], pH[:, 2:4], AF.Copy),
                      reads=[bpH], writes=[bue])
                ctx.append((cc, ue, bue, acc, bacc))
            for j in range(5):
                for (cc, ue, bue, acc, bacc) in ctx:
                    if j == 0:
                        kb.op("dve", lambda e, ue=ue, acc=acc, cc=cc: e.tensor_scalar(
                            acc[:], ue[:, 0:BT], scw[:, cc, 0:1], scb[:, cc:cc + 1], ALU.mult, ALU.add),
                            reads=[bue, bc], writes=[bacc])
                    else:
                        kb.op("dve", lambda e, ue=ue, acc=acc, cc=cc, j=j: e.scalar_tensor_tensor(
                            acc[:], ue[:, j:j + BT], scw[:, cc, j:j + 1], acc[:], ALU.mult, ALU.add),
                            reads=[bue, bc, bacc], writes=[bacc])
            for (cc, ue, bue, acc, bacc) in ctx:
                outs(cc, acc, bacc)

    def ssd_small(self, cst, la, bla, dirs, ps, sm, bsm):
        kb = self.kb
        cm, bc = cst["cm"], cst["b"]
        pH, bpH = ps["pH"], ps["bpH"]

        def mm(e):
            e.matmul(pH[:, 64:96], cm[:, 0, :], la[:, 0:32], start=True, stop=True)
            e.matmul(pH[:, 96:128], cm[:, 3, :], la[:, 32:64], start=True, stop=True)
            e.matmul(pH[:, 128:160], cm[:, 1, :], la[:, 32:64], start=True, stop=True)
            return e.matmul(pH[:, 160:224], cm[:, 4, :], la[:, 0:64], start=True, stop=True)
        kb.op("pe", mm, reads=[bla, bc], writes=[bpH])
        kb.op("act", lambda e: e.activation(sm[:, 0:160], pH[:, 64:224], AF.Copy), reads=[bpH], writes=[bsm])

    def ssd_dt(self, cst, W, bw, hnT, bh, sl, ps, tl):
        kb = self.kb
        pH, bpH = ps["pH"], ps["bpH"]
        bc = cst["b"]

        def mm(e):
            for k in range(8):
                ins = e.matmul(pH[:, 0:64], hnT[:, k, sl], W[:, k, 5120:5184], start=(k == 0), stop=(k == 7))
            return ins
        kb.op("pe", mm, reads=[bw, bh], writes=[bpH])
        dtr, dt, la, bdt = tl["dtr"], tl["dt"], tl["la"], tl["bdt"]
        kb.op("act", lambda e: e.activation(dtr[:], pH[:, 0:64], AF.Copy), reads=[bpH], writes=[bdt])
        kb.op("dve", lambda e: e.tensor_tensor(dtr[:], dtr[:], cst["dtb"][:], ALU.add), reads=[bdt, bc], writes=[bdt])
        kb.op("act", lambda e: e.activation(dtr[:], dtr[:], AF.Exp), reads=[bdt], writes=[bdt])
        kb.op("act", lambda e: e.activation(dt[:], dtr[:], AF.Ln, bias=1.0, scale=1.0), reads=[bdt], writes=[bdt])
        kb.op("dve", lambda e: e.tensor_tensor(la[:], dt[:], cst["A"][:], ALU.mult), reads=[bdt, bc], writes=[bdt])

    def ssd_alloc_common(self, BT, p2):
        kb = self.kb
        NT = BT // 128
        a = {}
        a["hnTs"] = [kb.sb("hnT", [128, 8, BT + 4], BF16) for _ in range(2)]
        a["bhnT"] = kb.bufs(2)
        a["ue_s"] = [(kb.sb("ue", [128, BT + 4]), kb.buf()) for _ in range(2)]
        a["acc_s"] = [(kb.sb("acc", [128, BT]), kb.buf()) for _ in range(2)]
        a["xsT"] = [(kb.sb("xsT", [128, BT], BF16), kb.buf()) for _ in range(2)]
        a["x_tok"] = [kb.sb("xtok", [128, NT, 2048], BF16)] * 2
        a["bx_tok"] = [kb.buf()] * 2
        a["B_tok"] = [kb.sb("btok", [128, NT, 512], BF16)] * 2
        a["bB_tok"] = [kb.buf()] * 2
        tl = {}
        for n in ("dtr", "dt", "la"):
            tl[n] = kb.sb(n, [128, 64])
        tl["bdt"] = kb.buf()
        tl["sm"] = kb.sb("sm", [128, 160])
        tl["bsm"] = kb.buf()
        a["tl"] = tl
        ps = {}
        ps["pA"] = [kb.ps("pA", [128, 512]) for _ in range(2)]
        ps["bpA"] = kb.bufs(2)
        ps["pH"] = kb.ps("pH", [128, 512])
        ps["bpH"] = kb.buf()
        ps["pHh"] = [kb.ps("pHh", [128, 512]) for _ in range(2)]
        ps["bpHh"] = kb.bufs(2)
        a["ps"] = ps
        return a

    def ssd_p1(self, i, l, Xin, BT=512):
        kb = self.kb
        T = self.T
        NT = BT // 128
        cst = self.ssd_consts(i, l)
        W, bw = self.ssd_load_win(i, [(0, 5184)])
        fe = self.make_fe(BT, 2)
        a = self.ssd_alloc_common(BT, False)
        ps, tl = a["ps"], a["tl"]
        pT, bpT = fe["pT"], fe["bpT"]
        Hb = kb.sb("Hb", [128, 2048])
        Hbb = kb.sb("Hbb", [128, 2048], BF16)
        bHb, bHbb = kb.bufs(4), kb.bufs(4)
        kb.op("pool", lambda e: e.memset(Hb[:], 0.0), writes=bHb)
        kb.op("pool", lambda e: e.memset(Hbb[:], 0.0), writes=bHbb)
        wgt = kb.sb("wgt", [128, 32])
        dtot = kb.sb("dtot", [128, 32])
        bwg = kb.buf()
        xws = [(kb.sb("xw", [128, 512], BF16), kb.buf()) for _ in range(2)]
        Ss = [(kb.sb("Ssb", [128, 512]), kb.buf()) for _ in range(2)]
        BTf = kb.sb("BTf", [128, 4, BT], BF16)
        CTf = kb.sb("CTf", [128, 4, BT], BF16)
        bBf, bCf = kb.buf(), kb.buf()
        zss = [(kb.sb("zs", [128, 2048], BF16), kb.buf()) for _ in range(2)]
        dtl = kb.sb("dtl", [128, 128])
        bdtl = kb.buf()
        gi = 0
        zi = 0
        for b in reversed(range(T // BT)):
            t0 = b * BT
            hnT, bh = a["hnTs"][b % 2], a["bhnT"][b % 2]
            x_tok, bxk = a["x_tok"][b % 2], a["bx_tok"][b % 2]
            B_tok, bBk = a["B_tok"][b % 2], a["bB_tok"][b % 2]

            def outs(cc, acc, bacc):
                if cc >= 20:
                    dstC = CTf[:, cc - 20, :]
                    kb.op("act", lambda e: e.activation(dstC, acc[:], AF.Silu), reads=[bacc], writes=[bCf])
                    return
                if cc >= 16:
                    xsT, bxs = BTf[:, cc - 16, :], bBf
                else:
                    xsT, bxs = a["xsT"][cc % 2]
                    xsT = xsT[:]
                kb.op("act", lambda e: e.activation(xsT, acc[:], AF.Silu), reads=[bacc], writes=[bxs])

                def tr(e):
                    for ct in range(NT):
                        ins = e.transpose(pT[:, ct * 128:(ct + 1) * 128], xsT[:, ct * 128:(ct + 1) * 128], self.ident[:])
                    return ins
                kb.op("pe", tr, reads=[bxs, self.bident], writes=[bpT])
                if cc < 16:
                    dst, bd = x_tok[:, :, cc * 128:(cc + 1) * 128], bxk
                else:
                    dst, bd = B_tok[:, :, (cc - 16) * 128:(cc - 15) * 128], bBk
                kb.op("act", lambda e: e.activation(dst, pT[:, 0:NT * 128].rearrange("p (c t) -> p c t", c=NT), AF.Copy),
                      reads=[bpT], writes=[bd])
            self.ssd_block_front(fe, cst, W, bw, Xin, t0, hnT, bh, list(range(24)), a["ue_s"], a["acc_s"], ps, outs)
            for ct in range(NT):
                r0 = t0 + ct * 128
                kb.dma("sp", self.XTK[r0:r0 + 128, :], x_tok[:, ct, :], reads=[bxk])
                kb.dma("act", self.BTK[r0:r0 + 128, :], B_tok[:, ct, :], reads=[bBk])
            kb.dma("sp", self.BTF[:, :, t0:t0 + BT].rearrange("g p t -> p g t"), BTf[:], reads=[bBf])
            kb.dma("act", self.CTF[:, :, t0:t0 + BT].rearrange("g p t -> p g t"), CTf[:], reads=[bCf])
            for ct in reversed(range(NT)):
                c = t0 // 128 + ct
                sl = slice(ct * 128, (ct + 1) * 128)
                self.ssd_dt(cst, W, bw, hnT, bh, sl, ps, tl)
                self.ssd_small(cst, tl["la"], tl["bdt"], None, ps, tl["sm"], tl["bsm"])
                sm, bsm, dt, bdt = tl["sm"], tl["bsm"], tl["dt"], tl["bdt"]
                kb.op("pool", lambda e: e.tensor_copy(dtl[:, 0:64], tl["dt"][:]), reads=[bdt], writes=[bdtl])
                kb.op("pool", lambda e: e.tensor_copy(dtl[:, 64:128], tl["la"][:]), reads=[bdt], writes=[bdtl])
                kb.dma("act", self.DTL[c * 128:(c + 1) * 128, :], dtl[:], reads=[bdtl])
                zs_, bz = zss[zi % 2]
                zi += 1
                for g in range(4):
                    pZ, bpZ = ps["pA"][gi % 2], ps["bpA"][gi % 2]
                    gi += 1

                    def mz(e, pZ=pZ, g=g, hnT=hnT, sl=sl):
                        for k in range(8):
                            ins = e.matmul(pZ[:, 0:512], hnT[:, k, sl], W[:, k, g * 512:(g + 1) * 512], start=(k == 0), stop=(k == 7))
                        return ins
                    kb.op("pe", mz, reads=[bw, bh], writes=[bpZ])
                    kb.op("act", lambda e, zs_=zs_, pZ=pZ, g=g: e.activation(zs_[:, g * 512:(g + 1) * 512], pZ[:, 0:512], AF.Silu),
                          reads=[bpZ], writes=[bz])
                kb.dma("sp", self.ZS[c * 128:(c + 1) * 128, :], zs_[:], reads=[bz])
                kb.op("act", lambda e: e.activation(wgt[:], sm[:, 64:96], AF.Exp), reads=[bsm], writes=[bwg])
                kb.op("dve", lambda e: e.tensor_tensor(wgt[:], wgt[:], dt[:, 32:64], ALU.mult), reads=[bwg, bdt], writes=[bwg])
                kb.op("act", lambda e: e.activation(dtot[:], sm[:, 128:160], AF.Exp), reads=[bsm], writes=[bwg])
                kb.dma("sp", self.HB[c], Hbb[:], reads=bHbb)
                for g in range(4):
                    xw, bxw = xws[gi % 2]
                    Ssb, bS = Ss[gi % 2]
                    pS, bpS = ps["pA"][gi % 2], ps["bpA"][gi % 2]
                    gi += 1
                    gs = slice(g * 512, (g + 1) * 512)
                    kb.op("pool", lambda e, xw=xw, gs=gs, g=g, ct=ct, x_tok=x_tok: e.tensor_tensor(
                        xw[:].rearrange("p (h d) -> p h d", h=8), x_tok[:, ct, gs].rearrange("p (h d) -> p h d", h=8),
                        wgt[:, g * 8:(g + 1) * 8].unsqueeze(2).to_broadcast([128, 8, 64]), ALU.mult),
                        reads=[bxk, bwg], writes=[bxw])
                    kb.op("pe", lambda e, pS=pS, xw=xw, g=g, ct=ct, B_tok=B_tok: e.matmul(
                        pS[:, 0:512], B_tok[:, ct, g * 128:(g + 1) * 128], xw[:], start=True, stop=True),
                        reads=[bBk, bxw], writes=[bpS])
                    kb.op("act", lambda e, Ssb=Ssb, pS=pS: e.activation(Ssb[:], pS[:, 0:512], AF.Copy),
                          reads=[bpS], writes=[bS])
                    kb.op("dve", lambda e, gs=gs, g=g: e.tensor_tensor(
                        Hb[:, gs].rearrange("p (h d) -> p h d", h=8), Hb[:, gs].rearrange("p (h d) -> p h d", h=8),
                        dtot[:, g * 8:(g + 1) * 8].unsqueeze(2).to_broadcast([128, 8, 64]), ALU.mult),
                        reads=[bHb[g], bwg], writes=[bHb[g]])
                    kb.op("pool", lambda e, gs=gs, Ssb=Ssb: e.tensor_tensor(Hb[:, gs], Hb[:, gs], Ssb[:], ALU.add),
                          reads=[bHb[g], bS], writes=[bHb[g]])
                    kb.op("pool", lambda e, gs=gs: e.tensor_copy(Hbb[:, gs], Hb[:, gs]), reads=[bHb[g]], writes=[bHbb[g]])

    def ssd_p2a(self, i, l, Xin, BT=256):
        kb = self.kb
        T = self.T
        NT = BT // 128
        cst = self.ssd_consts(i, l)
        bc = cst["b"]
        cm = cst["cm"]
        ps = {}
        ps["pA"] = [kb.ps("pA", [128, 512]) for _ in range(3)]
        ps["bpA"] = kb.bufs(3)
        ps["pH"] = kb.ps("pH", [128, 512])
        ps["bpH"] = kb.buf()
        P3 = kb.ps("P3", [128, 1536])
        bP3 = kb.bufs(3)
        dsk = kb.sb("dsk", [128, 32])
        ng = kb.sb("ng", [128, 2048])
        kb.dma("sp", dsk[:], self.sdsk[i], writes=[bc])
        kb.dma("sp", ng[:], self.sng[i], writes=[bc])
        xks = [(kb.sb("xk", [128, 2048], BF16), kb.buf()) for _ in range(2)]
        Bks = [(kb.sb("Bk", [128, 512], BF16), kb.buf()) for _ in range(2)]
        BTfs = [(kb.sb("BTf", [128, 4, 128], BF16), kb.buf()) for _ in range(2)]
        CTfs = [(kb.sb("CTf", [128, 4, 128], BF16), kb.buf()) for _ in range(2)]
        dtls = [(kb.sb("dtl", [128, 128]), kb.buf()) for _ in range(2)]
        zsl = [(kb.sb("zsl", [128, 2048], BF16), kb.buf()) for _ in range(2)]
        Hbbs = [(kb.sb("Hbb", [128, 2048], BF16), kb.buf()) for _ in range(2)]
        sms = [(kb.sb("sm", [128, 160]), kb.buf()) for _ in range(2)]
        Hf = kb.sb("Hf", [128, 2048])
        Hfb = kb.sb("Hfb", [128, 2048], BF16)
        bHf, bHfb = kb.bufs(4), kb.bufs(4)
        kb.op("pool", lambda e: e.memset(Hf[:], 0.0), writes=bHf)
        kb.op("pool", lambda e: e.memset(Hfb[:], 0.0), writes=bHfb)
        ecs = [(kb.sb("ec", [128, 64]), kb.sb("wgt", [128, 32]), kb.sb("dtot", [128, 32]), kb.sb("dd", [128, 32]), kb.buf())
               for _ in range(2)]
        qks = [(kb.sb("qk", [128, 4, 128]), kb.buf()) for _ in range(2)]
        AUf = [kb.sb("AUf", [128, 4, 128]) for _ in range(2)]
        AUb = [kb.sb("AUb", [128, 4, 128]) for _ in range(2)]
        bAU = kb.bufs(2)
        E = [kb.sb("E", [128, 4, 128]) for _ in range(2)]
        bE = kb.bufs(2)
        T1 = [kb.sb("T1", [128, 4, 128]) for _ in range(2)]
        bT1 = kb.bufs(2)
        Wb = [kb.sb("Wb", [128, 4, 128], BF16) for _ in range(2)]
        bWb = kb.bufs(2)
        Ys = [(kb.sb("Y", [128, 2048]), kb.bufs(4)) for _ in range(2)]
        Tt = [(kb.sb("Tt", [128, 512]), kb.buf()) for _ in range(3)]
        xds = [(kb.sb("xd", [128, 2048], BF16), kb.buf()) for _ in range(2)]
        xws = [(kb.sb("xw", [128, 512], BF16), kb.buf()) for _ in range(2)]
        Ss = [(kb.sb("Ssb", [128, 512]), kb.buf()) for _ in range(2)]
        jz = [(kb.sb("jz", [128, 512]), kb.buf()) for _ in range(2)]
        ssqs = [(kb.sb("ssq", [128, 4]), kb.buf()) for _ in range(2)]
        cnt = {"qi": 0, "gi": 0, "pa": 0}

        def nextpA():
            k = cnt["pa"] % 3
            cnt["pa"] += 1
            return ps["pA"][k], ps["bpA"][k]

        def do_chunk(c):
            if True:
                r0 = c * 128
                cp = c % 2
                x_tok, bxk = xks[cp]
                B_tok, bBk = Bks[cp]
                BTfb, bBf = BTfs[cp]
                CTfb, bCf = CTfs[cp]
                dtl, bdt = dtls[cp]
                zsb, bzs = zsl[cp]
                Hbb, bHbb = Hbbs[cp]
                sm, bsm = sms[cp]
                ec, wgt, dtot, dd, bsm2 = ecs[cp]
                qk, bqk = qks[cp]
                Y, bY = Ys[cp]
                xd, bxd = xds[cp]
                ssq, bssq = ssqs[cp]
                dt, la = dtl[:, 0:64], dtl[:, 64:128]
                kb.dma("sp", x_tok[:], self.XTK[r0:r0 + 128, :], writes=[bxk])
                kb.dma("act", B_tok[:], self.BTK[r0:r0 + 128, :], writes=[bBk])
                kb.dma("sp", BTfb[:], self.BTF[:, :, r0:r0 + 128].rearrange("g p t -> p g t"), writes=[bBf])
                kb.dma("act", CTfb[:], self.CTF[:, :, r0:r0 + 128].rearrange("g p t -> p g t"), writes=[bCf])
                kb.dma("sp", dtl[:], self.DTL[r0:r0 + 128, :], writes=[bdt])
                kb.dma("act", zsb[:], self.ZS[r0:r0 + 128, :], writes=[bzs])
                kb.dma("sp", Hbb[:], self.HB[c], writes=[bHbb])
                self.ssd_small(cst, la, bdt, None, ps, sm, bsm)
                kb.op("act", lambda e, ec=ec, sm=sm: e.activation(ec[:], sm[:, 0:64], AF.Exp), reads=[bsm], writes=[bsm2])
                kb.op("dve", lambda e, wgt=wgt, sm=sm: e.tensor_tensor(wgt[:], sm[:, 96:128], sm[:, 0:32], ALU.subtract), reads=[bsm], writes=[bsm2])
                kb.op("act", lambda e, wgt=wgt: e.activation(wgt[:], wgt[:], AF.Exp), reads=[bsm2], writes=[bsm2])
                kb.op("dve", lambda e, wgt=wgt, dt=dt: e.tensor_tensor(wgt[:], wgt[:], dt[:, 0:32], ALU.mult), reads=[bsm2, bdt], writes=[bsm2])
                kb.op("act", lambda e, dtot=dtot, sm=sm: e.activation(dtot[:], sm[:, 96:128], AF.Exp), reads=[bsm], writes=[bsm2])
                kb.op("dve", lambda e, dd=dd, dt=dt: e.tensor_tensor(dd[:], dt[:, 0:32], dt[:, 32:64], ALU.subtract), reads=[bdt], writes=[bsm2])
                pQ, bpQ = nextpA()

                def mq(e, pQ=pQ, BTfb=BTfb, CTfb=CTfb):
                    for g in range(4):
                        ins = e.matmul(pQ[:, g * 128:(g + 1) * 128], BTfb[:, g, :], CTfb[:, g, :], start=True, stop=True)
                    return ins
                kb.op("pe", mq, reads=[bBf, bCf], writes=[bpQ])
                kb.op("act", lambda e, pQ=pQ: e.activation(qk[:].rearrange("p g j -> p (g j)"), pQ[:, 0:512], AF.Copy),
                      reads=[bpQ], writes=[bqk])
                kb.op("pool", lambda e: e.tensor_tensor(
                    xd[:].rearrange("p (h d) -> p h d", h=32), x_tok[:, :].rearrange("p (h d) -> p h d", h=32),
                    dsk[:].unsqueeze(2).to_broadcast([128, 32, 64]), ALU.mult), reads=[bxk, bc], writes=[bxd])
                for g in range(4):
                    gs = slice(g * 512, (g + 1) * 512)
                    Pi, Pf, Pb = P3[:, 0:512], P3[:, 512:1024], P3[:, 1024:1536]
                    kb.op("pe", lambda e, gs=gs: e.matmul(Pi, self.ident[:], xd[:, gs], start=True, stop=False),
                          reads=[self.bident, bxd], writes=[bP3[0]])
                    halves = []
                    for half in range(2):
                        h0 = (g * 2 + half) * 4
                        par = cnt["qi"] % 2
                        cnt["qi"] += 1
                        pS, bpS = nextpA()
                        halves.append((half, h0, par, pS, bpS))
                    for (half, h0, par, pS, bpS) in halves:
                        kb.op("dve", lambda e, par=par, h0=h0: e.tensor_tensor(
                            AUf[par][:], la[:, h0:h0 + 4].unsqueeze(2).to_broadcast([128, 4, 128]),
                            cm[:, 0, :].unsqueeze(1).to_broadcast([128, 4, 128]), ALU.mult),
                            reads=[bdt, bc], writes=[bAU[par]])
                    for (half, h0, par, pS, bpS) in halves:
                        kb.op("pool", lambda e, par=par, h0=h0: e.tensor_tensor(
                            AUb[par][:], la[:, 32 + h0:32 + h0 + 4].unsqueeze(2).to_broadcast([128, 4, 128]),
                            cm[:, 3, :].unsqueeze(1).to_broadcast([128, 4, 128]), ALU.mult),
                            reads=[bdt, bc], writes=[bAU[par]])
                    for (half, h0, par, pS, bpS) in halves:
                        def ms(e, par=par, pS=pS):
                            e.matmul(pS[:, 0:512], cm[:, 2, :], AUf[par][:].rearrange("p h j -> p (h j)"), start=True, stop=False)
                            return e.matmul(pS[:, 0:512], cm[:, 1, :], AUb[par][:].rearrange("p h j -> p (h j)"),
                                            start=False, stop=True)
                        kb.op("pe", ms, reads=[bc, bAU[par]], writes=[bpS])
                    for (half, h0, par, pS, bpS) in halves:
                        kb.op("act", lambda e, par=par, pS=pS: e.activation(
                            E[par][:].rearrange("p h j -> p (h j)"), pS[:, 0:512], AF.Exp), reads=[bpS], writes=[bE[par]])
                    for (half, h0, par, pS, bpS) in halves:
                        kb.op("pool", lambda e, par=par, h0=h0: e.tensor_tensor(
                            T1[par][:], cm[:, 0, :].unsqueeze(1).to_broadcast([128, 4, 128]),
                            dd[:, h0:h0 + 4].unsqueeze(2).to_broadcast([128, 4, 128]), ALU.mult),
                            reads=[bc, bsm2], writes=[bT1[par]])
                    for (half, h0, par, pS, bpS) in halves:
                        kb.op("pool", lambda e, par=par, h0=h0: e.tensor_tensor(
                            T1[par][:], T1[par][:], dt[:, 32 + h0:32 + h0 + 4].unsqueeze(2).to_broadcast([128, 4, 128]),
                            ALU.add), reads=[bT1[par], bdt], writes=[bT1[par]])
                    for (half, h0, par, pS, bpS) in halves:
                        kb.op("dve", lambda e, par=par, g=g: e.tensor_tensor(
                            E[par][:], E[par][:], qk[:, g, :].unsqueeze(1).to_broadcast([128, 4, 128]), ALU.mult),
                            reads=[bE[par], bqk], writes=[bE[par]])
                    for (half, h0, par, pS, bpS) in halves:
                        kb.op("dve", lambda e, par=par: e.tensor_tensor(Wb[par][:], E[par][:], T1[par][:], ALU.mult),
                              reads=[bE[par], bT1[par]], writes=[bWb[par]])
                    for (half, h0, par, pS, bpS) in halves:
                        def mi(e, par=par, h0=h0, half=half):
                            for hh in range(4):
                                h = h0 + hh
                                cs_ = (half * 4 + hh) * 64
                                ins = e.matmul(Pi[:, cs_:cs_ + 64], Wb[par][:, hh, :], x_tok[:, h * 64:(h + 1) * 64],
                                               start=False, stop=(half == 1 and hh == 3))
                            return ins
                        kb.op("pe", mi, reads=[bWb[par], bxk], writes=[bP3[0]])
                    kb.op("pe", lambda e, g=g, gs=gs: e.matmul(
                        Pf, CTfb[:, g, :], Hfb[:, gs], start=True, stop=True), reads=[bCf, bHfb[g]], writes=[bP3[1]])
                    kb.op("pe", lambda e, g=g, gs=gs: e.matmul(
                        Pb, CTfb[:, g, :], Hbb[:, gs], start=True, stop=True), reads=[bCf, bHbb], writes=[bP3[2]])
                    kb.op("act", lambda e, gs=gs: e.activation(Y[:, gs], Pi, AF.Copy), reads=[bP3[0]], writes=[bY[g]])
                    for (Px, bPx, eoff) in ((Pf, bP3[1], 0), (Pb, bP3[2], 32)):
                        Tt_, bTt = Tt[cnt["gi"] % 3]
                        cnt["gi"] += 1
                        kb.op("act", lambda e, Tt_=Tt_, Px=Px: e.activation(Tt_[:], Px, AF.Copy), reads=[bPx], writes=[bTt])
                        kb.op("dve", lambda e, Tt_=Tt_, eoff=eoff, g=g: e.tensor_tensor(
                            Tt_[:].rearrange("p (h d) -> p h d", h=8), Tt_[:].rearrange("p (h d) -> p h d", h=8),
                            ec[:, eoff + g * 8:eoff + (g + 1) * 8].unsqueeze(2).to_broadcast([128, 8, 64]), ALU.mult),
                            reads=[bTt, bsm2], writes=[bTt])
                        kb.op("pool", lambda e, Tt_=Tt_, gs=gs: e.tensor_tensor(Y[:, gs], Y[:, gs], Tt_[:], ALU.add),
                              reads=[bTt, bY[g]], writes=[bY[g]])
                    xw, bxw = xws[g % 2]
                    Ssb, bS = Ss[g % 2]
                    pS, bpS = nextpA()
                    kb.op("pool", lambda e, xw=xw, gs=gs, g=g: e.tensor_tensor(
                        xw[:].rearrange("p (h d) -> p h d", h=8), x_tok[:, gs].rearrange("p (h d) -> p h d", h=8),
                        wgt[:, g * 8:(g + 1) * 8].unsqueeze(2).to_broadcast([128, 8, 64]), ALU.mult),
                        reads=[bxk, bsm2], writes=[bxw])
                    kb.op("pe", lambda e, pS=pS, xw=xw, g=g: e.matmul(
                        pS[:, 0:512], B_tok[:, g * 128:(g + 1) * 128], xw[:], start=True, stop=True),
                        reads=[bBk, bxw], writes=[bpS])
                    kb.op("act", lambda e, Ssb=Ssb, pS=pS: e.activation(Ssb[:], pS[:, 0:512], AF.Copy),
                          reads=[bpS], writes=[bS])
                    kb.op("dve", lambda e, gs=gs, g=g: e.tensor_tensor(
                        Hf[:, gs].rearrange("p (h d) -> p h d", h=8), Hf[:, gs].rearrange("p (h d) -> p h d", h=8),
                        dtot[:, g * 8:(g + 1) * 8].unsqueeze(2).to_broadcast([128, 8, 64]), ALU.mult),
                        reads=[bHf[g], bsm2], writes=[bHf[g]])
                    kb.op("pool", lambda e, gs=gs, Ssb=Ssb: e.tensor_tensor(Hf[:, gs], Hf[:, gs], Ssb[:], ALU.add),
                          reads=[bHf[g], bS], writes=[bHf[g]])
                    kb.op("pool", lambda e, gs=gs: e.tensor_copy(Hfb[:, gs], Hf[:, gs]), reads=[bHf[g]], writes=[bHfb[g]])
                    z_, bz = jz[g % 2]
                    kb.op("dve", lambda e, gs=gs: e.tensor_tensor(Y[:, gs], Y[:, gs], zsb[:, gs], ALU.mult),
                          reads=[bY[g], bzs], writes=[bY[g]])
                    if g == 0:
                        kb.op("dve", lambda e: e.memset(ssq[:], 0.0), writes=[bssq])
                    kb.op("act", lambda e, z_=z_, gs=gs, g=g: e.activation(z_[:], Y[:, gs], AF.Square, accum_out=ssq[:, g:g + 1]),
                          reads=[bY[g], bssq], writes=[bz, bssq])
                kb.op("act", lambda e: e.activation(ssq[:], ssq[:], AF.Sqrt, bias=EPS, scale=1.0 / 512), reads=[bssq], writes=[bssq])
                kb.op("dve", lambda e: e.reciprocal(ssq[:], ssq[:]), reads=[bssq], writes=[bssq])
                for g in range(4):
                    gs = slice(g * 512, (g + 1) * 512)
                    kb.op("dve", lambda e, g=g, gs=gs: e.scalar_tensor_tensor(
                        xd[:, gs], Y[:, gs], ssq[:, g:g + 1], ng[:, gs], ALU.mult, ALU.mult),
                        reads=[bY[g], bssq, bc], writes=[bxd])
                kb.dma("sp", self.YN[c * 128:(c + 1) * 128, 0:2048], xd[:], reads=[bxd])

        for c in range(T // 128):
            do_chunk(c)

    def even_consts(self, i, l):
        kb = self.kb
        c = {}
        b = kb.buf()
        c["b"] = b
        c["gpre"] = kb.sb("gpre", [128, D])
        kb.dma("sp", c["gpre"][:], self.gains[l * 4 + 0], writes=[b])
        lg = kb.sb("lg", [128, 16])
        kb.dma("sp", lg[:], self.rlogit[i], writes=[b])
        kb.op("act", lambda e: e.activation(lg[:], lg[:], AF.Sigmoid), reads=[b], writes=[b])
        kb.op("act", lambda e: e.activation(lg[:], lg[:], AF.Ln), reads=[b], writes=[b])
        c["lg"] = lg
        idx = kb.sb("idx", [128, 4])
        kb.dma("sp", idx[:], self.ridx, writes=[b])
        tabs = kb.sb("tabs", [128, 5, 8])
        for t, (col, off) in enumerate(((0, 0), (1, 8), (2, 0), (3, 8))):
            kb.op("dve", lambda e, t=t, col=col, off=off: e.tensor_scalar(
                tabs[:, t, :], lg[:, off:off + 8], idx[:, col:col + 1], None, ALU.mult), reads=[b], writes=[b])
        kb.op("act", lambda e: e.activation(tabs[:, 0:4, :], tabs[:, 0:4, :], AF.Exp), reads=[b], writes=[b])
        g128 = kb.sb("g128", [128, 16])
        kb.op("act", lambda e: e.activation(g128[:], lg[:], AF.Exp, scale=128.0), reads=[b], writes=[b])
        c["tabs"], c["g128"] = tabs, g128
        return c

    def even_load_win(self, i, cols_list):
        kb = self.kb
        W = kb.sb("ewin", [128, 8, 3584], BF16)
        bw = kb.buf()
        for k in range(8):
            for (a, bnd) in cols_list:
                kb.dma("pool", W[:, k, a:bnd], self.ab_w_in[i, k * 128:(k + 1) * 128, a:bnd], writes=[bw])
        return W, bw

    def proj_tok(self, W, bw, hnT, bh, col0, pP, bpP, width=512):
        def mm(e):
            for k in range(8):
                ins = e.matmul(pP[:, 0:width], hnT[:, k, 0:128], W[:, k, col0:col0 + width], start=(k == 0), stop=(k == 7))
            return ins
        self.kb.op("pe", mm, reads=[bw, bh], writes=[bpP])

    def rotary(self, src, bsrc, dst, bdst, cs, bcs, tmp):
        kb = self.kb
        s3 = src[:].rearrange("p (h d) -> p h d", h=8)
        d3 = dst[:].rearrange("p (h d) -> p h d", h=8)
        x1, x2 = s3[:, :, 0:32], s3[:, :, 32:64]
        cosb = cs[:, 0:32].unsqueeze(1).to_broadcast([128, 8, 32])
        sinb = cs[:, 32:64].unsqueeze(1).to_broadcast([128, 8, 32])
        (ta, tb_, tc, td), bt = tmp
        ta3, tb3, tc3, td3 = [t[:].rearrange("p (h d) -> p h d", h=8) for t in (ta, tb_, tc, td)]
        kb.op("dve", lambda e: e.tensor_tensor(ta3, x1, cosb, ALU.mult), reads=[bsrc, bcs], writes=[bt[0]])
        kb.op("dve", lambda e: e.tensor_tensor(tb3, x2, sinb, ALU.mult), reads=[bsrc, bcs], writes=[bt[1]])
        kb.op("dve", lambda e: e.tensor_tensor(d3[:, :, 0:32], ta3, tb3, ALU.subtract), reads=[bt[0], bt[1]], writes=[bdst])
        kb.op("pool", lambda e: e.tensor_tensor(tc3, x1, sinb, ALU.mult), reads=[bsrc, bcs], writes=[bt[2]])
        kb.op("pool", lambda e: e.tensor_tensor(td3, x2, cosb, ALU.mult), reads=[bsrc, bcs], writes=[bt[3]])
        kb.op("pool", lambda e: e.tensor_tensor(d3[:, :, 32:64], tc3, td3, ALU.add), reads=[bt[2], bt[3]], writes=[bdst])

    def even_p1(self, i, l, Xin):
        kb = self.kb
        T = self.T
        cst = self.even_consts(i, l)
        bc = cst["b"]
        tabs, g128 = cst["tabs"], cst["g128"]
        W, bw = self.even_load_win(i, [(512, 1536), (2048, 3584)])
        fe = self.make_fe(128, 0)
        hnTs = [kb.sb("hnT", [128, 8, 128], BF16) for _ in range(2)]
        bhnT = kb.bufs(2)
        pPs = [(kb.ps("pP", [128, 512]), kb.buf()) for _ in range(3)]
        kr = kb.sb("kr", [128, 512])
        bkr = kb.buf()
        krot = kb.sb("krot", [128, 512], BF16)
        bkrot = kb.buf()
        v = kb.sb("v", [128, 512])
        bv = kb.buf()
        vw = kb.sb("vw", [128, 512], BF16)
        bvw = kb.buf()
        cs = kb.sb("cs", [128, 64])
        bcs = kb.buf()
        tmp = ([kb.sb("rt", [128, 256]) for _ in range(4)], kb.bufs(4))
        Hb = kb.sb("Hb", [64, 512])
        Hbb = kb.sb("Hbb", [64, 512], BF16)
        bHb, bHbb = kb.buf(), kb.buf()
        kb.op("pool", lambda e: e.memset(Hb[:], 0.0), writes=[bHb])
        kb.op("pool", lambda e: e.memset(Hbb[:], 0.0), writes=[bHbb])
        Ssb = kb.sb("Ssb", [64, 512])
        bS = kb.buf()
        nqk = [(kb.sb("nqk", [128, 8, 128], BF16), kb.buf()) for _ in range(2)]
        nvs = [(kb.sb("nv", [128, 512], BF16), kb.buf()) for _ in range(2)]
        pi = 0
        for c in reversed(range(T // 128)):
            r0 = c * 128
            hnT, bh = hnTs[c % 2], bhnT[c % 2]
            self.run_fe(fe, Xin, r0, cst["gpre"], bc, hnT, bh)
            kb.dma("act", cs[:], self.rope[r0:r0 + 128, :], writes=[bcs])
            pP, bpP = pPs[pi % 3]
            pi += 1
            self.proj_tok(W, bw, hnT, bh, 512, pP, bpP)
            kb.op("act", lambda e, pP=pP: e.activation(kr[:], pP[:, 0:512], AF.Copy, scale=0.125), reads=[bpP], writes=[bkr])
            self.rotary(kr, bkr, krot, bkrot, cs, bcs, tmp)
            pP, bpP = pPs[pi % 3]
            pi += 1
            self.proj_tok(W, bw, hnT, bh, 1024, pP, bpP)
            kb.op("act", lambda e, pP=pP: e.activation(v[:], pP[:, 0:512], AF.Copy), reads=[bpP], writes=[bv])
            kb.op("pool", lambda e: e.tensor_tensor(
                vw[:].rearrange("p (h d) -> p h d", h=8), v[:].rearrange("p (h d) -> p h d", h=8),
                tabs[:, 3, :].unsqueeze(2).to_broadcast([128, 8, 64]), ALU.mult), reads=[bv, bc], writes=[bvw])
            kb.dma("sp", self.RHB[c], Hbb[:], reads=[bHbb])
            pP, bpP = pPs[pi % 3]
            pi += 1

            def ms(e, pP=pP):
                for h in range(8):
                    ins = e.matmul(pP[0:64, h * 64:(h + 1) * 64], krot[:, h * 64:(h + 1) * 64], vw[:, h * 64:(h + 1) * 64],
                                   start=True, stop=True)
                return ins
            kb.op("pe", ms, reads=[bkrot, bvw], writes=[bpP])
            kb.op("act", lambda e, pP=pP: e.activation(Ssb[:], pP[0:64, 0:512], AF.Copy), reads=[bpP], writes=[bS])
            kb.op("dve", lambda e: e.tensor_tensor(
                Hb[:].rearrange("p (h d) -> p h d", h=8), Hb[:].rearrange("p (h d) -> p h d", h=8),
                g128[0:64, 8:16].unsqueeze(2).to_broadcast([64, 8, 64]), ALU.mult), reads=[bHb, bc], writes=[bHb])
            kb.op("pool", lambda e: e.tensor_tensor(Hb[:], Hb[:], Ssb[:], ALU.add), reads=[bHb, bS], writes=[bHb])
            kb.op("pool", lambda e: e.tensor_copy(Hbb[:], Hb[:]), reads=[bHb], writes=[bHbb])
            nq_, bnq = nqk[c % 2]
            for pr in range(8):
                col = 2048 + pr * 128
                pP, bpP = pPs[pi % 3]
                pi += 1

                def mf(e, pP=pP, col=col, hnT=hnT):
                    for k in range(8):
                        ins = e.matmul(pP[:, 0:128], W[:, k, col:col + 128], hnT[:, k, 0:128], start=(k == 0), stop=(k == 7))
                    return ins
                kb.op("pe", mf, reads=[bw, bh], writes=[bpP])
                kb.op("act", lambda e, pP=pP, pr=pr, nq_=nq_: e.activation(
                    nq_[:, pr, :], pP[:, 0:128], AF.Copy, scale=(0.125 if pr < 4 else 1.0)), reads=[bpP], writes=[bnq])
            kb.dma("sp", self.NQT[:, :, r0:r0 + 128].rearrange("c p t -> p c t"), nq_[:, 0:4, :], reads=[bnq])
            kb.dma("sp", self.NKT[:, :, r0:r0 + 128].rearrange("c p t -> p c t"), nq_[:, 4:8, :], reads=[bnq])
            nv_, bnv = nvs[c % 2]
            pP, bpP = pPs[pi % 3]
            pi += 1
            self.proj_tok(W, bw, hnT, bh, 3072, pP, bpP)
            kb.op("act", lambda e, pP=pP, nv_=nv_: e.activation(nv_[:], pP[:, 0:512], AF.Copy), reads=[bpP], writes=[bnv])
            kb.dma("sp", self.NV[r0:r0 + 128, :], nv_[:], reads=[bnv])

    def even_p3(self, i, l, Xin):
        kb = self.kb
        T = self.T
        cst = self.even_consts(i, l)
        bc = cst["b"]
        tabs, g128, lg = cst["tabs"], cst["g128"], cst["lg"]
        W, bw = self.even_load_win(i, [(0, 2048)])
        fe = self.make_fe(128, 0)
        pT, bpT = fe["pT"], fe["bpT"]
        dj = kb.sb("dj", [128, 2, 128])
        kb.dma("sp", dj[:], self.rdj.rearrange("m p j -> p m j"), writes=[bc])
        DT = kb.sb("DT", [128, 8, 128])
        for h in range(8):
            kb.op("dve", lambda e, h=h: e.tensor_scalar(DT[:, h, :], dj[:, 0, :], lg[:, h:h + 1], None, ALU.mult),
                  reads=[bc], writes=[bc])
            kb.op("dve", lambda e, h=h: e.scalar_tensor_tensor(DT[:, h, :], dj[:, 1, :], lg[:, 8 + h:9 + h], DT[:, h, :],
                                                               ALU.mult, ALU.add), reads=[bc], writes=[bc])
        kb.op("act", lambda e: e.activation(DT[:], DT[:], AF.Exp), reads=[bc], writes=[bc])
        gng = kb.sb("gng", [128, 512])
        kb.dma("sp", gng[:], self.rgn[i], writes=[bc])
        hnTs = [kb.sb("hnT", [128, 8, 128], BF16) for _ in range(2)]
        bhnT = kb.bufs(2)
        pPs = [(kb.ps("pP", [128, 512]), kb.buf()) for _ in range(2)]
        pSc = kb.ps("pSc", [128, 1024])
        bpSc = kb.buf()
        P3 = kb.ps("P3", [128, 1536])
        bP3 = kb.bufs(3)
        Pi, Pf, Pb = P3[:, 0:512], P3[:, 512:1024], P3[:, 1024:1536]
        raw = [(kb.sb("raw", [128, 512]), kb.buf()) for _ in range(2)]
        qrot = kb.sb("qrot", [128, 512], BF16)
        krot = kb.sb("krot", [128, 512], BF16)
        bqrot, bkrot = kb.buf(), kb.buf()
        v = kb.sb("v", [128, 512])
        vb = kb.sb("vb", [128, 512], BF16)
        vw = kb.sb("vw", [128, 512], BF16)
        bv, bvb, bvw = kb.buf(), kb.buf(), kb.buf()
        sg = kb.sb("sg", [128, 512])
        bsg = kb.buf()
        cs = kb.sb("cs", [128, 64])
        bcs = kb.buf()
        tmp = ([kb.sb("rt", [128, 256]) for _ in range(4)], kb.bufs(4))
        qT = kb.sb("qT", [64, 8, 128], BF16)
        kT = kb.sb("kT", [64, 8, 128], BF16)
        bqT, bkT = kb.buf(), kb.buf()
        WT = kb.sb("WT", [128, 8, 128], BF16)
        bWT = kb.buf()
        Ssc = kb.sb("Ssc", [128, 1024])
        bSsc = kb.buf()
        Hf = kb.sb("Hf", [64, 512])
        Hfb = kb.sb("Hfb", [64, 512], BF16)
        Hbb = kb.sb("Hbb", [64, 512], BF16)
        bHf, bHfb, bHbb = kb.buf(), kb.buf(), kb.buf()
        kb.op("pool", lambda e: e.memset(Hf[:], 0.0), writes=[bHf])
        kb.op("pool", lambda e: e.memset(Hfb[:], 0.0), writes=[bHfb])
        Y = kb.sb("Y", [128, 512])
        bY = kb.buf()
        Tt = kb.sb("Tt", [128, 512])
        bTt = kb.buf()
        Ssb = kb.sb("Ssb", [64, 512])
        bS = kb.buf()
        st = kb.sb("st", [128, 16])
        bst = kb.buf()
        yo = kb.sb("yo", [128, 512], BF16)
        byo = kb.buf()
        pi = 0
        for c in range(T // 128):
            r0 = c * 128
            hnT, bh = hnTs[c % 2], bhnT[c % 2]
            self.run_fe(fe, Xin, r0, cst["gpre"], bc, hnT, bh)
            kb.dma("act", cs[:], self.rope[r0:r0 + 128, :], writes=[bcs])
            kb.dma("act", Hbb[:], self.RHB[c], writes=[bHbb])
            for (col, dst, bd, sc) in ((0, qrot, bqrot, 1.0), (512, krot, bkrot, 0.125)):
                pP, bpP = pPs[pi % 2]
                rw, brw = raw[pi % 2]
                pi += 1
                self.proj_tok(W, bw, hnT, bh, col, pP, bpP)
                kb.op("act", lambda e, pP=pP, rw=rw, sc=sc: e.activation(rw[:], pP[:, 0:512], AF.Copy, scale=sc),
                      reads=[bpP], writes=[brw])
                self.rotary(rw, brw, dst, bd, cs, bcs, tmp)
            pP, bpP = pPs[pi % 2]
            pi += 1
            self.proj_tok(W, bw, hnT, bh, 1024, pP, bpP)
            kb.op("act", lambda e, pP=pP: e.activation(v[:], pP[:, 0:512], AF.Copy), reads=[bpP], writes=[bv])
            kb.op("pool", lambda e: e.tensor_copy(vb[:], v[:]), reads=[bv], writes=[bvb])
            kb.op("pool", lambda e: e.tensor_tensor(
                vw[:].rearrange("p (h d) -> p h d", h=8), v[:].rearrange("p (h d) -> p h d", h=8),
                tabs[:, 2, :].unsqueeze(2).to_broadcast([128, 8, 64]), ALU.mult), reads=[bv, bc], writes=[bvw])
            pP, bpP = pPs[pi % 2]
            pi += 1
            self.proj_tok(W, bw, hnT, bh, 1536, pP, bpP)
            kb.op("act", lambda e, pP=pP: e.activation(sg[:], pP[:, 0:512], AF.Silu), reads=[bpP], writes=[bsg])
            for (src, bs_, dstT, bdT) in ((qrot, bqrot, qT, bqT), (krot, bkrot, kT, bkT)):
                def tr(e, src=src):
                    for h in range(8):
                        ins = e.transpose(pT[0:64, h * 128:(h + 1) * 128], src[:, h * 64:(h + 1) * 64], self.ident[:])
                    return ins
                kb.op("pe", tr, reads=[bs_, self.bident], writes=[bpT])
                kb.op("act", lambda e, dstT=dstT: e.activation(
                    dstT[:], pT[0:64, :].rearrange("p (c t) -> p c t", c=8), AF.Copy), reads=[bpT], writes=[bdT])

            def msc(e):
                for h in range(8):
                    ins = e.matmul(pSc[:, h * 128:(h + 1) * 128], kT[:, h, :], qT[:, h, :], start=True, stop=True)
                return ins
            kb.op("pe", msc, reads=[bkT, bqT], writes=[bpSc])
            kb.op("act", lambda e: e.activation(Ssc[:], pSc[:], AF.Copy), reads=[bpSc], writes=[bSsc])
            kb.op("dve", lambda e: e.tensor_tensor(WT[:].rearrange("p h j -> p (h j)"), Ssc[:],
                                                   DT[:].rearrange("p h j -> p (h j)"), ALU.mult),
                  reads=[bSsc, bc], writes=[bWT])

            def mi(e):
                for h in range(8):
                    ins = e.matmul(Pi[:, h * 64:(h + 1) * 64], WT[:, h, :], vb[:, h * 64:(h + 1) * 64], start=True, stop=True)
                return ins
            kb.op("pe", mi, reads=[bWT, bvb], writes=[bP3[0]])
            for (Px, bPx, Hx, bHx) in ((Pf, bP3[1], Hfb, bHfb), (Pb, bP3[2], Hbb, bHbb)):
                def mx(e, Px=Px, Hx=Hx):
                    for h in range(8):
                        ins = e.matmul(Px[:, h * 64:(h + 1) * 64], qT[:, h, :], Hx[:, h * 64:(h + 1) * 64],
                                       start=True, stop=True)
                    return ins
                kb.op("pe", mx, reads=[bqT, bHx], writes=[bPx])
            kb.op("act", lambda e: e.activation(Y[:], Pi, AF.Copy), reads=[bP3[0]], writes=[bY])
            for (Px, bPx, t) in ((Pf, bP3[1], 0), (Pb, bP3[2], 1)):
                kb.op("act", lambda e, Px=Px: e.activation(Tt[:], Px, AF.Copy), reads=[bPx], writes=[bTt])
                kb.op("dve", lambda e, t=t: e.tensor_tensor(
                    Tt[:].rearrange("p (h d) -> p h d", h=8), Tt[:].rearrange("p (h d) -> p h d", h=8),
                    tabs[:, t, :].unsqueeze(2).to_broadcast([128, 8, 64]), ALU.mult), reads=[bTt, bc], writes=[bTt])
                kb.op("pool", lambda e: e.tensor_tensor(Y[:], Y[:], Tt[:], ALU.add), reads=[bTt, bY], writes=[bY])
            pP, bpP = pPs[pi % 2]
            pi += 1

            def ms(e, pP=pP):
                for h in range(8):
                    ins = e.matmul(pP[0:64, h * 64:(h + 1) * 64], krot[:, h * 64:(h + 1) * 64], vw[:, h * 64:(h + 1) * 64],
                                   start=True, stop=True)
                return ins
            kb.op("pe", ms, reads=[bkrot, bvw], writes=[bpP])
            kb.op("act", lambda e, pP=pP: e.activation(Ssb[:], pP[0:64, 0:512], AF.Copy), reads=[bpP], writes=[bS])
            kb.op("dve", lambda e: e.tensor_tensor(
                Hf[:].rearrange("p (h d) -> p h d", h=8), Hf[:].rearrange("p (h d) -> p h d", h=8),
                g128[0:64, 0:8].unsqueeze(2).to_broadcast([64, 8, 64]), ALU.mult), reads=[bHf, bc], writes=[bHf])
            kb.op("pool", lambda e: e.tensor_tensor(Hf[:], Hf[:], Ssb[:], ALU.add), reads=[bHf, bS], writes=[bHf])
            kb.op("pool", lambda e: e.tensor_copy(Hfb[:], Hf[:]), reads=[bHf], writes=[bHfb])
            Y3 = Y[:].rearrange("p (h d) -> p h d", h=8)
            T3 = Tt[:].rearrange("p (h d) -> p h d", h=8)
            kb.op("dve", lambda e: e.reduce_sum(st[:, 0:8], Y3, AX.X), reads=[bY], writes=[bst])
            kb.op("dve", lambda e: e.tensor_scalar(st[:, 0:8], st[:, 0:8], 1.0 / 64, None, ALU.mult), reads=[bst], writes=[bst])
            kb.op("dve", lambda e: e.tensor_tensor(Y3, Y3, st[:, 0:8].unsqueeze(2).to_broadcast([128, 8, 64]), ALU.subtract),
                  reads=[bY, bst], writes=[bY])
            kb.op("act", lambda e: e.activation(Tt[:], Y[:], AF.Square), reads=[bY], writes=[bTt])
            kb.op("dve", lambda e: e.reduce_sum(st[:, 8:16], T3, AX.X), reads=[bTt], writes=[bst])
            kb.op("act", lambda e: e.activation(st[:, 8:16], st[:, 8:16], AF.Sqrt, bias=EPS, scale=1.0 / 64), reads=[bst], writes=[bst])
            kb.op("dve", lambda e: e.reciprocal(st[:, 8:16], st[:, 8:16]), reads=[bst], writes=[bst])
            kb.op("dve", lambda e: e.tensor_tensor(Y3, Y3, st[:, 8:16].unsqueeze(2).to_broadcast([128, 8, 64]), ALU.mult),
                  reads=[bY, bst], writes=[bY])
            kb.op("pool", lambda e: e.tensor_tensor(Y[:], Y[:], gng[:], ALU.mult), reads=[bY, bc], writes=[bY])
            kb.op("pool", lambda e: e.tensor_tensor(yo[:], Y[:], sg[:], ALU.mult), reads=[bY, bsg], writes=[byo])
            kb.dma("sp", self.YN[r0:r0 + 128, 0:512], yo[:], reads=[byo])

    def na_stage(self, i):
        kb = self.kb
        T = self.T
        rows = T // 64
        tb = kb.sb("tb", [128, 2048])
        btb = kb.buf()
        kws = [(kb.sb("kw", [64, 8, 512], BF16), kb.buf()) for _ in range(2)]
        qws = [(kb.sb("qw", [64, 8, 64], BF16), kb.buf()) for _ in range(2)]
        vws = [(kb.sb("vwin", [128, 4, 8, 80], BF16), kb.buf()) for _ in range(2)]
        for (vw_, bvw_) in vws:
            kb.op("pool", lambda e, vw_=vw_: e.memset(vw_[:], 1.0), writes=[bvw_])
        Sb = kb.sb("Sb", [128, 2048])
        bSb = kb.buf()
        PTs = [(kb.sb("PT", [128, 2048], BF16), kb.buf()) for _ in range(2)]
        Osb = kb.sb("Osb", [64, 8, 65])
        bO = kb.buf()
        rc = kb.sb("rc", [64, 8])
        brc = kb.buf()
        nas = [(kb.sb("na", [64, 512], BF16), kb.buf()) for _ in range(2)]
        pST = kb.ps("pST", [128, 2048])
        bpST = kb.buf()
        pO = kb.ps("pO", [128, 1024])
        bpO = kb.buf()
        prev_s = None
        import os
        NCUT = int(os.environ.get("NA_CUT", "9"))
        for r in range(rows):
            start = min(max(r - 4, 0), rows - 8)
            s = r - start
            s0 = start * 64
            if s != prev_s:
                kb.dma("sp", tb[:], self.natab[i, s], writes=[btb])
                prev_s = s
            kw, bkw = kws[r % 2]
            qw, bqw = qws[r % 2]
            vw_, bvw_ = vws[r % 2]
            kb.dma("sp", kw[:], self.NKT[:, :, s0:s0 + 512].rearrange("c (two p) t -> p (c two) t", two=2), writes=[bkw])
            kb.dma("act", qw[:], self.NQT[:, :, r * 64:(r + 1) * 64].rearrange("c (two p) t -> p (c two) t", two=2),
                   writes=[bqw])
            for ck in range(4):
                kb.dma("act", vw_[:, ck, :, 0:64],
                       self.NV[s0 + ck * 128:s0 + (ck + 1) * 128, :].rearrange("p (h d) -> p h d", h=8), writes=[bvw_])

            if NCUT < 2:
                continue

            def mst(e, kw=kw, qw=qw):
                for ck in range(4):
                    for h in range(8):
                        o = (ck * 8 + h) * 64
                        ins = e.matmul(pST[:, o:o + 64], kw[:, h, ck * 128:(ck + 1) * 128],
                                       qw[:, h, :], start=True, stop=True)
                return ins
            kb.op("pe", mst, reads=[bkw, bqw], writes=[bpST])
            if NCUT < 3:
                continue
            kb.op("act", lambda e: e.activation(Sb[:], pST[:], AF.Copy), reads=[bpST], writes=[bSb])
            kb.op("dve", lambda e: e.tensor_tensor(Sb[:], Sb[:], tb[:], ALU.add), reads=[bSb, btb], writes=[bSb])
            PT, bPT = PTs[r % 2]
            kb.op("act", lambda e, PT=PT: e.activation(PT[:], Sb[:], AF.Exp), reads=[bSb], writes=[bPT])

            if NCUT < 4:
                continue

            def mpv(e, PT=PT, vw_=vw_):
                for h in range(8):
                    oc = (h % 4) * 80 + (h // 4) * 512
                    for ck in range(4):
                        o = (ck * 8 + h) * 64
                        ins = e.matmul(pO[0:64, oc:oc + 65], PT[:, o:o + 64], vw_[:, ck, h, 0:65],
                                       start=(ck == 0), stop=(ck == 3))
                return ins
            kb.op("pe", mpv, reads=[bPT, bvw_], writes=[bpO])
            if NCUT < 5:
                continue
            kb.op("act", lambda e: e.activation(Osb[:, 0:4, :], pO[0:64, 0:320].rearrange("p (h d) -> p h d", h=4)[:, :, 0:65], AF.Copy),
                  reads=[bpO], writes=[bO])
            kb.op("act", lambda e: e.activation(Osb[:, 4:8, :], pO[0:64, 512:832].rearrange("p (h d) -> p h d", h=4)[:, :, 0:65], AF.Copy),
                  reads=[bpO], writes=[bO])
            kb.op("dve", lambda e: e.reciprocal(rc[:].unsqueeze(2), Osb[:, :, 64:65]), reads=[bO], writes=[brc])
            na, bna = nas[r % 2]
            kb.op("dve", lambda e, na=na: e.tensor_tensor(
                na[:].rearrange("p (h d) -> p h d", h=8), Osb[:, :, 0:64],
                rc[:].unsqueeze(2).to_broadcast([64, 8, 64]), ALU.mult), reads=[bO, brc], writes=[bna])
            kb.dma("sp", self.YN[r * 64:(r + 1) * 64, 512:1024], na[:], reads=[bna])

    def outproj_stage(self, l, w_src, Cin, YN, Xin, Xout):
        kb = self.kb
        T = self.T
        KC = Cin // 128
        W = kb.sb("wout", [128, KC, D], BF16)
        bw = kb.buf()
        for k in range(KC):
            kb.dma("pool", W[:, k, :], w_src[k * 128:(k + 1) * 128, :], writes=[bw])
        gpost = kb.sb("gpost", [128, D])
        bc = kb.buf()
        kb.dma("sp", gpost[:], self.gains[l * 4 + 1], writes=[bc])
        yns = [(kb.sb("yn", [128, Cin], BF16), kb.buf()) for _ in range(2)]
        YTs = [(kb.sb("YT", [128, KC, 128], BF16), kb.buf()) for _ in range(2)]
        xrs = [(kb.sb("xr", [128, D]), kb.buf()) for _ in range(2)]
        xos = [(kb.sb("xo", [128, D]), kb.buf()) for _ in range(2)]
        junk = kb.sb("junk", [128, D])
        bjunk = kb.buf()
        ss = kb.sb("ss", [128, 1])
        bss = kb.buf()
        pTs = [(kb.ps("pT", [128, D], BF16), kb.buf()) for _ in range(2)]
        pm = kb.ps("pm", [128, D])
        bpm = kb.buf()
        ti = 0
        for c in range(T // 128):
            r0 = c * 128
            yn, byn = yns[c % 2]
            YT, bYT = YTs[c % 2]
            xr, bxr = xrs[c % 2]
            xo, bxo = xos[c % 2]
            kb.dma("act", yn[:], YN[r0:r0 + 128, 0:Cin], writes=[byn])
            kb.dma("sp", xr[:], Xin[r0:r0 + 128, :], writes=[bxr])
            for r in range(KC // 8):
                pT, bpT = pTs[ti % 2]
                ti += 1

                def tr(e, yn=yn, pT=pT, r=r):
                    for k in range(8):
                        kk = r * 8 + k
                        ins = e.transpose(pT[:, k * 128:(k + 1) * 128], yn[:, kk * 128:(kk + 1) * 128], self.ident[:])
                    return ins
                kb.op("pe", tr, reads=[byn, self.bident], writes=[bpT])
                kb.op("act", lambda e, YT=YT, pT=pT, r=r: e.activation(
                    YT[:, r * 8:(r + 1) * 8, :], pT[:].rearrange("p (k t) -> p k t", k=8), AF.Copy),
                    reads=[bpT], writes=[bYT])

            def mm(e, YT=YT):
                for nh in range(2):
                    for k in range(KC):
                        ins = e.matmul(pm[:, nh * 512:(nh + 1) * 512], YT[:, k, :], W[:, k, nh * 512:(nh + 1) * 512],
                                       start=(k == 0), stop=(k == KC - 1))
                return ins
            kb.op("pe", mm, reads=[bYT, bw], writes=[bpm])
            self.rms_stats(pm[:], bpm, ss, bss, junk, bjunk)
            kb.op("act", lambda e, xo=xo: e.activation(xo[:], pm[:], AF.Copy, scale=ss[:, 0:1]),
                  reads=[bpm, bss], writes=[bxo])
            kb.op("dve", lambda e, xo=xo: e.tensor_tensor(xo[:], xo[:], gpost[:], ALU.mult),
                  reads=[bxo, bc], writes=[bxo])
            kb.op("pool", lambda e, xo=xo, xr=xr: e.tensor_tensor(xo[:], xo[:], xr[:], ALU.add),
                  reads=[bxo, bxr], writes=[bxo])
            kb.dma("sp", Xout[r0:r0 + 128, :], xo[:], reads=[bxo])

    def ffn_stage(self, l, Xin, Xout, BT=512):
        kb = self.kb
        T = self.T
        NT = BT // 128
        ident, bident = self.ident, self.bident
        W_up = kb.sb("wup", [128, 8, 2 * FF], BF16)
        W_dn = kb.sb("wdn", [128, NCH, D], BF16)
        bwu = kb.bufs(8)
        bwd = kb.bufs(NCH)
        for k in range(8):
            kb.dma("pool", W_up[:, k, :], self.ffn_w_up[l, k * 128:(k + 1) * 128, :], writes=[bwu[k]])
        for c in range(NCH):
            kb.dma("pool", W_dn[:, c, :], self.ffn_w_down[l, c * 128:(c + 1) * 128, :], writes=[bwd[c]])
        gpre = kb.sb("gpre", [128, D])
        gpost = kb.sb("gpost", [128, D])
        fcw = kb.sb("fcw", [128, 2 * NCH, 3])
        fcb = kb.sb("fcb", [128, 2 * NCH])
        bconst = kb.buf()
        kb.dma("sp", gpre[:], self.gains[l * 4 + 2], writes=[bconst])
        kb.dma("sp", gpost[:], self.gains[l * 4 + 3], writes=[bconst])
        kb.dma("sp", fcw[:], self.fcw[l].rearrange("p (c w) -> p c w", w=3), writes=[bconst])
        kb.dma("sp", fcb[:], self.fcb[l], writes=[bconst])

        xts = [kb.sb("xt", [128, D])] * 2
        bxt = [kb.buf()] * 2
        hns = [kb.sb("hn", [128, D], BF16)] * 2
        bhn = [kb.buf()] * 2
        sss = [kb.sb("ss", [128, 1]) for _ in range(2)]
        bss = kb.bufs(2)
        xh = kb.sb("xh", [2, D])
        bxh = kb.buf()
        hnh = kb.sb("hnh", [2, D], BF16)
        bhnh = kb.buf()
        ssh = kb.sb("ssh", [2, 1])
        bssh = kb.buf()
        hnTs = [kb.sb("hnT", [128, 8, BT + 2], BF16)] * 2
        bhnT = [kb.buf()] * 2
        gT = kb.sb("gT", [128, NCH, BT], BF16)
        bgT = kb.bufs(NCH)
        cgs = [kb.sb("cg", [128, BT]) for _ in range(2)]
        cvs = [kb.sb("cv", [128, BT]) for _ in range(2)]
        ggs = [kb.sb("gg", [128, BT])] * 2
        bcg, bcv, bgg = kb.bufs(2), kb.bufs(2), [kb.buf()] * 2
        ugs = [kb.sb("ug", [128, BT + 2])] * 2
        uvs = [kb.sb("uv", [128, BT + 2])] * 2
        bug, buv = [kb.buf()] * 2, [kb.buf()] * 2
        xrs = [kb.sb("xr", [128, D]) for _ in range(1)]
        bxr = kb.bufs(1)
        junk, bjunk = xrs[0], bxr[0]
        xos = [kb.sb("xo", [128, D])] * 2
        bxo = [kb.buf()] * 2
        ss2 = kb.sb("ss2", [128, 1])
        bss2 = kb.buf()

        psG = [kb.ps("psG", [128, 512]) for _ in range(2)]
        psV = [kb.ps("psV", [128, 512]) for _ in range(2)]
        bpsA = kb.bufs(2)
        psH = kb.ps("psH", [128, 512])
        bpsH = kb.buf()
        pT = kb.ps("pT", [128, D], BF16)
        bpT = kb.buf()
        pTh, bpTh = pT, bpT
        pf = kb.ps("pf", [128, D])
        bpf = kb.buf()

        ti = 0
        ci = 0
        import os
        CUT = int(os.environ.get("FFN_CUT", "9"))
        for b in range(T // BT if CUT > 1 else 0):
            t0 = b * BT
            hnT = hnTs[b % 2]
            bh = bhnT[b % 2]
            for j in range(NT):
                xt, bx = xts[ti % 2], bxt[ti % 2]
                hn, bn = hns[ti % 2], bhn[ti % 2]
                ss, bs = sss[ti % 2], bss[ti % 2]
                ti += 1
                r0 = t0 + j * 128
                kb.dma("sp", xt[:], Xin[r0:r0 + 128, :], writes=[bx])
                self.rms_stats(xt[:], bx, ss, bs, junk, bjunk)
                kb.op("dve", lambda e, hn=hn, xt=xt, ss=ss: e.scalar_tensor_tensor(
                    hn[:], xt[:], ss[:, 0:1], gpre[:], ALU.mult, ALU.mult),
                    reads=[bx, bs, bconst], writes=[bn])

                def tr(e, hn=hn):
                    for k in range(8):
                        ins = e.transpose(pT[:, k * 128:(k + 1) * 128], hn[:, k * 128:(k + 1) * 128], ident[:])
                    return ins
                kb.op("pe", tr, reads=[bn, bident], writes=[bpT])
                kb.op("act", lambda e, hnT=hnT, j=j: e.activation(
                    hnT[:, :, j * 128:(j + 1) * 128], pT[:].rearrange("p (k t) -> p k t", k=8), AF.Copy),
                    reads=[bpT], writes=[bh])
            kb.op("pool", lambda e: e.memset(xh[:], 0.0), writes=[bxh])
            if t0 - 1 >= 0:
                kb.dma("sp", xh[0:1, :], Xin[t0 - 1:t0, :], writes=[bxh])
            if t0 + BT < T:
                kb.dma("sp", xh[1:2, :], Xin[t0 + BT:t0 + BT + 1, :], writes=[bxh])
            self.rms_stats(xh[:], bxh, ssh, bssh, junk, bjunk, nparts=2)
            kb.op("dve", lambda e: e.scalar_tensor_tensor(
                hnh[:], xh[:], ssh[:, 0:1], gpre[0:2, :], ALU.mult, ALU.mult),
                reads=[bxh, bssh, bconst], writes=[bhnh])

            def trh(e):
                for k in range(8):
                    ins = e.transpose(pTh[:, k * 2:(k + 1) * 2], hnh[:, k * 128:(k + 1) * 128], ident[0:2, 0:2])
                return ins
            kb.op("pe", trh, reads=[bhnh, bident], writes=[bpTh])
            kb.op("act", lambda e, hnT=hnT: e.activation(
                hnT[:, :, BT:BT + 2], pTh[:, 0:16].rearrange("p (k t) -> p k t", k=8), AF.Copy),
                reads=[bpTh], writes=[bh])

            for c in range(NCH if CUT > 2 else 0):
                par = ci % 2
                ci += 1
                pg = psG[par][:, 0:BT]
                pv = psV[par][:, 0:BT]
                ph = psH[:, 0:4]
                cg, cv, gg = cgs[par], cvs[par], ggs[par]

                SUB = os.environ.get("FFN_SUB", "")

                def mm(e, c=c, pg=pg, pv=pv, ph=ph, hnT=hnT):
                    lst = ((pg, c * 128, slice(0, BT)), (pv, FF + c * 128, slice(0, BT)),
                           (ph[:, 0:2], c * 128, slice(BT, BT + 2)),
                           (ph[:, 2:4], FF + c * 128, slice(BT, BT + 2)))
                    if "h" in SUB:
                        lst = lst[:2]
                    for (dst, col, rhs_sl) in lst:
                        for k in range(8):
                            ins = e.matmul(dst, W_up[:, k, col:col + 128], hnT[:, k, rhs_sl],
                                           start=(k == 0), stop=(k == 7))
                    return ins
                kb.op("pe", mm, reads=bwu + [bh], writes=[bpsA[par], bpsH])
                pairs = ((pg, 0, cg, bcg[par], c, ugs[par], bug[par]), (pv, 2, cv, bcv[par], NCH + c, uvs[par], buv[par]))
                for (src, hoff, dst, bd, cc, u, bu) in pairs:
                    kb.op("act", lambda e, src=src, u=u: e.activation(u[:, 0:BT], src, AF.Copy),
                          reads=[bpsA[par]], writes=[bu])
                    kb.op("act", lambda e, ph=ph, hoff=hoff, u=u: e.activation(
                        u[:, BT:BT + 2], ph[:, hoff:hoff + 2], AF.Copy), reads=[bpsH], writes=[bu])
                    kb.op("act", lambda e, src=src, dst=dst, cc=cc: e.activation(
                        dst[:], src, AF.Identity, bias=fcb[:, cc:cc + 1], scale=fcw[:, cc, 1:2]),
                        reads=[bpsA[par], bconst], writes=[bd])
                for tap in range(4):
                    for (src, hoff, dst, bd, cc, u, bu) in pairs:
                        if tap == 0:
                            kb.op("dve", lambda e, u=u, dst=dst, cc=cc: e.scalar_tensor_tensor(
                                dst[:, 1:BT], u[:, 0:BT - 1], fcw[:, cc, 0:1], dst[:, 1:BT], ALU.mult, ALU.add),
                                reads=[bu, bconst, bd], writes=[bd])
                        elif tap == 1:
                            kb.op("dve", lambda e, u=u, dst=dst, cc=cc: e.scalar_tensor_tensor(
                                dst[:, 0:BT - 1], u[:, 1:BT], fcw[:, cc, 2:3], dst[:, 0:BT - 1], ALU.mult, ALU.add),
                                reads=[bu, bconst, bd], writes=[bd])
                        elif tap == 2:
                            kb.op("dve", lambda e, u=u, dst=dst, cc=cc: e.scalar_tensor_tensor(
                                dst[:, 0:1], u[:, BT:BT + 1], fcw[:, cc, 0:1], dst[:, 0:1], ALU.mult, ALU.add),
                                reads=[bu, bconst, bd], writes=[bd])
                        else:
                            kb.op("dve", lambda e, u=u, dst=dst, cc=cc: e.scalar_tensor_tensor(
                                dst[:, BT - 1:BT], u[:, BT + 1:BT + 2], fcw[:, cc, 2:3], dst[:, BT - 1:BT],
                                ALU.mult, ALU.add),
                                reads=[bu, bconst, bd], writes=[bd])
                kb.op("act", lambda e, gg=gg, cg=cg: e.activation(gg[:], cg[:], AF.Gelu_apprx_tanh),
                      reads=[bcg[par]], writes=[bgg[par]])
                kb.op("pool", lambda e, c=c, gg=gg, cv=cv: e.tensor_tensor(gT[:, c, :], gg[:], cv[:], ALU.mult),
                      reads=[bgg[par], bcv[par]], writes=[bgT[c]])

            for j in range(NT if CUT > 3 else 0):
                r0 = t0 + j * 128

                def dn(e, j=j):
                    for nh in range(2):
                        for c in range(NCH):
                            ins = e.matmul(pf[:, nh * 512:(nh + 1) * 512], gT[:, c, j * 128:(j + 1) * 128],
                                           W_dn[:, c, nh * 512:(nh + 1) * 512], start=(c == 0), stop=(c == NCH - 1))
                    return ins
                kb.op("pe", dn, reads=bgT + bwd, writes=[bpf])
                self.rms_stats(pf[:], bpf, ss2, bss2, junk, bjunk)
                xr, bxrr = xrs[0], bxr[0]
                xo, bxoo = xos[j % 2], bxo[j % 2]
                kb.dma("sp", xr[:], Xin[r0:r0 + 128, :], writes=[bxrr])
                kb.op("act", lambda e, xo=xo: e.activation(xo[:], pf[:], AF.Copy, scale=ss2[:, 0:1]),
                      reads=[bpf, bss2], writes=[bxoo])
                kb.op("dve", lambda e, xo=xo: e.tensor_tensor(xo[:], xo[:], gpost[:], ALU.mult),
                      reads=[bxoo, bconst], writes=[bxoo])
                kb.op("pool", lambda e, xo=xo, xr=xr: e.tensor_tensor(xo[:], xo[:], xr[:], ALU.add),
                      reads=[bxoo, bxrr], writes=[bxoo])
                kb.dma("sp", Xout[r0:r0 + 128, :], xo[:], reads=[bxoo])


def host_rope(pos):
    half = 32
    inv = (1.0 / (10000.0 ** (np.arange(half, dtype=np.float32) / half))).astype(np.float32)
    ang = pos.astype(np.float32)[:, None] * inv[None, :]
    return np.concatenate([np.cos(ang), np.sin(ang)], axis=1).astype(np.float32)


def host_even(inp):
    lgt = inp["ab_ret_decay_logit"].reshape(2, 1, 16)
    rlogit = np.ascontiguousarray(np.broadcast_to(lgt, (2, 128, 16))).astype(np.float32)
    p = np.arange(128, dtype=np.float32)
    ridx = np.stack([p + 1, 128 - p, 127 - p, p], axis=1).astype(np.float32)
    l_ = np.arange(128)[:, None]
    j_ = np.arange(128)[None, :]
    rdj = np.stack([np.maximum(j_ - l_, 0), np.maximum(l_ - j_, 0)], 0).astype(np.float32)
    rgn = np.ascontiguousarray(np.broadcast_to(inp["ab_ret_gn_g"].reshape(2, 1, 512), (2, 128, 512))).astype(np.float32)
    rpb = inp["ab_na_rpb"]
    kk = np.arange(512)
    w = kk // 64
    kc = kk % 64
    q = np.arange(64)
    cstart = np.clip(q - 8, 0, 48)
    valid = (kc[:, None] >= cstart[None, :]) & (kc[:, None] < cstart[None, :] + 16)
    dc = np.clip(kc[:, None] - q[None, :], -15, 15) + 15
    natab = np.empty((2, 8, 512, 8, 64), np.float32)
    for s in range(8):
        dr = np.clip(w - s + 7, 0, 14)
        g = rpb[:, :, dr[:, None], dc]
        g = np.where(valid[None, None], g, np.float32(-30000.0))
        natab[:, s] = g.transpose(0, 2, 1, 3)
    natab = natab.reshape(2, 8, 4, 128, 8, 64).transpose(0, 1, 3, 2, 4, 5).reshape(2, 8, 128, 2048)
    return {"ab_w_in": np.ascontiguousarray(inp["ab_w_in"]), "ab_w_out": np.ascontiguousarray(inp["ab_w_out"]),
            "rlogit": rlogit, "ridx": ridx, "rdj": rdj, "rgn": rgn, "natab": np.ascontiguousarray(natab)}

def host_prep(inp, layers=range(4)):
    L = L_TOTAL
    g = np.stack([inp["norm_mix_pre"], inp["norm_mix_post"], inp["norm_ffn_pre"], inp["norm_ffn_post"]], axis=1)
    gains = np.ascontiguousarray(np.broadcast_to(g.reshape(L * 4, 1, D), (L * 4, 128, D))).astype(np.float32)
    cw = inp["ffn_conv_w"]
    fcw = np.ascontiguousarray(cw.reshape(L, 3, 2 * NCH, 128).transpose(0, 3, 2, 1)).reshape(L, 128, 2 * NCH * 3)
    fcb = np.ascontiguousarray(inp["ffn_conv_b"].reshape(L, 2 * NCH, 128).transpose(0, 2, 1))
    cmats = np.zeros((5, 128, 128), np.float32)
    k = np.arange(128)[:, None]
    j = np.arange(128)[None, :]
    cmats[0] = k <= j
    cmats[1] = k < j
    cmats[2] = k > j
    cmats[3] = k >= j
    cmats[4] = 1.0
    ccw = inp["c_conv_w"]
    scw = np.ascontiguousarray(ccw.reshape(2, 5, 24, 128).transpose(0, 3, 2, 1)).reshape(2, 128, 120)
    scb = np.ascontiguousarray(inp["c_conv_b"].reshape(2, 24, 128).transpose(0, 2, 1))

    def rep(a, n):
        return np.ascontiguousarray(np.broadcast_to(a.reshape(2, 1, n), (2, 128, n))).astype(np.float32)
    extra = {"c_w_in": np.ascontiguousarray(inp["c_w_in"]), "c_w_out": np.ascontiguousarray(inp["c_w_out"]),
             "scw": scw.astype(np.float32), "scb": scb.astype(np.float32),
             "sdtb": rep(inp["c_dt_bias"], 64), "salog": rep(inp["c_a_log"], 64),
             "sdsk": rep(inp["c_d_skip"], 32), "sng": rep(inp["c_norm_g"], 2048), "cmats": cmats}
    extra.update(host_even(inp))
    return {**extra, "gains": gains, "ffn_w_up": np.ascontiguousarray(inp["ffn_w_up"]),
            "ffn_w_down": np.ascontiguousarray(inp["ffn_w_down"]),
            "fcw": fcw.astype(np.float32), "fcb": fcb.astype(np.float32)}


_CACHE = {}


def kernel(**inputs):
    x = np.asarray(inputs["x"], dtype=np.float32)
    B, S, _ = x.shape
    n_cores = 8
    cpb = n_cores // B
    if "prog" not in _CACHE:
        _CACHE["prog"] = Prog(S, [0, 1, 2, 3], do_mix=True, do_ffn=True)
    P = _CACHE["prog"]
    hp = host_prep({k: np.asarray(v, dtype=np.float32) for k, v in inputs.items()})
    hp["rope"] = host_rope(np.arange(S))
    in_maps = []
    for c in range(n_cores):
        m = dict(hp)
        m["x"] = np.ascontiguousarray(x[c // cpb])
        in_maps.append(m)
    res = run_bass_kernel_spmd(P.nc, in_maps, core_ids=list(range(n_cores)))
    out = np.stack([res.results[b * cpb]["out"] for b in range(B)], axis=0)
    return out.astype(np.float32)
```

```python
from contextlib import ExitStack
import numpy as np
import concourse.bass as bass
import concourse.mybir as mybir
from concourse.bass_utils import run_bass_kernel_spmd

F32 = mybir.dt.float32
BF16 = mybir.dt.bfloat16
ALU = mybir.AluOpType
AF = mybir.ActivationFunctionType
AX = mybir.AxisListType

EPOCH = 8192
SAME_ENGINE_SYNC = True
OWN_DIST = 10 ** 9


class Buf:
    __slots__ = ("name", "w", "r")

    def __init__(self, name):
        self.name = name
        self.w = None
        self.r = []


class Op:
    __slots__ = ("eng", "emit", "waits", "inc")


class KB:
    ENGS = ("sp", "act", "pool", "pe", "dve")

    def __init__(self, nc, n_dma_sems=16):
        self.nc = nc
        self.gstack = ExitStack()
        self.stack = self.gstack
        self.stream = {e: [] for e in self.ENGS}
        self.count = {e: 0 for e in self.ENGS}
        self.csems = {e: [] for e in self.ENGS}
        self.waited = {e: {} for e in self.ENGS}
        self.dma_pool = {}
        self.dma_next = {}
        self.n_dma_sems = n_dma_sems
        self.semobjs = {}
        self.nbuf = 0
        self.nalloc = 0

    def sem(self, name):
        s = self.gstack.enter_context(self.nc.semaphore(name))
        self.semobjs[name] = s
        return name

    def sb(self, name, shape, dt=F32):
        self.nalloc += 1
        return self.stack.enter_context(self.nc.sbuf_tensor(f"{name}_{self.nalloc}", list(shape), dt))

    def ps(self, name, shape, dt=F32):
        self.nalloc += 1
        return self.stack.enter_context(self.nc.psum_tensor(f"{name}_{self.nalloc}", list(shape), dt))

    def buf(self, name=None):
        self.nbuf += 1
        return Buf(name or f"b{self.nbuf}")

    def bufs(self, n):
        return [self.buf() for _ in range(n)]

    def _deps(self, reads, writes):
        ids = []
        for b in reads:
            if b.w is not None:
                ids.append(b.w)
        for b in writes:
            if b.w is not None:
                ids.append(b.w)
            ids.extend(b.r)
        return ids

    def _finish(self, eng, emit, reads, writes, cid3, ids):
        cid = cid3[:2]
        need = {}
        for t in ids:
            if t[1] > need.get(t[0], 0):
                need[t[0]] = t[1]
        waits = []
        wd = self.waited[eng]
        own = self.csems[eng]
        cur = self.count[eng] if cid3[2] == 1 else None
        for s, v in need.items():
            if s in own:
                if not SAME_ENGINE_SYNC:
                    continue
                if cur is not None and cur - (own.index(s) * EPOCH + v) >= OWN_DIST:
                    continue
            if wd.get(s, 0) >= v:
                continue
            wd[s] = v
            waits.append((s, v))
        op = Op()
        op.eng, op.emit, op.waits, op.inc = eng, emit, waits, cid3
        self.stream[eng].append(op)
        for b in reads:
            b.r.append(cid)
        for b in writes:
            b.w = cid
            b.r = []
        return cid

    def op(self, eng, emit, reads=(), writes=()):
        n = self.count[eng]
        self.count[eng] = n + 1
        k = n // EPOCH
        while len(self.csems[eng]) <= k:
            self.csems[eng].append(self.sem(f"c_{eng}_{len(self.csems[eng])}"))
        cid3 = (self.csems[eng][k], (n % EPOCH) + 1, 1)
        return self._finish(eng, emit, reads, writes, cid3, self._deps(reads, writes))

    def dma(self, eng, out, in_, reads=(), writes=(), **kw):
        if eng not in self.dma_pool:
            self.dma_pool[eng] = [[self.sem(f"d_{eng}_{i}"), 0] for i in range(self.n_dma_sems)]
            self.dma_next[eng] = 0
        i = self.dma_next[eng]
        self.dma_next[eng] = (i + 1) % self.n_dma_sems
        slot = self.dma_pool[eng][i]
        prev = slot[1]
        slot[1] = prev + 16
        cid3 = (slot[0], slot[1], 16)

        def emit(e, out=out, in_=in_, kw=kw):
            return e.dma_start(out=out, in_=in_, **kw)

        ids = self._deps(reads, writes)
        if prev > 0:
            ids = ids + [(slot[0], prev)]
        return self._finish(eng, emit, reads, writes, cid3, ids)

    def coll(self, emit, reads=(), writes=()):
        eng = "pool"
        if eng not in self.dma_pool:
            self.dma_pool[eng] = [[self.sem(f"d_{eng}_{i}"), 0] for i in range(self.n_dma_sems)]
            self.dma_next[eng] = 0
        i = self.dma_next[eng]
        self.dma_next[eng] = (i + 1) % self.n_dma_sems
        slot = self.dma_pool[eng][i]
        prev = slot[1]
        slot[1] = prev + 16
        cid3 = (slot[0], slot[1], 16)
        ids = self._deps(reads, writes)
        if prev > 0:
            ids = ids + [(slot[0], prev)]
        return self._finish(eng, emit, reads, writes, cid3, ids)

    def all_ids(self):
        ids = []
        for e in self.ENGS:
            n = self.count[e]
            if n > 0:
                ids.append((self.csems[e][(n - 1) // EPOCH], ((n - 1) % EPOCH) + 1))
        for e in self.dma_pool:
            for (s, v) in self.dma_pool[e]:
                if v > 0:
                    ids.append((s, v))
        return ids

    def barrier(self):
        ids = self.all_ids()
        for e in self.ENGS:
            waits = []
            wd = self.waited[e]
            for (s, v) in ids:
                if wd.get(s, 0) >= v:
                    continue
                wd[s] = v
                waits.append((s, v))
            op = Op()
            op.eng, op.emit, op.waits, op.inc = e, None, waits, None
            self.stream[e].append(op)

    def emit_all(self):
        nc = self.nc
        so = self.semobjs
        with nc.Block() as block:
            decos = {"sp": block.sync, "act": block.scalar, "pool": block.gpsimd,
                     "pe": block.tensor, "dve": block.vector}
            for name in self.ENGS:
                ops = self.stream[name]
                if not ops:
                    continue

                def f(eng, ops=ops):
                    for op in ops:
                        for (s, v) in op.waits:
                            eng.wait_ge(so[s], v)
                        if op.emit is None:
                            continue
                        ins = op.emit(eng)
                        if op.inc is not None:
                            ins.then_inc(so[op.inc[0]], op.inc[2])
                decos[name](f)
        self.stream = {e: [] for e in self.ENGS}

    def stage(self):
        return _Stage(self)


class _Stage:
    def __init__(self, kb):
        self.kb = kb

    def __enter__(self):
        self.st = ExitStack()
        self.kb.stack = self.st
        return self

    def __exit__(self, *a):
        self.kb.barrier()
        self.kb.emit_all()
        self.kb.stack = self.kb.gstack
        self.st.close()
        return False


D = 1024
FF = 2816
NCH = FF // 128
EPS = 1e-6
L_TOTAL = 4


class Prog:
    def __init__(self, T, layers, do_mix=True, do_ffn=True):
        self.T = T
        self.layers = layers
        nc = bass.Bass("TRN2", target_bir_lowering=False)
        self.nc = nc
        kb = KB(nc)
        self.kb = kb
        L = L_TOTAL

        def inp(name, shape):
            return nc.dram_tensor(name, list(shape), F32, kind="ExternalInput").ap()

        self.x = inp("x", [T, D])
        self.gains = inp("gains", [L * 4, 128, D])
        self.ffn_w_up = inp("ffn_w_up", [L, D, 2 * FF])
        self.ffn_w_down = inp("ffn_w_down", [L, FF, D])
        self.fcw = inp("fcw", [L, 128, 2 * NCH * 3])
        self.fcb = inp("fcb", [L, 128, 2 * NCH])
        self.c_w_in = inp("c_w_in", [2, D, 5184])
        self.c_w_out = inp("c_w_out", [2, 2048, D])
        self.scw = inp("scw", [2, 128, 24 * 5])
        self.scb = inp("scb", [2, 128, 24])
        self.sdtb = inp("sdtb", [2, 128, 64])
        self.salog = inp("salog", [2, 128, 64])
        self.sdsk = inp("sdsk", [2, 128, 32])
        self.sng = inp("sng", [2, 128, 2048])
        self.cmats = inp("cmats", [5, 128, 128])
        self.ab_w_in = inp("ab_w_in", [2, D, 3584])
        self.ab_w_out = inp("ab_w_out", [2, D, D])
        self.rlogit = inp("rlogit", [2, 128, 16])
        self.ridx = inp("ridx", [128, 4])
        self.rdj = inp("rdj", [2, 128, 128])
        self.rgn = inp("rgn", [2, 128, 512])
        self.rope = inp("rope", [T, 64])
        self.natab = inp("natab", [2, 8, 128, 2048])
        self.out = nc.dram_tensor("out", [T, D], F32, kind="ExternalOutput").ap()
        self.YN = nc.dram_tensor("YN", [T, 2048], BF16).ap()
        self.HB = nc.dram_tensor("HB", [T // 128, 128, 2048], BF16).ap()
        self.RHB = nc.dram_tensor("RHB", [T // 128, 64, 512], BF16).ap()
        self.XTK = nc.dram_tensor("XTK", [T, 2048], BF16).ap()
        self.BTK = nc.dram_tensor("BTK", [T, 512], BF16).ap()
        self.BTF = nc.dram_tensor("BTF", [4, 128, T], BF16).ap()
        self.CTF = nc.dram_tensor("CTF", [4, 128, T], BF16).ap()
        self.DTL = nc.dram_tensor("DTL", [T, 128], F32).ap()
        self.ZS = nc.dram_tensor("ZS", [T, 2048], BF16).ap()
        self.NQT = nc.dram_tensor("NQT", [4, 128, T], BF16).ap()
        self.NKT = nc.dram_tensor("NKT", [4, 128, T], BF16).ap()
        self.NV = nc.dram_tensor("NV", [T, 512], BF16).ap()
        self.Xa = nc.dram_tensor("Xa", [T, D], F32).ap()
        self.Xb = nc.dram_tensor("Xb", [T, D], F32).ap()

        self.ident = kb.sb("ident", [128, 128], BF16)
        self.bident = kb.buf()
        with kb.stage():
            one = kb.sb("one", [128, 128])
            idf = kb.sb("idf", [128, 128])
            b1, b2 = kb.buf(), kb.buf()
            kb.op("pool", lambda e: e.memset(one[:], 1.0), writes=[b1])
            kb.op("pool", lambda e: e.affine_select(idf[:], one[:], [[-1, 128]], ALU.is_equal, 0.0,
                                                    base=0, channel_multiplier=1), reads=[b1], writes=[b2])
            kb.op("pool", lambda e: e.tensor_copy(self.ident[:], idf[:]), reads=[b2], writes=[self.bident])

        cur = self.x
        for li, l in enumerate(layers):
            last = li == len(layers) - 1
            if do_mix:
                i = l // 2
                mdst = self.out if (last and not do_ffn) else self.Xa
                if l % 2 == 0:
                    import os
                    EVS = os.environ.get("EV_STAGES", "1234")
                    if "1" in EVS:
                        with kb.stage():
                            self.even_p1(i, l, cur)
                    if "2" in EVS:
                        with kb.stage():
                            self.na_stage(i)
                    if "3" in EVS:
                        with kb.stage():
                            self.even_p3(i, l, cur)
                    if "4" in EVS:
                        with kb.stage():
                            self.outproj_stage(l, self.ab_w_out[i], 1024, self.YN, cur, mdst)
                else:
                    import os
                    SDS = os.environ.get("SSD_STAGES", "123")
                    if "1" in SDS:
                        with kb.stage():
                            self.ssd_p1(i, l, cur)
                    if "2" in SDS:
                        with kb.stage():
                            self.ssd_p2a(i, l, cur)
                    if "3" in SDS:
                        with kb.stage():
                            self.outproj_stage(l, self.c_w_out[i], 2048, self.YN, cur, mdst)
                cur = mdst
            if do_ffn:
                dst = self.out if last else self.Xb
                with kb.stage():
                    self.ffn_stage(l, cur, dst)
                cur = dst
        kb.gstack.close()

    def rms_stats(self, src, bsrc, ss, bss, junk, bjunk, nparts=128, width=D):
        kb = self.kb
        kb.op("dve", lambda e: e.memset(ss[0:nparts, 0:1], 0.0), writes=[bss])
        kb.op("act", lambda e: e.activation(junk[0:nparts, 0:width], src, AF.Square,
                                            accum_out=ss[0:nparts, 0:1]),
              reads=[bsrc, bss], writes=[bjunk, bss])
        kb.op("act", lambda e: e.activation(ss[0:nparts, 0:1], ss[0:nparts, 0:1], AF.Sqrt,
                                            bias=EPS, scale=1.0 / width), reads=[bss], writes=[bss])
        kb.op("dve", lambda e: e.reciprocal(ss[0:nparts, 0:1], ss[0:nparts, 0:1]), reads=[bss], writes=[bss])

    def make_fe(self, BT, nh):
        kb = self.kb
        fe = {}
        fe["BT"], fe["nh"] = BT, nh
        fe["xts"] = [kb.sb("xt", [128, D]) for _ in range(2)]
        fe["bxt"] = kb.bufs(2)
        fe["hns"] = [kb.sb("hn", [128, D], BF16) for _ in range(2)]
        fe["bhn"] = kb.bufs(2)
        fe["junk"] = kb.sb("junk", [128, D])
        fe["bjunk"] = kb.buf()
        fe["sss"] = [kb.sb("ss", [128, 1]) for _ in range(2)]
        fe["bss"] = kb.bufs(2)
        nhh = max(nh, 1)
        fe["xh"] = kb.sb("xh", [2 * nhh, D]) if nh > 0 else None
        fe["bxh"] = kb.buf()
        fe["hnh"] = kb.sb("hnh", [2 * nhh, D], BF16) if nh > 0 else None
        fe["bhnh"] = kb.buf()
        fe["ssh"] = kb.sb("ssh", [2 * nhh, 1]) if nh > 0 else None
        fe["bssh"] = kb.buf()
        fe["pT"] = kb.ps("pT", [128, D], BF16)
        fe["bpT"] = kb.buf()
        if nh > 0:
            fe["pTh"] = kb.ps("pTh", [128, D], BF16)
            fe["bpTh"] = kb.buf()
        else:
            fe["pTh"], fe["bpTh"] = fe["pT"], fe["bpT"]
        fe["ti"] = 0
        return fe

    def run_fe(self, fe, Xin, t0, gpre, bconst, hnT, bh):
        kb = self.kb
        T = self.T
        BT, nh = fe["BT"], fe["nh"]
        ident, bident = self.ident, self.bident
        pT, bpT, pTh, bpTh = fe["pT"], fe["bpT"], fe["pTh"], fe["bpTh"]
        junk, bjunk = fe["junk"], fe["bjunk"]
        for j in range(BT // 128):
            i = fe["ti"] % 2
            fe["ti"] += 1
            xt, bx = fe["xts"][i], fe["bxt"][i]
            hn, bn = fe["hns"][i], fe["bhn"][i]
            ss, bs = fe["sss"][i], fe["bss"][i]
            r0 = t0 + j * 128
            kb.dma("sp", xt[:], Xin[r0:r0 + 128, :], writes=[bx])
            self.rms_stats(xt[:], bx, ss, bs, junk, bjunk)
            kb.op("dve", lambda e, hn=hn, xt=xt, ss=ss: e.scalar_tensor_tensor(
                hn[:], xt[:], ss[:, 0:1], gpre[:], ALU.mult, ALU.mult),
                reads=[bx, bs, bconst], writes=[bn])

            def tr(e, hn=hn):
                for k in range(8):
                    ins = e.transpose(pT[:, k * 128:(k + 1) * 128], hn[:, k * 128:(k + 1) * 128], ident[:])
                return ins
            kb.op("pe", tr, reads=[bn, bident], writes=[bpT])
            kb.op("act", lambda e, j=j: e.activation(
                hnT[:, :, j * 128:(j + 1) * 128], pT[:].rearrange("p (k t) -> p k t", k=8), AF.Copy),
                reads=[bpT], writes=[bh])
        if nh == 0:
            return
        xh, bxh, hnh, bhnh, ssh, bssh = fe["xh"], fe["bxh"], fe["hnh"], fe["bhnh"], fe["ssh"], fe["bssh"]
        kb.op("pool", lambda e: e.memset(xh[:], 0.0), writes=[bxh])
        if t0 - nh >= 0:
            kb.dma("sp", xh[0:nh, :], Xin[t0 - nh:t0, :], writes=[bxh])
        if t0 + BT + nh <= T:
            kb.dma("sp", xh[nh:2 * nh, :], Xin[t0 + BT:t0 + BT + nh, :], writes=[bxh])
        self.rms_stats(xh[:], bxh, ssh, bssh, junk, bjunk, nparts=2 * nh)
        kb.op("dve", lambda e: e.scalar_tensor_tensor(
            hnh[:], xh[:], ssh[:, 0:1], gpre[0:2 * nh, :], ALU.mult, ALU.mult),
            reads=[bxh, bssh, bconst], writes=[bhnh])
        w = 2 * nh

        def trh(e):
            for k in range(8):
                ins = e.transpose(pTh[:, k * w:(k + 1) * w], hnh[:, k * 128:(k + 1) * 128], ident[0:w, 0:w])
            return ins
        kb.op("pe", trh, reads=[bhnh, bident], writes=[bpTh])
        kb.op("act", lambda e: e.activation(
            hnT[:, :, BT:BT + w], pTh[:, 0:8 * w].rearrange("p (k t) -> p k t", k=8), AF.Copy),
            reads=[bpTh], writes=[bh])

    def ssd_consts(self, i, l):
        kb = self.kb
        c = {}
        c["b"] = kb.buf()
        b = c["b"]
        c["gpre"] = kb.sb("gpre", [128, D])
        kb.dma("sp", c["gpre"][:], self.gains[l * 4 + 0], writes=[b])
        c["scw"] = kb.sb("scw", [128, 24, 5])
        c["scb"] = kb.sb("scb", [128, 24])
        kb.dma("sp", c["scw"][:], self.scw[i].rearrange("p (c w) -> p c w", w=5), writes=[b])
        kb.dma("sp", c["scb"][:], self.scb[i], writes=[b])
        c["dtb"] = kb.sb("dtb", [128, 64])
        kb.dma("sp", c["dtb"][:], self.sdtb[i], writes=[b])
        c["A"] = kb.sb("A", [128, 64])
        kb.dma("sp", c["A"][:], self.salog[i], writes=[b])
        kb.op("act", lambda e: e.activation(c["A"][:], c["A"][:], AF.Exp), reads=[b], writes=[b])
        kb.op("dve", lambda e: e.tensor_scalar(c["A"][:], c["A"][:], -1.0, None, ALU.mult), reads=[b], writes=[b])
        c["cm"] = kb.sb("cm", [128, 5, 128])
        kb.dma("sp", c["cm"][:], self.cmats.rearrange("m p j -> p m j"), writes=[b])
        return c

    def ssd_load_win(self, i, cols_list):
        kb = self.kb
        W = kb.sb("swin", [128, 8, 5184], BF16)
        bw = kb.buf()
        for k in range(8):
            for (a, bnd) in cols_list:
                kb.dma("pool", W[:, k, a:bnd], self.c_w_in[i, k * 128:(k + 1) * 128, a:bnd], writes=[bw])
        return W, bw

    def ssd_block_front(self, fe, cst, W, bw, Xin, t0, hnT, bh, chunks, ue_s, acc_s, ps, outs):
        kb = self.kb
        BT = fe["BT"]
        NT = BT // 128
        scw, scb, bc = cst["scw"], cst["scb"], cst["b"]
        self.run_fe(fe, Xin, t0, cst["gpre"], bc, hnT, bh)
        for n0 in range(0, len(chunks), 2):
            ctx = []
            for n in range(n0, min(n0 + 2, len(chunks))):
                cc = chunks[n]
                par = n % 2
                pA, bpA = ps["pA"][par], ps["bpA"][par]
                pH, bpH = ps["pHh"][par], ps["bpHh"][par]
                ue, bue = ue_s[par]
                acc, bacc = acc_s[par]
                col = 2048 + cc * 128

                def mm(e, col=col, pA=pA):
                    for k in range(8):
                        ins = e.matmul(pA[:, 0:BT], W[:, k, col:col + 128], hnT[:, k, 0:BT], start=(k == 0), stop=(k == 7))
                    return ins
                kb.op("pe", mm, reads=[bw, bh], writes=[bpA])

                def mmh(e, col=col, pH=pH):
                    for k in range(8):
                        ins = e.matmul(pH[:, 0:4], W[:, k, col:col + 128], hnT[:, k, BT:BT + 4], start=(k == 0), stop=(k == 7))
                    return ins
                kb.op("pe", mmh, reads=[bw, bh], writes=[bpH])
                kb.op("act", lambda e, ue=ue, pA=pA: e.activation(ue[:, 2:2 + BT], pA[:, 0:BT], AF.Copy),
                      reads=[bpA], writes=[bue])
                kb.op("act", lambda e, ue=ue, pH=pH: e.activation(ue[:, 0:2], pH[:, 0:2], AF.Copy), reads=[bpH], writes=[bue])
                kb.op("act", lambda e, ue=ue, pH=pH: e.activation(ue[:, BT + 2:BT + 4], pH[:, 2:4], AF.Copy),
                      reads=[bpH], writes=[bue])
                ctx.append((cc, ue, bue, acc, bacc))
            for j in range(5):
                for (cc, ue, bue, acc, bacc) in ctx:
                    if j == 0:
                        kb.op("dve", lambda e, ue=ue, acc=acc, cc=cc: e.tensor_scalar(
                            acc[:], ue[:, 0:BT], scw[:, cc, 0:1], scb[:, cc:cc + 1], ALU.mult, ALU.add),
                            reads=[bue, bc], writes=[bacc])
                    else:
                        kb.op("dve", lambda e, ue=ue, acc=acc, cc=cc, j=j: e.scalar_tensor_tensor(
                            acc[:], ue[:, j:j + BT], scw[:, cc, j:j + 1], acc[:], ALU.mult, ALU.add),
                            reads=[bue, bc, bacc], writes=[bacc])
            for (cc, ue, bue, acc, bacc) in ctx:
                outs(cc, acc, bacc)

    def ssd_small(self, cst, la, bla, dirs, ps, sm, bsm):
        kb = self.kb
        cm, bc = cst["cm"], cst["b"]
        pH, bpH = ps["pH"], ps["bpH"]

        def mm(e):
            e.matmul(pH[:, 64:96], cm[:, 0, :], la[:, 0:32], start=True, stop=True)
            e.matmul(pH[:, 96:128], cm[:, 3, :], la[:, 32:64], start=True, stop=True)
            e.matmul(pH[:, 128:160], cm[:, 1, :], la[:, 32:64], start=True, stop=True)
            return e.matmul(pH[:, 160:224], cm[:, 4, :], la[:, 0:64], start=True, stop=True)
        kb.op("pe", mm, reads=[bla, bc], writes=[bpH])
        kb.op("act", lambda e: e.activation(sm[:, 0:160], pH[:, 64:224], AF.Copy), reads=[bpH], writes=[bsm])

    def ssd_dt(self, cst, W, bw, hnT, bh, sl, ps, tl):
        kb = self.kb
        pH, bpH = ps["pH"], ps["bpH"]
        bc = cst["b"]

        def mm(e):
            for k in range(8):
                ins = e.matmul(pH[:, 0:64], hnT[:, k, sl], W[:, k, 5120:5184], start=(k == 0), stop=(k == 7))
            return ins
        kb.op("pe", mm, reads=[bw, bh], writes=[bpH])
        dtr, dt, la, bdt = tl["dtr"], tl["dt"], tl["la"], tl["bdt"]
        kb.op("act", lambda e: e.activation(dtr[:], pH[:, 0:64], AF.Copy), reads=[bpH], writes=[bdt])
        kb.op("dve", lambda e: e.tensor_tensor(dtr[:], dtr[:], cst["dtb"][:], ALU.add), reads=[bdt, bc], writes=[bdt])
        kb.op("act", lambda e: e.activation(dtr[:], dtr[:], AF.Exp), reads=[bdt], writes=[bdt])
        kb.op("act", lambda e: e.activation(dt[:], dtr[:], AF.Ln, bias=1.0, scale=1.0), reads=[bdt], writes=[bdt])
        kb.op("dve", lambda e: e.tensor_tensor(la[:], dt[:], cst["A"][:], ALU.mult), reads=[bdt, bc], writes=[bdt])

    def ssd_alloc_common(self, BT, p2):
        kb = self.kb
        NT = BT // 128
        a = {}
        a["hnTs"] = [kb.sb("hnT", [128, 8, BT + 4], BF16) for _ in range(2)]
        a["bhnT"] = kb.bufs(2)
        a["ue_s"] = [(kb.sb("ue", [128, BT + 4]), kb.buf()) for _ in range(2)]
        a["acc_s"] = [(kb.sb("acc", [128, BT]), kb.buf()) for _ in range(2)]
        a["xsT"] = [(kb.sb("xsT", [128, BT], BF16), kb.buf()) for _ in range(2)]
        a["x_tok"] = [kb.sb("xtok", [128, NT, 2048], BF16)] * 2
        a["bx_tok"] = [kb.buf()] * 2
        a["B_tok"] = [kb.sb("btok", [128, NT, 512], BF16)] * 2
        a["bB_tok"] = [kb.buf()] * 2
        tl = {}
        for n in ("dtr", "dt", "la"):
            tl[n] = kb.sb(n, [128, 64])
        tl["bdt"] = kb.buf()
        tl["sm"] = kb.sb("sm", [128, 160])
        tl["bsm"] = kb.buf()
        a["tl"] = tl
        ps = {}
        ps["pA"] = [kb.ps("pA", [128, 512]) for _ in range(2)]
        ps["bpA"] = kb.bufs(2)
        ps["pH"] = kb.ps("pH", [128, 512])
        ps["bpH"] = kb.buf()
        ps["pHh"] = [kb.ps("pHh", [128, 512]) for _ in range(2)]
        ps["bpHh"] = kb.bufs(2)
        a["ps"] = ps
        return a

    def ssd_p1(self, i, l, Xin, BT=512):
        kb = self.kb
        T = self.T
        NT = BT // 128
        cst = self.ssd_consts(i, l)
        W, bw = self.ssd_load_win(i, [(0, 5184)])
        fe = self.make_fe(BT, 2)
        a = self.ssd_alloc_common(BT, False)
        ps, tl = a["ps"], a["tl"]
        pT, bpT = fe["pT"], fe["bpT"]
        Hb = kb.sb("Hb", [128, 2048])
        Hbb = kb.sb("Hbb", [128, 2048], BF16)
        bHb, bHbb = kb.bufs(4), kb.bufs(4)
        kb.op("pool", lambda e: e.memset(Hb[:], 0.0), writes=bHb)
        kb.op("pool", lambda e: e.memset(Hbb[:], 0.0), writes=bHbb)
        wgt = kb.sb("wgt", [128, 32])
        dtot = kb.sb("dtot", [128, 32])
        bwg = kb.buf()
        xws = [(kb.sb("xw", [128, 512], BF16), kb.buf()) for _ in range(2)]
        Ss = [(kb.sb("Ssb", [128, 512]), kb.buf()) for _ in range(2)]
        BTf = kb.sb("BTf", [128, 4, BT], BF16)
        CTf = kb.sb("CTf", [128, 4, BT], BF16)
        bBf, bCf = kb.buf(), kb.buf()
        zss = [(kb.sb("zs", [128, 2048], BF16), kb.buf()) for _ in range(2)]
        dtl = kb.sb("dtl", [128, 128])
        bdtl = kb.buf()
        gi = 0
        zi = 0
        for b in reversed(range(T // BT)):
            t0 = b * BT
            hnT, bh = a["hnTs"][b % 2], a["bhnT"][b % 2]
            x_tok, bxk = a["x_tok"][b % 2], a["bx_tok"][b % 2]
            B_tok, bBk = a["B_tok"][b % 2], a["bB_tok"][b % 2]

            def outs(cc, acc, bacc):
                if cc >= 20:
                    dstC = CTf[:, cc - 20, :]
                    kb.op("act", lambda e: e.activation(dstC, acc[:], AF.Silu), reads=[bacc], writes=[bCf])
                    return
                if cc >= 16:
                    xsT, bxs = BTf[:, cc - 16, :], bBf
                else:
                    xsT, bxs = a["xsT"][cc % 2]
                    xsT = xsT[:]
                kb.op("act", lambda e: e.activation(xsT, acc[:], AF.Silu), reads=[bacc], writes=[bxs])

                def tr(e):
                    for ct in range(NT):
                        ins = e.transpose(pT[:, ct * 128:(ct + 1) * 128], xsT[:, ct * 128:(ct + 1) * 128], self.ident[:])
                    return ins
                kb.op("pe", tr, reads=[bxs, self.bident], writes=[bpT])
                if cc < 16:
                    dst, bd = x_tok[:, :, cc * 128:(cc + 1) * 128], bxk
                else:
                    dst, bd = B_tok[:, :, (cc - 16) * 128:(cc - 15) * 128], bBk
                kb.op("act", lambda e: e.activation(dst, pT[:, 0:NT * 128].rearrange("p (c t) -> p c t", c=NT), AF.Copy),
                      reads=[bpT], writes=[bd])
            self.ssd_block_front(fe, cst, W, bw, Xin, t0, hnT, bh, list(range(24)), a["ue_s"], a["acc_s"], ps, outs)
            for ct in range(NT):
                r0 = t0 + ct * 128
                kb.dma("sp", self.XTK[r0:r0 + 128, :], x_tok[:, ct, :], reads=[bxk])
                kb.dma("act", self.BTK[r0:r0 + 128, :], B_tok[:, ct, :], reads=[bBk])
            kb.dma("sp", self.BTF[:, :, t0:t0 + BT].rearrange("g p t -> p g t"), BTf[:], reads=[bBf])
            kb.dma("act", self.CTF[:, :, t0:t0 + BT].rearrange("g p t -> p g t"), CTf[:], reads=[bCf])
            for ct in reversed(range(NT)):
                c = t0 // 128 + ct
                sl = slice(ct * 128, (ct + 1) * 128)
                self.ssd_dt(cst, W, bw, hnT, bh, sl, ps, tl)
                self.ssd_small(cst, tl["la"], tl["bdt"], None, ps, tl["sm"], tl["bsm"])
                sm, bsm, dt, bdt = tl["sm"], tl["bsm"], tl["dt"], tl["bdt"]
                kb.op("pool", lambda e: e.tensor_copy(dtl[:, 0:64], tl["dt"][:]), reads=[bdt], writes=[bdtl])
                kb.op("pool", lambda e: e.tensor_copy(dtl[:, 64:128], tl["la"][:]), reads=[bdt], writes=[bdtl])
                kb.dma("act", self.DTL[c * 128:(c + 1) * 128, :], dtl[:], reads=[bdtl])
                zs_, bz = zss[zi % 2]
                zi += 1
                for g in range(4):
                    pZ, bpZ = ps["pA"][gi % 2], ps["bpA"][gi % 2]
                    gi += 1

                    def mz(e, pZ=pZ, g=g, hnT=hnT, sl=sl):
                        for k in range(8):
                            ins = e.matmul(pZ[:, 0:512], hnT[:, k, sl], W[:, k, g * 512:(g + 1) * 512], start=(k == 0), stop=(k == 7))
                        return ins
                    kb.op("pe", mz, reads=[bw, bh], writes=[bpZ])
                    kb.op("act", lambda e, zs_=zs_, pZ=pZ, g=g: e.activation(zs_[:, g * 512:(g + 1) * 512], pZ[:, 0:512], AF.Silu),
                          reads=[bpZ], writes=[bz])
                kb.dma("sp", self.ZS[c * 128:(c + 1) * 128, :], zs_[:], reads=[bz])
                kb.op("act", lambda e: e.activation(wgt[:], sm[:, 64:96], AF.Exp), reads=[bsm], writes=[bwg])
                kb.op("dve", lambda e: e.tensor_tensor(wgt[:], wgt[:], dt[:, 32:64], ALU.mult), reads=[bwg, bdt], writes=[bwg])
                kb.op("act", lambda e: e.activation(dtot[:], sm[:, 128:160], AF.Exp), reads=[bsm], writes=[bwg])
                kb.dma("sp", self.HB[c], Hbb[:], reads=bHbb)
                for g in range(4):
                    xw, bxw = xws[gi % 2]
                    Ssb, bS = Ss[gi % 2]
                    pS, bpS = ps["pA"][gi % 2], ps["bpA"][gi % 2]
                    gi += 1
                    gs = slice(g * 512, (g + 1) * 512)
                    kb.op("pool", lambda e, xw=xw, gs=gs, g=g, ct=ct, x_tok=x_tok: e.tensor_tensor(
                        xw[:].rearrange("p (h d) -> p h d", h=8), x_tok[:, ct, gs].rearrange("p (h d) -> p h d", h=8),
                        wgt[:, g * 8:(g + 1) * 8].unsqueeze(2).to_broadcast([128, 8, 64]), ALU.mult),
                        reads=[bxk, bwg], writes=[bxw])
                    kb.op("pe", lambda e, pS=pS, xw=xw, g=g, ct=ct, B_tok=B_tok: e.matmul(
                        pS[:, 0:512], B_tok[:, ct, g * 128:(g + 1) * 128], xw[:], start=True, stop=True),
                        reads=[bBk, bxw], writes=[bpS])
                    kb.op("act", lambda e, Ssb=Ssb, pS=pS: e.activation(Ssb[:], pS[:, 0:512], AF.Copy),
                          reads=[bpS], writes=[bS])
                    kb.op("dve", lambda e, gs=gs, g=g: e.tensor_tensor(
                        Hb[:, gs].rearrange("p (h d) -> p h d", h=8), Hb[:, gs].rearrange("p (h d) -> p h d", h=8),
                        dtot[:, g * 8:(g + 1) * 8].unsqueeze(2).to_broadcast([128, 8, 64]), ALU.mult),
                        reads=[bHb[g], bwg], writes=[bHb[g]])
                    kb.op("pool", lambda e, gs=gs, Ssb=Ssb: e.tensor_tensor(Hb[:, gs], Hb[:, gs], Ssb[:], ALU.add),
                          reads=[bHb[g], bS], writes=[bHb[g]])
                    kb.op("act", lambda e, gs=gs: e.activation(Hbb[:, gs], Hb[:, gs], AF.Copy), reads=[bHb[g]], writes=[bHbb[g]])

    def ssd_p2a(self, i, l, Xin, BT=256):
        kb = self.kb
        T = self.T
        NT = BT // 128
        cst = self.ssd_consts(i, l)
        bc = cst["b"]
        cm = cst["cm"]
        ps = {}
        ps["pA"] = [kb.ps("pA", [128, 512]) for _ in range(3)]
        ps["bpA"] = kb.bufs(3)
        ps["pH"] = kb.ps("pH", [128, 512])
        ps["bpH"] = kb.buf()
        P3 = kb.ps("P3", [128, 1536])
        bP3 = kb.bufs(3)
        dsk = kb.sb("dsk", [128, 32])
        ng = kb.sb("ng", [128, 2048])
        kb.dma("sp", dsk[:], self.sdsk[i], writes=[bc])
        kb.dma("sp", ng[:], self.sng[i], writes=[bc])
        xks = [(kb.sb("xk", [128, 2048], BF16), kb.buf()) for _ in range(2)]
        Bks = [(kb.sb("Bk", [128, 512], BF16), kb.buf()) for _ in range(2)]
        BTfs = [(kb.sb("BTf", [128, 4, 128], BF16), kb.buf()) for _ in range(2)]
        CTfs = [(kb.sb("CTf", [128, 4, 128], BF16), kb.buf()) for _ in range(2)]
        dtls = [(kb.sb("dtl", [128, 128]), kb.buf()) for _ in range(2)]
        zsl = [(kb.sb("zsl", [128, 2048], BF16), kb.buf()) for _ in range(2)]
        Hbbs = [(kb.sb("Hbb", [128, 2048], BF16), kb.buf()) for _ in range(2)]
        sms = [(kb.sb("sm", [128, 160]), kb.buf()) for _ in range(2)]
        Hf = kb.sb("Hf", [128, 2048])
        Hfb = kb.sb("Hfb", [128, 2048], BF16)
        bHf, bHfb = kb.bufs(4), kb.bufs(4)
        kb.op("pool", lambda e: e.memset(Hf[:], 0.0), writes=bHf)
        kb.op("pool", lambda e: e.memset(Hfb[:], 0.0), writes=bHfb)
        ecs = [(kb.sb("ec", [128, 64]), kb.sb("wgt", [128, 32]), kb.sb("dtot", [128, 32]), kb.sb("dd", [128, 32]), kb.buf())
               for _ in range(2)]
        qks = [(kb.sb("qk", [128, 4, 128]), kb.buf()) for _ in range(2)]
        AUf = [kb.sb("AUf", [128, 4, 128]) for _ in range(2)]
        AUb = [kb.sb("AUb", [128, 4, 128]) for _ in range(2)]
        bAU = kb.bufs(2)
        E = [kb.sb("E", [128, 4, 128]) for _ in range(2)]
        bE = kb.bufs(2)
        T1 = [kb.sb("T1", [128, 4, 128]) for _ in range(2)]
        bT1 = kb.bufs(2)
        Wb = [kb.sb("Wb", [128, 4, 128], BF16) for _ in range(2)]
        bWb = kb.bufs(2)
        Ys = [(kb.sb("Y", [128, 2048]), kb.bufs(4)) for _ in range(2)]
        Tt = [(kb.sb("Tt", [128, 512]), kb.buf()) for _ in range(3)]
        xds = [(kb.sb("xd", [128, 2048], BF16), kb.buf()) for _ in range(2)]
        xws = [(kb.sb("xw", [128, 512], BF16), kb.buf()) for _ in range(2)]
        Ss = [(kb.sb("Ssb", [128, 512]), kb.buf()) for _ in range(2)]
        jz = [(kb.sb("jz", [128, 512]), kb.buf()) for _ in range(2)]
        ssqs = [(kb.sb("ssq", [128, 4]), kb.buf()) for _ in range(2)]
        cnt = {"qi": 0, "gi": 0, "pa": 0}

        def nextpA():
            k = cnt["pa"] % 3
            cnt["pa"] += 1
            return ps["pA"][k], ps["bpA"][k]

        def do_chunk(c):
            if True:
                r0 = c * 128
                cp = c % 2
                x_tok, bxk = xks[cp]
                B_tok, bBk = Bks[cp]
                BTfb, bBf = BTfs[cp]
                CTfb, bCf = CTfs[cp]
                dtl, bdt = dtls[cp]
                zsb, bzs = zsl[cp]
                Hbb, bHbb = Hbbs[cp]
                sm, bsm = sms[cp]
                ec, wgt, dtot, dd, bsm2 = ecs[cp]
                qk, bqk = qks[cp]
                Y, bY = Ys[cp]
                xd, bxd = xds[cp]
                ssq, bssq = ssqs[cp]
                dt, la = dtl[:, 0:64], dtl[:, 64:128]
                kb.dma("sp", x_tok[:], self.XTK[r0:r0 + 128, :], writes=[bxk])
                kb.dma("act", B_tok[:], self.BTK[r0:r0 + 128, :], writes=[bBk])
                kb.dma("sp", BTfb[:], self.BTF[:, :, r0:r0 + 128].rearrange("g p t -> p g t"), writes=[bBf])
                kb.dma("act", CTfb[:], self.CTF[:, :, r0:r0 + 128].rearrange("g p t -> p g t"), writes=[bCf])
                kb.dma("sp", dtl[:], self.DTL[r0:r0 + 128, :], writes=[bdt])
                kb.dma("act", zsb[:], self.ZS[r0:r0 + 128, :], writes=[bzs])
                kb.dma("sp", Hbb[:], self.HB[c], writes=[bHbb])
                self.ssd_small(cst, la, bdt, None, ps, sm, bsm)
                kb.op("act", lambda e, ec=ec, sm=sm: e.activation(ec[:], sm[:, 0:64], AF.Exp), reads=[bsm], writes=[bsm2])
                kb.op("dve", lambda e, wgt=wgt, sm=sm: e.tensor_tensor(wgt[:], sm[:, 96:128], sm[:, 0:32], ALU.subtract), reads=[bsm], writes=[bsm2])
                kb.op("act", lambda e, wgt=wgt: e.activation(wgt[:], wgt[:], AF.Exp), reads=[bsm2], writes=[bsm2])
                kb.op("dve", lambda e, wgt=wgt, dt=dt: e.tensor_tensor(wgt[:], wgt[:], dt[:, 0:32], ALU.mult), reads=[bsm2, bdt], writes=[bsm2])
                kb.op("act", lambda e, dtot=dtot, sm=sm: e.activation(dtot[:], sm[:, 96:128], AF.Exp), reads=[bsm], writes=[bsm2])
                kb.op("dve", lambda e, dd=dd, dt=dt: e.tensor_tensor(dd[:], dt[:, 0:32], dt[:, 32:64], ALU.subtract), reads=[bdt], writes=[bsm2])
                pQ, bpQ = nextpA()

                def mq(e, pQ=pQ, BTfb=BTfb, CTfb=CTfb):
                    for g in range(4):
                        ins = e.matmul(pQ[:, g * 128:(g + 1) * 128], BTfb[:, g, :], CTfb[:, g, :], start=True, stop=True)
                    return ins
                kb.op("pe", mq, reads=[bBf, bCf], writes=[bpQ])
                kb.op("act", lambda e, pQ=pQ: e.activation(qk[:].rearrange("p g j -> p (g j)"), pQ[:, 0:512], AF.Copy),
                      reads=[bpQ], writes=[bqk])
                kb.op("pool", lambda e: e.tensor_tensor(
                    xd[:].rearrange("p (h d) -> p h d", h=32), x_tok[:, :].rearrange("p (h d) -> p h d", h=32),
                    dsk[:].unsqueeze(2).to_broadcast([128, 32, 64]), ALU.mult), reads=[bxk, bc], writes=[bxd])
                for g in range(4):
                    gs = slice(g * 512, (g + 1) * 512)
                    Pi, Pf, Pb = P3[:, 0:512], P3[:, 512:1024], P3[:, 1024:1536]
                    kb.op("pe", lambda e, gs=gs: e.matmul(Pi, self.ident[:], xd[:, gs], start=True, stop=False),
                          reads=[self.bident, bxd], writes=[bP3[0]])
                    halves = []
                    for half in range(2):
                        h0 = (g * 2 + half) * 4
                        par = cnt["qi"] % 2
                        cnt["qi"] += 1
                        pS, bpS = nextpA()
                        halves.append((half, h0, par, pS, bpS))
                    for (half, h0, par, pS, bpS) in halves:
                        kb.op("dve", lambda e, par=par, h0=h0: e.tensor_tensor(
                            AUf[par][:], la[:, h0:h0 + 4].unsqueeze(2).to_broadcast([128, 4, 128]),
                            cm[:, 0, :].unsqueeze(1).to_broadcast([128, 4, 128]), ALU.mult),
                            reads=[bdt, bc], writes=[bAU[par]])
                    for (half, h0, par, pS, bpS) in halves:
                        kb.op("pool", lambda e, par=par, h0=h0: e.tensor_tensor(
                            AUb[par][:], la[:, 32 + h0:32 + h0 + 4].unsqueeze(2).to_broadcast([128, 4, 128]),
                            cm[:, 3, :].unsqueeze(1).to_broadcast([128, 4, 128]), ALU.mult),
                            reads=[bdt, bc], writes=[bAU[par]])
                    for (half, h0, par, pS, bpS) in halves:
                        def ms(e, par=par, pS=pS):
                            e.matmul(pS[:, 0:512], cm[:, 2, :], AUf[par][:].rearrange("p h j -> p (h j)"), start=True, stop=False)
                            return e.matmul(pS[:, 0:512], cm[:, 1, :], AUb[par][:].rearrange("p h j -> p (h j)"),
                                            start=False, stop=True)
                        kb.op("pe", ms, reads=[bc, bAU[par]], writes=[bpS])
                    for (half, h0, par, pS, bpS) in halves:
                        kb.op("act", lambda e, par=par, pS=pS: e.activation(
                            E[par][:].rearrange("p h j -> p (h j)"), pS[:, 0:512], AF.Exp), reads=[bpS], writes=[bE[par]])
                    for (half, h0, par, pS, bpS) in halves:
                        kb.op("pool", lambda e, par=par, h0=h0: e.tensor_tensor(
                            T1[par][:], cm[:, 0, :].unsqueeze(1).to_broadcast([128, 4, 128]),
                            dd[:, h0:h0 + 4].unsqueeze(2).to_broadcast([128, 4, 128]), ALU.mult),
                            reads=[bc, bsm2], writes=[bT1[par]])
                    for (half, h0, par, pS, bpS) in halves:
                        kb.op("pool", lambda e, par=par, h0=h0: e.tensor_tensor(
                            T1[par][:], T1[par][:], dt[:, 32 + h0:32 + h0 + 4].unsqueeze(2).to_broadcast([128, 4, 128]),
                            ALU.add), reads=[bT1[par], bdt], writes=[bT1[par]])
                    for (half, h0, par, pS, bpS) in halves:
                        kb.op("dve", lambda e, par=par, g=g: e.tensor_tensor(
                            E[par][:], E[par][:], qk[:, g, :].unsqueeze(1).to_broadcast([128, 4, 128]), ALU.mult),
                            reads=[bE[par], bqk], writes=[bE[par]])
                    for (half, h0, par, pS, bpS) in halves:
                        kb.op("dve", lambda e, par=par: e.tensor_tensor(Wb[par][:], E[par][:], T1[par][:], ALU.mult),
                              reads=[bE[par], bT1[par]], writes=[bWb[par]])
                    for (half, h0, par, pS, bpS) in halves:
                        def mi(e, par=par, h0=h0, half=half):
                            for hh in range(4):
                                h = h0 + hh
                                cs_ = (half * 4 + hh) * 64
                                ins = e.matmul(Pi[:, cs_:cs_ + 64], Wb[par][:, hh, :], x_tok[:, h * 64:(h + 1) * 64],
                                               start=False, stop=(half == 1 and hh == 3))
                            return ins
                        kb.op("pe", mi, reads=[bWb[par], bxk], writes=[bP3[0]])
                    kb.op("pe", lambda e, g=g, gs=gs: e.matmul(
                        Pf, CTfb[:, g, :], Hfb[:, gs], start=True, stop=True), reads=[bCf, bHfb[g]], writes=[bP3[1]])
                    kb.op("pe", lambda e, g=g, gs=gs: e.matmul(
                        Pb, CTfb[:, g, :], Hbb[:, gs], start=True, stop=True), reads=[bCf, bHbb], writes=[bP3[2]])
                    kb.op("act", lambda e, gs=gs: e.activation(Y[:, gs], Pi, AF.Copy), reads=[bP3[0]], writes=[bY[g]])
                    for (Px, bPx, eoff) in ((Pf, bP3[1], 0), (Pb, bP3[2], 32)):
                        Tt_, bTt = Tt[cnt["gi"] % 3]
                        cnt["gi"] += 1
                        kb.op("act", lambda e, Tt_=Tt_, Px=Px: e.activation(Tt_[:], Px, AF.Copy), reads=[bPx], writes=[bTt])
                        kb.op("dve", lambda e, Tt_=Tt_, eoff=eoff, g=g: e.tensor_tensor(
                            Tt_[:].rearrange("p (h d) -> p h d", h=8), Tt_[:].rearrange("p (h d) -> p h d", h=8),
                            ec[:, eoff + g * 8:eoff + (g + 1) * 8].unsqueeze(2).to_broadcast([128, 8, 64]), ALU.mult),
                            reads=[bTt, bsm2], writes=[bTt])
                        kb.op("pool", lambda e, Tt_=Tt_, gs=gs: e.tensor_tensor(Y[:, gs], Y[:, gs], Tt_[:], ALU.add),
                              reads=[bTt, bY[g]], writes=[bY[g]])
                    xw, bxw = xws[g % 2]
                    Ssb, bS = Ss[g % 2]
                    pS, bpS = nextpA()
                    kb.op("pool", lambda e, xw=xw, gs=gs, g=g: e.tensor_tensor(
                        xw[:].rearrange("p (h d) -> p h d", h=8), x_tok[:, gs].rearrange("p (h d) -> p h d", h=8),
                        wgt[:, g * 8:(g + 1) * 8].unsqueeze(2).to_broadcast([128, 8, 64]), ALU.mult),
                        reads=[bxk, bsm2], writes=[bxw])
                    kb.op("pe", lambda e, pS=pS, xw=xw, g=g: e.matmul(
                        pS[:, 0:512], B_tok[:, g * 128:(g + 1) * 128], xw[:], start=True, stop=True),
                        reads=[bBk, bxw], writes=[bpS])
                    kb.op("act", lambda e, Ssb=Ssb, pS=pS: e.activation(Ssb[:], pS[:, 0:512], AF.Copy),
                          reads=[bpS], writes=[bS])
                    kb.op("dve", lambda e, gs=gs, g=g: e.tensor_tensor(
                        Hf[:, gs].rearrange("p (h d) -> p h d", h=8), Hf[:, gs].rearrange("p (h d) -> p h d", h=8),
                        dtot[:, g * 8:(g + 1) * 8].unsqueeze(2).to_broadcast([128, 8, 64]), ALU.mult),
                        reads=[bHf[g], bsm2], writes=[bHf[g]])
                    kb.op("pool", lambda e, gs=gs, Ssb=Ssb: e.tensor_tensor(Hf[:, gs], Hf[:, gs], Ssb[:], ALU.add),
                          reads=[bHf[g], bS], writes=[bHf[g]])
                    kb.op("act", lambda e, gs=gs: e.activation(Hfb[:, gs], Hf[:, gs], AF.Copy), reads=[bHf[g]], writes=[bHfb[g]])
                    z_, bz = jz[g % 2]
                    kb.op("dve", lambda e, gs=gs: e.tensor_tensor(Y[:, gs], Y[:, gs], zsb[:, gs], ALU.mult),
                          reads=[bY[g], bzs], writes=[bY[g]])
                    if g == 0:
                        kb.op("dve", lambda e: e.memset(ssq[:], 0.0), writes=[bssq])
                    kb.op("act", lambda e, z_=z_, gs=gs, g=g: e.activation(z_[:], Y[:, gs], AF.Square, accum_out=ssq[:, g:g + 1]),
                          reads=[bY[g], bssq], writes=[bz, bssq])
                kb.op("act", lambda e: e.activation(ssq[:], ssq[:], AF.Sqrt, bias=EPS, scale=1.0 / 512), reads=[bssq], writes=[bssq])
                kb.op("dve", lambda e: e.reciprocal(ssq[:], ssq[:]), reads=[bssq], writes=[bssq])
                for g in range(4):
                    gs = slice(g * 512, (g + 1) * 512)
                    kb.op("dve", lambda e, g=g, gs=gs: e.scalar_tensor_tensor(
                        xd[:, gs], Y[:, gs], ssq[:, g:g + 1], ng[:, gs], ALU.mult, ALU.mult),
                        reads=[bY[g], bssq, bc], writes=[bxd])
                kb.dma("sp", self.YN[c * 128:(c + 1) * 128, 0:2048], xd[:], reads=[bxd])

        for c in range(T // 128):
            do_chunk(c)

    def even_consts(self, i, l):
        kb = self.kb
        c = {}
        b = kb.buf()
        c["b"] = b
        c["gpre"] = kb.sb("gpre", [128, D])
        kb.dma("sp", c["gpre"][:], self.gains[l * 4 + 0], writes=[b])
        lg = kb.sb("lg", [128, 16])
        kb.dma("sp", lg[:], self.rlogit[i], writes=[b])
        kb.op("act", lambda e: e.activation(lg[:], lg[:], AF.Sigmoid), reads=[b], writes=[b])
        kb.op("act", lambda e: e.activation(lg[:], lg[:], AF.Ln), reads=[b], writes=[b])
        c["lg"] = lg
        idx = kb.sb("idx", [128, 4])
        kb.dma("sp", idx[:], self.ridx, writes=[b])
        tabs = kb.sb("tabs", [128, 5, 8])
        for t, (col, off) in enumerate(((0, 0), (1, 8), (2, 0), (3, 8))):
            kb.op("dve", lambda e, t=t, col=col, off=off: e.tensor_scalar(
                tabs[:, t, :], lg[:, off:off + 8], idx[:, col:col + 1], None, ALU.mult), reads=[b], writes=[b])
        kb.op("act", lambda e: e.activation(tabs[:, 0:4, :], tabs[:, 0:4, :], AF.Exp), reads=[b], writes=[b])
        g128 = kb.sb("g128", [128, 16])
        kb.op("act", lambda e: e.activation(g128[:], lg[:], AF.Exp, scale=128.0), reads=[b], writes=[b])
        c["tabs"], c["g128"] = tabs, g128
        return c

    def even_load_win(self, i, cols_list):
        kb = self.kb
        W = kb.sb("ewin", [128, 8, 3584], BF16)
        bw = kb.buf()
        for k in range(8):
            for (a, bnd) in cols_list:
                kb.dma("pool", W[:, k, a:bnd], self.ab_w_in[i, k * 128:(k + 1) * 128, a:bnd], writes=[bw])
        return W, bw

    def proj_tok(self, W, bw, hnT, bh, col0, pP, bpP, width=512):
        def mm(e):
            for k in range(8):
                ins = e.matmul(pP[:, 0:width], hnT[:, k, 0:128], W[:, k, col0:col0 + width], start=(k == 0), stop=(k == 7))
            return ins
        self.kb.op("pe", mm, reads=[bw, bh], writes=[bpP])

    def rotary(self, src, bsrc, dst, bdst, cs, bcs, tmp):
        kb = self.kb
        s3 = src[:].rearrange("p (h d) -> p h d", h=8)
        d3 = dst[:].rearrange("p (h d) -> p h d", h=8)
        x1, x2 = s3[:, :, 0:32], s3[:, :, 32:64]
        cosb = cs[:, 0:32].unsqueeze(1).to_broadcast([128, 8, 32])
        sinb = cs[:, 32:64].unsqueeze(1).to_broadcast([128, 8, 32])
        (ta, tb_, tc, td), bt = tmp
        ta3, tb3, tc3, td3 = [t[:].rearrange("p (h d) -> p h d", h=8) for t in (ta, tb_, tc, td)]
        kb.op("dve", lambda e: e.tensor_tensor(ta3, x1, cosb, ALU.mult), reads=[bsrc, bcs], writes=[bt[0]])
        kb.op("dve", lambda e: e.tensor_tensor(tb3, x2, sinb, ALU.mult), reads=[bsrc, bcs], writes=[bt[1]])
        kb.op("dve", lambda e: e.tensor_tensor(d3[:, :, 0:32], ta3, tb3, ALU.subtract), reads=[bt[0], bt[1]], writes=[bdst])
        kb.op("pool", lambda e: e.tensor_tensor(tc3, x1, sinb, ALU.mult), reads=[bsrc, bcs], writes=[bt[2]])
        kb.op("pool", lambda e: e.tensor_tensor(td3, x2, cosb, ALU.mult), reads=[bsrc, bcs], writes=[bt[3]])
        kb.op("pool", lambda e: e.tensor_tensor(d3[:, :, 32:64], tc3, td3, ALU.add), reads=[bt[2], bt[3]], writes=[bdst])

    def even_p1(self, i, l, Xin):
        kb = self.kb
        T = self.T
        cst = self.even_consts(i, l)
        bc = cst["b"]
        tabs, g128 = cst["tabs"], cst["g128"]
        W, bw = self.even_load_win(i, [(512, 1536), (2048, 3584)])
        fe = self.make_fe(128, 0)
        hnTs = [kb.sb("hnT", [128, 8, 128], BF16) for _ in range(2)]
        bhnT = kb.bufs(2)
        pPs = [(kb.ps("pP", [128, 512]), kb.buf()) for _ in range(3)]
        kr = kb.sb("kr", [128, 512])
        bkr = kb.buf()
        krot = kb.sb("krot", [128, 512], BF16)
        bkrot = kb.buf()
        v = kb.sb("v", [128, 512])
        bv = kb.buf()
        vw = kb.sb("vw", [128, 512], BF16)
        bvw = kb.buf()
        cs = kb.sb("cs", [128, 64])
        bcs = kb.buf()
        tmp = ([kb.sb("rt", [128, 256]) for _ in range(4)], kb.bufs(4))
        Hb = kb.sb("Hb", [64, 512])
        Hbb = kb.sb("Hbb", [64, 512], BF16)
        bHb, bHbb = kb.buf(), kb.buf()
        kb.op("pool", lambda e: e.memset(Hb[:], 0.0), writes=[bHb])
        kb.op("pool", lambda e: e.memset(Hbb[:], 0.0), writes=[bHbb])
        Ssb = kb.sb("Ssb", [64, 512])
        bS = kb.buf()
        nqk = [(kb.sb("nqk", [128, 8, 128], BF16), kb.buf()) for _ in range(2)]
        nvs = [(kb.sb("nv", [128, 512], BF16), kb.buf()) for _ in range(2)]
        pi = 0
        for c in reversed(range(T // 128)):
            r0 = c * 128
            hnT, bh = hnTs[c % 2], bhnT[c % 2]
            self.run_fe(fe, Xin, r0, cst["gpre"], bc, hnT, bh)
            kb.dma("act", cs[:], self.rope[r0:r0 + 128, :], writes=[bcs])
            pP, bpP = pPs[pi % 3]
            pi += 1
            self.proj_tok(W, bw, hnT, bh, 512, pP, bpP)
            kb.op("act", lambda e, pP=pP: e.activation(kr[:], pP[:, 0:512], AF.Copy, scale=0.125), reads=[bpP], writes=[bkr])
            self.rotary(kr, bkr, krot, bkrot, cs, bcs, tmp)
            pP, bpP = pPs[pi % 3]
            pi += 1
            self.proj_tok(W, bw, hnT, bh, 1024, pP, bpP)
            kb.op("act", lambda e, pP=pP: e.activation(v[:], pP[:, 0:512], AF.Copy), reads=[bpP], writes=[bv])
            kb.op("pool", lambda e: e.tensor_tensor(
                vw[:].rearrange("p (h d) -> p h d", h=8), v[:].rearrange("p (h d) -> p h d", h=8),
                tabs[:, 3, :].unsqueeze(2).to_broadcast([128, 8, 64]), ALU.mult), reads=[bv, bc], writes=[bvw])
            kb.dma("sp", self.RHB[c], Hbb[:], reads=[bHbb])
            pP, bpP = pPs[pi % 3]
            pi += 1

            def ms(e, pP=pP):
                for h in range(8):
                    ins = e.matmul(pP[0:64, h * 64:(h + 1) * 64], krot[:, h * 64:(h + 1) * 64], vw[:, h * 64:(h + 1) * 64],
                                   start=True, stop=True)
                return ins
            kb.op("pe", ms, reads=[bkrot, bvw], writes=[bpP])
            kb.op("act", lambda e, pP=pP: e.activation(Ssb[:], pP[0:64, 0:512], AF.Copy), reads=[bpP], writes=[bS])
            kb.op("dve", lambda e: e.tensor_tensor(
                Hb[:].rearrange("p (h d) -> p h d", h=8), Hb[:].rearrange("p (h d) -> p h d", h=8),
                g128[0:64, 8:16].unsqueeze(2).to_broadcast([64, 8, 64]), ALU.mult), reads=[bHb, bc], writes=[bHb])
            kb.op("pool", lambda e: e.tensor_tensor(Hb[:], Hb[:], Ssb[:], ALU.add), reads=[bHb, bS], writes=[bHb])
            kb.op("pool", lambda e: e.tensor_copy(Hbb[:], Hb[:]), reads=[bHb], writes=[bHbb])
            nq_, bnq = nqk[c % 2]
            for pr in range(8):
                col = 2048 + pr * 128
                pP, bpP = pPs[pi % 3]
                pi += 1

                def mf(e, pP=pP, col=col, hnT=hnT):
                    for k in range(8):
                        ins = e.matmul(pP[:, 0:128], W[:, k, col:col + 128], hnT[:, k, 0:128], start=(k == 0), stop=(k == 7))
                    return ins
                kb.op("pe", mf, reads=[bw, bh], writes=[bpP])
                kb.op("act", lambda e, pP=pP, pr=pr, nq_=nq_: e.activation(
                    nq_[:, pr, :], pP[:, 0:128], AF.Copy, scale=(0.125 if pr < 4 else 1.0)), reads=[bpP], writes=[bnq])
            kb.dma("sp", self.NQT[:, :, r0:r0 + 128].rearrange("c p t -> p c t"), nq_[:, 0:4, :], reads=[bnq])
            kb.dma("sp", self.NKT[:, :, r0:r0 + 128].rearrange("c p t -> p c t"), nq_[:, 4:8, :], reads=[bnq])
            nv_, bnv = nvs[c % 2]
            pP, bpP = pPs[pi % 3]
            pi += 1
            self.proj_tok(W, bw, hnT, bh, 3072, pP, bpP)
            kb.op("act", lambda e, pP=pP, nv_=nv_: e.activation(nv_[:], pP[:, 0:512], AF.Copy), reads=[bpP], writes=[bnv])
            kb.dma("sp", self.NV[r0:r0 + 128, :], nv_[:], reads=[bnv])

    def even_p3(self, i, l, Xin):
        kb = self.kb
        T = self.T
        cst = self.even_consts(i, l)
        bc = cst["b"]
        tabs, g128, lg = cst["tabs"], cst["g128"], cst["lg"]
        W, bw = self.even_load_win(i, [(0, 2048)])
        fe = self.make_fe(128, 0)
        pT, bpT = fe["pT"], fe["bpT"]
        dj = kb.sb("dj", [128, 2, 128])
        kb.dma("sp", dj[:], self.rdj.rearrange("m p j -> p m j"), writes=[bc])
        DT = kb.sb("DT", [128, 8, 128])
        for h in range(8):
            kb.op("dve", lambda e, h=h: e.tensor_scalar(DT[:, h, :], dj[:, 0, :], lg[:, h:h + 1], None, ALU.mult),
                  reads=[bc], writes=[bc])
            kb.op("dve", lambda e, h=h: e.scalar_tensor_tensor(DT[:, h, :], dj[:, 1, :], lg[:, 8 + h:9 + h], DT[:, h, :],
                                                               ALU.mult, ALU.add), reads=[bc], writes=[bc])
        kb.op("act", lambda e: e.activation(DT[:], DT[:], AF.Exp), reads=[bc], writes=[bc])
        gng = kb.sb("gng", [128, 512])
        kb.dma("sp", gng[:], self.rgn[i], writes=[bc])
        hnTs = [kb.sb("hnT", [128, 8, 128], BF16) for _ in range(2)]
        bhnT = kb.bufs(2)
        pPs = [(kb.ps("pP", [128, 512]), kb.buf()) for _ in range(2)]
        pSc = kb.ps("pSc", [128, 1024])
        bpSc = kb.buf()
        P3 = kb.ps("P3", [128, 1536])
        bP3 = kb.bufs(3)
        Pi, Pf, Pb = P3[:, 0:512], P3[:, 512:1024], P3[:, 1024:1536]
        raw = [(kb.sb("raw", [128, 512]), kb.buf()) for _ in range(2)]
        qrot = kb.sb("qrot", [128, 512], BF16)
        krot = kb.sb("krot", [128, 512], BF16)
        bqrot, bkrot = kb.buf(), kb.buf()
        v = kb.sb("v", [128, 512])
        vb = kb.sb("vb", [128, 512], BF16)
        vw = kb.sb("vw", [128, 512], BF16)
        bv, bvb, bvw = kb.buf(), kb.buf(), kb.buf()
        sg = kb.sb("sg", [128, 512])
        bsg = kb.buf()
        cs = kb.sb("cs", [128, 64])
        bcs = kb.buf()
        tmp = ([kb.sb("rt", [128, 256]) for _ in range(4)], kb.bufs(4))
        qT = kb.sb("qT", [64, 8, 128], BF16)
        kT = kb.sb("kT", [64, 8, 128], BF16)
        bqT, bkT = kb.buf(), kb.buf()
        WT = kb.sb("WT", [128, 8, 128], BF16)
        bWT = kb.buf()
        Ssc = kb.sb("Ssc", [128, 1024])
        bSsc = kb.buf()
        Hf = kb.sb("Hf", [64, 512])
        Hfb = kb.sb("Hfb", [64, 512], BF16)
        Hbb = kb.sb("Hbb", [64, 512], BF16)
        bHf, bHfb, bHbb = kb.buf(), kb.buf(), kb.buf()
        kb.op("pool", lambda e: e.memset(Hf[:], 0.0), writes=[bHf])
        kb.op("pool", lambda e: e.memset(Hfb[:], 0.0), writes=[bHfb])
        Y = kb.sb("Y", [128, 512])
        bY = kb.buf()
        Tt = kb.sb("Tt", [128, 512])
        bTt = kb.buf()
        Ssb = kb.sb("Ssb", [64, 512])
        bS = kb.buf()
        st = kb.sb("st", [128, 16])
        bst = kb.buf()
        yo = kb.sb("yo", [128, 512], BF16)
        byo = kb.buf()
        pi = 0
        for c in range(T // 128):
            r0 = c * 128
            hnT, bh = hnTs[c % 2], bhnT[c % 2]
            self.run_fe(fe, Xin, r0, cst["gpre"], bc, hnT, bh)
            kb.dma("act", cs[:], self.rope[r0:r0 + 128, :], writes=[bcs])
            kb.dma("act", Hbb[:], self.RHB[c], writes=[bHbb])
            for (col, dst, bd, sc) in ((0, qrot, bqrot, 1.0), (512, krot, bkrot, 0.125)):
                pP, bpP = pPs[pi % 2]
                rw, brw = raw[pi % 2]
                pi += 1
                self.proj_tok(W, bw, hnT, bh, col, pP, bpP)
                kb.op("act", lambda e, pP=pP, rw=rw, sc=sc: e.activation(rw[:], pP[:, 0:512], AF.Copy, scale=sc),
                      reads=[bpP], writes=[brw])
                self.rotary(rw, brw, dst, bd, cs, bcs, tmp)
            pP, bpP = pPs[pi % 2]
            pi += 1
            self.proj_tok(W, bw, hnT, bh, 1024, pP, bpP)
            kb.op("act", lambda e, pP=pP: e.activation(v[:], pP[:, 0:512], AF.Copy), reads=[bpP], writes=[bv])
            kb.op("pool", lambda e: e.tensor_copy(vb[:], v[:]), reads=[bv], writes=[bvb])
            kb.op("pool", lambda e: e.tensor_tensor(
                vw[:].rearrange("p (h d) -> p h d", h=8), v[:].rearrange("p (h d) -> p h d", h=8),
                tabs[:, 2, :].unsqueeze(2).to_broadcast([128, 8, 64]), ALU.mult), reads=[bv, bc], writes=[bvw])
            pP, bpP = pPs[pi % 2]
            pi += 1
            self.proj_tok(W, bw, hnT, bh, 1536, pP, bpP)
            kb.op("act", lambda e, pP=pP: e.activation(sg[:], pP[:, 0:512], AF.Silu), reads=[bpP], writes=[bsg])
            for (src, bs_, dstT, bdT) in ((qrot, bqrot, qT, bqT), (krot, bkrot, kT, bkT)):
                def tr(e, src=src):
                    for h in range(8):
                        ins = e.transpose(pT[0:64, h * 128:(h + 1) * 128], src[:, h * 64:(h + 1) * 64], self.ident[:])
                    return ins
                kb.op("pe", tr, reads=[bs_, self.bident], writes=[bpT])
                kb.op("act", lambda e, dstT=dstT: e.activation(
                    dstT[:], pT[0:64, :].rearrange("p (c t) -> p c t", c=8), AF.Copy), reads=[bpT], writes=[bdT])

            def msc(e):
                for h in range(8):
                    ins = e.matmul(pSc[:, h * 128:(h + 1) * 128], kT[:, h, :], qT[:, h, :], start=True, stop=True)
                return ins
            kb.op("pe", msc, reads=[bkT, bqT], writes=[bpSc])
            kb.op("act", lambda e: e.activation(Ssc[:], pSc[:], AF.Copy), reads=[bpSc], writes=[bSsc])
            kb.op("dve", lambda e: e.tensor_tensor(WT[:].rearrange("p h j -> p (h j)"), Ssc[:],
                                                   DT[:].rearrange("p h j -> p (h j)"), ALU.mult),
                  reads=[bSsc, bc], writes=[bWT])

            def mi(e):
                for h in range(8):
                    ins = e.matmul(Pi[:, h * 64:(h + 1) * 64], WT[:, h, :], vb[:, h * 64:(h + 1) * 64], start=True, stop=True)
                return ins
            kb.op("pe", mi, reads=[bWT, bvb], writes=[bP3[0]])
            for (Px, bPx, Hx, bHx) in ((Pf, bP3[1], Hfb, bHfb), (Pb, bP3[2], Hbb, bHbb)):
                def mx(e, Px=Px, Hx=Hx):
                    for h in range(8):
                        ins = e.matmul(Px[:, h * 64:(h + 1) * 64], qT[:, h, :], Hx[:, h * 64:(h + 1) * 64],
                                       start=True, stop=True)
                    return ins
                kb.op("pe", mx, reads=[bqT, bHx], writes=[bPx])
            kb.op("act", lambda e: e.activation(Y[:], Pi, AF.Copy), reads=[bP3[0]], writes=[bY])
            for (Px, bPx, t) in ((Pf, bP3[1], 0), (Pb, bP3[2], 1)):
                kb.op("act", lambda e, Px=Px: e.activation(Tt[:], Px, AF.Copy), reads=[bPx], writes=[bTt])
                kb.op("dve", lambda e, t=t: e.tensor_tensor(
                    Tt[:].rearrange("p (h d) -> p h d", h=8), Tt[:].rearrange("p (h d) -> p h d", h=8),
                    tabs[:, t, :].unsqueeze(2).to_broadcast([128, 8, 64]), ALU.mult), reads=[bTt, bc], writes=[bTt])
                kb.op("pool", lambda e: e.tensor_tensor(Y[:], Y[:], Tt[:], ALU.add), reads=[bTt, bY], writes=[bY])
            pP, bpP = pPs[pi % 2]
            pi += 1

            def ms(e, pP=pP):
                for h in range(8):
                    ins = e.matmul(pP[0:64, h * 64:(h + 1) * 64], krot[:, h * 64:(h + 1) * 64], vw[:, h * 64:(h + 1) * 64],
                                   start=True, stop=True)
                return ins
            kb.op("pe", ms, reads=[bkrot, bvw], writes=[bpP])
            kb.op("act", lambda e, pP=pP: e.activation(Ssb[:], pP[0:64, 0:512], AF.Copy), reads=[bpP], writes=[bS])
            kb.op("dve", lambda e: e.tensor_tensor(
                Hf[:].rearrange("p (h d) -> p h d", h=8), Hf[:].rearrange("p (h d) -> p h d", h=8),
                g128[0:64, 0:8].unsqueeze(2).to_broadcast([64, 8, 64]), ALU.mult), reads=[bHf, bc], writes=[bHf])
            kb.op("pool", lambda e: e.tensor_tensor(Hf[:], Hf[:], Ssb[:], ALU.add), reads=[bHf, bS], writes=[bHf])
            kb.op("pool", lambda e: e.tensor_copy(Hfb[:], Hf[:]), reads=[bHf], writes=[bHfb])
            Y3 = Y[:].rearrange("p (h d) -> p h d", h=8)
            T3 = Tt[:].rearrange("p (h d) -> p h d", h=8)
            kb.op("dve", lambda e: e.reduce_sum(st[:, 0:8], Y3, AX.X), reads=[bY], writes=[bst])
            kb.op("dve", lambda e: e.tensor_scalar(st[:, 0:8], st[:, 0:8], 1.0 / 64, None, ALU.mult), reads=[bst], writes=[bst])
            kb.op("dve", lambda e: e.tensor_tensor(Y3, Y3, st[:, 0:8].unsqueeze(2).to_broadcast([128, 8, 64]), ALU.subtract),
                  reads=[bY, bst], writes=[bY])
            kb.op("act", lambda e: e.activation(Tt[:], Y[:], AF.Square), reads=[bY], writes=[bTt])
            kb.op("dve", lambda e: e.reduce_sum(st[:, 8:16], T3, AX.X), reads=[bTt], writes=[bst])
            kb.op("act", lambda e: e.activation(st[:, 8:16], st[:, 8:16], AF.Sqrt, bias=EPS, scale=1.0 / 64), reads=[bst], writes=[bst])
            kb.op("dve", lambda e: e.reciprocal(st[:, 8:16], st[:, 8:16]), reads=[bst], writes=[bst])
            kb.op("dve", lambda e: e.tensor_tensor(Y3, Y3, st[:, 8:16].unsqueeze(2).to_broadcast([128, 8, 64]), ALU.mult),
                  reads=[bY, bst], writes=[bY])
            kb.op("pool", lambda e: e.tensor_tensor(Y[:], Y[:], gng[:], ALU.mult), reads=[bY, bc], writes=[bY])
            kb.op("pool", lambda e: e.tensor_tensor(yo[:], Y[:], sg[:], ALU.mult), reads=[bY, bsg], writes=[byo])
            kb.dma("sp", self.YN[r0:r0 + 128, 0:512], yo[:], reads=[byo])

    def na_stage(self, i):
        kb = self.kb
        T = self.T
        rows = T // 64
        tb = kb.sb("tb", [128, 2048])
        btb = kb.buf()
        kws = [(kb.sb("kw", [64, 8, 512], BF16), kb.buf()) for _ in range(2)]
        qws = [(kb.sb("qw", [64, 8, 64], BF16), kb.buf()) for _ in range(2)]
        vws = [(kb.sb("vwin", [128, 4, 8, 80], BF16), kb.buf()) for _ in range(2)]
        for (vw_, bvw_) in vws:
            kb.op("pool", lambda e, vw_=vw_: e.memset(vw_[:], 1.0), writes=[bvw_])
        Sb = kb.sb("Sb", [128, 2048])
        bSb = kb.buf()
        PTs = [(kb.sb("PT", [128, 2048], BF16), kb.buf()) for _ in range(2)]
        Osb = kb.sb("Osb", [64, 8, 65])
        bO = kb.buf()
        rc = kb.sb("rc", [64, 8])
        brc = kb.buf()
        nas = [(kb.sb("na", [64, 512], BF16), kb.buf()) for _ in range(2)]
        pST = kb.ps("pST", [128, 2048])
        bpST = kb.buf()
        pO = kb.ps("pO", [128, 1024])
        bpO = kb.buf()
        prev_s = None
        import os
        NCUT = int(os.environ.get("NA_CUT", "9"))
        for r in range(rows):
            start = min(max(r - 4, 0), rows - 8)
            s = r - start
            s0 = start * 64
            if s != prev_s:
                kb.dma("sp", tb[:], self.natab[i, s], writes=[btb])
                prev_s = s
            kw, bkw = kws[r % 2]
            qw, bqw = qws[r % 2]
            vw_, bvw_ = vws[r % 2]
            kb.dma("sp", kw[:], self.NKT[:, :, s0:s0 + 512].rearrange("c (two p) t -> p (c two) t", two=2), writes=[bkw])
            kb.dma("act", qw[:], self.NQT[:, :, r * 64:(r + 1) * 64].rearrange("c (two p) t -> p (c two) t", two=2),
                   writes=[bqw])
            for ck in range(4):
                kb.dma("act", vw_[:, ck, :, 0:64],
                       self.NV[s0 + ck * 128:s0 + (ck + 1) * 128, :].rearrange("p (h d) -> p h d", h=8), writes=[bvw_])

            if NCUT < 2:
                continue

            def mst(e, kw=kw, qw=qw):
                for ck in range(4):
                    for h in range(8):
                        o = (ck * 8 + h) * 64
                        ins = e.matmul(pST[:, o:o + 64], kw[:, h, ck * 128:(ck + 1) * 128],
                                       qw[:, h, :], start=True, stop=True)
                return ins
            kb.op("pe", mst, reads=[bkw, bqw], writes=[bpST])
            if NCUT < 3:
                continue
            kb.op("act", lambda e: e.activation(Sb[:], pST[:], AF.Copy), reads=[bpST], writes=[bSb])
            kb.op("dve", lambda e: e.tensor_tensor(Sb[:], Sb[:], tb[:], ALU.add), reads=[bSb, btb], writes=[bSb])
            PT, bPT = PTs[r % 2]
            kb.op("act", lambda e, PT=PT: e.activation(PT[:], Sb[:], AF.Exp), reads=[bSb], writes=[bPT])

            if NCUT < 4:
                continue

            def mpv(e, PT=PT, vw_=vw_):
                for h in range(8):
                    oc = (h % 4) * 80 + (h // 4) * 512
                    for ck in range(4):
                        o = (ck * 8 + h) * 64
                        ins = e.matmul(pO[0:64, oc:oc + 65], PT[:, o:o + 64], vw_[:, ck, h, 0:65],
                                       start=(ck == 0), stop=(ck == 3))
                return ins
            kb.op("pe", mpv, reads=[bPT, bvw_], writes=[bpO])
            if NCUT < 5:
                continue
            kb.op("act", lambda e: e.activation(Osb[:, 0:4, :], pO[0:64, 0:320].rearrange("p (h d) -> p h d", h=4)[:, :, 0:65], AF.Copy),
                  reads=[bpO], writes=[bO])
            kb.op("act", lambda e: e.activation(Osb[:, 4:8, :], pO[0:64, 512:832].rearrange("p (h d) -> p h d", h=4)[:, :, 0:65], AF.Copy),
                  reads=[bpO], writes=[bO])
            kb.op("dve", lambda e: e.reciprocal(rc[:].unsqueeze(2), Osb[:, :, 64:65]), reads=[bO], writes=[brc])
            na, bna = nas[r % 2]
            kb.op("dve", lambda e, na=na: e.tensor_tensor(
                na[:].rearrange("p (h d) -> p h d", h=8), Osb[:, :, 0:64],
                rc[:].unsqueeze(2).to_broadcast([64, 8, 64]), ALU.mult), reads=[bO, brc], writes=[bna])
            kb.dma("sp", self.YN[r * 64:(r + 1) * 64, 512:1024], na[:], reads=[bna])

    def outproj_stage(self, l, w_src, Cin, YN, Xin, Xout):
        kb = self.kb
        T = self.T
        KC = Cin // 128
        W = kb.sb("wout", [128, KC, D], BF16)
        bw = kb.buf()
        for k in range(KC):
            kb.dma("pool", W[:, k, :], w_src[k * 128:(k + 1) * 128, :], writes=[bw])
        gpost = kb.sb("gpost", [128, D])
        bc = kb.buf()
        kb.dma("sp", gpost[:], self.gains[l * 4 + 1], writes=[bc])
        yns = [(kb.sb("yn", [128, Cin], BF16), kb.buf()) for _ in range(2)]
        YTs = [(kb.sb("YT", [128, KC, 128], BF16), kb.buf()) for _ in range(2)]
        xrs = [(kb.sb("xr", [128, D]), kb.buf()) for _ in range(2)]
        xos = [(kb.sb("xo", [128, D]), kb.buf()) for _ in range(2)]
        junk = kb.sb("junk", [128, D])
        bjunk = kb.buf()
        ss = kb.sb("ss", [128, 1])
        bss = kb.buf()
        pTs = [(kb.ps("pT", [128, D], BF16), kb.buf()) for _ in range(2)]
        pm = kb.ps("pm", [128, D])
        bpm = kb.buf()
        ti = 0
        for c in range(T // 128):
            r0 = c * 128
            yn, byn = yns[c % 2]
            YT, bYT = YTs[c % 2]
            xr, bxr = xrs[c % 2]
            xo, bxo = xos[c % 2]
            kb.dma("act", yn[:], YN[r0:r0 + 128, 0:Cin], writes=[byn])
            kb.dma("sp", xr[:], Xin[r0:r0 + 128, :], writes=[bxr])
            for r in range(KC // 8):
                pT, bpT = pTs[ti % 2]
                ti += 1

                def tr(e, yn=yn, pT=pT, r=r):
                    for k in range(8):
                        kk = r * 8 + k
                        ins = e.transpose(pT[:, k * 128:(k + 1) * 128], yn[:, kk * 128:(kk + 1) * 128], self.ident[:])
                    return ins
                kb.op("pe", tr, reads=[byn, self.bident], writes=[bpT])
                kb.op("act", lambda e, YT=YT, pT=pT, r=r: e.activation(
                    YT[:, r * 8:(r + 1) * 8, :], pT[:].rearrange("p (k t) -> p k t", k=8), AF.Copy),
                    reads=[bpT], writes=[bYT])

            def mm(e, YT=YT):
                for nh in range(2):
                    for k in range(KC):
                        ins = e.matmul(pm[:, nh * 512:(nh + 1) * 512], YT[:, k, :], W[:, k, nh * 512:(nh + 1) * 512],
                                       start=(k == 0), stop=(k == KC - 1))
                return ins
            kb.op("pe", mm, reads=[bYT, bw], writes=[bpm])
            self.rms_stats(pm[:], bpm, ss, bss, junk, bjunk)
            kb.op("act", lambda e, xo=xo: e.activation(xo[:], pm[:], AF.Copy, scale=ss[:, 0:1]),
                  reads=[bpm, bss], writes=[bxo])
            kb.op("dve", lambda e, xo=xo: e.tensor_tensor(xo[:], xo[:], gpost[:], ALU.mult),
                  reads=[bxo, bc], writes=[bxo])
            kb.op("pool", lambda e, xo=xo, xr=xr: e.tensor_tensor(xo[:], xo[:], xr[:], ALU.add),
                  reads=[bxo, bxr], writes=[bxo])
            kb.dma("sp", Xout[r0:r0 + 128, :], xo[:], reads=[bxo])

    def ffn_stage(self, l, Xin, Xout, BT=512):
        kb = self.kb
        T = self.T
        NT = BT // 128
        ident, bident = self.ident, self.bident
        W_up = kb.sb("wup", [128, 8, 2 * FF], BF16)
        W_dn = kb.sb("wdn", [128, NCH, D], BF16)
        bwu = kb.bufs(8)
        bwd = kb.bufs(NCH)
        for k in range(8):
            kb.dma("pool", W_up[:, k, :], self.ffn_w_up[l, k * 128:(k + 1) * 128, :], writes=[bwu[k]])
        for c in range(NCH):
            kb.dma("pool", W_dn[:, c, :], self.ffn_w_down[l, c * 128:(c + 1) * 128, :], writes=[bwd[c]])
        gpre = kb.sb("gpre", [128, D])
        gpost = kb.sb("gpost", [128, D])
        fcw = kb.sb("fcw", [128, 2 * NCH, 3])
        fcb = kb.sb("fcb", [128, 2 * NCH])
        bconst = kb.buf()
        kb.dma("sp", gpre[:], self.gains[l * 4 + 2], writes=[bconst])
        kb.dma("sp", gpost[:], self.gains[l * 4 + 3], writes=[bconst])
        kb.dma("sp", fcw[:], self.fcw[l].rearrange("p (c w) -> p c w", w=3), writes=[bconst])
        kb.dma("sp", fcb[:], self.fcb[l], writes=[bconst])

        xts = [kb.sb("xt", [128, D])] * 2
        bxt = [kb.buf()] * 2
        hns = [kb.sb("hn", [128, D], BF16)] * 2
        bhn = [kb.buf()] * 2
        sss = [kb.sb("ss", [128, 1]) for _ in range(2)]
        bss = kb.bufs(2)
        xh = kb.sb("xh", [2, D])
        bxh = kb.buf()
        hnh = kb.sb("hnh", [2, D], BF16)
        bhnh = kb.buf()
        ssh = kb.sb("ssh", [2, 1])
        bssh = kb.buf()
        hnTs = [kb.sb("hnT", [128, 8, BT + 2], BF16)] * 2
        bhnT = [kb.buf()] * 2
        gT = kb.sb("gT", [128, NCH, BT], BF16)
        bgT = kb.bufs(NCH)
        cgs = [kb.sb("cg", [128, BT]) for _ in range(2)]
        cvs = [kb.sb("cv", [128, BT]) for _ in range(2)]
        ggs = [kb.sb("gg", [128, BT])] * 2
        bcg, bcv, bgg = kb.bufs(2), kb.bufs(2), [kb.buf()] * 2
        ugs = [kb.sb("ug", [128, BT + 2])] * 2
        uvs = [kb.sb("uv", [128, BT + 2])] * 2
        bug, buv = [kb.buf()] * 2, [kb.buf()] * 2
        xrs = [kb.sb("xr", [128, D]) for _ in range(1)]
        bxr = kb.bufs(1)
        junk, bjunk = xrs[0], bxr[0]
        xos = [kb.sb("xo", [128, D])] * 2
        bxo = [kb.buf()] * 2
        ss2 = kb.sb("ss2", [128, 1])
        bss2 = kb.buf()

        psG = [kb.ps("psG", [128, 512]) for _ in range(2)]
        psV = [kb.ps("psV", [128, 512]) for _ in range(2)]
        bpsA = kb.bufs(2)
        psH = kb.ps("psH", [128, 512])
        bpsH = kb.buf()
        pT = kb.ps("pT", [128, D], BF16)
        bpT = kb.buf()
        pTh, bpTh = pT, bpT
        pf = kb.ps("pf", [128, D])
        bpf = kb.buf()

        ti = 0
        ci = 0
        import os
        CUT = int(os.environ.get("FFN_CUT", "9"))
        for b in range(T // BT if CUT > 1 else 0):
            t0 = b * BT
            hnT = hnTs[b % 2]
            bh = bhnT[b % 2]
            for j in range(NT):
                xt, bx = xts[ti % 2], bxt[ti % 2]
                hn, bn = hns[ti % 2], bhn[ti % 2]
                ss, bs = sss[ti % 2], bss[ti % 2]
                ti += 1
                r0 = t0 + j * 128
                kb.dma("sp", xt[:], Xin[r0:r0 + 128, :], writes=[bx])
                self.rms_stats(xt[:], bx, ss, bs, junk, bjunk)
                kb.op("dve", lambda e, hn=hn, xt=xt, ss=ss: e.scalar_tensor_tensor(
                    hn[:], xt[:], ss[:, 0:1], gpre[:], ALU.mult, ALU.mult),
                    reads=[bx, bs, bconst], writes=[bn])

                def tr(e, hn=hn):
                    for k in range(8):
                        ins = e.transpose(pT[:, k * 128:(k + 1) * 128], hn[:, k * 128:(k + 1) * 128], ident[:])
                    return ins
                kb.op("pe", tr, reads=[bn, bident], writes=[bpT])
                kb.op("act", lambda e, hnT=hnT, j=j: e.activation(
                    hnT[:, :, j * 128:(j + 1) * 128], pT[:].rearrange("p (k t) -> p k t", k=8), AF.Copy),
                    reads=[bpT], writes=[bh])
            kb.op("pool", lambda e: e.memset(xh[:], 0.0), writes=[bxh])
            if t0 - 1 >= 0:
                kb.dma("sp", xh[0:1, :], Xin[t0 - 1:t0, :], writes=[bxh])
            if t0 + BT < T:
                kb.dma("sp", xh[1:2, :], Xin[t0 + BT:t0 + BT + 1, :], writes=[bxh])
            self.rms_stats(xh[:], bxh, ssh, bssh, junk, bjunk, nparts=2)
            kb.op("dve", lambda e: e.scalar_tensor_tensor(
                hnh[:], xh[:], ssh[:, 0:1], gpre[0:2, :], ALU.mult, ALU.mult),
                reads=[bxh, bssh, bconst], writes=[bhnh])

            def trh(e):
                for k in range(8):
                    ins = e.transpose(pTh[:, k * 2:(k + 1) * 2], hnh[:, k * 128:(k + 1) * 128], ident[0:2, 0:2])
                return ins
            kb.op("pe", trh, reads=[bhnh, bident], writes=[bpTh])
            kb.op("act", lambda e, hnT=hnT: e.activation(
                hnT[:, :, BT:BT + 2], pTh[:, 0:16].rearrange("p (k t) -> p k t", k=8), AF.Copy),
                reads=[bpTh], writes=[bh])

            for c in range(NCH if CUT > 2 else 0):
                par = ci % 2
                ci += 1
                pg = psG[par][:, 0:BT]
                pv = psV[par][:, 0:BT]
                ph = psH[:, 0:4]
                cg, cv, gg = cgs[par], cvs[par], ggs[par]

                SUB = os.environ.get("FFN_SUB", "")

                def mm(e, c=c, pg=pg, pv=pv, ph=ph, hnT=hnT):
                    lst = ((pg, c * 128, slice(0, BT)), (pv, FF + c * 128, slice(0, BT)),
                           (ph[:, 0:2], c * 128, slice(BT, BT + 2)),
                           (ph[:, 2:4], FF + c * 128, slice(BT, BT + 2)))
                    if "h" in SUB:
                        lst = lst[:2]
                    for (dst, col, rhs_sl) in lst:
                        for k in range(8):
                            ins = e.matmul(dst, W_up[:, k, col:col + 128], hnT[:, k, rhs_sl],
                                           start=(k == 0), stop=(k == 7))
                    return ins
                kb.op("pe", mm, reads=bwu + [bh], writes=[bpsA[par], bpsH])
                pairs = ((pg, 0, cg, bcg[par], c, ugs[par], bug[par]), (pv, 2, cv, bcv[par], NCH + c, uvs[par], buv[par]))
                for (src, hoff, dst, bd, cc, u, bu) in pairs:
                    kb.op("act", lambda e, src=src, u=u: e.activation(u[:, 0:BT], src, AF.Copy),
                          reads=[bpsA[par]], writes=[bu])
                    kb.op("act", lambda e, ph=ph, hoff=hoff, u=u: e.activation(
                        u[:, BT:BT + 2], ph[:, hoff:hoff + 2], AF.Copy), reads=[bpsH], writes=[bu])
                    kb.op("act", lambda e, src=src, dst=dst, cc=cc: e.activation(
                        dst[:], src, AF.Identity, bias=fcb[:, cc:cc + 1], scale=fcw[:, cc, 1:2]),
                        reads=[bpsA[par], bconst], writes=[bd])
                for tap in range(4):
                    for (src, hoff, dst, bd, cc, u, bu) in pairs:
                        if tap == 0:
                            kb.op("dve", lambda e, u=u, dst=dst, cc=cc: e.scalar_tensor_tensor(
                                dst[:, 1:BT], u[:, 0:BT - 1], fcw[:, cc, 0:1], dst[:, 1:BT], ALU.mult, ALU.add),
                                reads=[bu, bconst, bd], writes=[bd])
                        elif tap == 1:
                            kb.op("dve", lambda e, u=u, dst=dst, cc=cc: e.scalar_tensor_tensor(
                                dst[:, 0:BT - 1], u[:, 1:BT], fcw[:, cc, 2:3], dst[:, 0:BT - 1], ALU.mult, ALU.add),
                                reads=[bu, bconst, bd], writes=[bd])
                        elif tap == 2:
                            kb.op("dve", lambda e, u=u, dst=dst, cc=cc: e.scalar_tensor_tensor(
                                dst[:, 0:1], u[:, BT:BT + 1], fcw[:, cc, 0:1], dst[:, 0:1], ALU.mult, ALU.add),
                                reads=[bu, bconst, bd], writes=[bd])
                        else:
                            kb.op("dve", lambda e, u=u, dst=dst, cc=cc: e.scalar_tensor_tensor(
                                dst[:, BT - 1:BT], u[:, BT + 1:BT + 2], fcw[:, cc, 2:3], dst[:, BT - 1:BT],
                                ALU.mult, ALU.add),
                                reads=[bu, bconst, bd], writes=[bd])
                kb.op("act", lambda e, gg=gg, cg=cg: e.activation(gg[:], cg[:], AF.Gelu_apprx_tanh),
                      reads=[bcg[par]], writes=[bgg[par]])
                kb.op("pool", lambda e, c=c, gg=gg, cv=cv: e.tensor_tensor(gT[:, c, :], gg[:], cv[:], ALU.mult),
                      reads=[bgg[par], bcv[par]], writes=[bgT[c]])

            for j in range(NT if CUT > 3 else 0):
                r0 = t0 + j * 128

                def dn(e, j=j):
                    for nh in range(2):
                        for c in range(NCH):
                            ins = e.matmul(pf[:, nh * 512:(nh + 1) * 512], gT[:, c, j * 128:(j + 1) * 128],
                                           W_dn[:, c, nh * 512:(nh + 1) * 512], start=(c == 0), stop=(c == NCH - 1))
                    return ins
                kb.op("pe", dn, reads=bgT + bwd, writes=[bpf])
                self.rms_stats(pf[:], bpf, ss2, bss2, junk, bjunk)
                xr, bxrr = xrs[0], bxr[0]
                xo, bxoo = xos[j % 2], bxo[j % 2]
                kb.dma("sp", xr[:], Xin[r0:r0 + 128, :], writes=[bxrr])
                kb.op("act", lambda e, xo=xo: e.activation(xo[:], pf[:], AF.Copy, scale=ss2[:, 0:1]),
                      reads=[bpf, bss2], writes=[bxoo])
                kb.op("dve", lambda e, xo=xo: e.tensor_tensor(xo[:], xo[:], gpost[:], ALU.mult),
                      reads=[bxoo, bconst], writes=[bxoo])
                kb.op("pool", lambda e, xo=xo, xr=xr: e.tensor_tensor(xo[:], xo[:], xr[:], ALU.add),
                      reads=[bxoo, bxrr], writes=[bxoo])
                kb.dma("sp", Xout[r0:r0 + 128, :], xo[:], reads=[bxoo])


def host_rope(pos):
    half = 32
    inv = (1.0 / (10000.0 ** (np.arange(half, dtype=np.float32) / half))).astype(np.float32)
    ang = pos.astype(np.float32)[:, None] * inv[None, :]
    return np.concatenate([np.cos(ang), np.sin(ang)], axis=1).astype(np.float32)


def host_even(inp):
    lgt = inp["ab_ret_decay_logit"].reshape(2, 1, 16)
    rlogit = np.ascontiguousarray(np.broadcast_to(lgt, (2, 128, 16))).astype(np.float32)
    p = np.arange(128, dtype=np.float32)
    ridx = np.stack([p + 1, 128 - p, 127 - p, p], axis=1).astype(np.float32)
    l_ = np.arange(128)[:, None]
    j_ = np.arange(128)[None, :]
    rdj = np.stack([np.maximum(j_ - l_, 0), np.maximum(l_ - j_, 0)], 0).astype(np.float32)
    rgn = np.ascontiguousarray(np.broadcast_to(inp["ab_ret_gn_g"].reshape(2, 1, 512), (2, 128, 512))).astype(np.float32)
    rpb = inp["ab_na_rpb"]
    kk = np.arange(512)
    w = kk // 64
    kc = kk % 64
    q = np.arange(64)
    cstart = np.clip(q - 8, 0, 48)
    valid = (kc[:, None] >= cstart[None, :]) & (kc[:, None] < cstart[None, :] + 16)
    dc = np.clip(kc[:, None] - q[None, :], -15, 15) + 15
    natab = np.empty((2, 8, 512, 8, 64), np.float32)
    for s in range(8):
        dr = np.clip(w - s + 7, 0, 14)
        g = rpb[:, :, dr[:, None], dc]
        g = np.where(valid[None, None], g, np.float32(-30000.0))
        natab[:, s] = g.transpose(0, 2, 1, 3)
    natab = natab.reshape(2, 8, 4, 128, 8, 64).transpose(0, 1, 3, 2, 4, 5).reshape(2, 8, 128, 2048)
    return {"ab_w_in": np.ascontiguousarray(inp["ab_w_in"]), "ab_w_out": np.ascontiguousarray(inp["ab_w_out"]),
            "rlogit": rlogit, "ridx": ridx, "rdj": rdj, "rgn": rgn, "natab": np.ascontiguousarray(natab)}

def host_prep(inp, layers=range(4)):
    L = L_TOTAL
    g = np.stack([inp["norm_mix_pre"], inp["norm_mix_post"], inp["norm_ffn_pre"], inp["norm_ffn_post"]], axis=1)
    gains = np.ascontiguousarray(np.broadcast_to(g.reshape(L * 4, 1, D), (L * 4, 128, D))).astype(np.float32)
    cw = inp["ffn_conv_w"]
    fcw = np.ascontiguousarray(cw.reshape(L, 3, 2 * NCH, 128).transpose(0, 3, 2, 1)).reshape(L, 128, 2 * NCH * 3)
    fcb = np.ascontiguousarray(inp["ffn_conv_b"].reshape(L, 2 * NCH, 128).transpose(0, 2, 1))
    cmats = np.zeros((5, 128, 128), np.float32)
    k = np.arange(128)[:, None]
    j = np.arange(128)[None, :]
    cmats[0] = k <= j
    cmats[1] = k < j
    cmats[2] = k > j
    cmats[3] = k >= j
    cmats[4] = 1.0
    ccw = inp["c_conv_w"]
    scw = np.ascontiguousarray(ccw.reshape(2, 5, 24, 128).transpose(0, 3, 2, 1)).reshape(2, 128, 120)
    scb = np.ascontiguousarray(inp["c_conv_b"].reshape(2, 24, 128).transpose(0, 2, 1))

    def rep(a, n):
        return np.ascontiguousarray(np.broadcast_to(a.reshape(2, 1, n), (2, 128, n))).astype(np.float32)
    extra = {"c_w_in": np.ascontiguousarray(inp["c_w_in"]), "c_w_out": np.ascontiguousarray(inp["c_w_out"]),
             "scw": scw.astype(np.float32), "scb": scb.astype(np.float32),
             "sdtb": rep(inp["c_dt_bias"], 64), "salog": rep(inp["c_a_log"], 64),
             "sdsk": rep(inp["c_d_skip"], 32), "sng": rep(inp["c_norm_g"], 2048), "cmats": cmats}
    extra.update(host_even(inp))
    return {**extra, "gains": gains, "ffn_w_up": np.ascontiguousarray(inp["ffn_w_up"]),
            "ffn_w_down": np.ascontiguousarray(inp["ffn_w_down"]),
            "fcw": fcw.astype(np.float32), "fcb": fcb.astype(np.float32)}


_CACHE = {}


def kernel(**inputs):
    x = np.asarray(inputs["x"], dtype=np.float32)
    B, S, _ = x.shape
    n_cores = 8
    cpb = n_cores // B
    if "prog" not in _CACHE:
        _CACHE["prog"] = Prog(S, [0, 1, 2, 3], do_mix=True, do_ffn=True)
    P = _CACHE["prog"]
    hp = host_prep({k: np.asarray(v, dtype=np.float32) for k, v in inputs.items()})
    hp["rope"] = host_rope(np.arange(S))
    in_maps = []
    for c in range(n_cores):
        m = dict(hp)
        m["x"] = np.ascontiguousarray(x[c // cpb])
        in_maps.append(m)
    res = run_bass_kernel_spmd(P.nc, in_maps, core_ids=list(range(n_cores)))
    out = np.stack([res.results[b * cpb]["out"] for b in range(B)], axis=0)
    return out.astype(np.float32)
```

```python
from contextlib import ExitStack
import numpy as np
import concourse.bass as bass
import concourse.mybir as mybir
from concourse.bass_utils import run_bass_kernel_spmd

F32 = mybir.dt.float32
BF16 = mybir.dt.bfloat16
ALU = mybir.AluOpType
AF = mybir.ActivationFunctionType
AX = mybir.AxisListType

EPOCH = 8192
SAME_ENGINE_SYNC = True
OWN_DIST = 10 ** 9


class Buf:
    __slots__ = ("name", "w", "r")

    def __init__(self, name):
        self.name = name
        self.w = None
        self.r = []


class Op:
    __slots__ = ("eng", "emit", "waits", "inc")


class KB:
    ENGS = ("sp", "act", "pool", "pe", "dve")

    def __init__(self, nc, n_dma_sems=16):
        self.nc = nc
        self.gstack = ExitStack()
        self.stack = self.gstack
        self.stream = {e: [] for e in self.ENGS}
        self.count = {e: 0 for e in self.ENGS}
        self.csems = {e: [] for e in self.ENGS}
        self.waited = {e: {} for e in self.ENGS}
        self.dma_pool = {}
        self.dma_next = {}
        self.n_dma_sems = n_dma_sems
        self.semobjs = {}
        self.nbuf = 0
        self.nalloc = 0

    def sem(self, name):
        s = self.gstack.enter_context(self.nc.semaphore(name))
        self.semobjs[name] = s
        return name

    def sb(self, name, shape, dt=F32):
        self.nalloc += 1
        return self.stack.enter_context(self.nc.sbuf_tensor(f"{name}_{self.nalloc}", list(shape), dt))

    def ps(self, name, shape, dt=F32):
        self.nalloc += 1
        return self.stack.enter_context(self.nc.psum_tensor(f"{name}_{self.nalloc}", list(shape), dt))

    def buf(self, name=None):
        self.nbuf += 1
        return Buf(name or f"b{self.nbuf}")

    def bufs(self, n):
        return [self.buf() for _ in range(n)]

    def _deps(self, reads, writes):
        ids = []
        for b in reads:
            if b.w is not None:
                ids.append(b.w)
        for b in writes:
            if b.w is not None:
                ids.append(b.w)
            ids.extend(b.r)
        return ids

    def _finish(self, eng, emit, reads, writes, cid3, ids):
        cid = cid3[:2]
        need = {}
        for t in ids:
            if t[1] > need.get(t[0], 0):
                need[t[0]] = t[1]
        waits = []
        wd = self.waited[eng]
        own = self.csems[eng]
        cur = self.count[eng] if cid3[2] == 1 else None
        for s, v in need.items():
            if s in own:
                if not SAME_ENGINE_SYNC:
                    continue
                if cur is not None and cur - (own.index(s) * EPOCH + v) >= OWN_DIST:
                    continue
            if wd.get(s, 0) >= v:
                continue
            wd[s] = v
            waits.append((s, v))
        op = Op()
        op.eng, op.emit, op.waits, op.inc = eng, emit, waits, cid3
        self.stream[eng].append(op)
        for b in reads:
            b.r.append(cid)
        for b in writes:
            b.w = cid
            b.r = []
        return cid

    def op(self, eng, emit, reads=(), writes=()):
        n = self.count[eng]
        self.count[eng] = n + 1
        k = n // EPOCH
        while len(self.csems[eng]) <= k:
            self.csems[eng].append(self.sem(f"c_{eng}_{len(self.csems[eng])}"))
        cid3 = (self.csems[eng][k], (n % EPOCH) + 1, 1)
        return self._finish(eng, emit, reads, writes, cid3, self._deps(reads, writes))

    def dma(self, eng, out, in_, reads=(), writes=(), **kw):
        if eng not in self.dma_pool:
            self.dma_pool[eng] = [[self.sem(f"d_{eng}_{i}"), 0] for i in range(self.n_dma_sems)]
            self.dma_next[eng] = 0
        i = self.dma_next[eng]
        self.dma_next[eng] = (i + 1) % self.n_dma_sems
        slot = self.dma_pool[eng][i]
        prev = slot[1]
        slot[1] = prev + 16
        cid3 = (slot[0], slot[1], 16)

        def emit(e, out=out, in_=in_, kw=kw):
            return e.dma_start(out=out, in_=in_, **kw)

        ids = self._deps(reads, writes)
        if prev > 0:
            ids = ids + [(slot[0], prev)]
        return self._finish(eng, emit, reads, writes, cid3, ids)

    def coll(self, emit, reads=(), writes=()):
        eng = "pool"
        if eng not in self.dma_pool:
            self.dma_pool[eng] = [[self.sem(f"d_{eng}_{i}"), 0] for i in range(self.n_dma_sems)]
            self.dma_next[eng] = 0
        i = self.dma_next[eng]
        self.dma_next[eng] = (i + 1) % self.n_dma_sems
        slot = self.dma_pool[eng][i]
        prev = slot[1]
        slot[1] = prev + 16
        cid3 = (slot[0], slot[1], 16)
        ids = self._deps(reads, writes)
        if prev > 0:
            ids = ids + [(slot[0], prev)]
        return self._finish(eng, emit, reads, writes, cid3, ids)

    def all_ids(self):
        ids = []
        for e in self.ENGS:
            n = self.count[e]
            if n > 0:
                ids.append((self.csems[e][(n - 1) // EPOCH], ((n - 1) % EPOCH) + 1))
        for e in self.dma_pool:
            for (s, v) in self.dma_pool[e]:
                if v > 0:
                    ids.append((s, v))
        return ids

    def barrier(self):
        ids = self.all_ids()
        for e in self.ENGS:
            waits = []
            wd = self.waited[e]
            for (s, v) in ids:
                if wd.get(s, 0) >= v:
                    continue
                wd[s] = v
                waits.append((s, v))
            op = Op()
            op.eng, op.emit, op.waits, op.inc = e, None, waits, None
            self.stream[e].append(op)

    def emit_all(self):
        nc = self.nc
        so = self.semobjs
        with nc.Block() as block:
            decos = {"sp": block.sync, "act": block.scalar, "pool": block.gpsimd,
                     "pe": block.tensor, "dve": block.vector}
            for name in self.ENGS:
                ops = self.stream[name]
                if not ops:
                    continue

                def f(eng, ops=ops):
                    for op in ops:
                        for (s, v) in op.waits:
                            eng.wait_ge(so[s], v)
                        if op.emit is None:
                            continue
                        ins = op.emit(eng)
                        if op.inc is not None:
                            ins.then_inc(so[op.inc[0]], op.inc[2])
                decos[name](f)
        self.stream = {e: [] for e in self.ENGS}

    def stage(self):
        return _Stage(self)


class _Stage:
    def __init__(self, kb):
        self.kb = kb

    def __enter__(self):
        self.st = ExitStack()
        self.kb.stack = self.st
        return self

    def __exit__(self, *a):
        self.kb.barrier()
        self.kb.emit_all()
        self.kb.stack = self.kb.gstack
        self.st.close()
        return False


D = 1024
FF = 2816
NCH = FF // 128
EPS = 1e-6
L_TOTAL = 4


class Prog:
    def __init__(self, T, layers, do_mix=True, do_ffn=True):
        self.T = T
        self.layers = layers
        nc = bass.Bass("TRN2", target_bir_lowering=False)
        self.nc = nc
        kb = KB(nc)
        self.kb = kb
        L = L_TOTAL

        def inp(name, shape):
            return nc.dram_tensor(name, list(shape), F32, kind="ExternalInput").ap()

        self.x = inp("x", [T, D])
        self.gains = inp("gains", [L * 4, 128, D])
        self.ffn_w_up = inp("ffn_w_up", [L, D, 2 * FF])
        self.ffn_w_down = inp("ffn_w_down", [L, FF, D])
        self.fcw = inp("fcw", [L, 128, 2 * NCH * 3])
        self.fcb = inp("fcb", [L, 128, 2 * NCH])
        self.c_w_in = inp("c_w_in", [2, D, 5184])
        self.c_w_out = inp("c_w_out", [2, 2048, D])
        self.scw = inp("scw", [2, 128, 24 * 5])
        self.scb = inp("scb", [2, 128, 24])
        self.sdtb = inp("sdtb", [2, 128, 64])
        self.salog = inp("salog", [2, 128, 64])
        self.sdsk = inp("sdsk", [2, 128, 32])
        self.sng = inp("sng", [2, 128, 2048])
        self.cmats = inp("cmats", [5, 128, 128])
        self.ab_w_in = inp("ab_w_in", [2, D, 3584])
        self.ab_w_out = inp("ab_w_out", [2, D, D])
        self.rlogit = inp("rlogit", [2, 128, 16])
        self.ridx = inp("ridx", [128, 4])
        self.rdj = inp("rdj", [2, 128, 128])
        self.rgn = inp("rgn", [2, 128, 512])
        self.rope = inp("rope", [T, 64])
        self.natab = inp("natab", [2, 8, 128, 2048])
        self.out = nc.dram_tensor("out", [T, D], F32, kind="ExternalOutput").ap()
        self.YN = nc.dram_tensor("YN", [T, 2048], BF16).ap()
        self.HB = nc.dram_tensor("HB", [T // 128, 128, 2048], BF16).ap()
        self.RHB = nc.dram_tensor("RHB", [T // 128, 64, 512], BF16).ap()
        self.XTK = nc.dram_tensor("XTK", [T, 2048], BF16).ap()
        self.BTK = nc.dram_tensor("BTK", [T, 512], BF16).ap()
        self.BTF = nc.dram_tensor("BTF", [4, 128, T], BF16).ap()
        self.CTF = nc.dram_tensor("CTF", [4, 128, T], BF16).ap()
        self.DTL = nc.dram_tensor("DTL", [T, 128], F32).ap()
        self.ZS = nc.dram_tensor("ZS", [T, 2048], BF16).ap()
        self.NQT = nc.dram_tensor("NQT", [4, 128, T], BF16).ap()
        self.NKT = nc.dram_tensor("NKT", [4, 128, T], BF16).ap()
        self.NV = nc.dram_tensor("NV", [T, 512], BF16).ap()
        self.Xa = nc.dram_tensor("Xa", [T, D], F32).ap()
        self.Xb = nc.dram_tensor("Xb", [T, D], F32).ap()

        self.ident = kb.sb("ident", [128, 128], BF16)
        self.bident = kb.buf()
        with kb.stage():
            one = kb.sb("one", [128, 128])
            idf = kb.sb("idf", [128, 128])
            b1, b2 = kb.buf(), kb.buf()
            kb.op("pool", lambda e: e.memset(one[:], 1.0), writes=[b1])
            kb.op("pool", lambda e: e.affine_select(idf[:], one[:], [[-1, 128]], ALU.is_equal, 0.0,
                                                    base=0, channel_multiplier=1), reads=[b1], writes=[b2])
            kb.op("pool", lambda e: e.tensor_copy(self.ident[:], idf[:]), reads=[b2], writes=[self.bident])

        cur = self.x
        for li, l in enumerate(layers):
            last = li == len(layers) - 1
            if do_mix:
                i = l // 2
                mdst = self.out if (last and not do_ffn) else self.Xa
                if l % 2 == 0:
                    import os
                    EVS = os.environ.get("EV_STAGES", "1234")
                    if "1" in EVS:
                        with kb.stage():
                            self.even_p1(i, l, cur)
                    if "2" in EVS:
                        with kb.stage():
                            self.na_stage(i)
                    if "3" in EVS:
                        with kb.stage():
                            self.even_p3(i, l, cur)
                    if "4" in EVS:
                        with kb.stage():
                            self.outproj_stage(l, self.ab_w_out[i], 1024, self.YN, cur, mdst)
                else:
                    import os
                    SDS = os.environ.get("SSD_STAGES", "123")
                    if "1" in SDS:
                        with kb.stage():
                            self.ssd_p1(i, l, cur)
                    if "2" in SDS:
                        with kb.stage():
                            self.ssd_p2a(i, l, cur)
                    if "3" in SDS:
                        with kb.stage():
                            self.outproj_stage(l, self.c_w_out[i], 2048, self.YN, cur, mdst)
                cur = mdst
            if do_ffn:
                dst = self.out if last else self.Xb
                with kb.stage():
                    self.ffn_stage(l, cur, dst)
                cur = dst
        kb.gstack.close()

    def rms_stats(self, src, bsrc, ss, bss, junk, bjunk, nparts=128, width=D):
        kb = self.kb
        kb.op("dve", lambda e: e.memset(ss[0:nparts, 0:1], 0.0), writes=[bss])
        kb.op("act", lambda e: e.activation(junk[0:nparts, 0:width], src, AF.Square,
                                            accum_out=ss[0:nparts, 0:1]),
              reads=[bsrc, bss], writes=[bjunk, bss])
        kb.op("act", lambda e: e.activation(ss[0:nparts, 0:1], ss[0:nparts, 0:1], AF.Sqrt,
                                            bias=EPS, scale=1.0 / width), reads=[bss], writes=[bss])
        kb.op("dve", lambda e: e.reciprocal(ss[0:nparts, 0:1], ss[0:nparts, 0:1]), reads=[bss], writes=[bss])

    def make_fe(self, BT, nh):
        kb = self.kb
        fe = {}
        fe["BT"], fe["nh"] = BT, nh
        fe["xts"] = [kb.sb("xt", [128, D]) for _ in range(2)]
        fe["bxt"] = kb.bufs(2)
        fe["hns"] = [kb.sb("hn", [128, D], BF16) for _ in range(2)]
        fe["bhn"] = kb.bufs(2)
        fe["junk"] = kb.sb("junk", [128, D])
        fe["bjunk"] = kb.buf()
        fe["sss"] = [kb.sb("ss", [128, 1]) for _ in range(2)]
        fe["bss"] = kb.bufs(2)
        nhh = max(nh, 1)
        fe["xh"] = kb.sb("xh", [2 * nhh, D]) if nh > 0 else None
        fe["bxh"] = kb.buf()
        fe["hnh"] = kb.sb("hnh", [2 * nhh, D], BF16) if nh > 0 else None
        fe["bhnh"] = kb.buf()
        fe["ssh"] = kb.sb("ssh", [2 * nhh, 1]) if nh > 0 else None
        fe["bssh"] = kb.buf()
        fe["pT"] = kb.ps("pT", [128, D], BF16)
        fe["bpT"] = kb.buf()
        if nh > 0:
            fe["pTh"] = kb.ps("pTh", [128, D], BF16)
            fe["bpTh"] = kb.buf()
        else:
            fe["pTh"], fe["bpTh"] = fe["pT"], fe["bpT"]
        fe["ti"] = 0
        return fe

    def run_fe(self, fe, Xin, t0, gpre, bconst, hnT, bh):
        kb = self.kb
        T = self.T
        BT, nh = fe["BT"], fe["nh"]
        ident, bident = self.ident, self.bident
        pT, bpT, pTh, bpTh = fe["pT"], fe["bpT"], fe["pTh"], fe["bpTh"]
        junk, bjunk = fe["junk"], fe["bjunk"]
        for j in range(BT // 128):
            i = fe["ti"] % 2
            fe["ti"] += 1
            xt, bx = fe["xts"][i], fe["bxt"][i]
            hn, bn = fe["hns"][i], fe["bhn"][i]
            ss, bs = fe["sss"][i], fe["bss"][i]
            r0 = t0 + j * 128
            kb.dma("sp", xt[:], Xin[r0:r0 + 128, :], writes=[bx])
            self.rms_stats(xt[:], bx, ss, bs, junk, bjunk)
            kb.op("dve", lambda e, hn=hn, xt=xt, ss=ss: e.scalar_tensor_tensor(
                hn[:], xt[:], ss[:, 0:1], gpre[:], ALU.mult, ALU.mult),
                reads=[bx, bs, bconst], writes=[bn])

            def tr(e, hn=hn):
                for k in range(8):
                    ins = e.transpose(pT[:, k * 128:(k + 1) * 128], hn[:, k * 128:(k + 1) * 128], ident[:])
                return ins
            kb.op("pe", tr, reads=[bn, bident], writes=[bpT])
            kb.op("act", lambda e, j=j: e.activation(
                hnT[:, :, j * 128:(j + 1) * 128], pT[:].rearrange("p (k t) -> p k t", k=8), AF.Copy),
                reads=[bpT], writes=[bh])
        if nh == 0:
            return
        xh, bxh, hnh, bhnh, ssh, bssh = fe["xh"], fe["bxh"], fe["hnh"], fe["bhnh"], fe["ssh"], fe["bssh"]
        kb.op("pool", lambda e: e.memset(xh[:], 0.0), writes=[bxh])
        if t0 - nh >= 0:
            kb.dma("sp", xh[0:nh, :], Xin[t0 - nh:t0, :], writes=[bxh])
        if t0 + BT + nh <= T:
            kb.dma("sp", xh[nh:2 * nh, :], Xin[t0 + BT:t0 + BT + nh, :], writes=[bxh])
        self.rms_stats(xh[:], bxh, ssh, bssh, junk, bjunk, nparts=2 * nh)
        kb.op("dve", lambda e: e.scalar_tensor_tensor(
            hnh[:], xh[:], ssh[:, 0:1], gpre[0:2 * nh, :], ALU.mult, ALU.mult),
            reads=[bxh, bssh, bconst], writes=[bhnh])
        w = 2 * nh

        def trh(e):
            for k in range(8):
                ins = e.transpose(pTh[:, k * w:(k + 1) * w], hnh[:, k * 128:(k + 1) * 128], ident[0:w, 0:w])
            return ins
        kb.op("pe", trh, reads=[bhnh, bident], writes=[bpTh])
        kb.op("act", lambda e: e.activation(
            hnT[:, :, BT:BT + w], pTh[:, 0:8 * w].rearrange("p (k t) -> p k t", k=8), AF.Copy),
            reads=[bpTh], writes=[bh])

    def ssd_consts(self, i, l):
        kb = self.kb
        c = {}
        c["b"] = kb.buf()
        b = c["b"]
        c["gpre"] = kb.sb("gpre", [128, D])
        kb.dma("sp", c["gpre"][:], self.gains[l * 4 + 0], writes=[b])
        c["scw"] = kb.sb("scw", [128, 24, 5])
        c["scb"] = kb.sb("scb", [128, 24])
        kb.dma("sp", c["scw"][:], self.scw[i].rearrange("p (c w) -> p c w", w=5), writes=[b])
        kb.dma("sp", c["scb"][:], self.scb[i], writes=[b])
        c["dtb"] = kb.sb("dtb", [128, 64])
        kb.dma("sp", c["dtb"][:], self.sdtb[i], writes=[b])
        c["A"] = kb.sb("A", [128, 64])
        kb.dma("sp", c["A"][:], self.salog[i], writes=[b])
        kb.op("act", lambda e: e.activation(c["A"][:], c["A"][:], AF.Exp), reads=[b], writes=[b])
        kb.op("dve", lambda e: e.tensor_scalar(c["A"][:], c["A"][:], -1.0, None, ALU.mult), reads=[b], writes=[b])
        c["cm"] = kb.sb("cm", [128, 5, 128])
        kb.dma("sp", c["cm"][:], self.cmats.rearrange("m p j -> p m j"), writes=[b])
        return c

    def ssd_load_win(self, i, cols_list):
        kb = self.kb
        W = kb.sb("swin", [128, 8, 5184], BF16)
        bw = kb.buf()
        for k in range(8):
            for (a, bnd) in cols_list:
                kb.dma("pool", W[:, k, a:bnd], self.c_w_in[i, k * 128:(k + 1) * 128, a:bnd], writes=[bw])
        return W, bw

    def ssd_block_front(self, fe, cst, W, bw, Xin, t0, hnT, bh, chunks, ue_s, acc_s, ps, outs):
        kb = self.kb
        BT = fe["BT"]
        NT = BT // 128
        scw, scb, bc = cst["scw"], cst["scb"], cst["b"]
        self.run_fe(fe, Xin, t0, cst["gpre"], bc, hnT, bh)
        for n0 in range(0, len(chunks), 2):
            ctx = []
            for n in range(n0, min(n0 + 2, len(chunks))):
                cc = chunks[n]
                par = n % 2
                pA, bpA = ps["pA"][par], ps["bpA"][par]
                pH, bpH = ps["pHh"][par], ps["bpHh"][par]
                ue, bue = ue_s[par]
                acc, bacc = acc_s[par]
                col = 2048 + cc * 128

                def mm(e, col=col, pA=pA):
                    for k in range(8):
                        ins = e.matmul(pA[:, 0:BT], W[:, k, col:col + 128], hnT[:, k, 0:BT], start=(k == 0), stop=(k == 7))
                    return ins
                kb.op("pe", mm, reads=[bw, bh], writes=[bpA])

                def mmh(e, col=col, pH=pH):
                    for k in range(8):
                        ins = e.matmul(pH[:, 0:4], W[:, k, col:col + 128], hnT[:, k, BT:BT + 4], start=(k == 0), stop=(k == 7))
                    return ins
                kb.op("pe", mmh, reads=[bw, bh], writes=[bpH])
                kb.op("act", lambda e, ue=ue, pA=pA: e.activation(ue[:, 2:2 + BT], pA[:, 0:BT], AF.Copy),
                      reads=[bpA], writes=[bue])
                kb.op("act", lambda e, ue=ue, pH=pH: e.activation(ue[:, 0:2], pH[:, 0:2], AF.Copy), reads=[bpH], writes=[bue])
                kb.op("act", lambda e, ue=ue, pH=pH: e.activation(ue[:, BT + 2:BT + 4], pH[:, 2:4], AF.Copy),
                      reads=[bpH], writes=[bue])
                ctx.append((cc, ue, bue, acc, bacc))
            for j in range(5):
                for (cc, ue, bue, acc, bacc) in ctx:
                    if j == 0:
                        kb.op("dve", lambda e, ue=ue, acc=acc, cc=cc: e.tensor_scalar(
                            acc[:], ue[:, 0:BT], scw[:, cc, 0:1], scb[:, cc:cc + 1], ALU.mult, ALU.add),
                            reads=[bue, bc], writes=[bacc])
                    else:
                        kb.op("dve", lambda e, ue=ue, acc=acc, cc=cc, j=j: e.scalar_tensor_tensor(
                            acc[:], ue[:, j:j + BT], scw[:, cc, j:j + 1], acc[:], ALU.mult, ALU.add),
                            reads=[bue, bc, bacc], writes=[bacc])
            for (cc, ue, bue, acc, bacc) in ctx:
                outs(cc, acc, bacc)

    def ssd_small(self, cst, la, bla, dirs, ps, sm, bsm):
        kb = self.kb
        cm, bc = cst["cm"], cst["b"]
        pH, bpH = ps["pH"], ps["bpH"]

        def mm(e):
            e.matmul(pH[:, 64:96], cm[:, 0, :], la[:, 0:32], start=True, stop=True)
            e.matmul(pH[:, 96:128], cm[:, 3, :], la[:, 32:64], start=True, stop=True)
            e.matmul(pH[:, 128:160], cm[:, 1, :], la[:, 32:64], start=True, stop=True)
            return e.matmul(pH[:, 160:224], cm[:, 4, :], la[:, 0:64], start=True, stop=True)
        kb.op("pe", mm, reads=[bla, bc], writes=[bpH])
        kb.op("act", lambda e: e.activation(sm[:, 0:160], pH[:, 64:224], AF.Copy), reads=[bpH], writes=[bsm])

    def ssd_dt(self, cst, W, bw, hnT, bh, sl, ps, tl):
        kb = self.kb
        pH, bpH = ps["pH"], ps["bpH"]
        bc = cst["b"]

        def mm(e):
            for k in range(8):
                ins = e.matmul(pH[:, 0:64], hnT[:, k, sl], W[:, k, 5120:5184], start=(k == 0), stop=(k == 7))
            return ins
        kb.op("pe", mm, reads=[bw, bh], writes=[bpH])
        dtr, dt, la, bdt = tl["dtr"], tl["dt"], tl["la"], tl["bdt"]
        kb.op("act", lambda e: e.activation(dtr[:], pH[:, 0:64], AF.Copy), reads=[bpH], writes=[bdt])
        kb.op("dve", lambda e: e.tensor_tensor(dtr[:], dtr[:], cst["dtb"][:], ALU.add), reads=[bdt, bc], writes=[bdt])
        kb.op("act", lambda e: e.activation(dtr[:], dtr[:], AF.Exp), reads=[bdt], writes=[bdt])
        kb.op("act", lambda e: e.activation(dt[:], dtr[:], AF.Ln, bias=1.0, scale=1.0), reads=[bdt], writes=[bdt])
        kb.op("dve", lambda e: e.tensor_tensor(la[:], dt[:], cst["A"][:], ALU.mult), reads=[bdt, bc], writes=[bdt])

    def ssd_alloc_common(self, BT, p2):
        kb = self.kb
        NT = BT // 128
        a = {}
        a["hnTs"] = [kb.sb("hnT", [128, 8, BT + 4], BF16) for _ in range(2)]
        a["bhnT"] = kb.bufs(2)
        a["ue_s"] = [(kb.sb("ue", [128, BT + 4]), kb.buf()) for _ in range(2)]
        a["acc_s"] = [(kb.sb("acc", [128, BT]), kb.buf()) for _ in range(2)]
        a["xsT"] = [(kb.sb("xsT", [128, BT], BF16), kb.buf()) for _ in range(2)]
        a["x_tok"] = [kb.sb("xtok", [128, NT, 2048], BF16)] * 2
        a["bx_tok"] = [kb.buf()] * 2
        a["B_tok"] = [kb.sb("btok", [128, NT, 512], BF16)] * 2
        a["bB_tok"] = [kb.buf()] * 2
        tl = {}
        for n in ("dtr", "dt", "la"):
            tl[n] = kb.sb(n, [128, 64])
        tl["bdt"] = kb.buf()
        tl["sm"] = kb.sb("sm", [128, 160])
        tl["bsm"] = kb.buf()
        a["tl"] = tl
        ps = {}
        ps["pA"] = [kb.ps("pA", [128, 512]) for _ in range(2)]
        ps["bpA"] = kb.bufs(2)
        ps["pH"] = kb.ps("pH", [128, 512])
        ps["bpH"] = kb.buf()
        ps["pHh"] = [kb.ps("pHh", [128, 512]) for _ in range(2)]
        ps["bpHh"] = kb.bufs(2)
        a["ps"] = ps
        return a

    def ssd_p1(self, i, l, Xin, BT=512):
        kb = self.kb
        T = self.T
        NT = BT // 128
        cst = self.ssd_consts(i, l)
        W, bw = self.ssd_load_win(i, [(0, 5184)])
        fe = self.make_fe(BT, 2)
        a = self.ssd_alloc_common(BT, False)
        ps, tl = a["ps"], a["tl"]
        pT, bpT = fe["pT"], fe["bpT"]
        Hb = kb.sb("Hb", [128, 2048])
        Hbb = kb.sb("Hbb", [128, 2048], BF16)
        bHb, bHbb = kb.bufs(4), kb.bufs(4)
        kb.op("pool", lambda e: e.memset(Hb[:], 0.0), writes=bHb)
        kb.op("pool", lambda e: e.memset(Hbb[:], 0.0), writes=bHbb)
        wgt = kb.sb("wgt", [128, 32])
        dtot = kb.sb("dtot", [128, 32])
        bwg = kb.buf()
        xws = [(kb.sb("xw", [128, 512], BF16), kb.buf()) for _ in range(2)]
        Ss = [(kb.sb("Ssb", [128, 512]), kb.buf()) for _ in range(2)]
        BTf = kb.sb("BTf", [128, 4, BT], BF16)
        CTf = kb.sb("CTf", [128, 4, BT], BF16)
        bBf, bCf = kb.buf(), kb.buf()
        zss = [(kb.sb("zs", [128, 2048], BF16), kb.buf()) for _ in range(2)]
        dtl = kb.sb("dtl", [128, 128])
        bdtl = kb.buf()
        gi = 0
        zi = 0
        for b in reversed(range(T // BT)):
            t0 = b * BT
            hnT, bh = a["hnTs"][b % 2], a["bhnT"][b % 2]
            x_tok, bxk = a["x_tok"][b % 2], a["bx_tok"][b % 2]
            B_tok, bBk = a["B_tok"][b % 2], a["bB_tok"][b % 2]

            def outs(cc, acc, bacc):
                if cc >= 20:
                    dstC = CTf[:, cc - 20, :]
                    kb.op("act", lambda e: e.activation(dstC, acc[:], AF.Silu), reads=[bacc], writes=[bCf])
                    return
                if cc >= 16:
                    xsT, bxs = BTf[:, cc - 16, :], bBf
                else:
                    xsT, bxs = a["xsT"][cc % 2]
                    xsT = xsT[:]
                kb.op("act", lambda e: e.activation(xsT, acc[:], AF.Silu), reads=[bacc], writes=[bxs])

                def tr(e):
                    for ct in range(NT):
                        ins = e.transpose(pT[:, ct * 128:(ct + 1) * 128], xsT[:, ct * 128:(ct + 1) * 128], self.ident[:])
                    return ins
                kb.op("pe", tr, reads=[bxs, self.bident], writes=[bpT])
                if cc < 16:
                    dst, bd = x_tok[:, :, cc * 128:(cc + 1) * 128], bxk
                else:
                    dst, bd = B_tok[:, :, (cc - 16) * 128:(cc - 15) * 128], bBk
                kb.op("act", lambda e: e.activation(dst, pT[:, 0:NT * 128].rearrange("p (c t) -> p c t", c=NT), AF.Copy),
                      reads=[bpT], writes=[bd])
            self.ssd_block_front(fe, cst, W, bw, Xin, t0, hnT, bh, list(range(24)), a["ue_s"], a["acc_s"], ps, outs)
            for ct in range(NT):
                r0 = t0 + ct * 128
                kb.dma("sp", self.XTK[r0:r0 + 128, :], x_tok[:, ct, :], reads=[bxk])
                kb.dma("act", self.BTK[r0:r0 + 128, :], B_tok[:, ct, :], reads=[bBk])
            kb.dma("sp", self.BTF[:, :, t0:t0 + BT].rearrange("g p t -> p g t"), BTf[:], reads=[bBf])
            kb.dma("act", self.CTF[:, :, t0:t0 + BT].rearrange("g p t -> p g t"), CTf[:], reads=[bCf])
            for ct in reversed(range(NT)):
                c = t0 // 128 + ct
                sl = slice(ct * 128, (ct + 1) * 128)
                self.ssd_dt(cst, W, bw, hnT, bh, sl, ps, tl)
                self.ssd_small(cst, tl["la"], tl["bdt"], None, ps, tl["sm"], tl["bsm"])
                sm, bsm, dt, bdt = tl["sm"], tl["bsm"], tl["dt"], tl["bdt"]
                kb.op("pool", lambda e: e.tensor_copy(dtl[:, 0:64], tl["dt"][:]), reads=[bdt], writes=[bdtl])
                kb.op("pool", lambda e: e.tensor_copy(dtl[:, 64:128], tl["la"][:]), reads=[bdt], writes=[bdtl])
                kb.dma("act", self.DTL[c * 128:(c + 1) * 128, :], dtl[:], reads=[bdtl])
                zs_, bz = zss[zi % 2]
                zi += 1
                for g in range(4):
                    pZ, bpZ = ps["pA"][gi % 2], ps["bpA"][gi % 2]
                    gi += 1

                    def mz(e, pZ=pZ, g=g, hnT=hnT, sl=sl):
                        for k in range(8):
                            ins = e.matmul(pZ[:, 0:512], hnT[:, k, sl], W[:, k, g * 512:(g + 1) * 512], start=(k == 0), stop=(k == 7))
                        return ins
                    kb.op("pe", mz, reads=[bw, bh], writes=[bpZ])
                    kb.op("act", lambda e, zs_=zs_, pZ=pZ, g=g: e.activation(zs_[:, g * 512:(g + 1) * 512], pZ[:, 0:512], AF.Silu),
                          reads=[bpZ], writes=[bz])
                kb.dma("sp", self.ZS[c * 128:(c + 1) * 128, :], zs_[:], reads=[bz])
                kb.op("act", lambda e: e.activation(wgt[:], sm[:, 64:96], AF.Exp), reads=[bsm], writes=[bwg])
                kb.op("dve", lambda e: e.tensor_tensor(wgt[:], wgt[:], dt[:, 32:64], ALU.mult), reads=[bwg, bdt], writes=[bwg])
                kb.op("act", lambda e: e.activation(dtot[:], sm[:, 128:160], AF.Exp), reads=[bsm], writes=[bwg])
                kb.dma("sp", self.HB[c], Hbb[:], reads=bHbb)
                for g in range(4):
                    xw, bxw = xws[gi % 2]
                    Ssb, bS = Ss[gi % 2]
                    pS, bpS = ps["pA"][gi % 2], ps["bpA"][gi % 2]
                    gi += 1
                    gs = slice(g * 512, (g + 1) * 512)
                    kb.op("pool", lambda e, xw=xw, gs=gs, g=g, ct=ct, x_tok=x_tok: e.tensor_tensor(
                        xw[:].rearrange("p (h d) -> p h d", h=8), x_tok[:, ct, gs].rearrange("p (h d) -> p h d", h=8),
                        wgt[:, g * 8:(g + 1) * 8].unsqueeze(2).to_broadcast([128, 8, 64]), ALU.mult),
                        reads=[bxk, bwg], writes=[bxw])
                    kb.op("pe", lambda e, pS=pS, xw=xw, g=g, ct=ct, B_tok=B_tok: e.matmul(
                        pS[:, 0:512], B_tok[:, ct, g * 128:(g + 1) * 128], xw[:], start=True, stop=True),
                        reads=[bBk, bxw], writes=[bpS])
                    kb.op("act", lambda e, Ssb=Ssb, pS=pS: e.activation(Ssb[:], pS[:, 0:512], AF.Copy),
                          reads=[bpS], writes=[bS])
                    kb.op("dve", lambda e, gs=gs, g=g: e.tensor_tensor(
                        Hb[:, gs].rearrange("p (h d) -> p h d", h=8), Hb[:, gs].rearrange("p (h d) -> p h d", h=8),
                        dtot[:, g * 8:(g + 1) * 8].unsqueeze(2).to_broadcast([128, 8, 64]), ALU.mult),
                        reads=[bHb[g], bwg], writes=[bHb[g]])
                    kb.op("pool", lambda e, gs=gs, Ssb=Ssb: e.tensor_tensor(Hb[:, gs], Hb[:, gs], Ssb[:], ALU.add),
                          reads=[bHb[g], bS], writes=[bHb[g]])
                    kb.op("act", lambda e, gs=gs: e.activation(Hbb[:, gs], Hb[:, gs], AF.Copy), reads=[bHb[g]], writes=[bHbb[g]])

    def ssd_p2a(self, i, l, Xin, BT=256):
        kb = self.kb
        T = self.T
        NT = BT // 128
        cst = self.ssd_consts(i, l)
        bc = cst["b"]
        cm = cst["cm"]
        ps = {}
        ps["pA"] = [kb.ps("pA", [128, 512]) for _ in range(3)]
        ps["bpA"] = kb.bufs(3)
        ps["pH"] = kb.ps("pH", [128, 512])
        ps["bpH"] = kb.buf()
        P3 = kb.ps("P3", [128, 1536])
        bP3 = kb.bufs(3)
        dsk = kb.sb("dsk", [128, 32])
        ng = kb.sb("ng", [128, 2048])
        kb.dma("sp", dsk[:], self.sdsk[i], writes=[bc])
        kb.dma("sp", ng[:], self.sng[i], writes=[bc])
        xks = [(kb.sb("xk", [128, 2048], BF16), kb.buf()) for _ in range(2)]
        Bks = [(kb.sb("Bk", [128, 512], BF16), kb.buf()) for _ in range(2)]
        BTfs = [(kb.sb("BTf", [128, 4, 128], BF16), kb.buf()) for _ in range(2)]
        CTfs = [(kb.sb("CTf", [128, 4, 128], BF16), kb.buf()) for _ in range(2)]
        dtls = [(kb.sb("dtl", [128, 128]), kb.buf()) for _ in range(2)]
        zsl = [(kb.sb("zsl", [128, 2048], BF16), kb.buf()) for _ in range(2)]
        Hbbs = [(kb.sb("Hbb", [128, 2048], BF16), kb.buf()) for _ in range(2)]
        sms = [(kb.sb("sm", [128, 160]), kb.buf()) for _ in range(2)]
        Hf = kb.sb("Hf", [128, 2048])
        Hfb = kb.sb("Hfb", [128, 2048], BF16)
        bHf, bHfb = kb.bufs(4), kb.bufs(4)
        kb.op("pool", lambda e: e.memset(Hf[:], 0.0), writes=bHf)
        kb.op("pool", lambda e: e.memset(Hfb[:], 0.0), writes=bHfb)
        ecs = [(kb.sb("ec", [128, 64]), kb.sb("wgt", [128, 32]), kb.sb("dtot", [128, 32]), kb.sb("dd", [128, 32]), kb.buf())
               for _ in range(2)]
        qks = [(kb.sb("qk", [128, 4, 128]), kb.buf()) for _ in range(2)]
        AUf = [kb.sb("AUf", [128, 4, 128]) for _ in range(2)]
        AUb = [kb.sb("AUb", [128, 4, 128]) for _ in range(2)]
        bAU = kb.bufs(2)
        E = [kb.sb("E", [128, 4, 128]) for _ in range(2)]
        bE = kb.bufs(2)
        T1 = [kb.sb("T1", [128, 4, 128]) for _ in range(2)]
        bT1 = kb.bufs(2)
        Wb = [kb.sb("Wb", [128, 4, 128], BF16) for _ in range(2)]
        bWb = kb.bufs(2)
        Ys = [(kb.sb("Y", [128, 2048]), kb.bufs(4)) for _ in range(2)]
        Tt = [(kb.sb("Tt", [128, 512]), kb.buf()) for _ in range(3)]
        xds = [(kb.sb("xd", [128, 2048], BF16), kb.buf()) for _ in range(2)]
        xws = [(kb.sb("xw", [128, 512], BF16), kb.buf()) for _ in range(2)]
        Ss = [(kb.sb("Ssb", [128, 512]), kb.buf()) for _ in range(2)]
        jz = [(kb.sb("jz", [128, 512]), kb.buf()) for _ in range(2)]
        ssqs = [(kb.sb("ssq", [128, 4]), kb.buf()) for _ in range(2)]
        cnt = {"qi": 0, "gi": 0, "pa": 0}

        def nextpA():
            k = cnt["pa"] % 3
            cnt["pa"] += 1
            return ps["pA"][k], ps["bpA"][k]

        def do_chunk(c):
            if True:
                r0 = c * 128
                cp = c % 2
                x_tok, bxk = xks[cp]
                B_tok, bBk = Bks[cp]
                BTfb, bBf = BTfs[cp]
                CTfb, bCf = CTfs[cp]
                dtl, bdt = dtls[cp]
                zsb, bzs = zsl[cp]
                Hbb, bHbb = Hbbs[cp]
                sm, bsm = sms[cp]
                ec, wgt, dtot, dd, bsm2 = ecs[cp]
                qk, bqk = qks[cp]
                Y, bY = Ys[cp]
                xd, bxd = xds[cp]
                ssq, bssq = ssqs[cp]
                dt, la = dtl[:, 0:64], dtl[:, 64:128]
                kb.dma("sp", x_tok[:], self.XTK[r0:r0 + 128, :], writes=[bxk])
                kb.dma("act", B_tok[:], self.BTK[r0:r0 + 128, :], writes=[bBk])
                kb.dma("sp", BTfb[:], self.BTF[:, :, r0:r0 + 128].rearrange("g p t -> p g t"), writes=[bBf])
                kb.dma("act", CTfb[:], self.CTF[:, :, r0:r0 + 128].rearrange("g p t -> p g t"), writes=[bCf])
                kb.dma("sp", dtl[:], self.DTL[r0:r0 + 128, :], writes=[bdt])
                kb.dma("act", zsb[:], self.ZS[r0:r0 + 128, :], writes=[bzs])
                kb.dma("sp", Hbb[:], self.HB[c], writes=[bHbb])
                self.ssd_small(cst, la, bdt, None, ps, sm, bsm)
                kb.op("act", lambda e, ec=ec, sm=sm: e.activation(ec[:], sm[:, 0:64], AF.Exp), reads=[bsm], writes=[bsm2])
                kb.op("dve", lambda e, wgt=wgt, sm=sm: e.tensor_tensor(wgt[:], sm[:, 96:128], sm[:, 0:32], ALU.subtract), reads=[bsm], writes=[bsm2])
                kb.op("act", lambda e, wgt=wgt: e.activation(wgt[:], wgt[:], AF.Exp), reads=[bsm2], writes=[bsm2])
                kb.op("dve", lambda e, wgt=wgt, dt=dt: e.tensor_tensor(wgt[:], wgt[:], dt[:, 0:32], ALU.mult), reads=[bsm2, bdt], writes=[bsm2])
                kb.op("act", lambda e, dtot=dtot, sm=sm: e.activation(dtot[:], sm[:, 96:128], AF.Exp), reads=[bsm], writes=[bsm2])
                kb.op("dve", lambda e, dd=dd, dt=dt: e.tensor_tensor(dd[:], dt[:, 0:32], dt[:, 32:64], ALU.subtract), reads=[bdt], writes=[bsm2])
                pQ, bpQ = nextpA()

                def mq(e, pQ=pQ, BTfb=BTfb, CTfb=CTfb):
                    for g in range(4):
                        ins = e.matmul(pQ[:, g * 128:(g + 1) * 128], BTfb[:, g, :], CTfb[:, g, :], start=True, stop=True)
                    return ins
                kb.op("pe", mq, reads=[bBf, bCf], writes=[bpQ])
                kb.op("act", lambda e, pQ=pQ: e.activation(qk[:].rearrange("p g j -> p (g j)"), pQ[:, 0:512], AF.Copy),
                      reads=[bpQ], writes=[bqk])
                kb.op("pool", lambda e: e.tensor_tensor(
                    xd[:].rearrange("p (h d) -> p h d", h=32), x_tok[:, :].rearrange("p (h d) -> p h d", h=32),
                    dsk[:].unsqueeze(2).to_broadcast([128, 32, 64]), ALU.mult), reads=[bxk, bc], writes=[bxd])
                for g in range(4):
                    gs = slice(g * 512, (g + 1) * 512)
                    Pi, Pf, Pb = P3[:, 0:512], P3[:, 512:1024], P3[:, 1024:1536]
                    kb.op("pe", lambda e, gs=gs: e.matmul(Pi, self.ident[:], xd[:, gs], start=True, stop=False),
                          reads=[self.bident, bxd], writes=[bP3[0]])
                    halves = []
                    for half in range(2):
                        h0 = (g * 2 + half) * 4
                        par = cnt["qi"] % 2
                        cnt["qi"] += 1
                        pS, bpS = nextpA()
                        halves.append((half, h0, par, pS, bpS))
                    for (half, h0, par, pS, bpS) in halves:
                        kb.op("dve", lambda e, par=par, h0=h0: e.tensor_tensor(
                            AUf[par][:], la[:, h0:h0 + 4].unsqueeze(2).to_broadcast([128, 4, 128]),
                            cm[:, 0, :].unsqueeze(1).to_broadcast([128, 4, 128]), ALU.mult),
                            reads=[bdt, bc], writes=[bAU[par]])
                    for (half, h0, par, pS, bpS) in halves:
                        kb.op("pool", lambda e, par=par, h0=h0: e.tensor_tensor(
                            AUb[par][:], la[:, 32 + h0:32 + h0 + 4].unsqueeze(2).to_broadcast([128, 4, 128]),
                            cm[:, 3, :].unsqueeze(1).to_broadcast([128, 4, 128]), ALU.mult),
                            reads=[bdt, bc], writes=[bAU[par]])
                    for (half, h0, par, pS, bpS) in halves:
                        def ms(e, par=par, pS=pS):
                            e.matmul(pS[:, 0:512], cm[:, 2, :], AUf[par][:].rearrange("p h j -> p (h j)"), start=True, stop=False)
                            return e.matmul(pS[:, 0:512], cm[:, 1, :], AUb[par][:].rearrange("p h j -> p (h j)"),
                                            start=False, stop=True)
                        kb.op("pe", ms, reads=[bc, bAU[par]], writes=[bpS])
                    for (half, h0, par, pS, bpS) in halves:
                        kb.op("act", lambda e, par=par, pS=pS: e.activation(
                            E[par][:].rearrange("p h j -> p (h j)"), pS[:, 0:512], AF.Exp), reads=[bpS], writes=[bE[par]])
                    for (half, h0, par, pS, bpS) in halves:
                        kb.op("pool", lambda e, par=par, h0=h0: e.tensor_tensor(
                            T1[par][:], cm[:, 0, :].unsqueeze(1).to_broadcast([128, 4, 128]),
                            dd[:, h0:h0 + 4].unsqueeze(2).to_broadcast([128, 4, 128]), ALU.mult),
                            reads=[bc, bsm2], writes=[bT1[par]])
                    for (half, h0, par, pS, bpS) in halves:
                        kb.op("pool", lambda e, par=par, h0=h0: e.tensor_tensor(
                            T1[par][:], T1[par][:], dt[:, 32 + h0:32 + h0 + 4].unsqueeze(2).to_broadcast([128, 4, 128]),
                            ALU.add), reads=[bT1[par], bdt], writes=[bT1[par]])
                    for (half, h0, par, pS, bpS) in halves:
                        kb.op("dve", lambda e, par=par, g=g: e.tensor_tensor(
                            E[par][:], E[par][:], qk[:, g, :].unsqueeze(1).to_broadcast([128, 4, 128]), ALU.mult),
                            reads=[bE[par], bqk], writes=[bE[par]])
                    for (half, h0, par, pS, bpS) in halves:
                        kb.op("dve", lambda e, par=par: e.tensor_tensor(Wb[par][:], E[par][:], T1[par][:], ALU.mult),
                              reads=[bE[par], bT1[par]], writes=[bWb[par]])
                    for (half, h0, par, pS, bpS) in halves:
                        def mi(e, par=par, h0=h0, half=half):
                            for hh in range(4):
                                h = h0 + hh
                                cs_ = (half * 4 + hh) * 64
                                ins = e.matmul(Pi[:, cs_:cs_ + 64], Wb[par][:, hh, :], x_tok[:, h * 64:(h + 1) * 64],
                                               start=False, stop=(half == 1 and hh == 3))
                            return ins
                        kb.op("pe", mi, reads=[bWb[par], bxk], writes=[bP3[0]])
                    kb.op("pe", lambda e, g=g, gs=gs: e.matmul(
                        Pf, CTfb[:, g, :], Hfb[:, gs], start=True, stop=True), reads=[bCf, bHfb[g]], writes=[bP3[1]])
                    kb.op("pe", lambda e, g=g, gs=gs: e.matmul(
                        Pb, CTfb[:, g, :], Hbb[:, gs], start=True, stop=True), reads=[bCf, bHbb], writes=[bP3[2]])
                    kb.op("act", lambda e, gs=gs: e.activation(Y[:, gs], Pi, AF.Copy), reads=[bP3[0]], writes=[bY[g]])
                    for (Px, bPx, eoff) in ((Pf, bP3[1], 0), (Pb, bP3[2], 32)):
                        Tt_, bTt = Tt[cnt["gi"] % 3]
                        cnt["gi"] += 1
                        kb.op("act", lambda e, Tt_=Tt_, Px=Px: e.activation(Tt_[:], Px, AF.Copy), reads=[bPx], writes=[bTt])
                        kb.op("dve", lambda e, Tt_=Tt_, eoff=eoff, g=g: e.tensor_tensor(
                            Tt_[:].rearrange("p (h d) -> p h d", h=8), Tt_[:].rearrange("p (h d) -> p h d", h=8),
                            ec[:, eoff + g * 8:eoff + (g + 1) * 8].unsqueeze(2).to_broadcast([128, 8, 64]), ALU.mult),
                            reads=[bTt, bsm2], writes=[bTt])
                        kb.op("pool", lambda e, Tt_=Tt_, gs=gs: e.tensor_tensor(Y[:, gs], Y[:, gs], Tt_[:], ALU.add),
                              reads=[bTt, bY[g]], writes=[bY[g]])
                    xw, bxw = xws[g % 2]
                    Ssb, bS = Ss[g % 2]
                    pS, bpS = nextpA()
                    kb.op("pool", lambda e, xw=xw, gs=gs, g=g: e.tensor_tensor(
                        xw[:].rearrange("p (h d) -> p h d", h=8), x_tok[:, gs].rearrange("p (h d) -> p h d", h=8),
                        wgt[:, g * 8:(g + 1) * 8].unsqueeze(2).to_broadcast([128, 8, 64]), ALU.mult),
                        reads=[bxk, bsm2], writes=[bxw])
                    kb.op("pe", lambda e, pS=pS, xw=xw, g=g: e.matmul(
                        pS[:, 0:512], B_tok[:, g * 128:(g + 1) * 128], xw[:], start=True, stop=True),
                        reads=[bBk, bxw], writes=[bpS])
                    kb.op("act", lambda e, Ssb=Ssb, pS=pS: e.activation(Ssb[:], pS[:, 0:512], AF.Copy),
                          reads=[bpS], writes=[bS])
                    kb.op("dve", lambda e, gs=gs, g=g: e.tensor_tensor(
                        Hf[:, gs].rearrange("p (h d) -> p h d", h=8), Hf[:, gs].rearrange("p (h d) -> p h d", h=8),
                        dtot[:, g * 8:(g + 1) * 8].unsqueeze(2).to_broadcast([128, 8, 64]), ALU.mult),
                        reads=[bHf[g], bsm2], writes=[bHf[g]])
                    kb.op("pool", lambda e, gs=gs, Ssb=Ssb: e.tensor_tensor(Hf[:, gs], Hf[:, gs], Ssb[:], ALU.add),
                          reads=[bHf[g], bS], writes=[bHf[g]])
                    kb.op("act", lambda e, gs=gs: e.activation(Hfb[:, gs], Hf[:, gs], AF.Copy), reads=[bHf[g]], writes=[bHfb[g]])
                    z_, bz = jz[g % 2]
                    kb.op("dve", lambda e, gs=gs: e.tensor_tensor(Y[:, gs], Y[:, gs], zsb[:, gs], ALU.mult),
                          reads=[bY[g], bzs], writes=[bY[g]])
                    if g == 0:
                        kb.op("dve", lambda e: e.memset(ssq[:], 0.0), writes=[bssq])
                    kb.op("act", lambda e, z_=z_, gs=gs, g=g: e.activation(z_[:], Y[:, gs], AF.Square, accum_out=ssq[:, g:g + 1]),
                          reads=[bY[g], bssq], writes=[bz, bssq])
                kb.op("act", lambda e: e.activation(ssq[:], ssq[:], AF.Sqrt, bias=EPS, scale=1.0 / 512), reads=[bssq], writes=[bssq])
                kb.op("dve", lambda e: e.reciprocal(ssq[:], ssq[:]), reads=[bssq], writes=[bssq])
                for g in range(4):
                    gs = slice(g * 512, (g + 1) * 512)
                    kb.op("dve", lambda e, g=g, gs=gs: e.scalar_tensor_tensor(
                        xd[:, gs], Y[:, gs], ssq[:, g:g + 1], ng[:, gs], ALU.mult, ALU.mult),
                        reads=[bY[g], bssq, bc], writes=[bxd])
                kb.dma("sp", self.YN[c * 128:(c + 1) * 128, 0:2048], xd[:], reads=[bxd])

        for c in range(T // 128):
            do_chunk(c)

    def even_consts(self, i, l):
        kb = self.kb
        c = {}
        b = kb.buf()
        c["b"] = b
        c["gpre"] = kb.sb("gpre", [128, D])
        kb.dma("sp", c["gpre"][:], self.gains[l * 4 + 0], writes=[b])
        lg = kb.sb("lg", [128, 16])
        kb.dma("sp", lg[:], self.rlogit[i], writes=[b])
        kb.op("act", lambda e: e.activation(lg[:], lg[:], AF.Sigmoid), reads=[b], writes=[b])
        kb.op("act", lambda e: e.activation(lg[:], lg[:], AF.Ln), reads=[b], writes=[b])
        c["lg"] = lg
        idx = kb.sb("idx", [128, 4])
        kb.dma("sp", idx[:], self.ridx, writes=[b])
        tabs = kb.sb("tabs", [128, 5, 8])
        for t, (col, off) in enumerate(((0, 0), (1, 8), (2, 0), (3, 8))):
            kb.op("dve", lambda e, t=t, col=col, off=off: e.tensor_scalar(
                tabs[:, t, :], lg[:, off:off + 8], idx[:, col:col + 1], None, ALU.mult), reads=[b], writes=[b])
        kb.op("act", lambda e: e.activation(tabs[:, 0:4, :], tabs[:, 0:4, :], AF.Exp), reads=[b], writes=[b])
        g128 = kb.sb("g128", [128, 16])
        kb.op("act", lambda e: e.activation(g128[:], lg[:], AF.Exp, scale=128.0), reads=[b], writes=[b])
        c["tabs"], c["g128"] = tabs, g128
        return c

    def even_load_win(self, i, cols_list):
        kb = self.kb
        W = kb.sb("ewin", [128, 8, 3584], BF16)
        bw = kb.buf()
        for k in range(8):
            for (a, bnd) in cols_list:
                kb.dma("pool", W[:, k, a:bnd], self.ab_w_in[i, k * 128:(k + 1) * 128, a:bnd], writes=[bw])
        return W, bw

    def proj_tok(self, W, bw, hnT, bh, col0, pP, bpP, width=512):
        def mm(e):
            for k in range(8):
                ins = e.matmul(pP[:, 0:width], hnT[:, k, 0:128], W[:, k, col0:col0 + width], start=(k == 0), stop=(k == 7))
            return ins
        self.kb.op("pe", mm, reads=[bw, bh], writes=[bpP])

    def rotary(self, src, bsrc, dst, bdst, cs, bcs, tmp):
        kb = self.kb
        s3 = src[:].rearrange("p (h d) -> p h d", h=8)
        d3 = dst[:].rearrange("p (h d) -> p h d", h=8)
        x1, x2 = s3[:, :, 0:32], s3[:, :, 32:64]
        cosb = cs[:, 0:32].unsqueeze(1).to_broadcast([128, 8, 32])
        sinb = cs[:, 32:64].unsqueeze(1).to_broadcast([128, 8, 32])
        (ta, tb_, tc, td), bt = tmp
        ta3, tb3, tc3, td3 = [t[:].rearrange("p (h d) -> p h d", h=8) for t in (ta, tb_, tc, td)]
        kb.op("dve", lambda e: e.tensor_tensor(ta3, x1, cosb, ALU.mult), reads=[bsrc, bcs], writes=[bt[0]])
        kb.op("dve", lambda e: e.tensor_tensor(tb3, x2, sinb, ALU.mult), reads=[bsrc, bcs], writes=[bt[1]])
        kb.op("dve", lambda e: e.tensor_tensor(d3[:, :, 0:32], ta3, tb3, ALU.subtract), reads=[bt[0], bt[1]], writes=[bdst])
        kb.op("pool", lambda e: e.tensor_tensor(tc3, x1, sinb, ALU.mult), reads=[bsrc, bcs], writes=[bt[2]])
        kb.op("pool", lambda e: e.tensor_tensor(td3, x2, cosb, ALU.mult), reads=[bsrc, bcs], writes=[bt[3]])
        kb.op("pool", lambda e: e.tensor_tensor(d3[:, :, 32:64], tc3, td3, ALU.add), reads=[bt[2], bt[3]], writes=[bdst])

    def even_p1(self, i, l, Xin):
        kb = self.kb
        T = self.T
        cst = self.even_consts(i, l)
        bc = cst["b"]
        tabs, g128 = cst["tabs"], cst["g128"]
        W, bw = self.even_load_win(i, [(512, 1536), (2048, 3584)])
        fe = self.make_fe(128, 0)
        hnTs = [kb.sb("hnT", [128, 8, 128], BF16) for _ in range(2)]
        bhnT = kb.bufs(2)
        pPs = [(kb.ps("pP", [128, 512]), kb.buf()) for _ in range(3)]
        kr = kb.sb("kr", [128, 512])
        bkr = kb.buf()
        krot = kb.sb("krot", [128, 512], BF16)
        bkrot = kb.buf()
        v = kb.sb("v", [128, 512])
        bv = kb.buf()
        vw = kb.sb("vw", [128, 512], BF16)
        bvw = kb.buf()
        cs = kb.sb("cs", [128, 64])
        bcs = kb.buf()
        tmp = ([kb.sb("rt", [128, 256]) for _ in range(4)], kb.bufs(4))
        Hb = kb.sb("Hb", [64, 512])
        Hbb = kb.sb("Hbb", [64, 512], BF16)
        bHb, bHbb = kb.buf(), kb.buf()
        kb.op("pool", lambda e: e.memset(Hb[:], 0.0), writes=[bHb])
        kb.op("pool", lambda e: e.memset(Hbb[:], 0.0), writes=[bHbb])
        Ssb = kb.sb("Ssb", [64, 512])
        bS = kb.buf()
        nqk = [(kb.sb("nqk", [128, 8, 128], BF16), kb.buf()) for _ in range(2)]
        nvs = [(kb.sb("nv", [128, 512], BF16), kb.buf()) for _ in range(2)]
        pi = 0
        for c in reversed(range(T // 128)):
            r0 = c * 128
            hnT, bh = hnTs[c % 2], bhnT[c % 2]
            self.run_fe(fe, Xin, r0, cst["gpre"], bc, hnT, bh)
            kb.dma("act", cs[:], self.rope[r0:r0 + 128, :], writes=[bcs])
            pP, bpP = pPs[pi % 3]
            pi += 1
            self.proj_tok(W, bw, hnT, bh, 512, pP, bpP)
            kb.op("act", lambda e, pP=pP: e.activation(kr[:], pP[:, 0:512], AF.Copy, scale=0.125), reads=[bpP], writes=[bkr])
            self.rotary(kr, bkr, krot, bkrot, cs, bcs, tmp)
            pP, bpP = pPs[pi % 3]
            pi += 1
            self.proj_tok(W, bw, hnT, bh, 1024, pP, bpP)
            kb.op("act", lambda e, pP=pP: e.activation(v[:], pP[:, 0:512], AF.Copy), reads=[bpP], writes=[bv])
            kb.op("pool", lambda e: e.tensor_tensor(
                vw[:].rearrange("p (h d) -> p h d", h=8), v[:].rearrange("p (h d) -> p h d", h=8),
                tabs[:, 3, :].unsqueeze(2).to_broadcast([128, 8, 64]), ALU.mult), reads=[bv, bc], writes=[bvw])
            kb.dma("sp", self.RHB[c], Hbb[:], reads=[bHbb])
            pP, bpP = pPs[pi % 3]
            pi += 1

            def ms(e, pP=pP):
                for h in range(8):
                    ins = e.matmul(pP[0:64, h * 64:(h + 1) * 64], krot[:, h * 64:(h + 1) * 64], vw[:, h * 64:(h + 1) * 64],
                                   start=True, stop=True)
                return ins
            kb.op("pe", ms, reads=[bkrot, bvw], writes=[bpP])
            kb.op("act", lambda e, pP=pP: e.activation(Ssb[:], pP[0:64, 0:512], AF.Copy), reads=[bpP], writes=[bS])
            kb.op("dve", lambda e: e.tensor_tensor(
                Hb[:].rearrange("p (h d) -> p h d", h=8), Hb[:].rearrange("p (h d) -> p h d", h=8),
                g128[0:64, 8:16].unsqueeze(2).to_broadcast([64, 8, 64]), ALU.mult), reads=[bHb, bc], writes=[bHb])
            kb.op("pool", lambda e: e.tensor_tensor(Hb[:], Hb[:], Ssb[:], ALU.add), reads=[bHb, bS], writes=[bHb])
            kb.op("pool", lambda e: e.tensor_copy(Hbb[:], Hb[:]), reads=[bHb], writes=[bHbb])
            nq_, bnq = nqk[c % 2]
            for pr in range(8):
                col = 2048 + pr * 128
                pP, bpP = pPs[pi % 3]
                pi += 1

                def mf(e, pP=pP, col=col, hnT=hnT):
                    for k in range(8):
                        ins = e.matmul(pP[:, 0:128], W[:, k, col:col + 128], hnT[:, k, 0:128], start=(k == 0), stop=(k == 7))
                    return ins
                kb.op("pe", mf, reads=[bw, bh], writes=[bpP])
                kb.op("act", lambda e, pP=pP, pr=pr, nq_=nq_: e.activation(
                    nq_[:, pr, :], pP[:, 0:128], AF.Copy, scale=(0.125 if pr < 4 else 1.0)), reads=[bpP], writes=[bnq])
            kb.dma("sp", self.NQT[:, :, r0:r0 + 128].rearrange("c p t -> p c t"), nq_[:, 0:4, :], reads=[bnq])
            kb.dma("sp", self.NKT[:, :, r0:r0 + 128].rearrange("c p t -> p c t"), nq_[:, 4:8, :], reads=[bnq])
            nv_, bnv = nvs[c % 2]
            pP, bpP = pPs[pi % 3]
            pi += 1
            self.proj_tok(W, bw, hnT, bh, 3072, pP, bpP)
            kb.op("act", lambda e, pP=pP, nv_=nv_: e.activation(nv_[:], pP[:, 0:512], AF.Copy), reads=[bpP], writes=[bnv])
            kb.dma("sp", self.NV[r0:r0 + 128, :], nv_[:], reads=[bnv])

    def even_p3(self, i, l, Xin):
        kb = self.kb
        T = self.T
        cst = self.even_consts(i, l)
        bc = cst["b"]
        tabs, g128, lg = cst["tabs"], cst["g128"], cst["lg"]
        W, bw = self.even_load_win(i, [(0, 2048)])
        fe = self.make_fe(128, 0)
        pT, bpT = fe["pT"], fe["bpT"]
        dj = kb.sb("dj", [128, 2, 128])
        kb.dma("sp", dj[:], self.rdj.rearrange("m p j -> p m j"), writes=[bc])
        DT = kb.sb("DT", [128, 8, 128])
        for h in range(8):
            kb.op("dve", lambda e, h=h: e.tensor_scalar(DT[:, h, :], dj[:, 0, :], lg[:, h:h + 1], None, ALU.mult),
                  reads=[bc], writes=[bc])
            kb.op("dve", lambda e, h=h: e.scalar_tensor_tensor(DT[:, h, :], dj[:, 1, :], lg[:, 8 + h:9 + h], DT[:, h, :],
                                                               ALU.mult, ALU.add), reads=[bc], writes=[bc])
        kb.op("act", lambda e: e.activation(DT[:], DT[:], AF.Exp), reads=[bc], writes=[bc])
        gng = kb.sb("gng", [128, 512])
        kb.dma("sp", gng[:], self.rgn[i], writes=[bc])
        hnTs = [kb.sb("hnT", [128, 8, 128], BF16) for _ in range(2)]
        bhnT = kb.bufs(2)
        pPs = [(kb.ps("pP", [128, 512]), kb.buf()) for _ in range(2)]
        pSc = kb.ps("pSc", [128, 1024])
        bpSc = kb.buf()
        P3 = kb.ps("P3", [128, 1536])
        bP3 = kb.bufs(3)
        Pi, Pf, Pb = P3[:, 0:512], P3[:, 512:1024], P3[:, 1024:1536]
        raw = [(kb.sb("raw", [128, 512]), kb.buf()) for _ in range(2)]
        qrot = kb.sb("qrot", [128, 512], BF16)
        krot = kb.sb("krot", [128, 512], BF16)
        bqrot, bkrot = kb.buf(), kb.buf()
        v = kb.sb("v", [128, 512])
        vb = kb.sb("vb", [128, 512], BF16)
        vw = kb.sb("vw", [128, 512], BF16)
        bv, bvb, bvw = kb.buf(), kb.buf(), kb.buf()
        sg = kb.sb("sg", [128, 512])
        bsg = kb.buf()
        cs = kb.sb("cs", [128, 64])
        bcs = kb.buf()
        tmp = ([kb.sb("rt", [128, 256]) for _ in range(4)], kb.bufs(4))
        qT = kb.sb("qT", [64, 8, 128], BF16)
        kT = kb.sb("kT", [64, 8, 128], BF16)
        bqT, bkT = kb.buf(), kb.buf()
        WT = kb.sb("WT", [128, 8, 128], BF16)
        bWT = kb.buf()
        Ssc = kb.sb("Ssc", [128, 1024])
        bSsc = kb.buf()
        Hf = kb.sb("Hf", [64, 512])
        Hfb = kb.sb("Hfb", [64, 512], BF16)
        Hbb = kb.sb("Hbb", [64, 512], BF16)
        bHf, bHfb, bHbb = kb.buf(), kb.buf(), kb.buf()
        kb.op("pool", lambda e: e.memset(Hf[:], 0.0), writes=[bHf])
        kb.op("pool", lambda e: e.memset(Hfb[:], 0.0), writes=[bHfb])
        Y = kb.sb("Y", [128, 512])
        bY = kb.buf()
        Tt = kb.sb("Tt", [128, 512])
        bTt = kb.buf()
        Ssb = kb.sb("Ssb", [64, 512])
        bS = kb.buf()
        st = kb.sb("st", [128, 16])
        bst = kb.buf()
        yo = kb.sb("yo", [128, 512], BF16)
        byo = kb.buf()
        pi = 0
        for c in range(T // 128):
            r0 = c * 128
            hnT, bh = hnTs[c % 2], bhnT[c % 2]
            self.run_fe(fe, Xin, r0, cst["gpre"], bc, hnT, bh)
            kb.dma("act", cs[:], self.rope[r0:r0 + 128, :], writes=[bcs])
            kb.dma("act", Hbb[:], self.RHB[c], writes=[bHbb])
            for (col, dst, bd, sc) in ((0, qrot, bqrot, 1.0), (512, krot, bkrot, 0.125)):
                pP, bpP = pPs[pi % 2]
                rw, brw = raw[pi % 2]
                pi += 1
                self.proj_tok(W, bw, hnT, bh, col, pP, bpP)
                kb.op("act", lambda e, pP=pP, rw=rw, sc=sc: e.activation(rw[:], pP[:, 0:512], AF.Copy, scale=sc),
                      reads=[bpP], writes=[brw])
                self.rotary(rw, brw, dst, bd, cs, bcs, tmp)
            pP, bpP = pPs[pi % 2]
            pi += 1
            self.proj_tok(W, bw, hnT, bh, 1024, pP, bpP)
            kb.op("act", lambda e, pP=pP: e.activation(v[:], pP[:, 0:512], AF.Copy), reads=[bpP], writes=[bv])
            kb.op("pool", lambda e: e.tensor_copy(vb[:], v[:]), reads=[bv], writes=[bvb])
            kb.op("pool", lambda e: e.tensor_tensor(
                vw[:].rearrange("p (h d) -> p h d", h=8), v[:].rearrange("p (h d) -> p h d", h=8),
                tabs[:, 2, :].unsqueeze(2).to_broadcast([128, 8, 64]), ALU.mult), reads=[bv, bc], writes=[bvw])
            pP, bpP = pPs[pi % 2]
            pi += 1
            self.proj_tok(W, bw, hnT, bh, 1536, pP, bpP)
            kb.op("act", lambda e, pP=pP: e.activation(sg[:], pP[:, 0:512], AF.Silu), reads=[bpP], writes=[bsg])
            for (src, bs_, dstT, bdT) in ((qrot, bqrot, qT, bqT), (krot, bkrot, kT, bkT)):
                def tr(e, src=src):
                    for h in range(8):
                        ins = e.transpose(pT[0:64, h * 128:(h + 1) * 128], src[:, h * 64:(h + 1) * 64], self.ident[:])
                    return ins
                kb.op("pe", tr, reads=[bs_, self.bident], writes=[bpT])
                kb.op("act", lambda e, dstT=dstT: e.activation(
                    dstT[:], pT[0:64, :].rearrange("p (c t) -> p c t", c=8), AF.Copy), reads=[bpT], writes=[bdT])

            def msc(e):
                for h in range(8):
                    ins = e.matmul(pSc[:, h * 128:(h + 1) * 128], kT[:, h, :], qT[:, h, :], start=True, stop=True)
                return ins
            kb.op("pe", msc, reads=[bkT, bqT], writes=[bpSc])
            kb.op("act", lambda e: e.activation(Ssc[:], pSc[:], AF.Copy), reads=[bpSc], writes=[bSsc])
            kb.op("dve", lambda e: e.tensor_tensor(WT[:].rearrange("p h j -> p (h j)"), Ssc[:],
                                                   DT[:].rearrange("p h j -> p (h j)"), ALU.mult),
                  reads=[bSsc, bc], writes=[bWT])

            def mi(e):
                for h in range(8):
                    ins = e.matmul(Pi[:, h * 64:(h + 1) * 64], WT[:, h, :], vb[:, h * 64:(h + 1) * 64], start=True, stop=True)
                return ins
            kb.op("pe", mi, reads=[bWT, bvb], writes=[bP3[0]])
            for (Px, bPx, Hx, bHx) in ((Pf, bP3[1], Hfb, bHfb), (Pb, bP3[2], Hbb, bHbb)):
                def mx(e, Px=Px, Hx=Hx):
                    for h in range(8):
                        ins = e.matmul(Px[:, h * 64:(h + 1) * 64], qT[:, h, :], Hx[:, h * 64:(h + 1) * 64],
                                       start=True, stop=True)
                    return ins
                kb.op("pe", mx, reads=[bqT, bHx], writes=[bPx])
            kb.op("act", lambda e: e.activation(Y[:], Pi, AF.Copy), reads=[bP3[0]], writes=[bY])
            for (Px, bPx, t) in ((Pf, bP3[1], 0), (Pb, bP3[2], 1)):
                kb.op("act", lambda e, Px=Px: e.activation(Tt[:], Px, AF.Copy), reads=[bPx], writes=[bTt])
                kb.op("dve", lambda e, t=t: e.tensor_tensor(
                    Tt[:].rearrange("p (h d) -> p h d", h=8), Tt[:].rearrange("p (h d) -> p h d", h=8),
                    tabs[:, t, :].unsqueeze(2).to_broadcast([128, 8, 64]), ALU.mult), reads=[bTt, bc], writes=[bTt])
                kb.op("pool", lambda e: e.tensor_tensor(Y[:], Y[:], Tt[:], ALU.add), reads=[bTt, bY], writes=[bY])
            pP, bpP = pPs[pi % 2]
            pi += 1

            def ms(e, pP=pP):
                for h in range(8):
                    ins = e.matmul(pP[0:64, h * 64:(h + 1) * 64], krot[:, h * 64:(h + 1) * 64], vw[:, h * 64:(h + 1) * 64],
                                   start=True, stop=True)
                return ins
            kb.op("pe", ms, reads=[bkrot, bvw], writes=[bpP])
            kb.op("act", lambda e, pP=pP: e.activation(Ssb[:], pP[0:64, 0:512], AF.Copy), reads=[bpP], writes=[bS])
            kb.op("dve", lambda e: e.tensor_tensor(
                Hf[:].rearrange("p (h d) -> p h d", h=8), Hf[:].rearrange("p (h d) -> p h d", h=8),
                g128[0:64, 0:8].unsqueeze(2).to_broadcast([64, 8, 64]), ALU.mult), reads=[bHf, bc], writes=[bHf])
            kb.op("pool", lambda e: e.tensor_tensor(Hf[:], Hf[:], Ssb[:], ALU.add), reads=[bHf, bS], writes=[bHf])
            kb.op("pool", lambda e: e.tensor_copy(Hfb[:], Hf[:]), reads=[bHf], writes=[bHfb])
            Y3 = Y[:].rearrange("p (h d) -> p h d", h=8)
            T3 = Tt[:].rearrange("p (h d) -> p h d", h=8)
            kb.op("dve", lambda e: e.reduce_sum(st[:, 0:8], Y3, AX.X), reads=[bY], writes=[bst])
            kb.op("dve", lambda e: e.tensor_scalar(st[:, 0:8], st[:, 0:8], 1.0 / 64, None, ALU.mult), reads=[bst], writes=[bst])
            kb.op("dve", lambda e: e.tensor_tensor(Y3, Y3, st[:, 0:8].unsqueeze(2).to_broadcast([128, 8, 64]), ALU.subtract),
                  reads=[bY, bst], writes=[bY])
            kb.op("act", lambda e: e.activation(Tt[:], Y[:], AF.Square), reads=[bY], writes=[bTt])
            kb.op("dve", lambda e: e.reduce_sum(st[:, 8:16], T3, AX.X), reads=[bTt], writes=[bst])
            kb.op("act", lambda e: e.activation(st[:, 8:16], st[:, 8:16], AF.Sqrt, bias=EPS, scale=1.0 / 64), reads=[bst], writes=[bst])
            kb.op("dve", lambda e: e.reciprocal(st[:, 8:16], st[:, 8:16]), reads=[bst], writes=[bst])
            kb.op("dve", lambda e: e.tensor_tensor(Y3, Y3, st[:, 8:16].unsqueeze(2).to_broadcast([128, 8, 64]), ALU.mult),
                  reads=[bY, bst], writes=[bY])
            kb.op("pool", lambda e: e.tensor_tensor(Y[:], Y[:], gng[:], ALU.mult), reads=[bY, bc], writes=[bY])
            kb.op("pool", lambda e: e.tensor_tensor(yo[:], Y[:], sg[:], ALU.mult), reads=[bY, bsg], writes=[byo])
            kb.dma("sp", self.YN[r0:r0 + 128, 0:512], yo[:], reads=[byo])

    def na_stage(self, i):
        kb = self.kb
        T = self.T
        rows = T // 64
        tb = kb.sb("tb", [128, 2048])
        btb = kb.buf()
        kws = [(kb.sb("kw", [64, 8, 512], BF16), kb.buf()) for _ in range(2)]
        qws = [(kb.sb("qw", [64, 8, 64], BF16), kb.buf()) for _ in range(2)]
        vws = [(kb.sb("vwin", [128, 4, 8, 80], BF16), kb.buf()) for _ in range(2)]
        for (vw_, bvw_) in vws:
            kb.op("pool", lambda e, vw_=vw_: e.memset(vw_[:], 1.0), writes=[bvw_])
        Sb = kb.sb("Sb", [128, 2048])
        bSb = kb.buf()
        PTs = [(kb.sb("PT", [128, 2048], BF16), kb.buf()) for _ in range(2)]
        Osb = kb.sb("Osb", [64, 8, 65])
        bO = kb.buf()
        rc = kb.sb("rc", [64, 8])
        brc = kb.buf()
        nas = [(kb.sb("na", [64, 512], BF16), kb.buf()) for _ in range(2)]
        pST = kb.ps("pST", [128, 2048])
        bpST = kb.buf()
        pO = kb.ps("pO", [128, 1024])
        bpO = kb.buf()
        prev_s = None
        import os
        NCUT = int(os.environ.get("NA_CUT", "9"))
        stt_ = {"prev_s": None}

        def s1(r):
            start = min(max(r - 4, 0), rows - 8)
            s = r - start
            s0 = start * 64
            if s != stt_["prev_s"]:
                kb.dma("sp", tb[:], self.natab[i, s], writes=[btb])
                stt_["prev_s"] = s
            kw, bkw = kws[r % 2]
            qw, bqw = qws[r % 2]
            vw_, bvw_ = vws[r % 2]
            kb.dma("sp", kw[:], self.NKT[:, :, s0:s0 + 512].rearrange("c (two p) t -> p (c two) t", two=2), writes=[bkw])
            kb.dma("act", qw[:], self.NQT[:, :, r * 64:(r + 1) * 64].rearrange("c (two p) t -> p (c two) t", two=2),
                   writes=[bqw])
            for ck in range(4):
                kb.dma("act", vw_[:, ck, :, 0:64],
                       self.NV[s0 + ck * 128:s0 + (ck + 1) * 128, :].rearrange("p (h d) -> p h d", h=8), writes=[bvw_])

            def mst(e):
                for ck in range(4):
                    for h in range(8):
                        o = (ck * 8 + h) * 64
                        ins = e.matmul(pST[:, o:o + 64], kw[:, h, ck * 128:(ck + 1) * 128],
                                       qw[:, h, :], start=True, stop=True)
                return ins
            kb.op("pe", mst, reads=[bkw, bqw], writes=[bpST])
            kb.op("act", lambda e: e.activation(Sb[:], pST[:], AF.Copy), reads=[bpST], writes=[bSb])
            kb.op("dve", lambda e: e.tensor_tensor(Sb[:], Sb[:], tb[:], ALU.add), reads=[bSb, btb], writes=[bSb])
            PT, bPT = PTs[r % 2]
            kb.op("act", lambda e: e.activation(PT[:], Sb[:], AF.Exp), reads=[bSb], writes=[bPT])

        def s2(r):
            vw_, bvw_ = vws[r % 2]
            PT, bPT = PTs[r % 2]

            def mpv(e):
                for h in range(8):
                    oc = (h % 4) * 80 + (h // 4) * 512
                    for ck in range(4):
                        o = (ck * 8 + h) * 64
                        ins = e.matmul(pO[0:64, oc:oc + 65], PT[:, o:o + 64], vw_[:, ck, h, 0:65],
                                       start=(ck == 0), stop=(ck == 3))
                return ins
            kb.op("pe", mpv, reads=[bPT, bvw_], writes=[bpO])
            kb.op("act", lambda e: e.activation(Osb[:, 0:4, :], pO[0:64, 0:320].rearrange("p (h d) -> p h d", h=4)[:, :, 0:65], AF.Copy),
                  reads=[bpO], writes=[bO])
            kb.op("act", lambda e: e.activation(Osb[:, 4:8, :], pO[0:64, 512:832].rearrange("p (h d) -> p h d", h=4)[:, :, 0:65], AF.Copy),
                  reads=[bpO], writes=[bO])
            kb.op("dve", lambda e: e.reciprocal(rc[:].unsqueeze(2), Osb[:, :, 64:65]), reads=[bO], writes=[brc])
            na, bna = nas[r % 2]
            kb.op("dve", lambda e: e.tensor_tensor(
                na[:].rearrange("p (h d) -> p h d", h=8), Osb[:, :, 0:64],
                rc[:].unsqueeze(2).to_broadcast([64, 8, 64]), ALU.mult), reads=[bO, brc], writes=[bna])
            kb.dma("sp", self.YN[r * 64:(r + 1) * 64, 512:1024], na[:], reads=[bna])

        s1(0)
        for r in range(rows):
            if r + 1 < rows:
                s1(r + 1)
            s2(r)

    def outproj_stage(self, l, w_src, Cin, YN, Xin, Xout):
        kb = self.kb
        T = self.T
        KC = Cin // 128
        W = kb.sb("wout", [128, KC, D], BF16)
        bw = kb.buf()
        for k in range(KC):
            kb.dma("pool", W[:, k, :], w_src[k * 128:(k + 1) * 128, :], writes=[bw])
        gpost = kb.sb("gpost", [128, D])
        bc = kb.buf()
        kb.dma("sp", gpost[:], self.gains[l * 4 + 1], writes=[bc])
        yns = [(kb.sb("yn", [128, Cin], BF16), kb.buf()) for _ in range(2)]
        YTs = [(kb.sb("YT", [128, KC, 128], BF16), kb.buf()) for _ in range(2)]
        xrs = [(kb.sb("xr", [128, D]), kb.buf()) for _ in range(2)]
        xos = [(kb.sb("xo", [128, D]), kb.buf()) for _ in range(2)]
        junk = kb.sb("junk", [128, D])
        bjunk = kb.buf()
        ss = kb.sb("ss", [128, 1])
        bss = kb.buf()
        pTs = [(kb.ps("pT", [128, D], BF16), kb.buf()) for _ in range(2)]
        pm = kb.ps("pm", [128, D])
        bpm = kb.buf()
        ti = 0
        for c in range(T // 128):
            r0 = c * 128
            yn, byn = yns[c % 2]
            YT, bYT = YTs[c % 2]
            xr, bxr = xrs[c % 2]
            xo, bxo = xos[c % 2]
            kb.dma("act", yn[:], YN[r0:r0 + 128, 0:Cin], writes=[byn])
            kb.dma("sp", xr[:], Xin[r0:r0 + 128, :], writes=[bxr])
            for r in range(KC // 8):
                pT, bpT = pTs[ti % 2]
                ti += 1

                def tr(e, yn=yn, pT=pT, r=r):
                    for k in range(8):
                        kk = r * 8 + k
                        ins = e.transpose(pT[:, k * 128:(k + 1) * 128], yn[:, kk * 128:(kk + 1) * 128], self.ident[:])
                    return ins
                kb.op("pe", tr, reads=[byn, self.bident], writes=[bpT])
                kb.op("act", lambda e, YT=YT, pT=pT, r=r: e.activation(
                    YT[:, r * 8:(r + 1) * 8, :], pT[:].rearrange("p (k t) -> p k t", k=8), AF.Copy),
                    reads=[bpT], writes=[bYT])

            def mm(e, YT=YT):
                for nh in range(2):
                    for k in range(KC):
                        ins = e.matmul(pm[:, nh * 512:(nh + 1) * 512], YT[:, k, :], W[:, k, nh * 512:(nh + 1) * 512],
                                       start=(k == 0), stop=(k == KC - 1))
                return ins
            kb.op("pe", mm, reads=[bYT, bw], writes=[bpm])
            self.rms_stats(pm[:], bpm, ss, bss, junk, bjunk)
            kb.op("act", lambda e, xo=xo: e.activation(xo[:], pm[:], AF.Copy, scale=ss[:, 0:1]),
                  reads=[bpm, bss], writes=[bxo])
            kb.op("dve", lambda e, xo=xo: e.tensor_tensor(xo[:], xo[:], gpost[:], ALU.mult),
                  reads=[bxo, bc], writes=[bxo])
            kb.op("pool", lambda e, xo=xo, xr=xr: e.tensor_tensor(xo[:], xo[:], xr[:], ALU.add),
                  reads=[bxo, bxr], writes=[bxo])
            kb.dma("sp", Xout[r0:r0 + 128, :], xo[:], reads=[bxo])

    def ffn_stage(self, l, Xin, Xout, BT=512):
        kb = self.kb
        T = self.T
        NT = BT // 128
        ident, bident = self.ident, self.bident
        W_up = kb.sb("wup", [128, 8, 2 * FF], BF16)
        W_dn = kb.sb("wdn", [128, NCH, D], BF16)
        bwu = kb.bufs(8)
        bwd = kb.bufs(NCH)
        for k in range(8):
            kb.dma("pool", W_up[:, k, :], self.ffn_w_up[l, k * 128:(k + 1) * 128, :], writes=[bwu[k]])
        for c in range(NCH):
            kb.dma("pool", W_dn[:, c, :], self.ffn_w_down[l, c * 128:(c + 1) * 128, :], writes=[bwd[c]])
        gpre = kb.sb("gpre", [128, D])
        gpost = kb.sb("gpost", [128, D])
        fcw = kb.sb("fcw", [128, 2 * NCH, 3])
        fcb = kb.sb("fcb", [128, 2 * NCH])
        bconst = kb.buf()
        kb.dma("sp", gpre[:], self.gains[l * 4 + 2], writes=[bconst])
        kb.dma("sp", gpost[:], self.gains[l * 4 + 3], writes=[bconst])
        kb.dma("sp", fcw[:], self.fcw[l].rearrange("p (c w) -> p c w", w=3), writes=[bconst])
        kb.dma("sp", fcb[:], self.fcb[l], writes=[bconst])

        xts = [kb.sb("xt", [128, D])] * 2
        bxt = [kb.buf()] * 2
        hns = [kb.sb("hn", [128, D], BF16)] * 2
        bhn = [kb.buf()] * 2
        sss = [kb.sb("ss", [128, 1]) for _ in range(2)]
        bss = kb.bufs(2)
        xh = kb.sb("xh", [2, D])
        bxh = kb.buf()
        hnh = kb.sb("hnh", [2, D], BF16)
        bhnh = kb.buf()
        ssh = kb.sb("ssh", [2, 1])
        bssh = kb.buf()
        hnTs = [kb.sb("hnT", [128, 8, BT + 2], BF16)] * 2
        bhnT = [kb.buf()] * 2
        gT = kb.sb("gT", [128, NCH, BT], BF16)
        bgT = kb.bufs(NCH)
        cgs = [kb.sb("cg", [128, BT]) for _ in range(2)]
        cvs = [kb.sb("cv", [128, BT]) for _ in range(2)]
        ggs = [kb.sb("gg", [128, BT])] * 2
        bcg, bcv, bgg = kb.bufs(2), kb.bufs(2), [kb.buf()] * 2
        ugs = [kb.sb("ug", [128, BT + 2])] * 2
        uvs = [kb.sb("uv", [128, BT + 2])] * 2
        bug, buv = [kb.buf()] * 2, [kb.buf()] * 2
        xrs = [kb.sb("xr", [128, D]) for _ in range(1)]
        bxr = kb.bufs(1)
        junk, bjunk = xrs[0], bxr[0]
        xos = [kb.sb("xo", [128, D])] * 2
        bxo = [kb.buf()] * 2
        ss2 = kb.sb("ss2", [128, 1])
        bss2 = kb.buf()

        psG = [kb.ps("psG", [128, 512]) for _ in range(2)]
        psV = [kb.ps("psV", [128, 512]) for _ in range(2)]
        bpsA = kb.bufs(2)
        psH = kb.ps("psH", [128, 512])
        bpsH = kb.buf()
        pT = kb.ps("pT", [128, D], BF16)
        bpT = kb.buf()
        pTh, bpTh = pT, bpT
        pf = kb.ps("pf", [128, D])
        bpf = kb.buf()

        ti = 0
        ci = 0
        import os
        CUT = int(os.environ.get("FFN_CUT", "9"))
        for b in range(T // BT if CUT > 1 else 0):
            t0 = b * BT
            hnT = hnTs[b % 2]
            bh = bhnT[b % 2]
            for j in range(NT):
                xt, bx = xts[ti % 2], bxt[ti % 2]
                hn, bn = hns[ti % 2], bhn[ti % 2]
                ss, bs = sss[ti % 2], bss[ti % 2]
                ti += 1
                r0 = t0 + j * 128
                kb.dma("sp", xt[:], Xin[r0:r0 + 128, :], writes=[bx])
                self.rms_stats(xt[:], bx, ss, bs, junk, bjunk)
                kb.op("dve", lambda e, hn=hn, xt=xt, ss=ss: e.scalar_tensor_tensor(
                    hn[:], xt[:], ss[:, 0:1], gpre[:], ALU.mult, ALU.mult),
                    reads=[bx, bs, bconst], writes=[bn])

                def tr(e, hn=hn):
                    for k in range(8):
                        ins = e.transpose(pT[:, k * 128:(k + 1) * 128], hn[:, k * 128:(k + 1) * 128], ident[:])
                    return ins
                kb.op("pe", tr, reads=[bn, bident], writes=[bpT])
                kb.op("act", lambda e, hnT=hnT, j=j: e.activation(
                    hnT[:, :, j * 128:(j + 1) * 128], pT[:].rearrange("p (k t) -> p k t", k=8), AF.Copy),
                    reads=[bpT], writes=[bh])
            kb.op("pool", lambda e: e.memset(xh[:], 0.0), writes=[bxh])
            if t0 - 1 >= 0:
                kb.dma("sp", xh[0:1, :], Xin[t0 - 1:t0, :], writes=[bxh])
            if t0 + BT < T:
                kb.dma("sp", xh[1:2, :], Xin[t0 + BT:t0 + BT + 1, :], writes=[bxh])
            self.rms_stats(xh[:], bxh, ssh, bssh, junk, bjunk, nparts=2)
            kb.op("dve", lambda e: e.scalar_tensor_tensor(
                hnh[:], xh[:], ssh[:, 0:1], gpre[0:2, :], ALU.mult, ALU.mult),
                reads=[bxh, bssh, bconst], writes=[bhnh])

            def trh(e):
                for k in range(8):
                    ins = e.transpose(pTh[:, k * 2:(k + 1) * 2], hnh[:, k * 128:(k + 1) * 128], ident[0:2, 0:2])
                return ins
            kb.op("pe", trh, reads=[bhnh, bident], writes=[bpTh])
            kb.op("act", lambda e, hnT=hnT: e.activation(
                hnT[:, :, BT:BT + 2], pTh[:, 0:16].rearrange("p (k t) -> p k t", k=8), AF.Copy),
                reads=[bpTh], writes=[bh])

            for c in range(NCH if CUT > 2 else 0):
                par = ci % 2
                ci += 1
                pg = psG[par][:, 0:BT]
                pv = psV[par][:, 0:BT]
                ph = psH[:, 0:4]
                cg, cv, gg = cgs[par], cvs[par], ggs[par]

                SUB = os.environ.get("FFN_SUB", "")

                def mm(e, c=c, pg=pg, pv=pv, ph=ph, hnT=hnT):
                    lst = ((pg, c * 128, slice(0, BT)), (pv, FF + c * 128, slice(0, BT)),
                           (ph[:, 0:2], c * 128, slice(BT, BT + 2)),
                           (ph[:, 2:4], FF + c * 128, slice(BT, BT + 2)))
                    if "h" in SUB:
                        lst = lst[:2]
                    for (dst, col, rhs_sl) in lst:
                        for k in range(8):
                            ins = e.matmul(dst, W_up[:, k, col:col + 128], hnT[:, k, rhs_sl],
                                           start=(k == 0), stop=(k == 7))
                    return ins
                kb.op("pe", mm, reads=bwu + [bh], writes=[bpsA[par], bpsH])
                pairs = ((pg, 0, cg, bcg[par], c, ugs[par], bug[par]), (pv, 2, cv, bcv[par], NCH + c, uvs[par], buv[par]))
                for (src, hoff, dst, bd, cc, u, bu) in pairs:
                    kb.op("act", lambda e, src=src, u=u: e.activation(u[:, 0:BT], src, AF.Copy),
                          reads=[bpsA[par]], writes=[bu])
                    kb.op("act", lambda e, ph=ph, hoff=hoff, u=u: e.activation(
                        u[:, BT:BT + 2], ph[:, hoff:hoff + 2], AF.Copy), reads=[bpsH], writes=[bu])
                    kb.op("act", lambda e, src=src, dst=dst, cc=cc: e.activation(
                        dst[:], src, AF.Identity, bias=fcb[:, cc:cc + 1], scale=fcw[:, cc, 1:2]),
                        reads=[bpsA[par], bconst], writes=[bd])
                for tap in range(4):
                    for (src, hoff, dst, bd, cc, u, bu) in pairs:
                        if tap == 0:
                            kb.op("dve", lambda e, u=u, dst=dst, cc=cc: e.scalar_tensor_tensor(
                                dst[:, 1:BT], u[:, 0:BT - 1], fcw[:, cc, 0:1], dst[:, 1:BT], ALU.mult, ALU.add),
                                reads=[bu, bconst, bd], writes=[bd])
                        elif tap == 1:
                            kb.op("dve", lambda e, u=u, dst=dst, cc=cc: e.scalar_tensor_tensor(
                                dst[:, 0:BT - 1], u[:, 1:BT], fcw[:, cc, 2:3], dst[:, 0:BT - 1], ALU.mult, ALU.add),
                                reads=[bu, bconst, bd], writes=[bd])
                        elif tap == 2:
                            kb.op("dve", lambda e, u=u, dst=dst, cc=cc: e.scalar_tensor_tensor(
                                dst[:, 0:1], u[:, BT:BT + 1], fcw[:, cc, 0:1], dst[:, 0:1], ALU.mult, ALU.add),
                                reads=[bu, bconst, bd], writes=[bd])
                        else:
                            kb.op("dve", lambda e, u=u, dst=dst, cc=cc: e.scalar_tensor_tensor(
                                dst[:, BT - 1:BT], u[:, BT + 1:BT + 2], fcw[:, cc, 2:3], dst[:, BT - 1:BT],
                                ALU.mult, ALU.add),
                                reads=[bu, bconst, bd], writes=[bd])
                kb.op("act", lambda e, gg=gg, cg=cg: e.activation(gg[:], cg[:], AF.Gelu_apprx_tanh),
                      reads=[bcg[par]], writes=[bgg[par]])
                kb.op("pool", lambda e, c=c, gg=gg, cv=cv: e.tensor_tensor(gT[:, c, :], gg[:], cv[:], ALU.mult),
                      reads=[bgg[par], bcv[par]], writes=[bgT[c]])

            for j in range(NT if CUT > 3 else 0):
                r0 = t0 + j * 128

                def dn(e, j=j):
                    for nh in range(2):
                        for c in range(NCH):
                            ins = e.matmul(pf[:, nh * 512:(nh + 1) * 512], gT[:, c, j * 128:(j + 1) * 128],
                                           W_dn[:, c, nh * 512:(nh + 1) * 512], start=(c == 0), stop=(c == NCH - 1))
                    return ins
                kb.op("pe", dn, reads=bgT + bwd, writes=[bpf])
                self.rms_stats(pf[:], bpf, ss2, bss2, junk, bjunk)
                xr, bxrr = xrs[0], bxr[0]
                xo, bxoo = xos[j % 2], bxo[j % 2]
                kb.dma("sp", xr[:], Xin[r0:r0 + 128, :], writes=[bxrr])
                kb.op("act", lambda e, xo=xo: e.activation(xo[:], pf[:], AF.Copy, scale=ss2[:, 0:1]),
                      reads=[bpf, bss2], writes=[bxoo])
                kb.op("dve", lambda e, xo=xo: e.tensor_tensor(xo[:], xo[:], gpost[:], ALU.mult),
                      reads=[bxoo, bconst], writes=[bxoo])
                kb.op("pool", lambda e, xo=xo, xr=xr: e.tensor_tensor(xo[:], xo[:], xr[:], ALU.add),
                      reads=[bxoo, bxrr], writes=[bxoo])
                kb.dma("sp", Xout[r0:r0 + 128, :], xo[:], reads=[bxoo])


def host_rope(pos):
    half = 32
    inv = (1.0 / (10000.0 ** (np.arange(half, dtype=np.float32) / half))).astype(np.float32)
    ang = pos.astype(np.float32)[:, None] * inv[None, :]
    return np.concatenate([np.cos(ang), np.sin(ang)], axis=1).astype(np.float32)


def host_even(inp):
    lgt = inp["ab_ret_decay_logit"].reshape(2, 1, 16)
    rlogit = np.ascontiguousarray(np.broadcast_to(lgt, (2, 128, 16))).astype(np.float32)
    p = np.arange(128, dtype=np.float32)
    ridx = np.stack([p + 1, 128 - p, 127 - p, p], axis=1).astype(np.float32)
    l_ = np.arange(128)[:, None]
    j_ = np.arange(128)[None, :]
    rdj = np.stack([np.maximum(j_ - l_, 0), np.maximum(l_ - j_, 0)], 0).astype(np.float32)
    rgn = np.ascontiguousarray(np.broadcast_to(inp["ab_ret_gn_g"].reshape(2, 1, 512), (2, 128, 512))).astype(np.float32)
    rpb = inp["ab_na_rpb"]
    kk = np.arange(512)
    w = kk // 64
    kc = kk % 64
    q = np.arange(64)
    cstart = np.clip(q - 8, 0, 48)
    valid = (kc[:, None] >= cstart[None, :]) & (kc[:, None] < cstart[None, :] + 16)
    dc = np.clip(kc[:, None] - q[None, :], -15, 15) + 15
    natab = np.empty((2, 8, 512, 8, 64), np.float32)
    for s in range(8):
        dr = np.clip(w - s + 7, 0, 14)
        g = rpb[:, :, dr[:, None], dc]
        g = np.where(valid[None, None], g, np.float32(-30000.0))
        natab[:, s] = g.transpose(0, 2, 1, 3)
    natab = natab.reshape(2, 8, 4, 128, 8, 64).transpose(0, 1, 3, 2, 4, 5).reshape(2, 8, 128, 2048)
    return {"ab_w_in": np.ascontiguousarray(inp["ab_w_in"]), "ab_w_out": np.ascontiguousarray(inp["ab_w_out"]),
            "rlogit": rlogit, "ridx": ridx, "rdj": rdj, "rgn": rgn, "natab": np.ascontiguousarray(natab)}

def host_prep(inp, layers=range(4)):
    L = L_TOTAL
    g = np.stack([inp["norm_mix_pre"], inp["norm_mix_post"], inp["norm_ffn_pre"], inp["norm_ffn_post"]], axis=1)
    gains = np.ascontiguousarray(np.broadcast_to(g.reshape(L * 4, 1, D), (L * 4, 128, D))).astype(np.float32)
    cw = inp["ffn_conv_w"]
    fcw = np.ascontiguousarray(cw.reshape(L, 3, 2 * NCH, 128).transpose(0, 3, 2, 1)).reshape(L, 128, 2 * NCH * 3)
    fcb = np.ascontiguousarray(inp["ffn_conv_b"].reshape(L, 2 * NCH, 128).transpose(0, 2, 1))
    cmats = np.zeros((5, 128, 128), np.float32)
    k = np.arange(128)[:, None]
    j = np.arange(128)[None, :]
    cmats[0] = k <= j
    cmats[1] = k < j
    cmats[2] = k > j
    cmats[3] = k >= j
    cmats[4] = 1.0
    ccw = inp["c_conv_w"]
    scw = np.ascontiguousarray(ccw.reshape(2, 5, 24, 128).transpose(0, 3, 2, 1)).reshape(2, 128, 120)
    scb = np.ascontiguousarray(inp["c_conv_b"].reshape(2, 24, 128).transpose(0, 2, 1))

    def rep(a, n):
        return np.ascontiguousarray(np.broadcast_to(a.reshape(2, 1, n), (2, 128, n))).astype(np.float32)
    extra = {"c_w_in": np.ascontiguousarray(inp["c_w_in"]), "c_w_out": np.ascontiguousarray(inp["c_w_out"]),
             "scw": scw.astype(np.float32), "scb": scb.astype(np.float32),
             "sdtb": rep(inp["c_dt_bias"], 64), "salog": rep(inp["c_a_log"], 64),
             "sdsk": rep(inp["c_d_skip"], 32), "sng": rep(inp["c_norm_g"], 2048), "cmats": cmats}
    extra.update(host_even(inp))
    return {**extra, "gains": gains, "ffn_w_up": np.ascontiguousarray(inp["ffn_w_up"]),
            "ffn_w_down": np.ascontiguousarray(inp["ffn_w_down"]),
            "fcw": fcw.astype(np.float32), "fcb": fcb.astype(np.float32)}


_CACHE = {}


def kernel(**inputs):
    x = np.asarray(inputs["x"], dtype=np.float32)
    B, S, _ = x.shape
    n_cores = 8
    cpb = n_cores // B
    if "prog" not in _CACHE:
        _CACHE["prog"] = Prog(S, [0, 1, 2, 3], do_mix=True, do_ffn=True)
    P = _CACHE["prog"]
    hp = host_prep({k: np.asarray(v, dtype=np.float32) for k, v in inputs.items()})
    hp["rope"] = host_rope(np.arange(S))
    in_maps = []
    for c in range(n_cores):
        m = dict(hp)
        m["x"] = np.ascontiguousarray(x[c // cpb])
        in_maps.append(m)
    res = run_bass_kernel_spmd(P.nc, in_maps, core_ids=list(range(n_cores)))
    out = np.stack([res.results[b * cpb]["out"] for b in range(B)], axis=0)
    return out.astype(np.float32)
```

```python
from contextlib import ExitStack
import numpy as np
import concourse.bass as bass
import concourse.mybir as mybir
from concourse.bass_utils import run_bass_kernel_spmd

F32 = mybir.dt.float32
BF16 = mybir.dt.bfloat16
ALU = mybir.AluOpType
AF = mybir.ActivationFunctionType
AX = mybir.AxisListType

EPOCH = 8192
SAME_ENGINE_SYNC = True
OWN_DIST = 10 ** 9


class Buf:
    __slots__ = ("name", "w", "r")

    def __init__(self, name):
        self.name = name
        self.w = None
        self.r = []


class Op:
    __slots__ = ("eng", "emit", "waits", "inc")


class KB:
    ENGS = ("sp", "act", "pool", "pe", "dve")

    def __init__(self, nc, n_dma_sems=16):
        self.nc = nc
        self.gstack = ExitStack()
        self.stack = self.gstack
        self.stream = {e: [] for e in self.ENGS}
        self.count = {e: 0 for e in self.ENGS}
        self.csems = {e: [] for e in self.ENGS}
        self.waited = {e: {} for e in self.ENGS}
        self.dma_pool = {}
        self.dma_next = {}
        self.n_dma_sems = n_dma_sems
        self.semobjs = {}
        self.nbuf = 0
        self.nalloc = 0

    def sem(self, name):
        s = self.gstack.enter_context(self.nc.semaphore(name))
        self.semobjs[name] = s
        return name

    def sb(self, name, shape, dt=F32):
        self.nalloc += 1
        return self.stack.enter_context(self.nc.sbuf_tensor(f"{name}_{self.nalloc}", list(shape), dt))

    def ps(self, name, shape, dt=F32):
        self.nalloc += 1
        return self.stack.enter_context(self.nc.psum_tensor(f"{name}_{self.nalloc}", list(shape), dt))

    def buf(self, name=None):
        self.nbuf += 1
        return Buf(name or f"b{self.nbuf}")

    def bufs(self, n):
        return [self.buf() for _ in range(n)]

    def _deps(self, reads, writes):
        ids = []
        for b in reads:
            if b.w is not None:
                ids.append(b.w)
        for b in writes:
            if b.w is not None:
                ids.append(b.w)
            ids.extend(b.r)
        return ids

    def _finish(self, eng, emit, reads, writes, cid3, ids):
        cid = cid3[:2]
        need = {}
        for t in ids:
            if t[1] > need.get(t[0], 0):
                need[t[0]] = t[1]
        waits = []
        wd = self.waited[eng]
        own = self.csems[eng]
        cur = self.count[eng] if cid3[2] == 1 else None
        for s, v in need.items():
            if s in own:
                if not SAME_ENGINE_SYNC:
                    continue
                if cur is not None and cur - (own.index(s) * EPOCH + v) >= OWN_DIST:
                    continue
            if wd.get(s, 0) >= v:
                continue
            wd[s] = v
            waits.append((s, v))
        op = Op()
        op.eng, op.emit, op.waits, op.inc = eng, emit, waits, cid3
        self.stream[eng].append(op)
        for b in reads:
            b.r.append(cid)
        for b in writes:
            b.w = cid
            b.r = []
        return cid

    def op(self, eng, emit, reads=(), writes=()):
        n = self.count[eng]
        self.count[eng] = n + 1
        k = n // EPOCH
        while len(self.csems[eng]) <= k:
            self.csems[eng].append(self.sem(f"c_{eng}_{len(self.csems[eng])}"))
        cid3 = (self.csems[eng][k], (n % EPOCH) + 1, 1)
        return self._finish(eng, emit, reads, writes, cid3, self._deps(reads, writes))

    def dma(self, eng, out, in_, reads=(), writes=(), **kw):
        if eng not in self.dma_pool:
            self.dma_pool[eng] = [[self.sem(f"d_{eng}_{i}"), 0] for i in range(self.n_dma_sems)]
            self.dma_next[eng] = 0
        i = self.dma_next[eng]
        self.dma_next[eng] = (i + 1) % self.n_dma_sems
        slot = self.dma_pool[eng][i]
        prev = slot[1]
        slot[1] = prev + 16
        cid3 = (slot[0], slot[1], 16)

        def emit(e, out=out, in_=in_, kw=kw):
            return e.dma_start(out=out, in_=in_, **kw)

        ids = self._deps(reads, writes)
        if prev > 0:
            ids = ids + [(slot[0], prev)]
        return self._finish(eng, emit, reads, writes, cid3, ids)

    def coll(self, emit, reads=(), writes=()):
        eng = "pool"
        if eng not in self.dma_pool:
            self.dma_pool[eng] = [[self.sem(f"d_{eng}_{i}"), 0] for i in range(self.n_dma_sems)]
            self.dma_next[eng] = 0
        i = self.dma_next[eng]
        self.dma_next[eng] = (i + 1) % self.n_dma_sems
        slot = self.dma_pool[eng][i]
        prev = slot[1]
        slot[1] = prev + 16
        cid3 = (slot[0], slot[1], 16)
        ids = self._deps(reads, writes)
        if prev > 0:
            ids = ids + [(slot[0], prev)]
        return self._finish(eng, emit, reads, writes, cid3, ids)

    def all_ids(self):
        ids = []
        for e in self.ENGS:
            n = self.count[e]
            if n > 0:
                ids.append((self.csems[e][(n - 1) // EPOCH], ((n - 1) % EPOCH) + 1))
        for e in self.dma_pool:
            for (s, v) in self.dma_pool[e]:
                if v > 0:
                    ids.append((s, v))
        return ids

    def barrier(self):
        ids = self.all_ids()
        for e in self.ENGS:
            waits = []
            wd = self.waited[e]
            for (s, v) in ids:
                if wd.get(s, 0) >= v:
                    continue
                wd[s] = v
                waits.append((s, v))
            op = Op()
            op.eng, op.emit, op.waits, op.inc = e, None, waits, None
            self.stream[e].append(op)

    def emit_all(self):
        nc = self.nc
        so = self.semobjs
        with nc.Block() as block:
            decos = {"sp": block.sync, "act": block.scalar, "pool": block.gpsimd,
                     "pe": block.tensor, "dve": block.vector}
            for name in self.ENGS:
                ops = self.stream[name]
                if not ops:
                    continue

                def f(eng, ops=ops):
                    for op in ops:
                        for (s, v) in op.waits:
                            eng.wait_ge(so[s], v)
                        if op.emit is None:
                            continue
                        ins = op.emit(eng)
                        if op.inc is not None:
                            ins.then_inc(so[op.inc[0]], op.inc[2])
                decos[name](f)
        self.stream = {e: [] for e in self.ENGS}

    def stage(self):
        return _Stage(self)


class _Stage:
    def __init__(self, kb):
        self.kb = kb

    def __enter__(self):
        self.st = ExitStack()
        self.kb.stack = self.st
        return self

    def __exit__(self, *a):
        self.kb.barrier()
        self.kb.emit_all()
        self.kb.stack = self.kb.gstack
        self.st.close()
        return False


D = 1024
FF = 2816
NCH = FF // 128
EPS = 1e-6
L_TOTAL = 4


class Prog:
    def __init__(self, T, layers, do_mix=True, do_ffn=True):
        self.T = T
        self.layers = layers
        nc = bass.Bass("TRN2", target_bir_lowering=False)
        self.nc = nc
        kb = KB(nc)
        self.kb = kb
        L = L_TOTAL

        def inp(name, shape):
            return nc.dram_tensor(name, list(shape), F32, kind="ExternalInput").ap()

        self.x = inp("x", [T, D])
        self.gains = inp("gains", [L * 4, 128, D])
        self.ffn_w_up = inp("ffn_w_up", [L, D, 2 * FF])
        self.ffn_w_down = inp("ffn_w_down", [L, FF, D])
        self.fcw = inp("fcw", [L, 128, 2 * NCH * 3])
        self.fcb = inp("fcb", [L, 128, 2 * NCH])
        self.c_w_in = inp("c_w_in", [2, D, 5184])
        self.c_w_out = inp("c_w_out", [2, 2048, D])
        self.scw = inp("scw", [2, 128, 24 * 5])
        self.scb = inp("scb", [2, 128, 24])
        self.sdtb = inp("sdtb", [2, 128, 64])
        self.salog = inp("salog", [2, 128, 64])
        self.sdsk = inp("sdsk", [2, 128, 32])
        self.sng = inp("sng", [2, 128, 2048])
        self.cmats = inp("cmats", [5, 128, 128])
        self.ab_w_in = inp("ab_w_in", [2, D, 3584])
        self.ab_w_out = inp("ab_w_out", [2, D, D])
        self.rlogit = inp("rlogit", [2, 128, 16])
        self.ridx = inp("ridx", [128, 4])
        self.rdj = inp("rdj", [2, 128, 128])
        self.rgn = inp("rgn", [2, 128, 512])
        self.rope = inp("rope", [T, 64])
        self.natab = inp("natab", [2, 8, 128, 2048])
        self.out = nc.dram_tensor("out", [T, D], F32, kind="ExternalOutput").ap()
        self.YN = nc.dram_tensor("YN", [T, 2048], BF16).ap()
        self.HB = nc.dram_tensor("HB", [T // 128, 128, 2048], BF16).ap()
        self.RHB = nc.dram_tensor("RHB", [T // 128, 64, 512], BF16).ap()
        self.XTK = nc.dram_tensor("XTK", [T, 2048], BF16).ap()
        self.BTK = nc.dram_tensor("BTK", [T, 512], BF16).ap()
        self.BTF = nc.dram_tensor("BTF", [4, 128, T], BF16).ap()
        self.CTF = nc.dram_tensor("CTF", [4, 128, T], BF16).ap()
        self.DTL = nc.dram_tensor("DTL", [T, 128], F32).ap()
        self.ZS = nc.dram_tensor("ZS", [T, 2048], BF16).ap()
        self.NQT = nc.dram_tensor("NQT", [4, 128, T], BF16).ap()
        self.NKT = nc.dram_tensor("NKT", [4, 128, T], BF16).ap()
        self.NV = nc.dram_tensor("NV", [T, 512], BF16).ap()
        self.Xa = nc.dram_tensor("Xa", [T, D], F32).ap()
        self.Xb = nc.dram_tensor("Xb", [T, D], F32).ap()

        self.ident = kb.sb("ident", [128, 128], BF16)
        self.bident = kb.buf()
        with kb.stage():
            one = kb.sb("one", [128, 128])
            idf = kb.sb("idf", [128, 128])
            b1, b2 = kb.buf(), kb.buf()
            kb.op("pool", lambda e: e.memset(one[:], 1.0), writes=[b1])
            kb.op("pool", lambda e: e.affine_select(idf[:], one[:], [[-1, 128]], ALU.is_equal, 0.0,
                                                    base=0, channel_multiplier=1), reads=[b1], writes=[b2])
            kb.op("pool", lambda e: e.tensor_copy(self.ident[:], idf[:]), reads=[b2], writes=[self.bident])

        cur = self.x
        for li, l in enumerate(layers):
            last = li == len(layers) - 1
            if do_mix:
                i = l // 2
                mdst = self.out if (last and not do_ffn) else self.Xa
                if l % 2 == 0:
                    import os
                    EVS = os.environ.get("EV_STAGES", "1234")
                    if "1" in EVS:
                        with kb.stage():
                            self.even_p1(i, l, cur)
                    if "2" in EVS:
                        with kb.stage():
                            self.na_stage(i)
                    if "3" in EVS:
                        with kb.stage():
                            self.even_p3(i, l, cur)
                    if "4" in EVS:
                        with kb.stage():
                            self.outproj_stage(l, self.ab_w_out[i], 1024, self.YN, cur, mdst)
                else:
                    import os
                    SDS = os.environ.get("SSD_STAGES", "123")
                    if "1" in SDS:
                        with kb.stage():
                            self.ssd_p1(i, l, cur)
                    if "2" in SDS:
                        with kb.stage():
                            self.ssd_p2a(i, l, cur)
                    if "3" in SDS:
                        with kb.stage():
                            self.outproj_stage(l, self.c_w_out[i], 2048, self.YN, cur, mdst)
                cur = mdst
            if do_ffn:
                dst = self.out if last else self.Xb
                with kb.stage():
                    self.ffn_stage(l, cur, dst)
                cur = dst
        kb.gstack.close()

    def rms_stats(self, src, bsrc, ss, bss, junk, bjunk, nparts=128, width=D):
        kb = self.kb
        kb.op("dve", lambda e: e.memset(ss[0:nparts, 0:1], 0.0), writes=[bss])
        kb.op("act", lambda e: e.activation(junk[0:nparts, 0:width], src, AF.Square,
                                            accum_out=ss[0:nparts, 0:1]),
              reads=[bsrc, bss], writes=[bjunk, bss])
        kb.op("act", lambda e: e.activation(ss[0:nparts, 0:1], ss[0:nparts, 0:1], AF.Sqrt,
                                            bias=EPS, scale=1.0 / width), reads=[bss], writes=[bss])
        kb.op("dve", lambda e: e.reciprocal(ss[0:nparts, 0:1], ss[0:nparts, 0:1]), reads=[bss], writes=[bss])

    def make_fe(self, BT, nh):
        kb = self.kb
        fe = {}
        fe["BT"], fe["nh"] = BT, nh
        fe["xts"] = [kb.sb("xt", [128, D]) for _ in range(2)]
        fe["bxt"] = kb.bufs(2)
        fe["hns"] = [kb.sb("hn", [128, D], BF16) for _ in range(2)]
        fe["bhn"] = kb.bufs(2)
        fe["junk"] = kb.sb("junk", [128, D])
        fe["bjunk"] = kb.buf()
        fe["sss"] = [kb.sb("ss", [128, 1]) for _ in range(2)]
        fe["bss"] = kb.bufs(2)
        nhh = max(nh, 1)
        fe["xh"] = kb.sb("xh", [2 * nhh, D]) if nh > 0 else None
        fe["bxh"] = kb.buf()
        fe["hnh"] = kb.sb("hnh", [2 * nhh, D], BF16) if nh > 0 else None
        fe["bhnh"] = kb.buf()
        fe["ssh"] = kb.sb("ssh", [2 * nhh, 1]) if nh > 0 else None
        fe["bssh"] = kb.buf()
        fe["pT"] = kb.ps("pT", [128, D], BF16)
        fe["bpT"] = kb.buf()
        if nh > 0:
            fe["pTh"] = kb.ps("pTh", [128, D], BF16)
            fe["bpTh"] = kb.buf()
        else:
            fe["pTh"], fe["bpTh"] = fe["pT"], fe["bpT"]
        fe["ti"] = 0
        return fe

    def run_fe(self, fe, Xin, t0, gpre, bconst, hnT, bh):
        kb = self.kb
        T = self.T
        BT, nh = fe["BT"], fe["nh"]
        ident, bident = self.ident, self.bident
        pT, bpT, pTh, bpTh = fe["pT"], fe["bpT"], fe["pTh"], fe["bpTh"]
        junk, bjunk = fe["junk"], fe["bjunk"]
        for j in range(BT // 128):
            i = fe["ti"] % 2
            fe["ti"] += 1
            xt, bx = fe["xts"][i], fe["bxt"][i]
            hn, bn = fe["hns"][i], fe["bhn"][i]
            ss, bs = fe["sss"][i], fe["bss"][i]
            r0 = t0 + j * 128
            kb.dma("sp", xt[:], Xin[r0:r0 + 128, :], writes=[bx])
            self.rms_stats(xt[:], bx, ss, bs, junk, bjunk)
            kb.op("dve", lambda e, hn=hn, xt=xt, ss=ss: e.scalar_tensor_tensor(
                hn[:], xt[:], ss[:, 0:1], gpre[:], ALU.mult, ALU.mult),
                reads=[bx, bs, bconst], writes=[bn])

            def tr(e, hn=hn):
                for k in range(8):
                    ins = e.transpose(pT[:, k * 128:(k + 1) * 128], hn[:, k * 128:(k + 1) * 128], ident[:])
                return ins
            kb.op("pe", tr, reads=[bn, bident], writes=[bpT])
            kb.op("act", lambda e, j=j: e.activation(
                hnT[:, :, j * 128:(j + 1) * 128], pT[:].rearrange("p (k t) -> p k t", k=8), AF.Copy),
                reads=[bpT], writes=[bh])
        if nh == 0:
            return
        xh, bxh, hnh, bhnh, ssh, bssh = fe["xh"], fe["bxh"], fe["hnh"], fe["bhnh"], fe["ssh"], fe["bssh"]
        kb.op("pool", lambda e: e.memset(xh[:], 0.0), writes=[bxh])
        if t0 - nh >= 0:
            kb.dma("sp", xh[0:nh, :], Xin[t0 - nh:t0, :], writes=[bxh])
        if t0 + BT + nh <= T:
            kb.dma("sp", xh[nh:2 * nh, :], Xin[t0 + BT:t0 + BT + nh, :], writes=[bxh])
        self.rms_stats(xh[:], bxh, ssh, bssh, junk, bjunk, nparts=2 * nh)
        kb.op("dve", lambda e: e.scalar_tensor_tensor(
            hnh[:], xh[:], ssh[:, 0:1], gpre[0:2 * nh, :], ALU.mult, ALU.mult),
            reads=[bxh, bssh, bconst], writes=[bhnh])
        w = 2 * nh

        def trh(e):
            for k in range(8):
                ins = e.transpose(pTh[:, k * w:(k + 1) * w], hnh[:, k * 128:(k + 1) * 128], ident[0:w, 0:w])
            return ins
        kb.op("pe", trh, reads=[bhnh, bident], writes=[bpTh])
        kb.op("act", lambda e: e.activation(
            hnT[:, :, BT:BT + w], pTh[:, 0:8 * w].rearrange("p (k t) -> p k t", k=8), AF.Copy),
            reads=[bpTh], writes=[bh])

    def ssd_consts(self, i, l):
        kb = self.kb
        c = {}
        c["b"] = kb.buf()
        b = c["b"]
        c["gpre"] = kb.sb("gpre", [128, D])
        kb.dma("sp", c["gpre"][:], self.gains[l * 4 + 0], writes=[b])
        c["scw"] = kb.sb("scw", [128, 24, 5])
        c["scb"] = kb.sb("scb", [128, 24])
        kb.dma("sp", c["scw"][:], self.scw[i].rearrange("p (c w) -> p c w", w=5), writes=[b])
        kb.dma("sp", c["scb"][:], self.scb[i], writes=[b])
        c["dtb"] = kb.sb("dtb", [128, 64])
        kb.dma("sp", c["dtb"][:], self.sdtb[i], writes=[b])
        c["A"] = kb.sb("A", [128, 64])
        kb.dma("sp", c["A"][:], self.salog[i], writes=[b])
        kb.op("act", lambda e: e.activation(c["A"][:], c["A"][:], AF.Exp), reads=[b], writes=[b])
        kb.op("dve", lambda e: e.tensor_scalar(c["A"][:], c["A"][:], -1.0, None, ALU.mult), reads=[b], writes=[b])
        c["cm"] = kb.sb("cm", [128, 5, 128])
        kb.dma("sp", c["cm"][:], self.cmats.rearrange("m p j -> p m j"), writes=[b])
        return c

    def ssd_load_win(self, i, cols_list):
        kb = self.kb
        W = kb.sb("swin", [128, 8, 5184], BF16)
        bw = kb.buf()
        for k in range(8):
            for (a, bnd) in cols_list:
                kb.dma("pool", W[:, k, a:bnd], self.c_w_in[i, k * 128:(k + 1) * 128, a:bnd], writes=[bw])
        return W, bw

    def ssd_block_front(self, fe, cst, W, bw, Xin, t0, hnT, bh, chunks, ue_s, acc_s, ps, outs):
        kb = self.kb
        BT = fe["BT"]
        NT = BT // 128
        scw, scb, bc = cst["scw"], cst["scb"], cst["b"]
        self.run_fe(fe, Xin, t0, cst["gpre"], bc, hnT, bh)
        for n0 in range(0, len(chunks), 2):
            ctx = []
            for n in range(n0, min(n0 + 2, len(chunks))):
                cc = chunks[n]
                par = n % 2
                pA, bpA = ps["pA"][par], ps["bpA"][par]
                pH, bpH = ps["pHh"][par], ps["bpHh"][par]
                ue, bue = ue_s[par]
                acc, bacc = acc_s[par]
                col = 2048 + cc * 128

                def mm(e, col=col, pA=pA):
                    for k in range(8):
                        ins = e.matmul(pA[:, 0:BT], W[:, k, col:col + 128], hnT[:, k, 0:BT], start=(k == 0), stop=(k == 7))
                    return ins
                kb.op("pe", mm, reads=[bw, bh], writes=[bpA])

                def mmh(e, col=col, pH=pH):
                    for k in range(8):
                        ins = e.matmul(pH[:, 0:4], W[:, k, col:col + 128], hnT[:, k, BT:BT + 4], start=(k == 0), stop=(k == 7))
                    return ins
                kb.op("pe", mmh, reads=[bw, bh], writes=[bpH])
                kb.op("act", lambda e, ue=ue, pA=pA: e.activation(ue[:, 2:2 + BT], pA[:, 0:BT], AF.Copy),
                      reads=[bpA], writes=[bue])
                kb.op("act", lambda e, ue=ue, pH=pH: e.activation(ue[:, 0:2], pH[:, 0:2], AF.Copy), reads=[bpH], writes=[bue])
                kb.op("act", lambda e, ue=ue, pH=pH: e.activation(ue[:, BT + 2:BT + 4], pH[:, 2:4], AF.Copy),
                      reads=[bpH], writes=[bue])
                ctx.append((cc, ue, bue, acc, bacc))
            for j in range(5):
                for (cc, ue, bue, acc, bacc) in ctx:
                    if j == 0:
                        kb.op("dve", lambda e, ue=ue, acc=acc, cc=cc: e.tensor_scalar(
                            acc[:], ue[:, 0:BT], scw[:, cc, 0:1], scb[:, cc:cc + 1], ALU.mult, ALU.add),
                            reads=[bue, bc], writes=[bacc])
                    else:
                        kb.op("dve", lambda e, ue=ue, acc=acc, cc=cc, j=j: e.scalar_tensor_tensor(
                            acc[:], ue[:, j:j + BT], scw[:, cc, j:j + 1], acc[:], ALU.mult, ALU.add),
                            reads=[bue, bc, bacc], writes=[bacc])
            for (cc, ue, bue, acc, bacc) in ctx:
                outs(cc, acc, bacc)

    def ssd_small(self, cst, la, bla, dirs, ps, sm, bsm):
        kb = self.kb
        cm, bc = cst["cm"], cst["b"]
        pH, bpH = ps["pH"], ps["bpH"]

        def mm(e):
            e.matmul(pH[:, 64:96], cm[:, 0, :], la[:, 0:32], start=True, stop=True)
            e.matmul(pH[:, 96:128], cm[:, 3, :], la[:, 32:64], start=True, stop=True)
            e.matmul(pH[:, 128:160], cm[:, 1, :], la[:, 32:64], start=True, stop=True)
            return e.matmul(pH[:, 160:224], cm[:, 4, :], la[:, 0:64], start=True, stop=True)
        kb.op("pe", mm, reads=[bla, bc], writes=[bpH])
        kb.op("act", lambda e: e.activation(sm[:, 0:160], pH[:, 64:224], AF.Copy), reads=[bpH], writes=[bsm])

    def ssd_dt(self, cst, W, bw, hnT, bh, sl, ps, tl):
        kb = self.kb
        pH, bpH = ps["pH"], ps["bpH"]
        bc = cst["b"]

        def mm(e):
            for k in range(8):
                ins = e.matmul(pH[:, 0:64], hnT[:, k, sl], W[:, k, 5120:5184], start=(k == 0), stop=(k == 7))
            return ins
        kb.op("pe", mm, reads=[bw, bh], writes=[bpH])
        dtr, dt, la, bdt = tl["dtr"], tl["dt"], tl["la"], tl["bdt"]
        kb.op("act", lambda e: e.activation(dtr[:], pH[:, 0:64], AF.Copy), reads=[bpH], writes=[bdt])
        kb.op("dve", lambda e: e.tensor_tensor(dtr[:], dtr[:], cst["dtb"][:], ALU.add), reads=[bdt, bc], writes=[bdt])
        kb.op("act", lambda e: e.activation(dtr[:], dtr[:], AF.Exp), reads=[bdt], writes=[bdt])
        kb.op("act", lambda e: e.activation(dt[:], dtr[:], AF.Ln, bias=1.0, scale=1.0), reads=[bdt], writes=[bdt])
        kb.op("dve", lambda e: e.tensor_tensor(la[:], dt[:], cst["A"][:], ALU.mult), reads=[bdt, bc], writes=[bdt])

    def ssd_alloc_common(self, BT, p2):
        kb = self.kb
        NT = BT // 128
        a = {}
        a["hnTs"] = [kb.sb("hnT", [128, 8, BT + 4], BF16) for _ in range(2)]
        a["bhnT"] = kb.bufs(2)
        a["ue_s"] = [(kb.sb("ue", [128, BT + 4]), kb.buf()) for _ in range(2)]
        a["acc_s"] = [(kb.sb("acc", [128, BT]), kb.buf()) for _ in range(2)]
        a["xsT"] = [(kb.sb("xsT", [128, BT], BF16), kb.buf()) for _ in range(2)]
        a["x_tok"] = [kb.sb("xtok", [128, NT, 2048], BF16)] * 2
        a["bx_tok"] = [kb.buf()] * 2
        a["B_tok"] = [kb.sb("btok", [128, NT, 512], BF16)] * 2
        a["bB_tok"] = [kb.buf()] * 2
        tl = {}
        for n in ("dtr", "dt", "la"):
            tl[n] = kb.sb(n, [128, 64])
        tl["bdt"] = kb.buf()
        tl["sm"] = kb.sb("sm", [128, 160])
        tl["bsm"] = kb.buf()
        a["tl"] = tl
        ps = {}
        ps["pA"] = [kb.ps("pA", [128, 512]) for _ in range(2)]
        ps["bpA"] = kb.bufs(2)
        ps["pH"] = kb.ps("pH", [128, 512])
        ps["bpH"] = kb.buf()
        ps["pHh"] = [kb.ps("pHh", [128, 512]) for _ in range(2)]
        ps["bpHh"] = kb.bufs(2)
        a["ps"] = ps
        return a

    def ssd_p1(self, i, l, Xin, BT=512):
        kb = self.kb
        T = self.T
        NT = BT // 128
        cst = self.ssd_consts(i, l)
        W, bw = self.ssd_load_win(i, [(0, 5184)])
        fe = self.make_fe(BT, 2)
        a = self.ssd_alloc_common(BT, False)
        ps, tl = a["ps"], a["tl"]
        pT, bpT = fe["pT"], fe["bpT"]
        Hb = kb.sb("Hb", [128, 2048])
        Hbb = kb.sb("Hbb", [128, 2048], BF16)
        bHb, bHbb = kb.bufs(4), kb.bufs(4)
        kb.op("pool", lambda e: e.memset(Hb[:], 0.0), writes=bHb)
        kb.op("pool", lambda e: e.memset(Hbb[:], 0.0), writes=bHbb)
        wgt = kb.sb("wgt", [128, 32])
        dtot = kb.sb("dtot", [128, 32])
        bwg = kb.buf()
        xws = [(kb.sb("xw", [128, 512], BF16), kb.buf()) for _ in range(2)]
        Ss = [(kb.sb("Ssb", [128, 512]), kb.buf()) for _ in range(2)]
        BTf = kb.sb("BTf", [128, 4, BT], BF16)
        CTf = kb.sb("CTf", [128, 4, BT], BF16)
        bBf, bCf = kb.buf(), kb.buf()
        zss = [(kb.sb("zs", [128, 2048], BF16), kb.buf()) for _ in range(2)]
        dtl = kb.sb("dtl", [128, 128])
        bdtl = kb.buf()
        gi = 0
        zi = 0
        for b in reversed(range(T // BT)):
            t0 = b * BT
            hnT, bh = a["hnTs"][b % 2], a["bhnT"][b % 2]
            x_tok, bxk = a["x_tok"][b % 2], a["bx_tok"][b % 2]
            B_tok, bBk = a["B_tok"][b % 2], a["bB_tok"][b % 2]

            def outs(cc, acc, bacc):
                if cc >= 20:
                    dstC = CTf[:, cc - 20, :]
                    kb.op("act", lambda e: e.activation(dstC, acc[:], AF.Silu), reads=[bacc], writes=[bCf])
                    return
                if cc >= 16:
                    xsT, bxs = BTf[:, cc - 16, :], bBf
                else:
                    xsT, bxs = a["xsT"][cc % 2]
                    xsT = xsT[:]
                kb.op("act", lambda e: e.activation(xsT, acc[:], AF.Silu), reads=[bacc], writes=[bxs])

                def tr(e):
                    for ct in range(NT):
                        ins = e.transpose(pT[:, ct * 128:(ct + 1) * 128], xsT[:, ct * 128:(ct + 1) * 128], self.ident[:])
                    return ins
                kb.op("pe", tr, reads=[bxs, self.bident], writes=[bpT])
                if cc < 16:
                    dst, bd = x_tok[:, :, cc * 128:(cc + 1) * 128], bxk
                else:
                    dst, bd = B_tok[:, :, (cc - 16) * 128:(cc - 15) * 128], bBk
                kb.op("act", lambda e: e.activation(dst, pT[:, 0:NT * 128].rearrange("p (c t) -> p c t", c=NT), AF.Copy),
                      reads=[bpT], writes=[bd])
            self.ssd_block_front(fe, cst, W, bw, Xin, t0, hnT, bh, list(range(24)), a["ue_s"], a["acc_s"], ps, outs)
            for ct in range(NT):
                r0 = t0 + ct * 128
                kb.dma("sp", self.XTK[r0:r0 + 128, :], x_tok[:, ct, :], reads=[bxk])
                kb.dma("act", self.BTK[r0:r0 + 128, :], B_tok[:, ct, :], reads=[bBk])
            kb.dma("sp", self.BTF[:, :, t0:t0 + BT].rearrange("g p t -> p g t"), BTf[:], reads=[bBf])
            kb.dma("act", self.CTF[:, :, t0:t0 + BT].rearrange("g p t -> p g t"), CTf[:], reads=[bCf])
            for ct in reversed(range(NT)):
                c = t0 // 128 + ct
                sl = slice(ct * 128, (ct + 1) * 128)
                self.ssd_dt(cst, W, bw, hnT, bh, sl, ps, tl)
                self.ssd_small(cst, tl["la"], tl["bdt"], None, ps, tl["sm"], tl["bsm"])
                sm, bsm, dt, bdt = tl["sm"], tl["bsm"], tl["dt"], tl["bdt"]
                kb.op("pool", lambda e: e.tensor_copy(dtl[:, 0:64], tl["dt"][:]), reads=[bdt], writes=[bdtl])
                kb.op("pool", lambda e: e.tensor_copy(dtl[:, 64:128], tl["la"][:]), reads=[bdt], writes=[bdtl])
                kb.dma("act", self.DTL[c * 128:(c + 1) * 128, :], dtl[:], reads=[bdtl])
                zs_, bz = zss[zi % 2]
                zi += 1
                for g in range(4):
                    pZ, bpZ = ps["pA"][gi % 2], ps["bpA"][gi % 2]
                    gi += 1

                    def mz(e, pZ=pZ, g=g, hnT=hnT, sl=sl):
                        for k in range(8):
                            ins = e.matmul(pZ[:, 0:512], hnT[:, k, sl], W[:, k, g * 512:(g + 1) * 512], start=(k == 0), stop=(k == 7))
                        return ins
                    kb.op("pe", mz, reads=[bw, bh], writes=[bpZ])
                    kb.op("act", lambda e, zs_=zs_, pZ=pZ, g=g: e.activation(zs_[:, g * 512:(g + 1) * 512], pZ[:, 0:512], AF.Silu),
                          reads=[bpZ], writes=[bz])
                kb.dma("sp", self.ZS[c * 128:(c + 1) * 128, :], zs_[:], reads=[bz])
                kb.op("act", lambda e: e.activation(wgt[:], sm[:, 64:96], AF.Exp), reads=[bsm], writes=[bwg])
                kb.op("dve", lambda e: e.tensor_tensor(wgt[:], wgt[:], dt[:, 32:64], ALU.mult), reads=[bwg, bdt], writes=[bwg])
                kb.op("act", lambda e: e.activation(dtot[:], sm[:, 128:160], AF.Exp), reads=[bsm], writes=[bwg])
                kb.dma("sp", self.HB[c], Hbb[:], reads=bHbb)
                for g in range(4):
                    xw, bxw = xws[gi % 2]
                    Ssb, bS = Ss[gi % 2]
                    pS, bpS = ps["pA"][gi % 2], ps["bpA"][gi % 2]
                    gi += 1
                    gs = slice(g * 512, (g + 1) * 512)
                    kb.op("pool", lambda e, xw=xw, gs=gs, g=g, ct=ct, x_tok=x_tok: e.tensor_tensor(
                        xw[:].rearrange("p (h d) -> p h d", h=8), x_tok[:, ct, gs].rearrange("p (h d) -> p h d", h=8),
                        wgt[:, g * 8:(g + 1) * 8].unsqueeze(2).to_broadcast([128, 8, 64]), ALU.mult),
                        reads=[bxk, bwg], writes=[bxw])
                    kb.op("pe", lambda e, pS=pS, xw=xw, g=g, ct=ct, B_tok=B_tok: e.matmul(
                        pS[:, 0:512], B_tok[:, ct, g * 128:(g + 1) * 128], xw[:], start=True, stop=True),
                        reads=[bBk, bxw], writes=[bpS])
                    kb.op("act", lambda e, Ssb=Ssb, pS=pS: e.activation(Ssb[:], pS[:, 0:512], AF.Copy),
                          reads=[bpS], writes=[bS])
                    kb.op("dve", lambda e, gs=gs, g=g: e.tensor_tensor(
                        Hb[:, gs].rearrange("p (h d) -> p h d", h=8), Hb[:, gs].rearrange("p (h d) -> p h d", h=8),
                        dtot[:, g * 8:(g + 1) * 8].unsqueeze(2).to_broadcast([128, 8, 64]), ALU.mult),
                        reads=[bHb[g], bwg], writes=[bHb[g]])
                    kb.op("pool", lambda e, gs=gs, Ssb=Ssb: e.tensor_tensor(Hb[:, gs], Hb[:, gs], Ssb[:], ALU.add),
                          reads=[bHb[g], bS], writes=[bHb[g]])
                    kb.op("act", lambda e, gs=gs: e.activation(Hbb[:, gs], Hb[:, gs], AF.Copy), reads=[bHb[g]], writes=[bHbb[g]])

    def ssd_p2a(self, i, l, Xin, BT=256):
        kb = self.kb
        T = self.T
        NT = BT // 128
        cst = self.ssd_consts(i, l)
        bc = cst["b"]
        cm = cst["cm"]
        ps = {}
        ps["pA"] = [kb.ps("pA", [128, 512]) for _ in range(3)]
        ps["bpA"] = kb.bufs(3)
        ps["pH"] = kb.ps("pH", [128, 512])
        ps["bpH"] = kb.buf()
        P3 = kb.ps("P3", [128, 1536])
        bP3 = kb.bufs(3)
        dsk = kb.sb("dsk", [128, 32])
        ng = kb.sb("ng", [128, 2048])
        kb.dma("sp", dsk[:], self.sdsk[i], writes=[bc])
        kb.dma("sp", ng[:], self.sng[i], writes=[bc])
        xks = [(kb.sb("xk", [128, 2048], BF16), kb.buf()) for _ in range(2)]
        Bks = [(kb.sb("Bk", [128, 512], BF16), kb.buf()) for _ in range(2)]
        BTfs = [(kb.sb("BTf", [128, 4, 128], BF16), kb.buf()) for _ in range(2)]
        CTfs = [(kb.sb("CTf", [128, 4, 128], BF16), kb.buf()) for _ in range(2)]
        dtls = [(kb.sb("dtl", [128, 128]), kb.buf()) for _ in range(2)]
        zsl = [(kb.sb("zsl", [128, 2048], BF16), kb.buf()) for _ in range(2)]
        Hbbs = [(kb.sb("Hbb", [128, 2048], BF16), kb.buf()) for _ in range(2)]
        sms = [(kb.sb("sm", [128, 160]), kb.buf()) for _ in range(2)]
        Hf = kb.sb("Hf", [128, 2048])
        Hfb = kb.sb("Hfb", [128, 2048], BF16)
        bHf, bHfb = kb.bufs(4), kb.bufs(4)
        kb.op("pool", lambda e: e.memset(Hf[:], 0.0), writes=bHf)
        kb.op("pool", lambda e: e.memset(Hfb[:], 0.0), writes=bHfb)
        ecs = [(kb.sb("ec", [128, 64]), kb.sb("wgt", [128, 32]), kb.sb("dtot", [128, 32]), kb.sb("dd", [128, 32]), kb.buf())
               for _ in range(2)]
        qks = [(kb.sb("qk", [128, 4, 128]), kb.buf()) for _ in range(2)]
        AUf = [kb.sb("AUf", [128, 4, 128]) for _ in range(2)]
        AUb = [kb.sb("AUb", [128, 4, 128]) for _ in range(2)]
        bAU = kb.bufs(2)
        E = [kb.sb("E", [128, 4, 128]) for _ in range(2)]
        bE = kb.bufs(2)
        T1 = [kb.sb("T1", [128, 4, 128]) for _ in range(2)]
        bT1 = kb.bufs(2)
        Wb = [kb.sb("Wb", [128, 4, 128], BF16) for _ in range(2)]
        bWb = kb.bufs(2)
        Ys = [(kb.sb("Y", [128, 2048]), kb.bufs(4)) for _ in range(2)]
        Tt = [(kb.sb("Tt", [128, 512]), kb.buf()) for _ in range(3)]
        xds = [(kb.sb("xd", [128, 2048], BF16), kb.buf()) for _ in range(2)]
        xws = [(kb.sb("xw", [128, 512], BF16), kb.buf()) for _ in range(2)]
        Ss = [(kb.sb("Ssb", [128, 512]), kb.buf()) for _ in range(2)]
        jz = [(kb.sb("jz", [128, 512]), kb.buf()) for _ in range(2)]
        ssqs = [(kb.sb("ssq", [128, 4]), kb.buf()) for _ in range(2)]
        cnt = {"qi": 0, "gi": 0, "pa": 0}

        def nextpA():
            k = cnt["pa"] % 3
            cnt["pa"] += 1
            return ps["pA"][k], ps["bpA"][k]

        def do_chunk(c):
            if True:
                r0 = c * 128
                cp = c % 2
                x_tok, bxk = xks[cp]
                B_tok, bBk = Bks[cp]
                BTfb, bBf = BTfs[cp]
                CTfb, bCf = CTfs[cp]
                dtl, bdt = dtls[cp]
                zsb, bzs = zsl[cp]
                Hbb, bHbb = Hbbs[cp]
                sm, bsm = sms[cp]
                ec, wgt, dtot, dd, bsm2 = ecs[cp]
                qk, bqk = qks[cp]
                Y, bY = Ys[cp]
                xd, bxd = xds[cp]
                ssq, bssq = ssqs[cp]
                dt, la = dtl[:, 0:64], dtl[:, 64:128]
                kb.dma("sp", x_tok[:], self.XTK[r0:r0 + 128, :], writes=[bxk])
                kb.dma("act", B_tok[:], self.BTK[r0:r0 + 128, :], writes=[bBk])
                kb.dma("sp", BTfb[:], self.BTF[:, :, r0:r0 + 128].rearrange("g p t -> p g t"), writes=[bBf])
                kb.dma("act", CTfb[:], self.CTF[:, :, r0:r0 + 128].rearrange("g p t -> p g t"), writes=[bCf])
                kb.dma("sp", dtl[:], self.DTL[r0:r0 + 128, :], writes=[bdt])
                kb.dma("act", zsb[:], self.ZS[r0:r0 + 128, :], writes=[bzs])
                kb.dma("sp", Hbb[:], self.HB[c], writes=[bHbb])
                self.ssd_small(cst, la, bdt, None, ps, sm, bsm)
                kb.op("act", lambda e, ec=ec, sm=sm: e.activation(ec[:], sm[:, 0:64], AF.Exp), reads=[bsm], writes=[bsm2])
                kb.op("dve", lambda e, wgt=wgt, sm=sm: e.tensor_tensor(wgt[:], sm[:, 96:128], sm[:, 0:32], ALU.subtract), reads=[bsm], writes=[bsm2])
                kb.op("act", lambda e, wgt=wgt: e.activation(wgt[:], wgt[:], AF.Exp), reads=[bsm2], writes=[bsm2])
                kb.op("dve", lambda e, wgt=wgt, dt=dt: e.tensor_tensor(wgt[:], wgt[:], dt[:, 0:32], ALU.mult), reads=[bsm2, bdt], writes=[bsm2])
                kb.op("act", lambda e, dtot=dtot, sm=sm: e.activation(dtot[:], sm[:, 96:128], AF.Exp), reads=[bsm], writes=[bsm2])
                kb.op("dve", lambda e, dd=dd, dt=dt: e.tensor_tensor(dd[:], dt[:, 0:32], dt[:, 32:64], ALU.subtract), reads=[bdt], writes=[bsm2])
                pQ, bpQ = nextpA()

                def mq(e, pQ=pQ, BTfb=BTfb, CTfb=CTfb):
                    for g in range(4):
                        ins = e.matmul(pQ[:, g * 128:(g + 1) * 128], BTfb[:, g, :], CTfb[:, g, :], start=True, stop=True)
                    return ins
                kb.op("pe", mq, reads=[bBf, bCf], writes=[bpQ])
                kb.op("act", lambda e, pQ=pQ: e.activation(qk[:].rearrange("p g j -> p (g j)"), pQ[:, 0:512], AF.Copy),
                      reads=[bpQ], writes=[bqk])
                kb.op("pool", lambda e: e.tensor_tensor(
                    xd[:].rearrange("p (h d) -> p h d", h=32), x_tok[:, :].rearrange("p (h d) -> p h d", h=32),
                    dsk[:].unsqueeze(2).to_broadcast([128, 32, 64]), ALU.mult), reads=[bxk, bc], writes=[bxd])
                for g in range(4):
                    gs = slice(g * 512, (g + 1) * 512)
                    Pi, Pf, Pb = P3[:, 0:512], P3[:, 512:1024], P3[:, 1024:1536]
                    kb.op("pe", lambda e, gs=gs: e.matmul(Pi, self.ident[:], xd[:, gs], start=True, stop=False),
                          reads=[self.bident, bxd], writes=[bP3[0]])
                    halves = []
                    for half in range(2):
                        h0 = (g * 2 + half) * 4
                        par = cnt["qi"] % 2
                        cnt["qi"] += 1
                        pS, bpS = nextpA()
                        halves.append((half, h0, par, pS, bpS))
                    for (half, h0, par, pS, bpS) in halves:
                        kb.op("dve", lambda e, par=par, h0=h0: e.tensor_tensor(
                            AUf[par][:], la[:, h0:h0 + 4].unsqueeze(2).to_broadcast([128, 4, 128]),
                            cm[:, 0, :].unsqueeze(1).to_broadcast([128, 4, 128]), ALU.mult),
                            reads=[bdt, bc], writes=[bAU[par]])
                    for (half, h0, par, pS, bpS) in halves:
                        kb.op("pool", lambda e, par=par, h0=h0: e.tensor_tensor(
                            AUb[par][:], la[:, 32 + h0:32 + h0 + 4].unsqueeze(2).to_broadcast([128, 4, 128]),
                            cm[:, 3, :].unsqueeze(1).to_broadcast([128, 4, 128]), ALU.mult),
                            reads=[bdt, bc], writes=[bAU[par]])
                    for (half, h0, par, pS, bpS) in halves:
                        def ms(e, par=par, pS=pS):
                            e.matmul(pS[:, 0:512], cm[:, 2, :], AUf[par][:].rearrange("p h j -> p (h j)"), start=True, stop=False)
                            return e.matmul(pS[:, 0:512], cm[:, 1, :], AUb[par][:].rearrange("p h j -> p (h j)"),
                                            start=False, stop=True)
                        kb.op("pe", ms, reads=[bc, bAU[par]], writes=[bpS])
                    for (half, h0, par, pS, bpS) in halves:
                        kb.op("act", lambda e, par=par, pS=pS: e.activation(
                            E[par][:].rearrange("p h j -> p (h j)"), pS[:, 0:512], AF.Exp), reads=[bpS], writes=[bE[par]])
                    for (half, h0, par, pS, bpS) in halves:
                        kb.op("pool", lambda e, par=par, h0=h0: e.tensor_tensor(
                            T1[par][:], cm[:, 0, :].unsqueeze(1).to_broadcast([128, 4, 128]),
                            dd[:, h0:h0 + 4].unsqueeze(2).to_broadcast([128, 4, 128]), ALU.mult),
                            reads=[bc, bsm2], writes=[bT1[par]])
                    for (half, h0, par, pS, bpS) in halves:
                        kb.op("pool", lambda e, par=par, h0=h0: e.tensor_tensor(
                            T1[par][:], T1[par][:], dt[:, 32 + h0:32 + h0 + 4].unsqueeze(2).to_broadcast([128, 4, 128]),
                            ALU.add), reads=[bT1[par], bdt], writes=[bT1[par]])
                    for (half, h0, par, pS, bpS) in halves:
                        kb.op("dve", lambda e, par=par, g=g: e.tensor_tensor(
                            E[par][:], E[par][:], qk[:, g, :].unsqueeze(1).to_broadcast([128, 4, 128]), ALU.mult),
                            reads=[bE[par], bqk], writes=[bE[par]])
                    for (half, h0, par, pS, bpS) in halves:
                        kb.op("dve", lambda e, par=par: e.tensor_tensor(Wb[par][:], E[par][:], T1[par][:], ALU.mult),
                              reads=[bE[par], bT1[par]], writes=[bWb[par]])
                    for (half, h0, par, pS, bpS) in halves:
                        def mi(e, par=par, h0=h0, half=half):
                            for hh in range(4):
                                h = h0 + hh
                                cs_ = (half * 4 + hh) * 64
                                ins = e.matmul(Pi[:, cs_:cs_ + 64], Wb[par][:, hh, :], x_tok[:, h * 64:(h + 1) * 64],
                                               start=False, stop=(half == 1 and hh == 3))
                            return ins
                        kb.op("pe", mi, reads=[bWb[par], bxk], writes=[bP3[0]])
                    kb.op("pe", lambda e, g=g, gs=gs: e.matmul(
                        Pf, CTfb[:, g, :], Hfb[:, gs], start=True, stop=True), reads=[bCf, bHfb[g]], writes=[bP3[1]])
                    kb.op("pe", lambda e, g=g, gs=gs: e.matmul(
                        Pb, CTfb[:, g, :], Hbb[:, gs], start=True, stop=True), reads=[bCf, bHbb], writes=[bP3[2]])
                    kb.op("act", lambda e, gs=gs: e.activation(Y[:, gs], Pi, AF.Copy), reads=[bP3[0]], writes=[bY[g]])
                    for (Px, bPx, eoff) in ((Pf, bP3[1], 0), (Pb, bP3[2], 32)):
                        Tt_, bTt = Tt[cnt["gi"] % 3]
                        cnt["gi"] += 1
                        kb.op("act", lambda e, Tt_=Tt_, Px=Px: e.activation(Tt_[:], Px, AF.Copy), reads=[bPx], writes=[bTt])
                        kb.op("dve", lambda e, Tt_=Tt_, eoff=eoff, g=g: e.tensor_tensor(
                            Tt_[:].rearrange("p (h d) -> p h d", h=8), Tt_[:].rearrange("p (h d) -> p h d", h=8),
                            ec[:, eoff + g * 8:eoff + (g + 1) * 8].unsqueeze(2).to_broadcast([128, 8, 64]), ALU.mult),
                            reads=[bTt, bsm2], writes=[bTt])
                        kb.op("pool", lambda e, Tt_=Tt_, gs=gs: e.tensor_tensor(Y[:, gs], Y[:, gs], Tt_[:], ALU.add),
                              reads=[bTt, bY[g]], writes=[bY[g]])
                    xw, bxw = xws[g % 2]
                    Ssb, bS = Ss[g % 2]
                    pS, bpS = nextpA()
                    kb.op("pool", lambda e, xw=xw, gs=gs, g=g: e.tensor_tensor(
                        xw[:].rearrange("p (h d) -> p h d", h=8), x_tok[:, gs].rearrange("p (h d) -> p h d", h=8),
                        wgt[:, g * 8:(g + 1) * 8].unsqueeze(2).to_broadcast([128, 8, 64]), ALU.mult),
                        reads=[bxk, bsm2], writes=[bxw])
                    kb.op("pe", lambda e, pS=pS, xw=xw, g=g: e.matmul(
                        pS[:, 0:512], B_tok[:, g * 128:(g + 1) * 128], xw[:], start=True, stop=True),
                        reads=[bBk, bxw], writes=[bpS])
                    kb.op("act", lambda e, Ssb=Ssb, pS=pS: e.activation(Ssb[:], pS[:, 0:512], AF.Copy),
                          reads=[bpS], writes=[bS])
                    kb.op("dve", lambda e, gs=gs, g=g: e.tensor_tensor(
                        Hf[:, gs].rearrange("p (h d) -> p h d", h=8), Hf[:, gs].rearrange("p (h d) -> p h d", h=8),
                        dtot[:, g * 8:(g + 1) * 8].unsqueeze(2).to_broadcast([128, 8, 64]), ALU.mult),
                        reads=[bHf[g], bsm2], writes=[bHf[g]])
                    kb.op("pool", lambda e, gs=gs, Ssb=Ssb: e.tensor_tensor(Hf[:, gs], Hf[:, gs], Ssb[:], ALU.add),
                          reads=[bHf[g], bS], writes=[bHf[g]])
                    kb.op("act", lambda e, gs=gs: e.activation(Hfb[:, gs], Hf[:, gs], AF.Copy), reads=[bHf[g]], writes=[bHfb[g]])
                    z_, bz = jz[g % 2]
                    kb.op("dve", lambda e, gs=gs: e.tensor_tensor(Y[:, gs], Y[:, gs], zsb[:, gs], ALU.mult),
                          reads=[bY[g], bzs], writes=[bY[g]])
                    if g == 0:
                        kb.op("dve", lambda e: e.memset(ssq[:], 0.0), writes=[bssq])
                    kb.op("act", lambda e, z_=z_, gs=gs, g=g: e.activation(z_[:], Y[:, gs], AF.Square, accum_out=ssq[:, g:g + 1]),
                          reads=[bY[g], bssq], writes=[bz, bssq])
                kb.op("act", lambda e: e.activation(ssq[:], ssq[:], AF.Sqrt, bias=EPS, scale=1.0 / 512), reads=[bssq], writes=[bssq])
                kb.op("dve", lambda e: e.reciprocal(ssq[:], ssq[:]), reads=[bssq], writes=[bssq])
                for g in range(4):
                    gs = slice(g * 512, (g + 1) * 512)
                    kb.op("dve", lambda e, g=g, gs=gs: e.scalar_tensor_tensor(
                        xd[:, gs], Y[:, gs], ssq[:, g:g + 1], ng[:, gs], ALU.mult, ALU.mult),
                        reads=[bY[g], bssq, bc], writes=[bxd])
                kb.dma("sp", self.YN[c * 128:(c + 1) * 128, 0:2048], xd[:], reads=[bxd])

        for c in range(T // 128):
            do_chunk(c)

    def even_consts(self, i, l):
        kb = self.kb
        c = {}
        b = kb.buf()
        c["b"] = b
        c["gpre"] = kb.sb("gpre", [128, D])
        kb.dma("sp", c["gpre"][:], self.gains[l * 4 + 0], writes=[b])
        lg = kb.sb("lg", [128, 16])
        kb.dma("sp", lg[:], self.rlogit[i], writes=[b])
        kb.op("act", lambda e: e.activation(lg[:], lg[:], AF.Sigmoid), reads=[b], writes=[b])
        kb.op("act", lambda e: e.activation(lg[:], lg[:], AF.Ln), reads=[b], writes=[b])
        c["lg"] = lg
        idx = kb.sb("idx", [128, 4])
        kb.dma("sp", idx[:], self.ridx, writes=[b])
        tabs = kb.sb("tabs", [128, 5, 8])
        for t, (col, off) in enumerate(((0, 0), (1, 8), (2, 0), (3, 8))):
            kb.op("dve", lambda e, t=t, col=col, off=off: e.tensor_scalar(
                tabs[:, t, :], lg[:, off:off + 8], idx[:, col:col + 1], None, ALU.mult), reads=[b], writes=[b])
        kb.op("act", lambda e: e.activation(tabs[:, 0:4, :], tabs[:, 0:4, :], AF.Exp), reads=[b], writes=[b])
        g128 = kb.sb("g128", [128, 16])
        kb.op("act", lambda e: e.activation(g128[:], lg[:], AF.Exp, scale=128.0), reads=[b], writes=[b])
        c["tabs"], c["g128"] = tabs, g128
        return c

    def even_load_win(self, i, cols_list):
        kb = self.kb
        W = kb.sb("ewin", [128, 8, 3584], BF16)
        bw = kb.buf()
        for k in range(8):
            for (a, bnd) in cols_list:
                kb.dma("pool", W[:, k, a:bnd], self.ab_w_in[i, k * 128:(k + 1) * 128, a:bnd], writes=[bw])
        return W, bw

    def proj_tok(self, W, bw, hnT, bh, col0, pP, bpP, width=512):
        def mm(e):
            for k in range(8):
                ins = e.matmul(pP[:, 0:width], hnT[:, k, 0:128], W[:, k, col0:col0 + width], start=(k == 0), stop=(k == 7))
            return ins
        self.kb.op("pe", mm, reads=[bw, bh], writes=[bpP])

    def rotary(self, src, bsrc, dst, bdst, cs, bcs, tmp):
        kb = self.kb
        s3 = src[:].rearrange("p (h d) -> p h d", h=8)
        d3 = dst[:].rearrange("p (h d) -> p h d", h=8)
        x1, x2 = s3[:, :, 0:32], s3[:, :, 32:64]
        cosb = cs[:, 0:32].unsqueeze(1).to_broadcast([128, 8, 32])
        sinb = cs[:, 32:64].unsqueeze(1).to_broadcast([128, 8, 32])
        (ta, tb_, tc, td), bt = tmp
        ta3, tb3, tc3, td3 = [t[:].rearrange("p (h d) -> p h d", h=8) for t in (ta, tb_, tc, td)]
        kb.op("dve", lambda e: e.tensor_tensor(ta3, x1, cosb, ALU.mult), reads=[bsrc, bcs], writes=[bt[0]])
        kb.op("dve", lambda e: e.tensor_tensor(tb3, x2, sinb, ALU.mult), reads=[bsrc, bcs], writes=[bt[1]])
        kb.op("dve", lambda e: e.tensor_tensor(d3[:, :, 0:32], ta3, tb3, ALU.subtract), reads=[bt[0], bt[1]], writes=[bdst])
        kb.op("pool", lambda e: e.tensor_tensor(tc3, x1, sinb, ALU.mult), reads=[bsrc, bcs], writes=[bt[2]])
        kb.op("pool", lambda e: e.tensor_tensor(td3, x2, cosb, ALU.mult), reads=[bsrc, bcs], writes=[bt[3]])
        kb.op("pool", lambda e: e.tensor_tensor(d3[:, :, 32:64], tc3, td3, ALU.add), reads=[bt[2], bt[3]], writes=[bdst])

    def even_p1(self, i, l, Xin):
        kb = self.kb
        T = self.T
        cst = self.even_consts(i, l)
        bc = cst["b"]
        tabs, g128 = cst["tabs"], cst["g128"]
        W, bw = self.even_load_win(i, [(512, 1536), (2048, 3584)])
        fe = self.make_fe(128, 0)
        hnTs = [kb.sb("hnT", [128, 8, 128], BF16) for _ in range(2)]
        bhnT = kb.bufs(2)
        pPs = [(kb.ps("pP", [128, 512]), kb.buf()) for _ in range(3)]
        kr = kb.sb("kr", [128, 512])
        bkr = kb.buf()
        krot = kb.sb("krot", [128, 512], BF16)
        bkrot = kb.buf()
        v = kb.sb("v", [128, 512])
        bv = kb.buf()
        vw = kb.sb("vw", [128, 512], BF16)
        bvw = kb.buf()
        cs = kb.sb("cs", [128, 64])
        bcs = kb.buf()
        tmp = ([kb.sb("rt", [128, 256]) for _ in range(4)], kb.bufs(4))
        Hb = kb.sb("Hb", [64, 512])
        Hbb = kb.sb("Hbb", [64, 512], BF16)
        bHb, bHbb = kb.buf(), kb.buf()
        kb.op("pool", lambda e: e.memset(Hb[:], 0.0), writes=[bHb])
        kb.op("pool", lambda e: e.memset(Hbb[:], 0.0), writes=[bHbb])
        Ssb = kb.sb("Ssb", [64, 512])
        bS = kb.buf()
        nqk = [(kb.sb("nqk", [128, 8, 128], BF16), kb.buf()) for _ in range(2)]
        nvs = [(kb.sb("nv", [128, 512], BF16), kb.buf()) for _ in range(2)]
        pi = 0
        for c in reversed(range(T // 128)):
            r0 = c * 128
            hnT, bh = hnTs[c % 2], bhnT[c % 2]
            self.run_fe(fe, Xin, r0, cst["gpre"], bc, hnT, bh)
            kb.dma("act", cs[:], self.rope[r0:r0 + 128, :], writes=[bcs])
            pP, bpP = pPs[pi % 3]
            pi += 1
            self.proj_tok(W, bw, hnT, bh, 512, pP, bpP)
            kb.op("act", lambda e, pP=pP: e.activation(kr[:], pP[:, 0:512], AF.Copy, scale=0.125), reads=[bpP], writes=[bkr])
            self.rotary(kr, bkr, krot, bkrot, cs, bcs, tmp)
            pP, bpP = pPs[pi % 3]
            pi += 1
            self.proj_tok(W, bw, hnT, bh, 1024, pP, bpP)
            kb.op("act", lambda e, pP=pP: e.activation(v[:], pP[:, 0:512], AF.Copy), reads=[bpP], writes=[bv])
            kb.op("pool", lambda e: e.tensor_tensor(
                vw[:].rearrange("p (h d) -> p h d", h=8), v[:].rearrange("p (h d) -> p h d", h=8),
                tabs[:, 3, :].unsqueeze(2).to_broadcast([128, 8, 64]), ALU.mult), reads=[bv, bc], writes=[bvw])
            kb.dma("sp", self.RHB[c], Hbb[:], reads=[bHbb])
            pP, bpP = pPs[pi % 3]
            pi += 1

            def ms(e, pP=pP):
                for h in range(8):
                    ins = e.matmul(pP[0:64, h * 64:(h + 1) * 64], krot[:, h * 64:(h + 1) * 64], vw[:, h * 64:(h + 1) * 64],
                                   start=True, stop=True)
                return ins
            kb.op("pe", ms, reads=[bkrot, bvw], writes=[bpP])
            kb.op("act", lambda e, pP=pP: e.activation(Ssb[:], pP[0:64, 0:512], AF.Copy), reads=[bpP], writes=[bS])
            kb.op("dve", lambda e: e.tensor_tensor(
                Hb[:].rearrange("p (h d) -> p h d", h=8), Hb[:].rearrange("p (h d) -> p h d", h=8),
                g128[0:64, 8:16].unsqueeze(2).to_broadcast([64, 8, 64]), ALU.mult), reads=[bHb, bc], writes=[bHb])
            kb.op("pool", lambda e: e.tensor_tensor(Hb[:], Hb[:], Ssb[:], ALU.add), reads=[bHb, bS], writes=[bHb])
            kb.op("pool", lambda e: e.tensor_copy(Hbb[:], Hb[:]), reads=[bHb], writes=[bHbb])
            nq_, bnq = nqk[c % 2]
            for pr in range(8):
                col = 2048 + pr * 128
                pP, bpP = pPs[pi % 3]
                pi += 1

                def mf(e, pP=pP, col=col, hnT=hnT):
                    for k in range(8):
                        ins = e.matmul(pP[:, 0:128], W[:, k, col:col + 128], hnT[:, k, 0:128], start=(k == 0), stop=(k == 7))
                    return ins
                kb.op("pe", mf, reads=[bw, bh], writes=[bpP])
                kb.op("act", lambda e, pP=pP, pr=pr, nq_=nq_: e.activation(
                    nq_[:, pr, :], pP[:, 0:128], AF.Copy, scale=(0.125 if pr < 4 else 1.0)), reads=[bpP], writes=[bnq])
            kb.dma("sp", self.NQT[:, :, r0:r0 + 128].rearrange("c p t -> p c t"), nq_[:, 0:4, :], reads=[bnq])
            kb.dma("sp", self.NKT[:, :, r0:r0 + 128].rearrange("c p t -> p c t"), nq_[:, 4:8, :], reads=[bnq])
            nv_, bnv = nvs[c % 2]
            pP, bpP = pPs[pi % 3]
            pi += 1
            self.proj_tok(W, bw, hnT, bh, 3072, pP, bpP)
            kb.op("act", lambda e, pP=pP, nv_=nv_: e.activation(nv_[:], pP[:, 0:512], AF.Copy), reads=[bpP], writes=[bnv])
            kb.dma("sp", self.NV[r0:r0 + 128, :], nv_[:], reads=[bnv])

    def even_p3(self, i, l, Xin):
        kb = self.kb
        T = self.T
        cst = self.even_consts(i, l)
        bc = cst["b"]
        tabs, g128, lg = cst["tabs"], cst["g128"], cst["lg"]
        W, bw = self.even_load_win(i, [(0, 2048)])
        fe = self.make_fe(128, 0)
        pT, bpT = fe["pT"], fe["bpT"]
        dj = kb.sb("dj", [128, 2, 128])
        kb.dma("sp", dj[:], self.rdj.rearrange("m p j -> p m j"), writes=[bc])
        DT = kb.sb("DT", [128, 8, 128])
        for h in range(8):
            kb.op("dve", lambda e, h=h: e.tensor_scalar(DT[:, h, :], dj[:, 0, :], lg[:, h:h + 1], None, ALU.mult),
                  reads=[bc], writes=[bc])
            kb.op("dve", lambda e, h=h: e.scalar_tensor_tensor(DT[:, h, :], dj[:, 1, :], lg[:, 8 + h:9 + h], DT[:, h, :],
                                                               ALU.mult, ALU.add), reads=[bc], writes=[bc])
        kb.op("act", lambda e: e.activation(DT[:], DT[:], AF.Exp), reads=[bc], writes=[bc])
        gng = kb.sb("gng", [128, 512])
        kb.dma("sp", gng[:], self.rgn[i], writes=[bc])
        hnTs = [kb.sb("hnT", [128, 8, 128], BF16) for _ in range(2)]
        bhnT = kb.bufs(2)
        pPs = [(kb.ps("pP", [128, 512]), kb.buf()) for _ in range(2)]
        pSc = kb.ps("pSc", [128, 1024])
        bpSc = kb.buf()
        P3 = kb.ps("P3", [128, 1536])
        bP3 = kb.bufs(3)
        Pi, Pf, Pb = P3[:, 0:512], P3[:, 512:1024], P3[:, 1024:1536]
        raw = [(kb.sb("raw", [128, 512]), kb.buf()) for _ in range(2)]
        qrot = kb.sb("qrot", [128, 512], BF16)
        krot = kb.sb("krot", [128, 512], BF16)
        bqrot, bkrot = kb.buf(), kb.buf()
        v = kb.sb("v", [128, 512])
        vb = kb.sb("vb", [128, 512], BF16)
        vw = kb.sb("vw", [128, 512], BF16)
        bv, bvb, bvw = kb.buf(), kb.buf(), kb.buf()
        sg = kb.sb("sg", [128, 512])
        bsg = kb.buf()
        cs = kb.sb("cs", [128, 64])
        bcs = kb.buf()
        tmp = ([kb.sb("rt", [128, 256]) for _ in range(4)], kb.bufs(4))
        qT = kb.sb("qT", [64, 8, 128], BF16)
        kT = kb.sb("kT", [64, 8, 128], BF16)
        bqT, bkT = kb.buf(), kb.buf()
        WT = kb.sb("WT", [128, 8, 128], BF16)
        bWT = kb.buf()
        Ssc = kb.sb("Ssc", [128, 1024])
        bSsc = kb.buf()
        Hf = kb.sb("Hf", [64, 512])
        Hfb = kb.sb("Hfb", [64, 512], BF16)
        Hbb = kb.sb("Hbb", [64, 512], BF16)
        bHf, bHfb, bHbb = kb.buf(), kb.buf(), kb.buf()
        kb.op("pool", lambda e: e.memset(Hf[:], 0.0), writes=[bHf])
        kb.op("pool", lambda e: e.memset(Hfb[:], 0.0), writes=[bHfb])
        Y = kb.sb("Y", [128, 512])
        bY = kb.buf()
        Tt = kb.sb("Tt", [128, 512])
        bTt = kb.buf()
        Ssb = kb.sb("Ssb", [64, 512])
        bS = kb.buf()
        st = kb.sb("st", [128, 16])
        bst = kb.buf()
        yo = kb.sb("yo", [128, 512], BF16)
        byo = kb.buf()
        pi = 0
        for c in range(T // 128):
            r0 = c * 128
            hnT, bh = hnTs[c % 2], bhnT[c % 2]
            self.run_fe(fe, Xin, r0, cst["gpre"], bc, hnT, bh)
            kb.dma("act", cs[:], self.rope[r0:r0 + 128, :], writes=[bcs])
            kb.dma("act", Hbb[:], self.RHB[c], writes=[bHbb])
            for (col, dst, bd, sc) in ((0, qrot, bqrot, 1.0), (512, krot, bkrot, 0.125)):
                pP, bpP = pPs[pi % 2]
                rw, brw = raw[pi % 2]
                pi += 1
                self.proj_tok(W, bw, hnT, bh, col, pP, bpP)
                kb.op("act", lambda e, pP=pP, rw=rw, sc=sc: e.activation(rw[:], pP[:, 0:512], AF.Copy, scale=sc),
                      reads=[bpP], writes=[brw])
                self.rotary(rw, brw, dst, bd, cs, bcs, tmp)
            pP, bpP = pPs[pi % 2]
            pi += 1
            self.proj_tok(W, bw, hnT, bh, 1024, pP, bpP)
            kb.op("act", lambda e, pP=pP: e.activation(v[:], pP[:, 0:512], AF.Copy), reads=[bpP], writes=[bv])
            kb.op("pool", lambda e: e.tensor_copy(vb[:], v[:]), reads=[bv], writes=[bvb])
            kb.op("pool", lambda e: e.tensor_tensor(
                vw[:].rearrange("p (h d) -> p h d", h=8), v[:].rearrange("p (h d) -> p h d", h=8),
                tabs[:, 2, :].unsqueeze(2).to_broadcast([128, 8, 64]), ALU.mult), reads=[bv, bc], writes=[bvw])
            pP, bpP = pPs[pi % 2]
            pi += 1
            self.proj_tok(W, bw, hnT, bh, 1536, pP, bpP)
            kb.op("act", lambda e, pP=pP: e.activation(sg[:], pP[:, 0:512], AF.Silu), reads=[bpP], writes=[bsg])
            for (src, bs_, dstT, bdT) in ((qrot, bqrot, qT, bqT), (krot, bkrot, kT, bkT)):
                def tr(e, src=src):
                    for h in range(8):
                        ins = e.transpose(pT[0:64, h * 128:(h + 1) * 128], src[:, h * 64:(h + 1) * 64], self.ident[:])
                    return ins
                kb.op("pe", tr, reads=[bs_, self.bident], writes=[bpT])
                kb.op("act", lambda e, dstT=dstT: e.activation(
                    dstT[:], pT[0:64, :].rearrange("p (c t) -> p c t", c=8), AF.Copy), reads=[bpT], writes=[bdT])

            def msc(e):
                for h in range(8):
                    ins = e.matmul(pSc[:, h * 128:(h + 1) * 128], kT[:, h, :], qT[:, h, :], start=True, stop=True)
                return ins
            kb.op("pe", msc, reads=[bkT, bqT], writes=[bpSc])
            kb.op("act", lambda e: e.activation(Ssc[:], pSc[:], AF.Copy), reads=[bpSc], writes=[bSsc])
            kb.op("dve", lambda e: e.tensor_tensor(WT[:].rearrange("p h j -> p (h j)"), Ssc[:],
                                                   DT[:].rearrange("p h j -> p (h j)"), ALU.mult),
                  reads=[bSsc, bc], writes=[bWT])

            def mi(e):
                for h in range(8):
                    ins = e.matmul(Pi[:, h * 64:(h + 1) * 64], WT[:, h, :], vb[:, h * 64:(h + 1) * 64], start=True, stop=True)
                return ins
            kb.op("pe", mi, reads=[bWT, bvb], writes=[bP3[0]])
            for (Px, bPx, Hx, bHx) in ((Pf, bP3[1], Hfb, bHfb), (Pb, bP3[2], Hbb, bHbb)):
                def mx(e, Px=Px, Hx=Hx):
                    for h in range(8):
                        ins = e.matmul(Px[:, h * 64:(h + 1) * 64], qT[:, h, :], Hx[:, h * 64:(h + 1) * 64],
                                       start=True, stop=True)
                    return ins
                kb.op("pe", mx, reads=[bqT, bHx], writes=[bPx])
            kb.op("act", lambda e: e.activation(Y[:], Pi, AF.Copy), reads=[bP3[0]], writes=[bY])
            for (Px, bPx, t) in ((Pf, bP3[1], 0), (Pb, bP3[2], 1)):
                kb.op("act", lambda e, Px=Px: e.activation(Tt[:], Px, AF.Copy), reads=[bPx], writes=[bTt])
                kb.op("dve", lambda e, t=t: e.tensor_tensor(
                    Tt[:].rearrange("p (h d) -> p h d", h=8), Tt[:].rearrange("p (h d) -> p h d", h=8),
                    tabs[:, t, :].unsqueeze(2).to_broadcast([128, 8, 64]), ALU.mult), reads=[bTt, bc], writes=[bTt])
                kb.op("pool", lambda e: e.tensor_tensor(Y[:], Y[:], Tt[:], ALU.add), reads=[bTt, bY], writes=[bY])
            pP, bpP = pPs[pi % 2]
            pi += 1

            def ms(e, pP=pP):
                for h in range(8):
                    ins = e.matmul(pP[0:64, h * 64:(h + 1) * 64], krot[:, h * 64:(h + 1) * 64], vw[:, h * 64:(h + 1) * 64],
                                   start=True, stop=True)
                return ins
            kb.op("pe", ms, reads=[bkrot, bvw], writes=[bpP])
            kb.op("act", lambda e, pP=pP: e.activation(Ssb[:], pP[0:64, 0:512], AF.Copy), reads=[bpP], writes=[bS])
            kb.op("dve", lambda e: e.tensor_tensor(
                Hf[:].rearrange("p (h d) -> p h d", h=8), Hf[:].rearrange("p (h d) -> p h d", h=8),
                g128[0:64, 0:8].unsqueeze(2).to_broadcast([64, 8, 64]), ALU.mult), reads=[bHf, bc], writes=[bHf])
            kb.op("pool", lambda e: e.tensor_tensor(Hf[:], Hf[:], Ssb[:], ALU.add), reads=[bHf, bS], writes=[bHf])
            kb.op("pool", lambda e: e.tensor_copy(Hfb[:], Hf[:]), reads=[bHf], writes=[bHfb])
            Y3 = Y[:].rearrange("p (h d) -> p h d", h=8)
            T3 = Tt[:].rearrange("p (h d) -> p h d", h=8)
            kb.op("dve", lambda e: e.reduce_sum(st[:, 0:8], Y3, AX.X), reads=[bY], writes=[bst])
            kb.op("dve", lambda e: e.tensor_scalar(st[:, 0:8], st[:, 0:8], 1.0 / 64, None, ALU.mult), reads=[bst], writes=[bst])
            kb.op("dve", lambda e: e.tensor_tensor(Y3, Y3, st[:, 0:8].unsqueeze(2).to_broadcast([128, 8, 64]), ALU.subtract),
                  reads=[bY, bst], writes=[bY])
            kb.op("act", lambda e: e.activation(Tt[:], Y[:], AF.Square), reads=[bY], writes=[bTt])
            kb.op("dve", lambda e: e.reduce_sum(st[:, 8:16], T3, AX.X), reads=[bTt], writes=[bst])
            kb.op("act", lambda e: e.activation(st[:, 8:16], st[:, 8:16], AF.Sqrt, bias=EPS, scale=1.0 / 64), reads=[bst], writes=[bst])
            kb.op("dve", lambda e: e.reciprocal(st[:, 8:16], st[:, 8:16]), reads=[bst], writes=[bst])
            kb.op("dve", lambda e: e.tensor_tensor(Y3, Y3, st[:, 8:16].unsqueeze(2).to_broadcast([128, 8, 64]), ALU.mult),
                  reads=[bY, bst], writes=[bY])
            kb.op("pool", lambda e: e.tensor_tensor(Y[:], Y[:], gng[:], ALU.mult), reads=[bY, bc], writes=[bY])
            kb.op("pool", lambda e: e.tensor_tensor(yo[:], Y[:], sg[:], ALU.mult), reads=[bY, bsg], writes=[byo])
            kb.dma("sp", self.YN[r0:r0 + 128, 0:512], yo[:], reads=[byo])

    def na_stage(self, i):
        kb = self.kb
        T = self.T
        rows = T // 64
        tb = kb.sb("tb", [128, 2048])
        btb = kb.buf()
        kws = [(kb.sb("kw", [64, 8, 512], BF16), kb.buf()) for _ in range(2)]
        qws = [(kb.sb("qw", [64, 8, 64], BF16), kb.buf()) for _ in range(2)]
        vws = [(kb.sb("vwin", [128, 4, 8, 80], BF16), kb.buf()) for _ in range(2)]
        for (vw_, bvw_) in vws:
            kb.op("pool", lambda e, vw_=vw_: e.memset(vw_[:], 1.0), writes=[bvw_])
        Sb = kb.sb("Sb", [128, 2048])
        bSb = kb.buf()
        PTs = [(kb.sb("PT", [128, 2048], BF16), kb.buf()) for _ in range(2)]
        Osb = kb.sb("Osb", [64, 8, 65])
        bO = kb.buf()
        rc = kb.sb("rc", [64, 8])
        brc = kb.buf()
        nas = [(kb.sb("na", [64, 512], BF16), kb.buf()) for _ in range(2)]
        pST = kb.ps("pST", [128, 2048])
        bpST = kb.buf()
        pO = kb.ps("pO", [128, 1024])
        bpO = kb.buf()
        prev_s = None
        import os
        NCUT = int(os.environ.get("NA_CUT", "9"))
        stt_ = {"prev_s": None}

        def s1(r):
            start = min(max(r - 4, 0), rows - 8)
            s = r - start
            s0 = start * 64
            if s != stt_["prev_s"]:
                kb.dma("sp", tb[:], self.natab[i, s], writes=[btb])
                stt_["prev_s"] = s
            kw, bkw = kws[r % 2]
            qw, bqw = qws[r % 2]
            vw_, bvw_ = vws[r % 2]
            kb.dma("sp", kw[:], self.NKT[:, :, s0:s0 + 512].rearrange("c (two p) t -> p (c two) t", two=2), writes=[bkw])
            kb.dma("act", qw[:], self.NQT[:, :, r * 64:(r + 1) * 64].rearrange("c (two p) t -> p (c two) t", two=2),
                   writes=[bqw])
            for ck in range(4):
                kb.dma("act", vw_[:, ck, :, 0:64],
                       self.NV[s0 + ck * 128:s0 + (ck + 1) * 128, :].rearrange("p (h d) -> p h d", h=8), writes=[bvw_])

            def mst(e):
                for ck in range(4):
                    for h in range(8):
                        o = (ck * 8 + h) * 64
                        ins = e.matmul(pST[:, o:o + 64], kw[:, h, ck * 128:(ck + 1) * 128],
                                       qw[:, h, :], start=True, stop=True)
                return ins
            kb.op("pe", mst, reads=[bkw, bqw], writes=[bpST])
            kb.op("act", lambda e: e.activation(Sb[:], pST[:], AF.Copy), reads=[bpST], writes=[bSb])
            kb.op("dve", lambda e: e.tensor_tensor(Sb[:], Sb[:], tb[:], ALU.add), reads=[bSb, btb], writes=[bSb])
            PT, bPT = PTs[r % 2]
            kb.op("act", lambda e: e.activation(PT[:], Sb[:], AF.Exp), reads=[bSb], writes=[bPT])

        def s2(r):
            vw_, bvw_ = vws[r % 2]
            PT, bPT = PTs[r % 2]

            def mpv(e):
                for h in range(8):
                    oc = (h % 4) * 80 + (h // 4) * 512
                    for ck in range(4):
                        o = (ck * 8 + h) * 64
                        ins = e.matmul(pO[0:64, oc:oc + 65], PT[:, o:o + 64], vw_[:, ck, h, 0:65],
                                       start=(ck == 0), stop=(ck == 3))
                return ins
            kb.op("pe", mpv, reads=[bPT, bvw_], writes=[bpO])
            kb.op("act", lambda e: e.activation(Osb[:, 0:4, :], pO[0:64, 0:320].rearrange("p (h d) -> p h d", h=4)[:, :, 0:65], AF.Copy),
                  reads=[bpO], writes=[bO])
            kb.op("act", lambda e: e.activation(Osb[:, 4:8, :], pO[0:64, 512:832].rearrange("p (h d) -> p h d", h=4)[:, :, 0:65], AF.Copy),
                  reads=[bpO], writes=[bO])
            kb.op("dve", lambda e: e.reciprocal(rc[:].unsqueeze(2), Osb[:, :, 64:65]), reads=[bO], writes=[brc])
            na, bna = nas[r % 2]
            kb.op("dve", lambda e: e.tensor_tensor(
                na[:].rearrange("p (h d) -> p h d", h=8), Osb[:, :, 0:64],
                rc[:].unsqueeze(2).to_broadcast([64, 8, 64]), ALU.mult), reads=[bO, brc], writes=[bna])
            kb.dma("sp", self.YN[r * 64:(r + 1) * 64, 512:1024], na[:], reads=[bna])

        s1(0)
        for r in range(rows):
            if r + 1 < rows:
                s1(r + 1)
            s2(r)

    def outproj_stage(self, l, w_src, Cin, YN, Xin, Xout):
        kb = self.kb
        T = self.T
        KC = Cin // 128
        W = kb.sb("wout", [128, KC, D], BF16)
        bw = kb.buf()
        for k in range(KC):
            kb.dma("pool", W[:, k, :], w_src[k * 128:(k + 1) * 128, :], writes=[bw])
        gpost = kb.sb("gpost", [128, D])
        bc = kb.buf()
        kb.dma("sp", gpost[:], self.gains[l * 4 + 1], writes=[bc])
        yns = [(kb.sb("yn", [128, Cin], BF16), kb.buf()) for _ in range(2)]
        YTs = [(kb.sb("YT", [128, KC, 128], BF16), kb.buf()) for _ in range(2)]
        xrs = [(kb.sb("xr", [128, D]), kb.buf()) for _ in range(2)]
        xos = [(kb.sb("xo", [128, D]), kb.buf()) for _ in range(2)]
        junk = kb.sb("junk", [128, D])
        bjunk = kb.buf()
        ss = kb.sb("ss", [128, 1])
        bss = kb.buf()
        pTs = [(kb.ps("pT", [128, D], BF16), kb.buf()) for _ in range(2)]
        pms = [(kb.ps("pm", [128, D]), kb.buf()) for _ in range(2)]
        st = {"ti": 0}

        def s1(c):
            r0 = c * 128
            pm, bpm = pms[c % 2]
            yn, byn = yns[c % 2]
            YT, bYT = YTs[c % 2]
            xr, bxr = xrs[c % 2]
            xo, bxo = xos[c % 2]
            kb.dma("act", yn[:], YN[r0:r0 + 128, 0:Cin], writes=[byn])
            kb.dma("sp", xr[:], Xin[r0:r0 + 128, :], writes=[bxr])
            for r in range(KC // 8):
                pT, bpT = pTs[st["ti"] % 2]
                st["ti"] += 1

                def tr(e, yn=yn, pT=pT, r=r):
                    for k in range(8):
                        kk = r * 8 + k
                        ins = e.transpose(pT[:, k * 128:(k + 1) * 128], yn[:, kk * 128:(kk + 1) * 128], self.ident[:])
                    return ins
                kb.op("pe", tr, reads=[byn, self.bident], writes=[bpT])
                kb.op("act", lambda e, YT=YT, pT=pT, r=r: e.activation(
                    YT[:, r * 8:(r + 1) * 8, :], pT[:].rearrange("p (k t) -> p k t", k=8), AF.Copy),
                    reads=[bpT], writes=[bYT])

            def mm(e, YT=YT):
                for nh in range(2):
                    for k in range(KC):
                        ins = e.matmul(pm[:, nh * 512:(nh + 1) * 512], YT[:, k, :], W[:, k, nh * 512:(nh + 1) * 512],
                                       start=(k == 0), stop=(k == KC - 1))
                return ins
            kb.op("pe", mm, reads=[bYT, bw], writes=[bpm])

        def s2(c):
            r0 = c * 128
            pm, bpm = pms[c % 2]
            xr, bxr = xrs[c % 2]
            xo, bxo = xos[c % 2]
            self.rms_stats(pm[:], bpm, ss, bss, junk, bjunk)
            kb.op("act", lambda e, xo=xo: e.activation(xo[:], pm[:], AF.Copy, scale=ss[:, 0:1]),
                  reads=[bpm, bss], writes=[bxo])
            kb.op("dve", lambda e, xo=xo: e.tensor_tensor(xo[:], xo[:], gpost[:], ALU.mult),
                  reads=[bxo, bc], writes=[bxo])
            kb.op("pool", lambda e, xo=xo, xr=xr: e.tensor_tensor(xo[:], xo[:], xr[:], ALU.add),
                  reads=[bxo, bxr], writes=[bxo])
            kb.dma("sp", Xout[r0:r0 + 128, :], xo[:], reads=[bxo])

        NCk = T // 128
        s1(0)
        for c in range(NCk):
            if c + 1 < NCk:
                s1(c + 1)
            s2(c)

    def ffn_stage(self, l, Xin, Xout, BT=512):
        kb = self.kb
        T = self.T
        NT = BT // 128
        ident, bident = self.ident, self.bident
        W_up = kb.sb("wup", [128, 8, 2 * FF], BF16)
        W_dn = kb.sb("wdn", [128, NCH, D], BF16)
        bwu = kb.bufs(8)
        bwd = kb.bufs(NCH)
        for k in range(8):
            kb.dma("pool", W_up[:, k, :], self.ffn_w_up[l, k * 128:(k + 1) * 128, :], writes=[bwu[k]])
        for c in range(NCH):
            kb.dma("pool", W_dn[:, c, :], self.ffn_w_down[l, c * 128:(c + 1) * 128, :], writes=[bwd[c]])
        gpre = kb.sb("gpre", [128, D])
        gpost = kb.sb("gpost", [128, D])
        fcw = kb.sb("fcw", [128, 2 * NCH, 3])
        fcb = kb.sb("fcb", [128, 2 * NCH])
        bconst = kb.buf()
        kb.dma("sp", gpre[:], self.gains[l * 4 + 2], writes=[bconst])
        kb.dma("sp", gpost[:], self.gains[l * 4 + 3], writes=[bconst])
        kb.dma("sp", fcw[:], self.fcw[l].rearrange("p (c w) -> p c w", w=3), writes=[bconst])
        kb.dma("sp", fcb[:], self.fcb[l], writes=[bconst])

        xts = [kb.sb("xt", [128, D])] * 2
        bxt = [kb.buf()] * 2
        hns = [kb.sb("hn", [128, D], BF16)] * 2
        bhn = [kb.buf()] * 2
        sss = [kb.sb("ss", [128, 1]) for _ in range(2)]
        bss = kb.bufs(2)
        xh = kb.sb("xh", [2, D])
        bxh = kb.buf()
        hnh = kb.sb("hnh", [2, D], BF16)
        bhnh = kb.buf()
        ssh = kb.sb("ssh", [2, 1])
        bssh = kb.buf()
        hnTs = [kb.sb("hnT", [128, 8, BT + 2], BF16)] * 2
        bhnT = [kb.buf()] * 2
        gT = kb.sb("gT", [128, NCH, BT], BF16)
        bgT = kb.bufs(NCH)
        cgs = [kb.sb("cg", [128, BT]) for _ in range(2)]
        cvs = [kb.sb("cv", [128, BT]) for _ in range(2)]
        ggs = [kb.sb("gg", [128, BT])] * 2
        bcg, bcv, bgg = kb.bufs(2), kb.bufs(2), [kb.buf()] * 2
        ugs = [kb.sb("ug", [128, BT + 2])] * 2
        uvs = [kb.sb("uv", [128, BT + 2])] * 2
        bug, buv = [kb.buf()] * 2, [kb.buf()] * 2
        xrs = [kb.sb("xr", [128, D]) for _ in range(1)]
        bxr = kb.bufs(1)
        junk, bjunk = xrs[0], bxr[0]
        xos = [kb.sb("xo", [128, D])] * 2
        bxo = [kb.buf()] * 2
        ss2 = kb.sb("ss2", [128, 1])
        bss2 = kb.buf()

        psG = [kb.ps("psG", [128, 512]) for _ in range(2)]
        psV = [kb.ps("psV", [128, 512]) for _ in range(2)]
        bpsA = kb.bufs(2)
        psH = kb.ps("psH", [128, 512])
        bpsH = kb.buf()
        pT = kb.ps("pT", [128, D], BF16)
        bpT = kb.buf()
        pTh, bpTh = pT, bpT
        pf = kb.ps("pf", [128, D])
        bpf = kb.buf()

        ti = 0
        ci = 0
        import os
        CUT = int(os.environ.get("FFN_CUT", "9"))
        for b in range(T // BT if CUT > 1 else 0):
            t0 = b * BT
            hnT = hnTs[b % 2]
            bh = bhnT[b % 2]
            for j in range(NT):
                xt, bx = xts[ti % 2], bxt[ti % 2]
                hn, bn = hns[ti % 2], bhn[ti % 2]
                ss, bs = sss[ti % 2], bss[ti % 2]
                ti += 1
                r0 = t0 + j * 128
                kb.dma("sp", xt[:], Xin[r0:r0 + 128, :], writes=[bx])
                self.rms_stats(xt[:], bx, ss, bs, junk, bjunk)
                kb.op("dve", lambda e, hn=hn, xt=xt, ss=ss: e.scalar_tensor_tensor(
                    hn[:], xt[:], ss[:, 0:1], gpre[:], ALU.mult, ALU.mult),
                    reads=[bx, bs, bconst], writes=[bn])

                def tr(e, hn=hn):
                    for k in range(8):
                        ins = e.transpose(pT[:, k * 128:(k + 1) * 128], hn[:, k * 128:(k + 1) * 128], ident[:])
                    return ins
                kb.op("pe", tr, reads=[bn, bident], writes=[bpT])
                kb.op("act", lambda e, hnT=hnT, j=j: e.activation(
                    hnT[:, :, j * 128:(j + 1) * 128], pT[:].rearrange("p (k t) -> p k t", k=8), AF.Copy),
                    reads=[bpT], writes=[bh])
            kb.op("pool", lambda e: e.memset(xh[:], 0.0), writes=[bxh])
            if t0 - 1 >= 0:
                kb.dma("sp", xh[0:1, :], Xin[t0 - 1:t0, :], writes=[bxh])
            if t0 + BT < T:
                kb.dma("sp", xh[1:2, :], Xin[t0 + BT:t0 + BT + 1, :], writes=[bxh])
            self.rms_stats(xh[:], bxh, ssh, bssh, junk, bjunk, nparts=2)
            kb.op("dve", lambda e: e.scalar_tensor_tensor(
                hnh[:], xh[:], ssh[:, 0:1], gpre[0:2, :], ALU.mult, ALU.mult),
                reads=[bxh, bssh, bconst], writes=[bhnh])

            def trh(e):
                for k in range(8):
                    ins = e.transpose(pTh[:, k * 2:(k + 1) * 2], hnh[:, k * 128:(k + 1) * 128], ident[0:2, 0:2])
                return ins
            kb.op("pe", trh, reads=[bhnh, bident], writes=[bpTh])
            kb.op("act", lambda e, hnT=hnT: e.activation(
                hnT[:, :, BT:BT + 2], pTh[:, 0:16].rearrange("p (k t) -> p k t", k=8), AF.Copy),
                reads=[bpTh], writes=[bh])

            for c in range(NCH if CUT > 2 else 0):
                par = ci % 2
                ci += 1
                pg = psG[par][:, 0:BT]
                pv = psV[par][:, 0:BT]
                ph = psH[:, 0:4]
                cg, cv, gg = cgs[par], cvs[par], ggs[par]

                SUB = os.environ.get("FFN_SUB", "")

                def mm(e, c=c, pg=pg, pv=pv, ph=ph, hnT=hnT):
                    lst = ((pg, c * 128, slice(0, BT)), (pv, FF + c * 128, slice(0, BT)),
                           (ph[:, 0:2], c * 128, slice(BT, BT + 2)),
                           (ph[:, 2:4], FF + c * 128, slice(BT, BT + 2)))
                    if "h" in SUB:
                        lst = lst[:2]
                    for (dst, col, rhs_sl) in lst:
                        for k in range(8):
                            ins = e.matmul(dst, W_up[:, k, col:col + 128], hnT[:, k, rhs_sl],
                                           start=(k == 0), stop=(k == 7))
                    return ins
                kb.op("pe", mm, reads=bwu + [bh], writes=[bpsA[par], bpsH])
                pairs = ((pg, 0, cg, bcg[par], c, ugs[par], bug[par]), (pv, 2, cv, bcv[par], NCH + c, uvs[par], buv[par]))
                for (src, hoff, dst, bd, cc, u, bu) in pairs:
                    kb.op("act", lambda e, src=src, u=u: e.activation(u[:, 0:BT], src, AF.Copy),
                          reads=[bpsA[par]], writes=[bu])
                    kb.op("act", lambda e, ph=ph, hoff=hoff, u=u: e.activation(
                        u[:, BT:BT + 2], ph[:, hoff:hoff + 2], AF.Copy), reads=[bpsH], writes=[bu])
                    kb.op("act", lambda e, src=src, dst=dst, cc=cc: e.activation(
                        dst[:], src, AF.Identity, bias=fcb[:, cc:cc + 1], scale=fcw[:, cc, 1:2]),
                        reads=[bpsA[par], bconst], writes=[bd])
                for tap in range(4):
                    for (src, hoff, dst, bd, cc, u, bu) in pairs:
                        if tap == 0:
                            kb.op("dve", lambda e, u=u, dst=dst, cc=cc: e.scalar_tensor_tensor(
                                dst[:, 1:BT], u[:, 0:BT - 1], fcw[:, cc, 0:1], dst[:, 1:BT], ALU.mult, ALU.add),
                                reads=[bu, bconst, bd], writes=[bd])
                        elif tap == 1:
                            kb.op("dve", lambda e, u=u, dst=dst, cc=cc: e.scalar_tensor_tensor(
                                dst[:, 0:BT - 1], u[:, 1:BT], fcw[:, cc, 2:3], dst[:, 0:BT - 1], ALU.mult, ALU.add),
                                reads=[bu, bconst, bd], writes=[bd])
                        elif tap == 2:
                            kb.op("dve", lambda e, u=u, dst=dst, cc=cc: e.scalar_tensor_tensor(
                                dst[:, 0:1], u[:, BT:BT + 1], fcw[:, cc, 0:1], dst[:, 0:1], ALU.mult, ALU.add),
                                reads=[bu, bconst, bd], writes=[bd])
                        else:
                            kb.op("dve", lambda e, u=u, dst=dst, cc=cc: e.scalar_tensor_tensor(
                                dst[:, BT - 1:BT], u[:, BT + 1:BT + 2], fcw[:, cc, 2:3], dst[:, BT - 1:BT],
                                ALU.mult, ALU.add),
                                reads=[bu, bconst, bd], writes=[bd])
                kb.op("act", lambda e, gg=gg, cg=cg: e.activation(gg[:], cg[:], AF.Gelu_apprx_tanh),
                      reads=[bcg[par]], writes=[bgg[par]])
                kb.op("pool", lambda e, c=c, gg=gg, cv=cv: e.tensor_tensor(gT[:, c, :], gg[:], cv[:], ALU.mult),
                      reads=[bgg[par], bcv[par]], writes=[bgT[c]])

            for j in range(NT if CUT > 3 else 0):
                r0 = t0 + j * 128

                def dn(e, j=j):
                    for nh in range(2):
                        for c in range(NCH):
                            ins = e.matmul(pf[:, nh * 512:(nh + 1) * 512], gT[:, c, j * 128:(j + 1) * 128],
                                           W_dn[:, c, nh * 512:(nh + 1) * 512], start=(c == 0), stop=(c == NCH - 1))
                    return ins
                kb.op("pe", dn, reads=bgT + bwd, writes=[bpf])
                xr, bxrr = xrs[0], bxr[0]
                xo, bxoo = xos[j % 2], bxo[j % 2]
                kb.op("act", lambda e, xo=xo: e.activation(xo[:], pf[:], AF.Copy), reads=[bpf], writes=[bxoo])
                self.rms_stats(xo[:], bxoo, ss2, bss2, hns[0], bhn[0])
                kb.dma("sp", xr[:], Xin[r0:r0 + 128, :], writes=[bxrr])
                kb.op("dve", lambda e, xo=xo: e.scalar_tensor_tensor(xo[:], xo[:], ss2[:, 0:1], gpost[:], ALU.mult, ALU.mult),
                      reads=[bxoo, bss2, bconst], writes=[bxoo])
                kb.op("pool", lambda e, xo=xo, xr=xr: e.tensor_tensor(xo[:], xo[:], xr[:], ALU.add),
                      reads=[bxoo, bxrr], writes=[bxoo])
                kb.dma("sp", Xout[r0:r0 + 128, :], xo[:], reads=[bxoo])


def host_rope(pos):
    half = 32
    inv = (1.0 / (10000.0 ** (np.arange(half, dtype=np.float32) / half))).astype(np.float32)
    ang = pos.astype(np.float32)[:, None] * inv[None, :]
    return np.concatenate([np.cos(ang), np.sin(ang)], axis=1).astype(np.float32)


def host_even(inp):
    lgt = inp["ab_ret_decay_logit"].reshape(2, 1, 16)
    rlogit = np.ascontiguousarray(np.broadcast_to(lgt, (2, 128, 16))).astype(np.float32)
    p = np.arange(128, dtype=np.float32)
    ridx = np.stack([p + 1, 128 - p, 127 - p, p], axis=1).astype(np.float32)
    l_ = np.arange(128)[:, None]
    j_ = np.arange(128)[None, :]
    rdj = np.stack([np.maximum(j_ - l_, 0), np.maximum(l_ - j_, 0)], 0).astype(np.float32)
    rgn = np.ascontiguousarray(np.broadcast_to(inp["ab_ret_gn_g"].reshape(2, 1, 512), (2, 128, 512))).astype(np.float32)
    rpb = inp["ab_na_rpb"]
    kk = np.arange(512)
    w = kk // 64
    kc = kk % 64
    q = np.arange(64)
    cstart = np.clip(q - 8, 0, 48)
    valid = (kc[:, None] >= cstart[None, :]) & (kc[:, None] < cstart[None, :] + 16)
    dc = np.clip(kc[:, None] - q[None, :], -15, 15) + 15
    natab = np.empty((2, 8, 512, 8, 64), np.float32)
    for s in range(8):
        dr = np.clip(w - s + 7, 0, 14)
        g = rpb[:, :, dr[:, None], dc]
        g = np.where(valid[None, None], g, np.float32(-30000.0))
        natab[:, s] = g.transpose(0, 2, 1, 3)
    natab = natab.reshape(2, 8, 4, 128, 8, 64).transpose(0, 1, 3, 2, 4, 5).reshape(2, 8, 128, 2048)
    return {"ab_w_in": np.ascontiguousarray(inp["ab_w_in"]), "ab_w_out": np.ascontiguousarray(inp["ab_w_out"]),
            "rlogit": rlogit, "ridx": ridx, "rdj": rdj, "rgn": rgn, "natab": np.ascontiguousarray(natab)}

def host_prep(inp, layers=range(4)):
    L = L_TOTAL
    g = np.stack([inp["norm_mix_pre"], inp["norm_mix_post"], inp["norm_ffn_pre"], inp["norm_ffn_post"]], axis=1)
    gains = np.ascontiguousarray(np.broadcast_to(g.reshape(L * 4, 1, D), (L * 4, 128, D))).astype(np.float32)
    cw = inp["ffn_conv_w"]
    fcw = np.ascontiguousarray(cw.reshape(L, 3, 2 * NCH, 128).transpose(0, 3, 2, 1)).reshape(L, 128, 2 * NCH * 3)
    fcb = np.ascontiguousarray(inp["ffn_conv_b"].reshape(L, 2 * NCH, 128).transpose(0, 2, 1))
    cmats = np.zeros((5, 128, 128), np.float32)
    k = np.arange(128)[:, None]
    j = np.arange(128)[None, :]
    cmats[0] = k <= j
    cmats[1] = k < j
    cmats[2] = k > j
    cmats[3] = k >= j
    cmats[4] = 1.0
    ccw = inp["c_conv_w"]
    scw = np.ascontiguousarray(ccw.reshape(2, 5, 24, 128).transpose(0, 3, 2, 1)).reshape(2, 128, 120)
    scb = np.ascontiguousarray(inp["c_conv_b"].reshape(2, 24, 128).transpose(0, 2, 1))

    def rep(a, n):
        return np.ascontiguousarray(np.broadcast_to(a.reshape(2, 1, n), (2, 128, n))).astype(np.float32)
    extra = {"c_w_in": np.ascontiguousarray(inp["c_w_in"]), "c_w_out": np.ascontiguousarray(inp["c_w_out"]),
             "scw": scw.astype(np.float32), "scb": scb.astype(np.float32),
             "sdtb": rep(inp["c_dt_bias"], 64), "salog": rep(inp["c_a_log"], 64),
             "sdsk": rep(inp["c_d_skip"], 32), "sng": rep(inp["c_norm_g"], 2048), "cmats": cmats}
    extra.update(host_even(inp))
    return {**extra, "gains": gains, "ffn_w_up": np.ascontiguousarray(inp["ffn_w_up"]),
            "ffn_w_down": np.ascontiguousarray(inp["ffn_w_down"]),
            "fcw": fcw.astype(np.float32), "fcb": fcb.astype(np.float32)}


_CACHE = {}


def kernel(**inputs):
    x = np.asarray(inputs["x"], dtype=np.float32)
    B, S, _ = x.shape
    n_cores = 8
    cpb = n_cores // B
    if "prog" not in _CACHE:
        _CACHE["prog"] = Prog(S, [0, 1, 2, 3], do_mix=True, do_ffn=True)
    P = _CACHE["prog"]
    hp = host_prep({k: np.asarray(v, dtype=np.float32) for k, v in inputs.items()})
    hp["rope"] = host_rope(np.arange(S))
    in_maps = []
    for c in range(n_cores):
        m = dict(hp)
        m["x"] = np.ascontiguousarray(x[c // cpb])
        in_maps.append(m)
    res = run_bass_kernel_spmd(P.nc, in_maps, core_ids=list(range(n_cores)))
    out = np.stack([res.results[b * cpb]["out"] for b in range(B)], axis=0)
    return out.astype(np.float32)
```

```python
from contextlib import ExitStack
import numpy as np
import concourse.bass as bass
import concourse.mybir as mybir
from concourse.bass_utils import run_bass_kernel_spmd

F32 = mybir.dt.float32
BF16 = mybir.dt.bfloat16
ALU = mybir.AluOpType
AF = mybir.ActivationFunctionType
AX = mybir.AxisListType

EPOCH = 8192
SAME_ENGINE_SYNC = True
OWN_DIST = 10 ** 9


class Buf:
    __slots__ = ("name", "w", "r")

    def __init__(self, name):
        self.name = name
        self.w = None
        self.r = []


class Op:
    __slots__ = ("eng", "emit", "waits", "inc")


class KB:
    ENGS = ("sp", "act", "pool", "pe", "dve")

    def __init__(self, nc, n_dma_sems=16):
        self.nc = nc
        self.gstack = ExitStack()
        self.stack = self.gstack
        self.stream = {e: [] for e in self.ENGS}
        self.count = {e: 0 for e in self.ENGS}
        self.csems = {e: [] for e in self.ENGS}
        self.waited = {e: {} for e in self.ENGS}
        self.dma_pool = {}
        self.dma_next = {}
        self.n_dma_sems = n_dma_sems
        self.semobjs = {}
        self.nbuf = 0
        self.nalloc = 0

    def sem(self, name):
        s = self.gstack.enter_context(self.nc.semaphore(name))
        self.semobjs[name] = s
        return name

    def sb(self, name, shape, dt=F32):
        self.nalloc += 1
        return self.stack.enter_context(self.nc.sbuf_tensor(f"{name}_{self.nalloc}", list(shape), dt))

    def ps(self, name, shape, dt=F32):
        self.nalloc += 1
        return self.stack.enter_context(self.nc.psum_tensor(f"{name}_{self.nalloc}", list(shape), dt))

    def buf(self, name=None):
        self.nbuf += 1
        return Buf(name or f"b{self.nbuf}")

    def bufs(self, n):
        return [self.buf() for _ in range(n)]

    def _deps(self, reads, writes):
        ids = []
        for b in reads:
            if b.w is not None:
                ids.append(b.w)
        for b in writes:
            if b.w is not None:
                ids.append(b.w)
            ids.extend(b.r)
        return ids

    def _finish(self, eng, emit, reads, writes, cid3, ids):
        cid = cid3[:2]
        need = {}
        for t in ids:
            if t[1] > need.get(t[0], 0):
                need[t[0]] = t[1]
        waits = []
        wd = self.waited[eng]
        own = self.csems[eng]
        cur = self.count[eng] if cid3[2] == 1 else None
        for s, v in need.items():
            if s in own:
                if not SAME_ENGINE_SYNC:
                    continue
                if cur is not None and cur - (own.index(s) * EPOCH + v) >= OWN_DIST:
                    continue
            if wd.get(s, 0) >= v:
                continue
            wd[s] = v
            waits.append((s, v))
        op = Op()
        op.eng, op.emit, op.waits, op.inc = eng, emit, waits, cid3
        self.stream[eng].append(op)
        for b in reads:
            b.r.append(cid)
        for b in writes:
            b.w = cid
            b.r = []
        return cid

    def op(self, eng, emit, reads=(), writes=()):
        n = self.count[eng]
        self.count[eng] = n + 1
        k = n // EPOCH
        while len(self.csems[eng]) <= k:
            self.csems[eng].append(self.sem(f"c_{eng}_{len(self.csems[eng])}"))
        cid3 = (self.csems[eng][k], (n % EPOCH) + 1, 1)
        return self._finish(eng, emit, reads, writes, cid3, self._deps(reads, writes))

    def dma(self, eng, out, in_, reads=(), writes=(), **kw):
        if eng not in self.dma_pool:
            self.dma_pool[eng] = [[self.sem(f"d_{eng}_{i}"), 0] for i in range(self.n_dma_sems)]
            self.dma_next[eng] = 0
        i = self.dma_next[eng]
        self.dma_next[eng] = (i + 1) % self.n_dma_sems
        slot = self.dma_pool[eng][i]
        prev = slot[1]
        slot[1] = prev + 16
        cid3 = (slot[0], slot[1], 16)

        def emit(e, out=out, in_=in_, kw=kw):
            return e.dma_start(out=out, in_=in_, **kw)

        ids = self._deps(reads, writes)
        if prev > 0:
            ids = ids + [(slot[0], prev)]
        return self._finish(eng, emit, reads, writes, cid3, ids)

    def coll(self, emit, reads=(), writes=()):
        eng = "pool"
        if eng not in self.dma_pool:
            self.dma_pool[eng] = [[self.sem(f"d_{eng}_{i}"), 0] for i in range(self.n_dma_sems)]
            self.dma_next[eng] = 0
        i = self.dma_next[eng]
        self.dma_next[eng] = (i + 1) % self.n_dma_sems
        slot = self.dma_pool[eng][i]
        prev = slot[1]
        slot[1] = prev + 16
        cid3 = (slot[0], slot[1], 16)
        ids = self._deps(reads, writes)
        if prev > 0:
            ids = ids + [(slot[0], prev)]
        return self._finish(eng, emit, reads, writes, cid3, ids)

    def all_ids(self):
        ids = []
        for e in self.ENGS:
            n = self.count[e]
            if n > 0:
                ids.append((self.csems[e][(n - 1) // EPOCH], ((n - 1) % EPOCH) + 1))
        for e in self.dma_pool:
            for (s, v) in self.dma_pool[e]:
                if v > 0:
                    ids.append((s, v))
        return ids

    def barrier(self):
        ids = self.all_ids()
        for e in self.ENGS:
            waits = []
            wd = self.waited[e]
            for (s, v) in ids:
                if wd.get(s, 0) >= v:
                    continue
                wd[s] = v
                waits.append((s, v))
            op = Op()
            op.eng, op.emit, op.waits, op.inc = e, None, waits, None
            self.stream[e].append(op)

    def emit_all(self):
        nc = self.nc
        so = self.semobjs
        with nc.Block() as block:
            decos = {"sp": block.sync, "act": block.scalar, "pool": block.gpsimd,
                     "pe": block.tensor, "dve": block.vector}
            for name in self.ENGS:
                ops = self.stream[name]
                if not ops:
                    continue

                def f(eng, ops=ops):
                    for op in ops:
                        for (s, v) in op.waits:
                            eng.wait_ge(so[s], v)
                        if op.emit is None:
                            continue
                        ins = op.emit(eng)
                        if op.inc is not None:
                            ins.then_inc(so[op.inc[0]], op.inc[2])
                decos[name](f)
        self.stream = {e: [] for e in self.ENGS}

    def stage(self):
        return _Stage(self)


class _Stage:
    def __init__(self, kb):
        self.kb = kb

    def __enter__(self):
        self.st = ExitStack()
        self.kb.stack = self.st
        return self

    def __exit__(self, *a):
        self.kb.barrier()
        self.kb.emit_all()
        self.kb.stack = self.kb.gstack
        self.st.close()
        return False


D = 1024
FF = 2816
NCH = FF // 128
EPS = 1e-6
L_TOTAL = 4


class Prog:
    def __init__(self, T, layers, do_mix=True, do_ffn=True):
        self.T = T
        self.layers = layers
        nc = bass.Bass("TRN2", target_bir_lowering=False)
        self.nc = nc
        kb = KB(nc)
        self.kb = kb
        L = L_TOTAL

        def inp(name, shape):
            return nc.dram_tensor(name, list(shape), F32, kind="ExternalInput").ap()

        self.x = inp("x", [T, D])
        self.gains = inp("gains", [L * 4, 128, D])
        self.ffn_w_up = inp("ffn_w_up", [L, D, 2 * FF])
        self.ffn_w_down = inp("ffn_w_down", [L, FF, D])
        self.fcw = inp("fcw", [L, 128, 2 * NCH * 3])
        self.fcb = inp("fcb", [L, 128, 2 * NCH])
        self.c_w_in = inp("c_w_in", [2, D, 5184])
        self.c_w_out = inp("c_w_out", [2, 2048, D])
        self.scw = inp("scw", [2, 128, 24 * 5])
        self.scb = inp("scb", [2, 128, 24])
        self.sdtb = inp("sdtb", [2, 128, 64])
        self.salog = inp("salog", [2, 128, 64])
        self.sdsk = inp("sdsk", [2, 128, 32])
        self.sng = inp("sng", [2, 128, 2048])
        self.cmats = inp("cmats", [5, 128, 128])
        self.ab_w_in = inp("ab_w_in", [2, D, 3584])
        self.ab_w_out = inp("ab_w_out", [2, D, D])
        self.rlogit = inp("rlogit", [2, 128, 16])
        self.ridx = inp("ridx", [128, 4])
        self.rdj = inp("rdj", [2, 128, 128])
        self.rgn = inp("rgn", [2, 128, 512])
        self.rope = inp("rope", [T, 64])
        self.natab = inp("natab", [2, 8, 128, 2048])
        self.out = nc.dram_tensor("out", [T, D], F32, kind="ExternalOutput").ap()
        self.YN = nc.dram_tensor("YN", [T, 2048], BF16).ap()
        self.HB = nc.dram_tensor("HB", [T // 128, 128, 2048], BF16).ap()
        self.RHB = nc.dram_tensor("RHB", [T // 128, 64, 512], BF16).ap()
        self.XTK = nc.dram_tensor("XTK", [T, 2048], BF16).ap()
        self.BTK = nc.dram_tensor("BTK", [T, 512], BF16).ap()
        self.BTF = nc.dram_tensor("BTF", [4, 128, T], BF16).ap()
        self.CTF = nc.dram_tensor("CTF", [4, 128, T], BF16).ap()
        self.DTL = nc.dram_tensor("DTL", [T, 128], F32).ap()
        self.ZS = nc.dram_tensor("ZS", [T, 2048], BF16).ap()
        self.NQT = nc.dram_tensor("NQT", [4, 128, T], BF16).ap()
        self.NKT = nc.dram_tensor("NKT", [4, 128, T], BF16).ap()
        self.NV = nc.dram_tensor("NV", [T, 512], BF16).ap()
        self.Xa = nc.dram_tensor("Xa", [T, D], F32).ap()
        self.Xb = nc.dram_tensor("Xb", [T, D], F32).ap()

        self.ident = kb.sb("ident", [128, 128], BF16)
        self.bident = kb.buf()
        with kb.stage():
            one = kb.sb("one", [128, 128])
            idf = kb.sb("idf", [128, 128])
            b1, b2 = kb.buf(), kb.buf()
            kb.op("pool", lambda e: e.memset(one[:], 1.0), writes=[b1])
            kb.op("pool", lambda e: e.affine_select(idf[:], one[:], [[-1, 128]], ALU.is_equal, 0.0,
                                                    base=0, channel_multiplier=1), reads=[b1], writes=[b2])
            kb.op("pool", lambda e: e.tensor_copy(self.ident[:], idf[:]), reads=[b2], writes=[self.bident])

        cur = self.x
        for li, l in enumerate(layers):
            last = li == len(layers) - 1
            if do_mix:
                i = l // 2
                mdst = self.out if (last and not do_ffn) else self.Xa
                if l % 2 == 0:
                    import os
                    EVS = os.environ.get("EV_STAGES", "1234")
                    if "1" in EVS:
                        with kb.stage():
                            self.even_p1(i, l, cur)
                    if "2" in EVS:
                        with kb.stage():
                            self.na_stage(i)
                    if "3" in EVS:
                        with kb.stage():
                            self.even_p3(i, l, cur)
                    if "4" in EVS:
                        with kb.stage():
                            self.outproj_stage(l, self.ab_w_out[i], 1024, self.YN, cur, mdst)
                else:
                    import os
                    SDS = os.environ.get("SSD_STAGES", "123")
                    if "1" in SDS:
                        with kb.stage():
                            self.ssd_p1(i, l, cur)
                    if "2" in SDS:
                        with kb.stage():
                            self.ssd_p2a(i, l, cur)
                    if "3" in SDS:
                        with kb.stage():
                            self.outproj_stage(l, self.c_w_out[i], 2048, self.YN, cur, mdst)
                cur = mdst
            if do_ffn:
                dst = self.out if last else self.Xb
                with kb.stage():
                    self.ffn_stage(l, cur, dst)
                cur = dst
        kb.gstack.close()

    def rms_stats(self, src, bsrc, ss, bss, junk, bjunk, nparts=128, width=D):
        kb = self.kb
        kb.op("dve", lambda e: e.memset(ss[0:nparts, 0:1], 0.0), writes=[bss])
        kb.op("act", lambda e: e.activation(junk[0:nparts, 0:width], src, AF.Square,
                                            accum_out=ss[0:nparts, 0:1]),
              reads=[bsrc, bss], writes=[bjunk, bss])
        kb.op("act", lambda e: e.activation(ss[0:nparts, 0:1], ss[0:nparts, 0:1], AF.Sqrt,
                                            bias=EPS, scale=1.0 / width), reads=[bss], writes=[bss])
        kb.op("dve", lambda e: e.reciprocal(ss[0:nparts, 0:1], ss[0:nparts, 0:1]), reads=[bss], writes=[bss])

    def make_fe(self, BT, nh):
        kb = self.kb
        fe = {}
        fe["BT"], fe["nh"] = BT, nh
        fe["xts"] = [kb.sb("xt", [128, D]) for _ in range(2)]
        fe["bxt"] = kb.bufs(2)
        fe["hns"] = [kb.sb("hn", [128, D], BF16) for _ in range(2)]
        fe["bhn"] = kb.bufs(2)
        fe["junk"] = kb.sb("junk", [128, D])
        fe["bjunk"] = kb.buf()
        fe["sss"] = [kb.sb("ss", [128, 1]) for _ in range(2)]
        fe["bss"] = kb.bufs(2)
        nhh = max(nh, 1)
        fe["xh"] = kb.sb("xh", [2 * nhh, D]) if nh > 0 else None
        fe["bxh"] = kb.buf()
        fe["hnh"] = kb.sb("hnh", [2 * nhh, D], BF16) if nh > 0 else None
        fe["bhnh"] = kb.buf()
        fe["ssh"] = kb.sb("ssh", [2 * nhh, 1]) if nh > 0 else None
        fe["bssh"] = kb.buf()
        fe["pT"] = kb.ps("pT", [128, D], BF16)
        fe["bpT"] = kb.buf()
        if nh > 0:
            fe["pTh"] = kb.ps("pTh", [128, D], BF16)
            fe["bpTh"] = kb.buf()
        else:
            fe["pTh"], fe["bpTh"] = fe["pT"], fe["bpT"]
        fe["ti"] = 0
        return fe

    def run_fe(self, fe, Xin, t0, gpre, bconst, hnT, bh):
        kb = self.kb
        T = self.T
        BT, nh = fe["BT"], fe["nh"]
        ident, bident = self.ident, self.bident
        pT, bpT, pTh, bpTh = fe["pT"], fe["bpT"], fe["pTh"], fe["bpTh"]
        junk, bjunk = fe["junk"], fe["bjunk"]
        for j in range(BT // 128):
            i = fe["ti"] % 2
            fe["ti"] += 1
            xt, bx = fe["xts"][i], fe["bxt"][i]
            hn, bn = fe["hns"][i], fe["bhn"][i]
            ss, bs = fe["sss"][i], fe["bss"][i]
            r0 = t0 + j * 128
            kb.dma("sp", xt[:], Xin[r0:r0 + 128, :], writes=[bx])
            self.rms_stats(xt[:], bx, ss, bs, junk, bjunk)
            kb.op("dve", lambda e, hn=hn, xt=xt, ss=ss: e.scalar_tensor_tensor(
                hn[:], xt[:], ss[:, 0:1], gpre[:], ALU.mult, ALU.mult),
                reads=[bx, bs, bconst], writes=[bn])

            def tr(e, hn=hn):
                for k in range(8):
                    ins = e.transpose(pT[:, k * 128:(k + 1) * 128], hn[:, k * 128:(k + 1) * 128], ident[:])
                return ins
            kb.op("pe", tr, reads=[bn, bident], writes=[bpT])
            kb.op("act", lambda e, j=j: e.activation(
                hnT[:, :, j * 128:(j + 1) * 128], pT[:].rearrange("p (k t) -> p k t", k=8), AF.Copy),
                reads=[bpT], writes=[bh])
        if nh == 0:
            return
        xh, bxh, hnh, bhnh, ssh, bssh = fe["xh"], fe["bxh"], fe["hnh"], fe["bhnh"], fe["ssh"], fe["bssh"]
        kb.op("pool", lambda e: e.memset(xh[:], 0.0), writes=[bxh])
        if t0 - nh >= 0:
            kb.dma("sp", xh[0:nh, :], Xin[t0 - nh:t0, :], writes=[bxh])
        if t0 + BT + nh <= T:
            kb.dma("sp", xh[nh:2 * nh, :], Xin[t0 + BT:t0 + BT + nh, :], writes=[bxh])
        self.rms_stats(xh[:], bxh, ssh, bssh, junk, bjunk, nparts=2 * nh)
        kb.op("dve", lambda e: e.scalar_tensor_tensor(
            hnh[:], xh[:], ssh[:, 0:1], gpre[0:2 * nh, :], ALU.mult, ALU.mult),
            reads=[bxh, bssh, bconst], writes=[bhnh])
        w = 2 * nh

        def trh(e):
            for k in range(8):
                ins = e.transpose(pTh[:, k * w:(k + 1) * w], hnh[:, k * 128:(k + 1) * 128], ident[0:w, 0:w])
            return ins
        kb.op("pe", trh, reads=[bhnh, bident], writes=[bpTh])
        kb.op("act", lambda e: e.activation(
            hnT[:, :, BT:BT + w], pTh[:, 0:8 * w].rearrange("p (k t) -> p k t", k=8), AF.Copy),
            reads=[bpTh], writes=[bh])

    def ssd_consts(self, i, l):
        kb = self.kb
        c = {}
        c["b"] = kb.buf()
        b = c["b"]
        c["gpre"] = kb.sb("gpre", [128, D])
        kb.dma("sp", c["gpre"][:], self.gains[l * 4 + 0], writes=[b])
        c["scw"] = kb.sb("scw", [128, 24, 5])
        c["scb"] = kb.sb("scb", [128, 24])
        kb.dma("sp", c["scw"][:], self.scw[i].rearrange("p (c w) -> p c w", w=5), writes=[b])
        kb.dma("sp", c["scb"][:], self.scb[i], writes=[b])
        c["dtb"] = kb.sb("dtb", [128, 64])
        kb.dma("sp", c["dtb"][:], self.sdtb[i], writes=[b])
        c["A"] = kb.sb("A", [128, 64])
        kb.dma("sp", c["A"][:], self.salog[i], writes=[b])
        kb.op("act", lambda e: e.activation(c["A"][:], c["A"][:], AF.Exp), reads=[b], writes=[b])
        kb.op("dve", lambda e: e.tensor_scalar(c["A"][:], c["A"][:], -1.0, None, ALU.mult), reads=[b], writes=[b])
        c["cm"] = kb.sb("cm", [128, 5, 128])
        kb.dma("sp", c["cm"][:], self.cmats.rearrange("m p j -> p m j"), writes=[b])
        return c

    def ssd_load_win(self, i, cols_list):
        kb = self.kb
        W = kb.sb("swin", [128, 8, 5184], BF16)
        bw = kb.buf()
        for k in range(8):
            for (a, bnd) in cols_list:
                kb.dma("pool", W[:, k, a:bnd], self.c_w_in[i, k * 128:(k + 1) * 128, a:bnd], writes=[bw])
        return W, bw

    def ssd_block_front(self, fe, cst, W, bw, Xin, t0, hnT, bh, chunks, ue_s, acc_s, ps, outs):
        kb = self.kb
        BT = fe["BT"]
        NT = BT // 128
        scw, scb, bc = cst["scw"], cst["scb"], cst["b"]
        self.run_fe(fe, Xin, t0, cst["gpre"], bc, hnT, bh)
        for n0 in range(0, len(chunks), 2):
            ctx = []
            for n in range(n0, min(n0 + 2, len(chunks))):
                cc = chunks[n]
                par = n % 2
                pA, bpA = ps["pA"][par], ps["bpA"][par]
                pH, bpH = ps["pHh"][par], ps["bpHh"][par]
                ue, bue = ue_s[par]
                acc, bacc = acc_s[par]
                col = 2048 + cc * 128

                def mm(e, col=col, pA=pA):
                    for k in range(8):
                        ins = e.matmul(pA[:, 0:BT], W[:, k, col:col + 128], hnT[:, k, 0:BT], start=(k == 0), stop=(k == 7))
                    return ins
                kb.op("pe", mm, reads=[bw, bh], writes=[bpA])

                def mmh(e, col=col, pH=pH):
                    for k in range(8):
                        ins = e.matmul(pH[:, 0:4], W[:, k, col:col + 128], hnT[:, k, BT:BT + 4], start=(k == 0), stop=(k == 7))
                    return ins
                kb.op("pe", mmh, reads=[bw, bh], writes=[bpH])
                kb.op("act", lambda e, ue=ue, pA=pA: e.activation(ue[:, 2:2 + BT], pA[:, 0:BT], AF.Copy),
                      reads=[bpA], writes=[bue])
                kb.op("act", lambda e, ue=ue, pH=pH: e.activation(ue[:, 0:2], pH[:, 0:2], AF.Copy), reads=[bpH], writes=[bue])
                kb.op("act", lambda e, ue=ue, pH=pH: e.activation(ue[:, BT + 2:BT + 4], pH[:, 2:4], AF.Copy),
                      reads=[bpH], writes=[bue])
                ctx.append((cc, ue, bue, acc, bacc))
            for j in range(5):
                for (cc, ue, bue, acc, bacc) in ctx:
                    if j == 0:
                        kb.op("dve", lambda e, ue=ue, acc=acc, cc=cc: e.tensor_scalar(
                            acc[:], ue[:, 0:BT], scw[:, cc, 0:1], scb[:, cc:cc + 1], ALU.mult, ALU.add),
                            reads=[bue, bc], writes=[bacc])
                    else:
                        kb.op("dve", lambda e, ue=ue, acc=acc, cc=cc, j=j: e.scalar_tensor_tensor(
                            acc[:], ue[:, j:j + BT], scw[:, cc, j:j + 1], acc[:], ALU.mult, ALU.add),
                            reads=[bue, bc, bacc], writes=[bacc])
            for (cc, ue, bue, acc, bacc) in ctx:
                outs(cc, acc, bacc)

    def ssd_small(self, cst, la, bla, dirs, ps, sm, bsm):
        kb = self.kb
        cm, bc = cst["cm"], cst["b"]
        pH, bpH = ps["pH"], ps["bpH"]

        def mm(e):
            e.matmul(pH[:, 64:96], cm[:, 0, :], la[:, 0:32], start=True, stop=True)
            e.matmul(pH[:, 96:128], cm[:, 3, :], la[:, 32:64], start=True, stop=True)
            e.matmul(pH[:, 128:160], cm[:, 1, :], la[:, 32:64], start=True, stop=True)
            return e.matmul(pH[:, 160:224], cm[:, 4, :], la[:, 0:64], start=True, stop=True)
        kb.op("pe", mm, reads=[bla, bc], writes=[bpH])
        kb.op("act", lambda e: e.activation(sm[:, 0:160], pH[:, 64:224], AF.Copy), reads=[bpH], writes=[bsm])

    def ssd_dt(self, cst, W, bw, hnT, bh, sl, ps, tl):
        kb = self.kb
        pH, bpH = ps["pH"], ps["bpH"]
        bc = cst["b"]

        def mm(e):
            for k in range(8):
                ins = e.matmul(pH[:, 0:64], hnT[:, k, sl], W[:, k, 5120:5184], start=(k == 0), stop=(k == 7))
            return ins
        kb.op("pe", mm, reads=[bw, bh], writes=[bpH])
        dtr, dt, la, bdt = tl["dtr"], tl["dt"], tl["la"], tl["bdt"]
        kb.op("act", lambda e: e.activation(dtr[:], pH[:, 0:64], AF.Copy), reads=[bpH], writes=[bdt])
        kb.op("dve", lambda e: e.tensor_tensor(dtr[:], dtr[:], cst["dtb"][:], ALU.add), reads=[bdt, bc], writes=[bdt])
        kb.op("act", lambda e: e.activation(dtr[:], dtr[:], AF.Exp), reads=[bdt], writes=[bdt])
        kb.op("act", lambda e: e.activation(dt[:], dtr[:], AF.Ln, bias=1.0, scale=1.0), reads=[bdt], writes=[bdt])
        kb.op("dve", lambda e: e.tensor_tensor(la[:], dt[:], cst["A"][:], ALU.mult), reads=[bdt, bc], writes=[bdt])

    def ssd_alloc_common(self, BT, p2):
        kb = self.kb
        NT = BT // 128
        a = {}
        a["hnTs"] = [kb.sb("hnT", [128, 8, BT + 4], BF16) for _ in range(2)]
        a["bhnT"] = kb.bufs(2)
        a["ue_s"] = [(kb.sb("ue", [128, BT + 4]), kb.buf()) for _ in range(2)]
        a["acc_s"] = [(kb.sb("acc", [128, BT]), kb.buf()) for _ in range(2)]
        a["xsT"] = [(kb.sb("xsT", [128, BT], BF16), kb.buf()) for _ in range(2)]
        a["x_tok"] = [kb.sb("xtok", [128, NT, 2048], BF16)] * 2
        a["bx_tok"] = [kb.buf()] * 2
        a["B_tok"] = [kb.sb("btok", [128, NT, 512], BF16)] * 2
        a["bB_tok"] = [kb.buf()] * 2
        tl = {}
        for n in ("dtr", "dt", "la"):
            tl[n] = kb.sb(n, [128, 64])
        tl["bdt"] = kb.buf()
        tl["sm"] = kb.sb("sm", [128, 160])
        tl["bsm"] = kb.buf()
        a["tl"] = tl
        ps = {}
        ps["pA"] = [kb.ps("pA", [128, 512]) for _ in range(2)]
        ps["bpA"] = kb.bufs(2)
        ps["pH"] = kb.ps("pH", [128, 512])
        ps["bpH"] = kb.buf()
        ps["pHh"] = [kb.ps("pHh", [128, 512]) for _ in range(2)]
        ps["bpHh"] = kb.bufs(2)
        a["ps"] = ps
        return a

    def ssd_p1(self, i, l, Xin, BT=512):
        kb = self.kb
        T = self.T
        NT = BT // 128
        cst = self.ssd_consts(i, l)
        W, bw = self.ssd_load_win(i, [(0, 5184)])
        fe = self.make_fe(BT, 2)
        a = self.ssd_alloc_common(BT, False)
        ps, tl = a["ps"], a["tl"]
        pT, bpT = fe["pT"], fe["bpT"]
        Hb = kb.sb("Hb", [128, 2048])
        Hbb = kb.sb("Hbb", [128, 2048], BF16)
        bHb, bHbb = kb.bufs(4), kb.bufs(4)
        kb.op("pool", lambda e: e.memset(Hb[:], 0.0), writes=bHb)
        kb.op("pool", lambda e: e.memset(Hbb[:], 0.0), writes=bHbb)
        wgt = kb.sb("wgt", [128, 32])
        dtot = kb.sb("dtot", [128, 32])
        bwg = kb.buf()
        xws = [(kb.sb("xw", [128, 512], BF16), kb.buf()) for _ in range(2)]
        Ss = [(kb.sb("Ssb", [128, 512]), kb.buf()) for _ in range(2)]
        BTf = kb.sb("BTf", [128, 4, BT], BF16)
        CTf = kb.sb("CTf", [128, 4, BT], BF16)
        bBf, bCf = kb.buf(), kb.buf()
        zss = [(kb.sb("zs", [128, 2048], BF16), kb.buf()) for _ in range(2)]
        dtl = kb.sb("dtl", [128, 128])
        bdtl = kb.buf()
        gi = 0
        zi = 0
        for b in reversed(range(T // BT)):
            t0 = b * BT
            hnT, bh = a["hnTs"][b % 2], a["bhnT"][b % 2]
            x_tok, bxk = a["x_tok"][b % 2], a["bx_tok"][b % 2]
            B_tok, bBk = a["B_tok"][b % 2], a["bB_tok"][b % 2]

            def outs(cc, acc, bacc):
                if cc >= 20:
                    dstC = CTf[:, cc - 20, :]
                    kb.op("act", lambda e: e.activation(dstC, acc[:], AF.Silu), reads=[bacc], writes=[bCf])
                    return
                if cc >= 16:
                    xsT, bxs = BTf[:, cc - 16, :], bBf
                else:
                    xsT, bxs = a["xsT"][cc % 2]
                    xsT = xsT[:]
                kb.op("act", lambda e: e.activation(xsT, acc[:], AF.Silu), reads=[bacc], writes=[bxs])

                def tr(e):
                    for ct in range(NT):
                        ins = e.transpose(pT[:, ct * 128:(ct + 1) * 128], xsT[:, ct * 128:(ct + 1) * 128], self.ident[:])
                    return ins
                kb.op("pe", tr, reads=[bxs, self.bident], writes=[bpT])
                if cc < 16:
                    dst, bd = x_tok[:, :, cc * 128:(cc + 1) * 128], bxk
                else:
                    dst, bd = B_tok[:, :, (cc - 16) * 128:(cc - 15) * 128], bBk
                kb.op("act", lambda e: e.activation(dst, pT[:, 0:NT * 128].rearrange("p (c t) -> p c t", c=NT), AF.Copy),
                      reads=[bpT], writes=[bd])
            self.ssd_block_front(fe, cst, W, bw, Xin, t0, hnT, bh, list(range(24)), a["ue_s"], a["acc_s"], ps, outs)
            for ct in range(NT):
                r0 = t0 + ct * 128
                kb.dma("sp", self.XTK[r0:r0 + 128, :], x_tok[:, ct, :], reads=[bxk])
                kb.dma("act", self.BTK[r0:r0 + 128, :], B_tok[:, ct, :], reads=[bBk])
            kb.dma("sp", self.BTF[:, :, t0:t0 + BT].rearrange("g p t -> p g t"), BTf[:], reads=[bBf])
            kb.dma("act", self.CTF[:, :, t0:t0 + BT].rearrange("g p t -> p g t"), CTf[:], reads=[bCf])
            for ct in reversed(range(NT)):
                c = t0 // 128 + ct
                sl = slice(ct * 128, (ct + 1) * 128)
                self.ssd_dt(cst, W, bw, hnT, bh, sl, ps, tl)
                self.ssd_small(cst, tl["la"], tl["bdt"], None, ps, tl["sm"], tl["bsm"])
                sm, bsm, dt, bdt = tl["sm"], tl["bsm"], tl["dt"], tl["bdt"]
                kb.op("pool", lambda e: e.tensor_copy(dtl[:, 0:64], tl["dt"][:]), reads=[bdt], writes=[bdtl])
                kb.op("pool", lambda e: e.tensor_copy(dtl[:, 64:128], tl["la"][:]), reads=[bdt], writes=[bdtl])
                kb.dma("act", self.DTL[c * 128:(c + 1) * 128, :], dtl[:], reads=[bdtl])
                zs_, bz = zss[zi % 2]
                zi += 1
                for g in range(4):
                    pZ, bpZ = ps["pA"][gi % 2], ps["bpA"][gi % 2]
                    gi += 1

                    def mz(e, pZ=pZ, g=g, hnT=hnT, sl=sl):
                        for k in range(8):
                            ins = e.matmul(pZ[:, 0:512], hnT[:, k, sl], W[:, k, g * 512:(g + 1) * 512], start=(k == 0), stop=(k == 7))
                        return ins
                    kb.op("pe", mz, reads=[bw, bh], writes=[bpZ])
                    kb.op("act", lambda e, zs_=zs_, pZ=pZ, g=g: e.activation(zs_[:, g * 512:(g + 1) * 512], pZ[:, 0:512], AF.Silu),
                          reads=[bpZ], writes=[bz])
                kb.dma("sp", self.ZS[c * 128:(c + 1) * 128, :], zs_[:], reads=[bz])
                kb.op("act", lambda e: e.activation(wgt[:], sm[:, 64:96], AF.Exp), reads=[bsm], writes=[bwg])
                kb.op("dve", lambda e: e.tensor_tensor(wgt[:], wgt[:], dt[:, 32:64], ALU.mult), reads=[bwg, bdt], writes=[bwg])
                kb.op("act", lambda e: e.activation(dtot[:], sm[:, 128:160], AF.Exp), reads=[bsm], writes=[bwg])
                kb.dma("sp", self.HB[c], Hbb[:], reads=bHbb)
                for g in range(4):
                    xw, bxw = xws[gi % 2]
                    Ssb, bS = Ss[gi % 2]
                    pS, bpS = ps["pA"][gi % 2], ps["bpA"][gi % 2]
                    gi += 1
                    gs = slice(g * 512, (g + 1) * 512)
                    kb.op("pool", lambda e, xw=xw, gs=gs, g=g, ct=ct, x_tok=x_tok: e.tensor_tensor(
                        xw[:].rearrange("p (h d) -> p h d", h=8), x_tok[:, ct, gs].rearrange("p (h d) -> p h d", h=8),
                        wgt[:, g * 8:(g + 1) * 8].unsqueeze(2).to_broadcast([128, 8, 64]), ALU.mult),
                        reads=[bxk, bwg], writes=[bxw])
                    kb.op("pe", lambda e, pS=pS, xw=xw, g=g, ct=ct, B_tok=B_tok: e.matmul(
                        pS[:, 0:512], B_tok[:, ct, g * 128:(g + 1) * 128], xw[:], start=True, stop=True),
                        reads=[bBk, bxw], writes=[bpS])
                    kb.op("act", lambda e, Ssb=Ssb, pS=pS: e.activation(Ssb[:], pS[:, 0:512], AF.Copy),
                          reads=[bpS], writes=[bS])
                    kb.op("dve", lambda e, gs=gs, g=g: e.tensor_tensor(
                        Hb[:, gs].rearrange("p (h d) -> p h d", h=8), Hb[:, gs].rearrange("p (h d) -> p h d", h=8),
                        dtot[:, g * 8:(g + 1) * 8].unsqueeze(2).to_broadcast([128, 8, 64]), ALU.mult),
                        reads=[bHb[g], bwg], writes=[bHb[g]])
                    kb.op("pool", lambda e, gs=gs, Ssb=Ssb: e.tensor_tensor(Hb[:, gs], Hb[:, gs], Ssb[:], ALU.add),
                          reads=[bHb[g], bS], writes=[bHb[g]])
                    kb.op("act", lambda e, gs=gs: e.activation(Hbb[:, gs], Hb[:, gs], AF.Copy), reads=[bHb[g]], writes=[bHbb[g]])

    def ssd_p2a(self, i, l, Xin, BT=256):
        kb = self.kb
        T = self.T
        NT = BT // 128
        cst = self.ssd_consts(i, l)
        bc = cst["b"]
        cm = cst["cm"]
        ps = {}
        ps["pA"] = [kb.ps("pA", [128, 512]) for _ in range(3)]
        ps["bpA"] = kb.bufs(3)
        ps["pH"] = kb.ps("pH", [128, 512])
        ps["bpH"] = kb.buf()
        P3 = kb.ps("P3", [128, 1536])
        bP3 = kb.bufs(3)
        dsk = kb.sb("dsk", [128, 32])
        ng = kb.sb("ng", [128, 2048])
        kb.dma("sp", dsk[:], self.sdsk[i], writes=[bc])
        kb.dma("sp", ng[:], self.sng[i], writes=[bc])
        xks = [(kb.sb("xk", [128, 2048], BF16), kb.buf()) for _ in range(2)]
        Bks = [(kb.sb("Bk", [128, 512], BF16), kb.buf()) for _ in range(2)]
        BTfs = [(kb.sb("BTf", [128, 4, 128], BF16), kb.buf()) for _ in range(2)]
        CTfs = [(kb.sb("CTf", [128, 4, 128], BF16), kb.buf()) for _ in range(2)]
        dtls = [(kb.sb("dtl", [128, 128]), kb.buf()) for _ in range(2)]
        zsl = [(kb.sb("zsl", [128, 2048], BF16), kb.buf()) for _ in range(2)]
        Hbbs = [(kb.sb("Hbb", [128, 2048], BF16), kb.buf()) for _ in range(2)]
        sms = [(kb.sb("sm", [128, 160]), kb.buf()) for _ in range(2)]
        Hf = kb.sb("Hf", [128, 2048])
        Hfb = kb.sb("Hfb", [128, 2048], BF16)
        bHf, bHfb = kb.bufs(4), kb.bufs(4)
        kb.op("pool", lambda e: e.memset(Hf[:], 0.0), writes=bHf)
        kb.op("pool", lambda e: e.memset(Hfb[:], 0.0), writes=bHfb)
        ecs = [(kb.sb("ec", [128, 64]), kb.sb("wgt", [128, 32]), kb.sb("dtot", [128, 32]), kb.sb("dd", [128, 32]), kb.buf())
               for _ in range(2)]
        qks = [(kb.sb("qk", [128, 4, 128]), kb.buf()) for _ in range(2)]
        AUf = [kb.sb("AUf", [128, 4, 128]) for _ in range(2)]
        AUb = [kb.sb("AUb", [128, 4, 128]) for _ in range(2)]
        bAU = kb.bufs(2)
        E = [kb.sb("E", [128, 4, 128]) for _ in range(2)]
        bE = kb.bufs(2)
        T1 = [kb.sb("T1", [128, 4, 128]) for _ in range(2)]
        bT1 = kb.bufs(2)
        Wb = [kb.sb("Wb", [128, 4, 128], BF16) for _ in range(2)]
        bWb = kb.bufs(2)
        Ys = [(kb.sb("Y", [128, 2048]), kb.bufs(4)) for _ in range(2)]
        Tt = [(kb.sb("Tt", [128, 512]), kb.buf()) for _ in range(3)]
        xds = [(kb.sb("xd", [128, 2048], BF16), kb.buf()) for _ in range(2)]
        xws = [(kb.sb("xw", [128, 512], BF16), kb.buf()) for _ in range(2)]
        Ss = [(kb.sb("Ssb", [128, 512]), kb.buf()) for _ in range(2)]
        jz = [(kb.sb("jz", [128, 512]), kb.buf()) for _ in range(2)]
        ssqs = [(kb.sb("ssq", [128, 4]), kb.buf()) for _ in range(2)]
        cnt = {"qi": 0, "gi": 0, "pa": 0}

        def nextpA():
            k = cnt["pa"] % 3
            cnt["pa"] += 1
            return ps["pA"][k], ps["bpA"][k]

        def do_chunk(c):
            if True:
                r0 = c * 128
                cp = c % 2
                x_tok, bxk = xks[cp]
                B_tok, bBk = Bks[cp]
                BTfb, bBf = BTfs[cp]
                CTfb, bCf = CTfs[cp]
                dtl, bdt = dtls[cp]
                zsb, bzs = zsl[cp]
                Hbb, bHbb = Hbbs[cp]
                sm, bsm = sms[cp]
                ec, wgt, dtot, dd, bsm2 = ecs[cp]
                qk, bqk = qks[cp]
                Y, bY = Ys[cp]
                xd, bxd = xds[cp]
                ssq, bssq = ssqs[cp]
                dt, la = dtl[:, 0:64], dtl[:, 64:128]
                kb.dma("sp", x_tok[:], self.XTK[r0:r0 + 128, :], writes=[bxk])
                kb.dma("act", B_tok[:], self.BTK[r0:r0 + 128, :], writes=[bBk])
                kb.dma("sp", BTfb[:], self.BTF[:, :, r0:r0 + 128].rearrange("g p t -> p g t"), writes=[bBf])
                kb.dma("act", CTfb[:], self.CTF[:, :, r0:r0 + 128].rearrange("g p t -> p g t"), writes=[bCf])
                kb.dma("sp", dtl[:], self.DTL[r0:r0 + 128, :], writes=[bdt])
                kb.dma("act", zsb[:], self.ZS[r0:r0 + 128, :], writes=[bzs])
                kb.dma("sp", Hbb[:], self.HB[c], writes=[bHbb])
                self.ssd_small(cst, la, bdt, None, ps, sm, bsm)
                kb.op("act", lambda e, ec=ec, sm=sm: e.activation(ec[:], sm[:, 0:64], AF.Exp), reads=[bsm], writes=[bsm2])
                kb.op("dve", lambda e, wgt=wgt, sm=sm: e.tensor_tensor(wgt[:], sm[:, 96:128], sm[:, 0:32], ALU.subtract), reads=[bsm], writes=[bsm2])
                kb.op("act", lambda e, wgt=wgt: e.activation(wgt[:], wgt[:], AF.Exp), reads=[bsm2], writes=[bsm2])
                kb.op("dve", lambda e, wgt=wgt, dt=dt: e.tensor_tensor(wgt[:], wgt[:], dt[:, 0:32], ALU.mult), reads=[bsm2, bdt], writes=[bsm2])
                kb.op("act", lambda e, dtot=dtot, sm=sm: e.activation(dtot[:], sm[:, 96:128], AF.Exp), reads=[bsm], writes=[bsm2])
                kb.op("dve", lambda e, dd=dd, dt=dt: e.tensor_tensor(dd[:], dt[:, 0:32], dt[:, 32:64], ALU.subtract), reads=[bdt], writes=[bsm2])
                pQ, bpQ = nextpA()

                def mq(e, pQ=pQ, BTfb=BTfb, CTfb=CTfb):
                    for g in range(4):
                        ins = e.matmul(pQ[:, g * 128:(g + 1) * 128], BTfb[:, g, :], CTfb[:, g, :], start=True, stop=True)
                    return ins
                kb.op("pe", mq, reads=[bBf, bCf], writes=[bpQ])
                kb.op("act", lambda e, pQ=pQ: e.activation(qk[:].rearrange("p g j -> p (g j)"), pQ[:, 0:512], AF.Copy),
                      reads=[bpQ], writes=[bqk])
                kb.op("pool", lambda e: e.tensor_tensor(
                    xd[:].rearrange("p (h d) -> p h d", h=32), x_tok[:, :].rearrange("p (h d) -> p h d", h=32),
                    dsk[:].unsqueeze(2).to_broadcast([128, 32, 64]), ALU.mult), reads=[bxk, bc], writes=[bxd])
                for g in range(4):
                    gs = slice(g * 512, (g + 1) * 512)
                    Pi, Pf, Pb = P3[:, 0:512], P3[:, 512:1024], P3[:, 1024:1536]
                    kb.op("pe", lambda e, gs=gs: e.matmul(Pi, self.ident[:], xd[:, gs], start=True, stop=False),
                          reads=[self.bident, bxd], writes=[bP3[0]])
                    halves = []
                    for half in range(2):
                        h0 = (g * 2 + half) * 4
                        par = cnt["qi"] % 2
                        cnt["qi"] += 1
                        pS, bpS = nextpA()
                        halves.append((half, h0, par, pS, bpS))
                    for (half, h0, par, pS, bpS) in halves:
                        kb.op("dve", lambda e, par=par, h0=h0: e.tensor_tensor(
                            AUf[par][:], la[:, h0:h0 + 4].unsqueeze(2).to_broadcast([128, 4, 128]),
                            cm[:, 0, :].unsqueeze(1).to_broadcast([128, 4, 128]), ALU.mult),
                            reads=[bdt, bc], writes=[bAU[par]])
                    for (half, h0, par, pS, bpS) in halves:
                        kb.op("pool", lambda e, par=par, h0=h0: e.tensor_tensor(
                            AUb[par][:], la[:, 32 + h0:32 + h0 + 4].unsqueeze(2).to_broadcast([128, 4, 128]),
                            cm[:, 3, :].unsqueeze(1).to_broadcast([128, 4, 128]), ALU.mult),
                            reads=[bdt, bc], writes=[bAU[par]])
                    for (half, h0, par, pS, bpS) in halves:
                        def ms(e, par=par, pS=pS):
                            e.matmul(pS[:, 0:512], cm[:, 2, :], AUf[par][:].rearrange("p h j -> p (h j)"), start=True, stop=False)
                            return e.matmul(pS[:, 0:512], cm[:, 1, :], AUb[par][:].rearrange("p h j -> p (h j)"),
                                            start=False, stop=True)
                        kb.op("pe", ms, reads=[bc, bAU[par]], writes=[bpS])
                    for (half, h0, par, pS, bpS) in halves:
                        kb.op("act", lambda e, par=par, pS=pS: e.activation(
                            E[par][:].rearrange("p h j -> p (h j)"), pS[:, 0:512], AF.Exp), reads=[bpS], writes=[bE[par]])
                    for (half, h0, par, pS, bpS) in halves:
                        kb.op("pool", lambda e, par=par, h0=h0: e.tensor_tensor(
                            T1[par][:], cm[:, 0, :].unsqueeze(1).to_broadcast([128, 4, 128]),
                            dd[:, h0:h0 + 4].unsqueeze(2).to_broadcast([128, 4, 128]), ALU.mult),
                            reads=[bc, bsm2], writes=[bT1[par]])
                    for (half, h0, par, pS, bpS) in halves:
                        kb.op("pool", lambda e, par=par, h0=h0: e.tensor_tensor(
                            T1[par][:], T1[par][:], dt[:, 32 + h0:32 + h0 + 4].unsqueeze(2).to_broadcast([128, 4, 128]),
                            ALU.add), reads=[bT1[par], bdt], writes=[bT1[par]])
                    for (half, h0, par, pS, bpS) in halves:
                        kb.op("dve", lambda e, par=par, g=g: e.tensor_tensor(
                            E[par][:], E[par][:], qk[:, g, :].unsqueeze(1).to_broadcast([128, 4, 128]), ALU.mult),
                            reads=[bE[par], bqk], writes=[bE[par]])
                    for (half, h0, par, pS, bpS) in halves:
                        kb.op("dve", lambda e, par=par: e.tensor_tensor(Wb[par][:], E[par][:], T1[par][:], ALU.mult),
                              reads=[bE[par], bT1[par]], writes=[bWb[par]])
                    for (half, h0, par, pS, bpS) in halves:
                        def mi(e, par=par, h0=h0, half=half):
                            for hh in range(4):
                                h = h0 + hh
                                cs_ = (half * 4 + hh) * 64
                                ins = e.matmul(Pi[:, cs_:cs_ + 64], Wb[par][:, hh, :], x_tok[:, h * 64:(h + 1) * 64],
                                               start=False, stop=(half == 1 and hh == 3))
                            return ins
                        kb.op("pe", mi, reads=[bWb[par], bxk], writes=[bP3[0]])
                    kb.op("pe", lambda e, g=g, gs=gs: e.matmul(
                        Pf, CTfb[:, g, :], Hfb[:, gs], start=True, stop=True), reads=[bCf, bHfb[g]], writes=[bP3[1]])
                    kb.op("pe", lambda e, g=g, gs=gs: e.matmul(
                        Pb, CTfb[:, g, :], Hbb[:, gs], start=True, stop=True), reads=[bCf, bHbb], writes=[bP3[2]])
                    kb.op("act", lambda e, gs=gs: e.activation(Y[:, gs], Pi, AF.Copy), reads=[bP3[0]], writes=[bY[g]])
                    for (Px, bPx, eoff) in ((Pf, bP3[1], 0), (Pb, bP3[2], 32)):
                        Tt_, bTt = Tt[cnt["gi"] % 3]
                        cnt["gi"] += 1
                        kb.op("act", lambda e, Tt_=Tt_, Px=Px: e.activation(Tt_[:], Px, AF.Copy), reads=[bPx], writes=[bTt])
                        kb.op("dve", lambda e, Tt_=Tt_, eoff=eoff, g=g: e.tensor_tensor(
                            Tt_[:].rearrange("p (h d) -> p h d", h=8), Tt_[:].rearrange("p (h d) -> p h d", h=8),
                            ec[:, eoff + g * 8:eoff + (g + 1) * 8].unsqueeze(2).to_broadcast([128, 8, 64]), ALU.mult),
                            reads=[bTt, bsm2], writes=[bTt])
                        kb.op("dve", lambda e, Tt_=Tt_, gs=gs: e.tensor_tensor(Y[:, gs], Y[:, gs], Tt_[:], ALU.add),
                              reads=[bTt, bY[g]], writes=[bY[g]])
                    xw, bxw = xws[g % 2]
                    Ssb, bS = Ss[g % 2]
                    pS, bpS = nextpA()
                    kb.op("pool", lambda e, xw=xw, gs=gs, g=g: e.tensor_tensor(
                        xw[:].rearrange("p (h d) -> p h d", h=8), x_tok[:, gs].rearrange("p (h d) -> p h d", h=8),
                        wgt[:, g * 8:(g + 1) * 8].unsqueeze(2).to_broadcast([128, 8, 64]), ALU.mult),
                        reads=[bxk, bsm2], writes=[bxw])
                    kb.op("pe", lambda e, pS=pS, xw=xw, g=g: e.matmul(
                        pS[:, 0:512], B_tok[:, g * 128:(g + 1) * 128], xw[:], start=True, stop=True),
                        reads=[bBk, bxw], writes=[bpS])
                    kb.op("act", lambda e, Ssb=Ssb, pS=pS: e.activation(Ssb[:], pS[:, 0:512], AF.Copy),
                          reads=[bpS], writes=[bS])
                    kb.op("dve", lambda e, gs=gs, g=g: e.tensor_tensor(
                        Hf[:, gs].rearrange("p (h d) -> p h d", h=8), Hf[:, gs].rearrange("p (h d) -> p h d", h=8),
                        dtot[:, g * 8:(g + 1) * 8].unsqueeze(2).to_broadcast([128, 8, 64]), ALU.mult),
                        reads=[bHf[g], bsm2], writes=[bHf[g]])
                    kb.op("pool", lambda e, gs=gs, Ssb=Ssb: e.tensor_tensor(Hf[:, gs], Hf[:, gs], Ssb[:], ALU.add),
                          reads=[bHf[g], bS], writes=[bHf[g]])
                    kb.op("act", lambda e, gs=gs: e.activation(Hfb[:, gs], Hf[:, gs], AF.Copy), reads=[bHf[g]], writes=[bHfb[g]])
                    z_, bz = jz[g % 2]
                    kb.op("dve", lambda e, gs=gs: e.tensor_tensor(Y[:, gs], Y[:, gs], zsb[:, gs], ALU.mult),
                          reads=[bY[g], bzs], writes=[bY[g]])
                    if g == 0:
                        kb.op("dve", lambda e: e.memset(ssq[:], 0.0), writes=[bssq])
                    kb.op("act", lambda e, z_=z_, gs=gs, g=g: e.activation(z_[:], Y[:, gs], AF.Square, accum_out=ssq[:, g:g + 1]),
                          reads=[bY[g], bssq], writes=[bz, bssq])
                kb.op("act", lambda e: e.activation(ssq[:], ssq[:], AF.Sqrt, bias=EPS, scale=1.0 / 512), reads=[bssq], writes=[bssq])
                kb.op("dve", lambda e: e.reciprocal(ssq[:], ssq[:]), reads=[bssq], writes=[bssq])
                for g in range(4):
                    gs = slice(g * 512, (g + 1) * 512)
                    kb.op("dve", lambda e, g=g, gs=gs: e.scalar_tensor_tensor(
                        xd[:, gs], Y[:, gs], ssq[:, g:g + 1], ng[:, gs], ALU.mult, ALU.mult),
                        reads=[bY[g], bssq, bc], writes=[bxd])
                kb.dma("sp", self.YN[c * 128:(c + 1) * 128, 0:2048], xd[:], reads=[bxd])

        for c in range(T // 128):
            do_chunk(c)

    def even_consts(self, i, l):
        kb = self.kb
        c = {}
        b = kb.buf()
        c["b"] = b
        c["gpre"] = kb.sb("gpre", [128, D])
        kb.dma("sp", c["gpre"][:], self.gains[l * 4 + 0], writes=[b])
        lg = kb.sb("lg", [128, 16])
        kb.dma("sp", lg[:], self.rlogit[i], writes=[b])
        kb.op("act", lambda e: e.activation(lg[:], lg[:], AF.Sigmoid), reads=[b], writes=[b])
        kb.op("act", lambda e: e.activation(lg[:], lg[:], AF.Ln), reads=[b], writes=[b])
        c["lg"] = lg
        idx = kb.sb("idx", [128, 4])
        kb.dma("sp", idx[:], self.ridx, writes=[b])
        tabs = kb.sb("tabs", [128, 5, 8])
        for t, (col, off) in enumerate(((0, 0), (1, 8), (2, 0), (3, 8))):
            kb.op("dve", lambda e, t=t, col=col, off=off: e.tensor_scalar(
                tabs[:, t, :], lg[:, off:off + 8], idx[:, col:col + 1], None, ALU.mult), reads=[b], writes=[b])
        kb.op("act", lambda e: e.activation(tabs[:, 0:4, :], tabs[:, 0:4, :], AF.Exp), reads=[b], writes=[b])
        g128 = kb.sb("g128", [128, 16])
        kb.op("act", lambda e: e.activation(g128[:], lg[:], AF.Exp, scale=128.0), reads=[b], writes=[b])
        c["tabs"], c["g128"] = tabs, g128
        return c

    def even_load_win(self, i, cols_list):
        kb = self.kb
        W = kb.sb("ewin", [128, 8, 3584], BF16)
        bw = kb.buf()
        for k in range(8):
            for (a, bnd) in cols_list:
                kb.dma("pool", W[:, k, a:bnd], self.ab_w_in[i, k * 128:(k + 1) * 128, a:bnd], writes=[bw])
        return W, bw

    def proj_tok(self, W, bw, hnT, bh, col0, pP, bpP, width=512):
        def mm(e):
            for k in range(8):
                ins = e.matmul(pP[:, 0:width], hnT[:, k, 0:128], W[:, k, col0:col0 + width], start=(k == 0), stop=(k == 7))
            return ins
        self.kb.op("pe", mm, reads=[bw, bh], writes=[bpP])

    def rotary(self, src, bsrc, dst, bdst, cs, bcs, tmp):
        kb = self.kb
        s3 = src[:].rearrange("p (h d) -> p h d", h=8)
        d3 = dst[:].rearrange("p (h d) -> p h d", h=8)
        x1, x2 = s3[:, :, 0:32], s3[:, :, 32:64]
        cosb = cs[:, 0:32].unsqueeze(1).to_broadcast([128, 8, 32])
        sinb = cs[:, 32:64].unsqueeze(1).to_broadcast([128, 8, 32])
        (ta, tb_, tc, td), bt = tmp
        ta3, tb3, tc3, td3 = [t[:].rearrange("p (h d) -> p h d", h=8) for t in (ta, tb_, tc, td)]
        kb.op("dve", lambda e: e.tensor_tensor(ta3, x1, cosb, ALU.mult), reads=[bsrc, bcs], writes=[bt[0]])
        kb.op("dve", lambda e: e.tensor_tensor(tb3, x2, sinb, ALU.mult), reads=[bsrc, bcs], writes=[bt[1]])
        kb.op("dve", lambda e: e.tensor_tensor(d3[:, :, 0:32], ta3, tb3, ALU.subtract), reads=[bt[0], bt[1]], writes=[bdst])
        kb.op("pool", lambda e: e.tensor_tensor(tc3, x1, sinb, ALU.mult), reads=[bsrc, bcs], writes=[bt[2]])
        kb.op("pool", lambda e: e.tensor_tensor(td3, x2, cosb, ALU.mult), reads=[bsrc, bcs], writes=[bt[3]])
        kb.op("pool", lambda e: e.tensor_tensor(d3[:, :, 32:64], tc3, td3, ALU.add), reads=[bt[2], bt[3]], writes=[bdst])

    def even_p1(self, i, l, Xin):
        kb = self.kb
        T = self.T
        cst = self.even_consts(i, l)
        bc = cst["b"]
        tabs, g128 = cst["tabs"], cst["g128"]
        W, bw = self.even_load_win(i, [(512, 1536), (2048, 3584)])
        fe = self.make_fe(128, 0)
        hnTs = [kb.sb("hnT", [128, 8, 128], BF16) for _ in range(2)]
        bhnT = kb.bufs(2)
        pPs = [(kb.ps("pP", [128, 512]), kb.buf()) for _ in range(3)]
        kr = kb.sb("kr", [128, 512])
        bkr = kb.buf()
        krot = kb.sb("krot", [128, 512], BF16)
        bkrot = kb.buf()
        v = kb.sb("v", [128, 512])
        bv = kb.buf()
        vw = kb.sb("vw", [128, 512], BF16)
        bvw = kb.buf()
        cs = kb.sb("cs", [128, 64])
        bcs = kb.buf()
        tmp = ([kb.sb("rt", [128, 256]) for _ in range(4)], kb.bufs(4))
        Hb = kb.sb("Hb", [64, 512])
        Hbb = kb.sb("Hbb", [64, 512], BF16)
        bHb, bHbb = kb.buf(), kb.buf()
        kb.op("pool", lambda e: e.memset(Hb[:], 0.0), writes=[bHb])
        kb.op("pool", lambda e: e.memset(Hbb[:], 0.0), writes=[bHbb])
        Ssb = kb.sb("Ssb", [64, 512])
        bS = kb.buf()
        nqk = [(kb.sb("nqk", [128, 8, 128], BF16), kb.buf()) for _ in range(2)]
        nvs = [(kb.sb("nv", [128, 512], BF16), kb.buf()) for _ in range(2)]
        pi = 0
        for c in reversed(range(T // 128)):
            r0 = c * 128
            hnT, bh = hnTs[c % 2], bhnT[c % 2]
            self.run_fe(fe, Xin, r0, cst["gpre"], bc, hnT, bh)
            kb.dma("act", cs[:], self.rope[r0:r0 + 128, :], writes=[bcs])
            pP, bpP = pPs[pi % 3]
            pi += 1
            self.proj_tok(W, bw, hnT, bh, 512, pP, bpP)
            kb.op("act", lambda e, pP=pP: e.activation(kr[:], pP[:, 0:512], AF.Copy, scale=0.125), reads=[bpP], writes=[bkr])
            self.rotary(kr, bkr, krot, bkrot, cs, bcs, tmp)
            pP, bpP = pPs[pi % 3]
            pi += 1
            self.proj_tok(W, bw, hnT, bh, 1024, pP, bpP)
            kb.op("act", lambda e, pP=pP: e.activation(v[:], pP[:, 0:512], AF.Copy), reads=[bpP], writes=[bv])
            kb.op("pool", lambda e: e.tensor_tensor(
                vw[:].rearrange("p (h d) -> p h d", h=8), v[:].rearrange("p (h d) -> p h d", h=8),
                tabs[:, 3, :].unsqueeze(2).to_broadcast([128, 8, 64]), ALU.mult), reads=[bv, bc], writes=[bvw])
            kb.dma("sp", self.RHB[c], Hbb[:], reads=[bHbb])
            pP, bpP = pPs[pi % 3]
            pi += 1

            def ms(e, pP=pP):
                for h in range(8):
                    ins = e.matmul(pP[0:64, h * 64:(h + 1) * 64], krot[:, h * 64:(h + 1) * 64], vw[:, h * 64:(h + 1) * 64],
                                   start=True, stop=True)
                return ins
            kb.op("pe", ms, reads=[bkrot, bvw], writes=[bpP])
            kb.op("act", lambda e, pP=pP: e.activation(Ssb[:], pP[0:64, 0:512], AF.Copy), reads=[bpP], writes=[bS])
            kb.op("dve", lambda e: e.tensor_tensor(
                Hb[:].rearrange("p (h d) -> p h d", h=8), Hb[:].rearrange("p (h d) -> p h d", h=8),
                g128[0:64, 8:16].unsqueeze(2).to_broadcast([64, 8, 64]), ALU.mult), reads=[bHb, bc], writes=[bHb])
            kb.op("pool", lambda e: e.tensor_tensor(Hb[:], Hb[:], Ssb[:], ALU.add), reads=[bHb, bS], writes=[bHb])
            kb.op("pool", lambda e: e.tensor_copy(Hbb[:], Hb[:]), reads=[bHb], writes=[bHbb])
            nq_, bnq = nqk[c % 2]
            for pr in range(8):
                col = 2048 + pr * 128
                pP, bpP = pPs[pi % 3]
                pi += 1

                def mf(e, pP=pP, col=col, hnT=hnT):
                    for k in range(8):
                        ins = e.matmul(pP[:, 0:128], W[:, k, col:col + 128], hnT[:, k, 0:128], start=(k == 0), stop=(k == 7))
                    return ins
                kb.op("pe", mf, reads=[bw, bh], writes=[bpP])
                kb.op("act", lambda e, pP=pP, pr=pr, nq_=nq_: e.activation(
                    nq_[:, pr, :], pP[:, 0:128], AF.Copy, scale=(0.125 if pr < 4 else 1.0)), reads=[bpP], writes=[bnq])
            kb.dma("sp", self.NQT[:, :, r0:r0 + 128].rearrange("c p t -> p c t"), nq_[:, 0:4, :], reads=[bnq])
            kb.dma("sp", self.NKT[:, :, r0:r0 + 128].rearrange("c p t -> p c t"), nq_[:, 4:8, :], reads=[bnq])
            nv_, bnv = nvs[c % 2]
            pP, bpP = pPs[pi % 3]
            pi += 1
            self.proj_tok(W, bw, hnT, bh, 3072, pP, bpP)
            kb.op("act", lambda e, pP=pP, nv_=nv_: e.activation(nv_[:], pP[:, 0:512], AF.Copy), reads=[bpP], writes=[bnv])
            kb.dma("sp", self.NV[r0:r0 + 128, :], nv_[:], reads=[bnv])

    def even_p3(self, i, l, Xin):
        kb = self.kb
        T = self.T
        cst = self.even_consts(i, l)
        bc = cst["b"]
        tabs, g128, lg = cst["tabs"], cst["g128"], cst["lg"]
        W, bw = self.even_load_win(i, [(0, 2048)])
        fe = self.make_fe(128, 0)
        pT, bpT = fe["pT"], fe["bpT"]
        dj = kb.sb("dj", [128, 2, 128])
        kb.dma("sp", dj[:], self.rdj.rearrange("m p j -> p m j"), writes=[bc])
        DT = kb.sb("DT", [128, 8, 128])
        for h in range(8):
            kb.op("dve", lambda e, h=h: e.tensor_scalar(DT[:, h, :], dj[:, 0, :], lg[:, h:h + 1], None, ALU.mult),
                  reads=[bc], writes=[bc])
            kb.op("dve", lambda e, h=h: e.scalar_tensor_tensor(DT[:, h, :], dj[:, 1, :], lg[:, 8 + h:9 + h], DT[:, h, :],
                                                               ALU.mult, ALU.add), reads=[bc], writes=[bc])
        kb.op("act", lambda e: e.activation(DT[:], DT[:], AF.Exp), reads=[bc], writes=[bc])
        gng = kb.sb("gng", [128, 512])
        kb.dma("sp", gng[:], self.rgn[i], writes=[bc])
        hnTs = [kb.sb("hnT", [128, 8, 128], BF16) for _ in range(2)]
        bhnT = kb.bufs(2)
        pPs = [(kb.ps("pP", [128, 512]), kb.buf()) for _ in range(2)]
        pSc = kb.ps("pSc", [128, 1024])
        bpSc = kb.buf()
        P3 = kb.ps("P3", [128, 1536])
        bP3 = kb.bufs(3)
        Pi, Pf, Pb = P3[:, 0:512], P3[:, 512:1024], P3[:, 1024:1536]
        raw = [(kb.sb("raw", [128, 512]), kb.buf()) for _ in range(2)]
        qrot = kb.sb("qrot", [128, 512], BF16)
        krot = kb.sb("krot", [128, 512], BF16)
        bqrot, bkrot = kb.buf(), kb.buf()
        v = kb.sb("v", [128, 512])
        vb = kb.sb("vb", [128, 512], BF16)
        vw = kb.sb("vw", [128, 512], BF16)
        bv, bvb, bvw = kb.buf(), kb.buf(), kb.buf()
        sg = kb.sb("sg", [128, 512])
        bsg = kb.buf()
        cs = kb.sb("cs", [128, 64])
        bcs = kb.buf()
        tmp = ([kb.sb("rt", [128, 256]) for _ in range(4)], kb.bufs(4))
        qT = kb.sb("qT", [64, 8, 128], BF16)
        kT = kb.sb("kT", [64, 8, 128], BF16)
        bqT, bkT = kb.buf(), kb.buf()
        WT = kb.sb("WT", [128, 8, 128], BF16)
        bWT = kb.buf()
        Ssc = kb.sb("Ssc", [128, 1024])
        bSsc = kb.buf()
        Hf = kb.sb("Hf", [64, 512])
        Hfb = kb.sb("Hfb", [64, 512], BF16)
        Hbb = kb.sb("Hbb", [64, 512], BF16)
        bHf, bHfb, bHbb = kb.buf(), kb.buf(), kb.buf()
        kb.op("pool", lambda e: e.memset(Hf[:], 0.0), writes=[bHf])
        kb.op("pool", lambda e: e.memset(Hfb[:], 0.0), writes=[bHfb])
        Y = kb.sb("Y", [128, 512])
        bY = kb.buf()
        Tt = kb.sb("Tt", [128, 512])
        bTt = kb.buf()
        Ssb = kb.sb("Ssb", [64, 512])
        bS = kb.buf()
        st = kb.sb("st", [128, 16])
        bst = kb.buf()
        yo = kb.sb("yo", [128, 512], BF16)
        byo = kb.buf()
        pi = 0
        for c in range(T // 128):
            r0 = c * 128
            hnT, bh = hnTs[c % 2], bhnT[c % 2]
            self.run_fe(fe, Xin, r0, cst["gpre"], bc, hnT, bh)
            kb.dma("act", cs[:], self.rope[r0:r0 + 128, :], writes=[bcs])
            kb.dma("act", Hbb[:], self.RHB[c], writes=[bHbb])
            for (col, dst, bd, sc) in ((0, qrot, bqrot, 1.0), (512, krot, bkrot, 0.125)):
                pP, bpP = pPs[pi % 2]
                rw, brw = raw[pi % 2]
                pi += 1
                self.proj_tok(W, bw, hnT, bh, col, pP, bpP)
                kb.op("act", lambda e, pP=pP, rw=rw, sc=sc: e.activation(rw[:], pP[:, 0:512], AF.Copy, scale=sc),
                      reads=[bpP], writes=[brw])
                self.rotary(rw, brw, dst, bd, cs, bcs, tmp)
            pP, bpP = pPs[pi % 2]
            pi += 1
            self.proj_tok(W, bw, hnT, bh, 1024, pP, bpP)
            kb.op("act", lambda e, pP=pP: e.activation(v[:], pP[:, 0:512], AF.Copy), reads=[bpP], writes=[bv])
            kb.op("pool", lambda e: e.tensor_copy(vb[:], v[:]), reads=[bv], writes=[bvb])
            kb.op("pool", lambda e: e.tensor_tensor(
                vw[:].rearrange("p (h d) -> p h d", h=8), v[:].rearrange("p (h d) -> p h d", h=8),
                tabs[:, 2, :].unsqueeze(2).to_broadcast([128, 8, 64]), ALU.mult), reads=[bv, bc], writes=[bvw])
            pP, bpP = pPs[pi % 2]
            pi += 1
            self.proj_tok(W, bw, hnT, bh, 1536, pP, bpP)
            kb.op("act", lambda e, pP=pP: e.activation(sg[:], pP[:, 0:512], AF.Silu), reads=[bpP], writes=[bsg])
            for (src, bs_, dstT, bdT) in ((qrot, bqrot, qT, bqT), (krot, bkrot, kT, bkT)):
                def tr(e, src=src):
                    for h in range(8):
                        ins = e.transpose(pT[0:64, h * 128:(h + 1) * 128], src[:, h * 64:(h + 1) * 64], self.ident[:])
                    return ins
                kb.op("pe", tr, reads=[bs_, self.bident], writes=[bpT])
                kb.op("act", lambda e, dstT=dstT: e.activation(
                    dstT[:], pT[0:64, :].rearrange("p (c t) -> p c t", c=8), AF.Copy), reads=[bpT], writes=[bdT])

            def msc(e):
                for h in range(8):
                    ins = e.matmul(pSc[:, h * 128:(h + 1) * 128], kT[:, h, :], qT[:, h, :], start=True, stop=True)
                return ins
            kb.op("pe", msc, reads=[bkT, bqT], writes=[bpSc])
            kb.op("act", lambda e: e.activation(Ssc[:], pSc[:], AF.Copy), reads=[bpSc], writes=[bSsc])
            kb.op("dve", lambda e: e.tensor_tensor(WT[:].rearrange("p h j -> p (h j)"), Ssc[:],
                                                   DT[:].rearrange("p h j -> p (h j)"), ALU.mult),
                  reads=[bSsc, bc], writes=[bWT])

            def mi(e):
                for h in range(8):
                    ins = e.matmul(Pi[:, h * 64:(h + 1) * 64], WT[:, h, :], vb[:, h * 64:(h + 1) * 64], start=True, stop=True)
                return ins
            kb.op("pe", mi, reads=[bWT, bvb], writes=[bP3[0]])
            for (Px, bPx, Hx, bHx) in ((Pf, bP3[1], Hfb, bHfb), (Pb, bP3[2], Hbb, bHbb)):
                def mx(e, Px=Px, Hx=Hx):
                    for h in range(8):
                        ins = e.matmul(Px[:, h * 64:(h + 1) * 64], qT[:, h, :], Hx[:, h * 64:(h + 1) * 64],
                                       start=True, stop=True)
                    return ins
                kb.op("pe", mx, reads=[bqT, bHx], writes=[bPx])
            kb.op("act", lambda e: e.activation(Y[:], Pi, AF.Copy), reads=[bP3[0]], writes=[bY])
            for (Px, bPx, t) in ((Pf, bP3[1], 0), (Pb, bP3[2], 1)):
                kb.op("act", lambda e, Px=Px: e.activation(Tt[:], Px, AF.Copy), reads=[bPx], writes=[bTt])
                kb.op("dve", lambda e, t=t: e.tensor_tensor(
                    Tt[:].rearrange("p (h d) -> p h d", h=8), Tt[:].rearrange("p (h d) -> p h d", h=8),
                    tabs[:, t, :].unsqueeze(2).to_broadcast([128, 8, 64]), ALU.mult), reads=[bTt, bc], writes=[bTt])
                kb.op("pool", lambda e: e.tensor_tensor(Y[:], Y[:], Tt[:], ALU.add), reads=[bTt, bY], writes=[bY])
            pP, bpP = pPs[pi % 2]
            pi += 1

            def ms(e, pP=pP):
                for h in range(8):
                    ins = e.matmul(pP[0:64, h * 64:(h + 1) * 64], krot[:, h * 64:(h + 1) * 64], vw[:, h * 64:(h + 1) * 64],
                                   start=True, stop=True)
                return ins
            kb.op("pe", ms, reads=[bkrot, bvw], writes=[bpP])
            kb.op("act", lambda e, pP=pP: e.activation(Ssb[:], pP[0:64, 0:512], AF.Copy), reads=[bpP], writes=[bS])
            kb.op("dve", lambda e: e.tensor_tensor(
                Hf[:].rearrange("p (h d) -> p h d", h=8), Hf[:].rearrange("p (h d) -> p h d", h=8),
                g128[0:64, 0:8].unsqueeze(2).to_broadcast([64, 8, 64]), ALU.mult), reads=[bHf, bc], writes=[bHf])
            kb.op("pool", lambda e: e.tensor_tensor(Hf[:], Hf[:], Ssb[:], ALU.add), reads=[bHf, bS], writes=[bHf])
            kb.op("pool", lambda e: e.tensor_copy(Hfb[:], Hf[:]), reads=[bHf], writes=[bHfb])
            Y3 = Y[:].rearrange("p (h d) -> p h d", h=8)
            T3 = Tt[:].rearrange("p (h d) -> p h d", h=8)
            kb.op("dve", lambda e: e.reduce_sum(st[:, 0:8], Y3, AX.X), reads=[bY], writes=[bst])
            kb.op("dve", lambda e: e.tensor_scalar(st[:, 0:8], st[:, 0:8], 1.0 / 64, None, ALU.mult), reads=[bst], writes=[bst])
            kb.op("dve", lambda e: e.tensor_tensor(Y3, Y3, st[:, 0:8].unsqueeze(2).to_broadcast([128, 8, 64]), ALU.subtract),
                  reads=[bY, bst], writes=[bY])
            kb.op("act", lambda e: e.activation(Tt[:], Y[:], AF.Square), reads=[bY], writes=[bTt])
            kb.op("dve", lambda e: e.reduce_sum(st[:, 8:16], T3, AX.X), reads=[bTt], writes=[bst])
            kb.op("act", lambda e: e.activation(st[:, 8:16], st[:, 8:16], AF.Sqrt, bias=EPS, scale=1.0 / 64), reads=[bst], writes=[bst])
            kb.op("dve", lambda e: e.reciprocal(st[:, 8:16], st[:, 8:16]), reads=[bst], writes=[bst])
            kb.op("dve", lambda e: e.tensor_tensor(Y3, Y3, st[:, 8:16].unsqueeze(2).to_broadcast([128, 8, 64]), ALU.mult),
                  reads=[bY, bst], writes=[bY])
            kb.op("pool", lambda e: e.tensor_tensor(Y[:], Y[:], gng[:], ALU.mult), reads=[bY, bc], writes=[bY])
            kb.op("pool", lambda e: e.tensor_tensor(yo[:], Y[:], sg[:], ALU.mult), reads=[bY, bsg], writes=[byo])
            kb.dma("sp", self.YN[r0:r0 + 128, 0:512], yo[:], reads=[byo])

    def na_stage(self, i):
        kb = self.kb
        T = self.T
        rows = T // 64
        tb = kb.sb("tb", [128, 2048])
        btb = kb.buf()
        kws = [(kb.sb("kw", [64, 8, 512], BF16), kb.buf()) for _ in range(2)]
        qws = [(kb.sb("qw", [64, 8, 64], BF16), kb.buf()) for _ in range(2)]
        vws = [(kb.sb("vwin", [128, 4, 8, 80], BF16), kb.buf()) for _ in range(2)]
        for (vw_, bvw_) in vws:
            kb.op("pool", lambda e, vw_=vw_: e.memset(vw_[:], 1.0), writes=[bvw_])
        Sb = kb.sb("Sb", [128, 2048])
        bSb = kb.buf()
        PTs = [(kb.sb("PT", [128, 2048], BF16), kb.buf()) for _ in range(2)]
        Osb = kb.sb("Osb", [64, 8, 65])
        bO = kb.buf()
        rc = kb.sb("rc", [64, 8])
        brc = kb.buf()
        nas = [(kb.sb("na", [64, 512], BF16), kb.buf()) for _ in range(2)]
        pST = kb.ps("pST", [128, 2048])
        bpST = kb.buf()
        pO = kb.ps("pO", [128, 1024])
        bpO = kb.buf()
        prev_s = None
        import os
        NCUT = int(os.environ.get("NA_CUT", "9"))
        stt_ = {"prev_s": None}

        def s1(r):
            start = min(max(r - 4, 0), rows - 8)
            s = r - start
            s0 = start * 64
            if s != stt_["prev_s"]:
                kb.dma("sp", tb[:], self.natab[i, s], writes=[btb])
                stt_["prev_s"] = s
            kw, bkw = kws[r % 2]
            qw, bqw = qws[r % 2]
            vw_, bvw_ = vws[r % 2]
            kb.dma("sp", kw[:], self.NKT[:, :, s0:s0 + 512].rearrange("c (two p) t -> p (c two) t", two=2), writes=[bkw])
            kb.dma("act", qw[:], self.NQT[:, :, r * 64:(r + 1) * 64].rearrange("c (two p) t -> p (c two) t", two=2),
                   writes=[bqw])
            for ck in range(4):
                kb.dma("act", vw_[:, ck, :, 0:64],
                       self.NV[s0 + ck * 128:s0 + (ck + 1) * 128, :].rearrange("p (h d) -> p h d", h=8), writes=[bvw_])

            def mst(e):
                for ck in range(4):
                    for h in range(8):
                        o = (ck * 8 + h) * 64
                        ins = e.matmul(pST[:, o:o + 64], kw[:, h, ck * 128:(ck + 1) * 128],
                                       qw[:, h, :], start=True, stop=True)
                return ins
            kb.op("pe", mst, reads=[bkw, bqw], writes=[bpST])
            kb.op("act", lambda e: e.activation(Sb[:], pST[:], AF.Copy), reads=[bpST], writes=[bSb])
            kb.op("dve", lambda e: e.tensor_tensor(Sb[:], Sb[:], tb[:], ALU.add), reads=[bSb, btb], writes=[bSb])
            PT, bPT = PTs[r % 2]
            kb.op("act", lambda e: e.activation(PT[:], Sb[:], AF.Exp), reads=[bSb], writes=[bPT])

        def s2(r):
            vw_, bvw_ = vws[r % 2]
            PT, bPT = PTs[r % 2]

            def mpv(e):
                for h in range(8):
                    oc = (h % 4) * 80 + (h // 4) * 512
                    for ck in range(4):
                        o = (ck * 8 + h) * 64
                        ins = e.matmul(pO[0:64, oc:oc + 65], PT[:, o:o + 64], vw_[:, ck, h, 0:65],
                                       start=(ck == 0), stop=(ck == 3))
                return ins
            kb.op("pe", mpv, reads=[bPT, bvw_], writes=[bpO])
            kb.op("act", lambda e: e.activation(Osb[:, 0:4, :], pO[0:64, 0:320].rearrange("p (h d) -> p h d", h=4)[:, :, 0:65], AF.Copy),
                  reads=[bpO], writes=[bO])
            kb.op("act", lambda e: e.activation(Osb[:, 4:8, :], pO[0:64, 512:832].rearrange("p (h d) -> p h d", h=4)[:, :, 0:65], AF.Copy),
                  reads=[bpO], writes=[bO])
            kb.op("dve", lambda e: e.reciprocal(rc[:].unsqueeze(2), Osb[:, :, 64:65]), reads=[bO], writes=[brc])
            na, bna = nas[r % 2]
            kb.op("dve", lambda e: e.tensor_tensor(
                na[:].rearrange("p (h d) -> p h d", h=8), Osb[:, :, 0:64],
                rc[:].unsqueeze(2).to_broadcast([64, 8, 64]), ALU.mult), reads=[bO, brc], writes=[bna])
            kb.dma("sp", self.YN[r * 64:(r + 1) * 64, 512:1024], na[:], reads=[bna])

        s1(0)
        for r in range(rows):
            if r + 1 < rows:
                s1(r + 1)
            s2(r)

    def outproj_stage(self, l, w_src, Cin, YN, Xin, Xout):
        kb = self.kb
        T = self.T
        KC = Cin // 128
        W = kb.sb("wout", [128, KC, D], BF16)
        bw = kb.buf()
        for k in range(KC):
            kb.dma("pool", W[:, k, :], w_src[k * 128:(k + 1) * 128, :], writes=[bw])
        gpost = kb.sb("gpost", [128, D])
        bc = kb.buf()
        kb.dma("sp", gpost[:], self.gains[l * 4 + 1], writes=[bc])
        yns = [(kb.sb("yn", [128, Cin], BF16), kb.buf()) for _ in range(2)]
        YTs = [(kb.sb("YT", [128, KC, 128], BF16), kb.buf()) for _ in range(2)]
        xrs = [(kb.sb("xr", [128, D]), kb.buf()) for _ in range(2)]
        xos = [(kb.sb("xo", [128, D]), kb.buf()) for _ in range(2)]
        junk = kb.sb("junk", [128, D])
        bjunk = kb.buf()
        ss = kb.sb("ss", [128, 1])
        bss = kb.buf()
        pTs = [(kb.ps("pT", [128, D], BF16), kb.buf()) for _ in range(2)]
        pms = [(kb.ps("pm", [128, D]), kb.buf()) for _ in range(2)]
        st = {"ti": 0}

        def s1(c):
            r0 = c * 128
            pm, bpm = pms[c % 2]
            yn, byn = yns[c % 2]
            YT, bYT = YTs[c % 2]
            xr, bxr = xrs[c % 2]
            xo, bxo = xos[c % 2]
            kb.dma("act", yn[:], YN[r0:r0 + 128, 0:Cin], writes=[byn])
            kb.dma("sp", xr[:], Xin[r0:r0 + 128, :], writes=[bxr])
            for r in range(KC // 8):
                pT, bpT = pTs[st["ti"] % 2]
                st["ti"] += 1

                def tr(e, yn=yn, pT=pT, r=r):
                    for k in range(8):
                        kk = r * 8 + k
                        ins = e.transpose(pT[:, k * 128:(k + 1) * 128], yn[:, kk * 128:(kk + 1) * 128], self.ident[:])
                    return ins
                kb.op("pe", tr, reads=[byn, self.bident], writes=[bpT])
                kb.op("act", lambda e, YT=YT, pT=pT, r=r: e.activation(
                    YT[:, r * 8:(r + 1) * 8, :], pT[:].rearrange("p (k t) -> p k t", k=8), AF.Copy),
                    reads=[bpT], writes=[bYT])

            def mm(e, YT=YT):
                for nh in range(2):
                    for k in range(KC):
                        ins = e.matmul(pm[:, nh * 512:(nh + 1) * 512], YT[:, k, :], W[:, k, nh * 512:(nh + 1) * 512],
                                       start=(k == 0), stop=(k == KC - 1))
                return ins
            kb.op("pe", mm, reads=[bYT, bw], writes=[bpm])

        def s2(c):
            r0 = c * 128
            pm, bpm = pms[c % 2]
            xr, bxr = xrs[c % 2]
            xo, bxo = xos[c % 2]
            self.rms_stats(pm[:], bpm, ss, bss, junk, bjunk)
            kb.op("act", lambda e, xo=xo: e.activation(xo[:], pm[:], AF.Copy, scale=ss[:, 0:1]),
                  reads=[bpm, bss], writes=[bxo])
            kb.op("dve", lambda e, xo=xo: e.tensor_tensor(xo[:], xo[:], gpost[:], ALU.mult),
                  reads=[bxo, bc], writes=[bxo])
            kb.op("pool", lambda e, xo=xo, xr=xr: e.tensor_tensor(xo[:], xo[:], xr[:], ALU.add),
                  reads=[bxo, bxr], writes=[bxo])
            kb.dma("sp", Xout[r0:r0 + 128, :], xo[:], reads=[bxo])

        NCk = T // 128
        s1(0)
        for c in range(NCk):
            if c + 1 < NCk:
                s1(c + 1)
            s2(c)

    def ffn_stage(self, l, Xin, Xout, BT=512):
        kb = self.kb
        T = self.T
        NT = BT // 128
        ident, bident = self.ident, self.bident
        W_up = kb.sb("wup", [128, 8, 2 * FF], BF16)
        W_dn = kb.sb("wdn", [128, NCH, D], BF16)
        bwu = kb.bufs(8)
        bwd = kb.bufs(NCH)
        for k in range(8):
            kb.dma("pool", W_up[:, k, :], self.ffn_w_up[l, k * 128:(k + 1) * 128, :], writes=[bwu[k]])
        for c in range(NCH):
            kb.dma("pool", W_dn[:, c, :], self.ffn_w_down[l, c * 128:(c + 1) * 128, :], writes=[bwd[c]])
        gpre = kb.sb("gpre", [128, D])
        gpost = kb.sb("gpost", [128, D])
        fcw = kb.sb("fcw", [128, 2 * NCH, 3])
        fcb = kb.sb("fcb", [128, 2 * NCH])
        bconst = kb.buf()
        kb.dma("sp", gpre[:], self.gains[l * 4 + 2], writes=[bconst])
        kb.dma("sp", gpost[:], self.gains[l * 4 + 3], writes=[bconst])
        kb.dma("sp", fcw[:], self.fcw[l].rearrange("p (c w) -> p c w", w=3), writes=[bconst])
        kb.dma("sp", fcb[:], self.fcb[l], writes=[bconst])

        xts = [kb.sb("xt", [128, D])] * 2
        bxt = [kb.buf()] * 2
        hns = [kb.sb("hn", [128, D], BF16)] * 2
        bhn = [kb.buf()] * 2
        sss = [kb.sb("ss", [128, 1]) for _ in range(2)]
        bss = kb.bufs(2)
        xh = kb.sb("xh", [2, D])
        bxh = kb.buf()
        hnh = kb.sb("hnh", [2, D], BF16)
        bhnh = kb.buf()
        ssh = kb.sb("ssh", [2, 1])
        bssh = kb.buf()
        hnTs = [kb.sb("hnT", [128, 8, BT + 2], BF16)] * 2
        bhnT = [kb.buf()] * 2
        gT = kb.sb("gT", [128, NCH, BT], BF16)
        bgT = kb.bufs(NCH)
        cgs = [kb.sb("cg", [128, BT]) for _ in range(2)]
        cvs = [kb.sb("cv", [128, BT]) for _ in range(2)]
        ggs = [kb.sb("gg", [128, BT])] * 2
        bcg, bcv, bgg = kb.bufs(2), kb.bufs(2), [kb.buf()] * 2
        ugs = [kb.sb("ug", [128, BT + 2])] * 2
        uvs = [kb.sb("uv", [128, BT + 2])] * 2
        bug, buv = [kb.buf()] * 2, [kb.buf()] * 2
        xrs = [kb.sb("xr", [128, D]) for _ in range(1)]
        bxr = kb.bufs(1)
        junk, bjunk = xrs[0], bxr[0]
        xos = [kb.sb("xo", [128, D])] * 2
        bxo = [kb.buf()] * 2
        ss2 = kb.sb("ss2", [128, 1])
        bss2 = kb.buf()

        psG = [kb.ps("psG", [128, 512]) for _ in range(2)]
        psV = [kb.ps("psV", [128, 512]) for _ in range(2)]
        bpsA = kb.bufs(2)
        psH = kb.ps("psH", [128, 512])
        bpsH = kb.buf()
        pT = kb.ps("pT", [128, D], BF16)
        bpT = kb.buf()
        pTh, bpTh = pT, bpT
        pf = kb.ps("pf", [128, D])
        bpf = kb.buf()

        ti = 0
        ci = 0
        import os
        CUT = int(os.environ.get("FFN_CUT", "9"))
        for b in range(T // BT if CUT > 1 else 0):
            t0 = b * BT
            hnT = hnTs[b % 2]
            bh = bhnT[b % 2]
            for j in range(NT):
                xt, bx = xts[ti % 2], bxt[ti % 2]
                hn, bn = hns[ti % 2], bhn[ti % 2]
                ss, bs = sss[ti % 2], bss[ti % 2]
                ti += 1
                r0 = t0 + j * 128
                kb.dma("sp", xt[:], Xin[r0:r0 + 128, :], writes=[bx])
                self.rms_stats(xt[:], bx, ss, bs, junk, bjunk)
                kb.op("dve", lambda e, hn=hn, xt=xt, ss=ss: e.scalar_tensor_tensor(
                    hn[:], xt[:], ss[:, 0:1], gpre[:], ALU.mult, ALU.mult),
                    reads=[bx, bs, bconst], writes=[bn])

                def tr(e, hn=hn):
                    for k in range(8):
                        ins = e.transpose(pT[:, k * 128:(k + 1) * 128], hn[:, k * 128:(k + 1) * 128], ident[:])
                    return ins
                kb.op("pe", tr, reads=[bn, bident], writes=[bpT])
                kb.op("act", lambda e, hnT=hnT, j=j: e.activation(
                    hnT[:, :, j * 128:(j + 1) * 128], pT[:].rearrange("p (k t) -> p k t", k=8), AF.Copy),
                    reads=[bpT], writes=[bh])
            kb.op("pool", lambda e: e.memset(xh[:], 0.0), writes=[bxh])
            if t0 - 1 >= 0:
                kb.dma("sp", xh[0:1, :], Xin[t0 - 1:t0, :], writes=[bxh])
            if t0 + BT < T:
                kb.dma("sp", xh[1:2, :], Xin[t0 + BT:t0 + BT + 1, :], writes=[bxh])
            self.rms_stats(xh[:], bxh, ssh, bssh, junk, bjunk, nparts=2)
            kb.op("dve", lambda e: e.scalar_tensor_tensor(
                hnh[:], xh[:], ssh[:, 0:1], gpre[0:2, :], ALU.mult, ALU.mult),
                reads=[bxh, bssh, bconst], writes=[bhnh])

            def trh(e):
                for k in range(8):
                    ins = e.transpose(pTh[:, k * 2:(k + 1) * 2], hnh[:, k * 128:(k + 1) * 128], ident[0:2, 0:2])
                return ins
            kb.op("pe", trh, reads=[bhnh, bident], writes=[bpTh])
            kb.op("act", lambda e, hnT=hnT: e.activation(
                hnT[:, :, BT:BT + 2], pTh[:, 0:16].rearrange("p (k t) -> p k t", k=8), AF.Copy),
                reads=[bpTh], writes=[bh])

            for c in range(NCH if CUT > 2 else 0):
                par = ci % 2
                ci += 1
                pg = psG[par][:, 0:BT]
                pv = psV[par][:, 0:BT]
                ph = psH[:, 0:4]
                cg, cv, gg = cgs[par], cvs[par], ggs[par]

                SUB = os.environ.get("FFN_SUB", "")

                def mm(e, c=c, pg=pg, pv=pv, ph=ph, hnT=hnT):
                    lst = ((pg, c * 128, slice(0, BT)), (pv, FF + c * 128, slice(0, BT)),
                           (ph[:, 0:2], c * 128, slice(BT, BT + 2)),
                           (ph[:, 2:4], FF + c * 128, slice(BT, BT + 2)))
                    if "h" in SUB:
                        lst = lst[:2]
                    for (dst, col, rhs_sl) in lst:
                        for k in range(8):
                            ins = e.matmul(dst, W_up[:, k, col:col + 128], hnT[:, k, rhs_sl],
                                           start=(k == 0), stop=(k == 7))
                    return ins
                kb.op("pe", mm, reads=bwu + [bh], writes=[bpsA[par], bpsH])
                pairs = ((pg, 0, cg, bcg[par], c, ugs[par], bug[par]), (pv, 2, cv, bcv[par], NCH + c, uvs[par], buv[par]))
                for (src, hoff, dst, bd, cc, u, bu) in pairs:
                    kb.op("act", lambda e, src=src, u=u: e.activation(u[:, 0:BT], src, AF.Copy),
                          reads=[bpsA[par]], writes=[bu])
                    kb.op("act", lambda e, ph=ph, hoff=hoff, u=u: e.activation(
                        u[:, BT:BT + 2], ph[:, hoff:hoff + 2], AF.Copy), reads=[bpsH], writes=[bu])
                    kb.op("act", lambda e, src=src, dst=dst, cc=cc: e.activation(
                        dst[:], src, AF.Identity, bias=fcb[:, cc:cc + 1], scale=fcw[:, cc, 1:2]),
                        reads=[bpsA[par], bconst], writes=[bd])
                for tap in range(4):
                    for (src, hoff, dst, bd, cc, u, bu) in pairs:
                        if tap == 0:
                            kb.op("dve", lambda e, u=u, dst=dst, cc=cc: e.scalar_tensor_tensor(
                                dst[:, 1:BT], u[:, 0:BT - 1], fcw[:, cc, 0:1], dst[:, 1:BT], ALU.mult, ALU.add),
                                reads=[bu, bconst, bd], writes=[bd])
                        elif tap == 1:
                            kb.op("dve", lambda e, u=u, dst=dst, cc=cc: e.scalar_tensor_tensor(
                                dst[:, 0:BT - 1], u[:, 1:BT], fcw[:, cc, 2:3], dst[:, 0:BT - 1], ALU.mult, ALU.add),
                                reads=[bu, bconst, bd], writes=[bd])
                        elif tap == 2:
                            kb.op("dve", lambda e, u=u, dst=dst, cc=cc: e.scalar_tensor_tensor(
                                dst[:, 0:1], u[:, BT:BT + 1], fcw[:, cc, 0:1], dst[:, 0:1], ALU.mult, ALU.add),
                                reads=[bu, bconst, bd], writes=[bd])
                        else:
                            kb.op("dve", lambda e, u=u, dst=dst, cc=cc: e.scalar_tensor_tensor(
                                dst[:, BT - 1:BT], u[:, BT + 1:BT + 2], fcw[:, cc, 2:3], dst[:, BT - 1:BT],
                                ALU.mult, ALU.add),
                                reads=[bu, bconst, bd], writes=[bd])
                kb.op("act", lambda e, gg=gg, cg=cg: e.activation(gg[:], cg[:], AF.Gelu_apprx_tanh),
                      reads=[bcg[par]], writes=[bgg[par]])
                kb.op("pool", lambda e, c=c, gg=gg, cv=cv: e.tensor_tensor(gT[:, c, :], gg[:], cv[:], ALU.mult),
                      reads=[bgg[par], bcv[par]], writes=[bgT[c]])

            for j in range(NT if CUT > 3 else 0):
                r0 = t0 + j * 128

                def dn(e, j=j):
                    for nh in range(2):
                        for c in range(NCH):
                            ins = e.matmul(pf[:, nh * 512:(nh + 1) * 512], gT[:, c, j * 128:(j + 1) * 128],
                                           W_dn[:, c, nh * 512:(nh + 1) * 512], start=(c == 0), stop=(c == NCH - 1))
                    return ins
                kb.op("pe", dn, reads=bgT + bwd, writes=[bpf])
                xr, bxrr = xrs[0], bxr[0]
                xo, bxoo = xos[j % 2], bxo[j % 2]
                kb.op("act", lambda e, xo=xo: e.activation(xo[:], pf[:], AF.Copy), reads=[bpf], writes=[bxoo])
                self.rms_stats(xo[:], bxoo, ss2, bss2, hns[0], bhn[0])
                kb.dma("sp", xr[:], Xin[r0:r0 + 128, :], writes=[bxrr])
                kb.op("dve", lambda e, xo=xo: e.scalar_tensor_tensor(xo[:], xo[:], ss2[:, 0:1], gpost[:], ALU.mult, ALU.mult),
                      reads=[bxoo, bss2, bconst], writes=[bxoo])
                kb.op("pool", lambda e, xo=xo, xr=xr: e.tensor_tensor(xo[:], xo[:], xr[:], ALU.add),
                      reads=[bxoo, bxrr], writes=[bxoo])
                kb.dma("sp", Xout[r0:r0 + 128, :], xo[:], reads=[bxoo])


def host_rope(pos):
    half = 32
    inv = (1.0 / (10000.0 ** (np.arange(half, dtype=np.float32) / half))).astype(np.float32)
    ang = pos.astype(np.float32)[:, None] * inv[None, :]
    return np.concatenate([np.cos(ang), np.sin(ang)], axis=1).astype(np.float32)


def host_even(inp):
    lgt = inp["ab_ret_decay_logit"].reshape(2, 1, 16)
    rlogit = np.ascontiguousarray(np.broadcast_to(lgt, (2, 128, 16))).astype(np.float32)
    p = np.arange(128, dtype=np.float32)
    ridx = np.stack([p + 1, 128 - p, 127 - p, p], axis=1).astype(np.float32)
    l_ = np.arange(128)[:, None]
    j_ = np.arange(128)[None, :]
    rdj = np.stack([np.maximum(j_ - l_, 0), np.maximum(l_ - j_, 0)], 0).astype(np.float32)
    rgn = np.ascontiguousarray(np.broadcast_to(inp["ab_ret_gn_g"].reshape(2, 1, 512), (2, 128, 512))).astype(np.float32)
    rpb = inp["ab_na_rpb"]
    kk = np.arange(512)
    w = kk // 64
    kc = kk % 64
    q = np.arange(64)
    cstart = np.clip(q - 8, 0, 48)
    valid = (kc[:, None] >= cstart[None, :]) & (kc[:, None] < cstart[None, :] + 16)
    dc = np.clip(kc[:, None] - q[None, :], -15, 15) + 15
    natab = np.empty((2, 8, 512, 8, 64), np.float32)
    for s in range(8):
        dr = np.clip(w - s + 7, 0, 14)
        g = rpb[:, :, dr[:, None], dc]
        g = np.where(valid[None, None], g, np.float32(-30000.0))
        natab[:, s] = g.transpose(0, 2, 1, 3)
    natab = natab.reshape(2, 8, 4, 128, 8, 64).transpose(0, 1, 3, 2, 4, 5).reshape(2, 8, 128, 2048)
    return {"ab_w_in": np.ascontiguousarray(inp["ab_w_in"]), "ab_w_out": np.ascontiguousarray(inp["ab_w_out"]),
            "rlogit": rlogit, "ridx": ridx, "rdj": rdj, "rgn": rgn, "natab": np.ascontiguousarray(natab)}

def host_prep(inp, layers=range(4)):
    L = L_TOTAL
    g = np.stack([inp["norm_mix_pre"], inp["norm_mix_post"], inp["norm_ffn_pre"], inp["norm_ffn_post"]], axis=1)
    gains = np.ascontiguousarray(np.broadcast_to(g.reshape(L * 4, 1, D), (L * 4, 128, D))).astype(np.float32)
    cw = inp["ffn_conv_w"]
    fcw = np.ascontiguousarray(cw.reshape(L, 3, 2 * NCH, 128).transpose(0, 3, 2, 1)).reshape(L, 128, 2 * NCH * 3)
    fcb = np.ascontiguousarray(inp["ffn_conv_b"].reshape(L, 2 * NCH, 128).transpose(0, 2, 1))
    cmats = np.zeros((5, 128, 128), np.float32)
    k = np.arange(128)[:, None]
    j = np.arange(128)[None, :]
    cmats[0] = k <= j
    cmats[1] = k < j
    cmats[2] = k > j
    cmats[3] = k >= j
    cmats[4] = 1.0
    ccw = inp["c_conv_w"]
    scw = np.ascontiguousarray(ccw.reshape(2, 5, 24, 128).transpose(0, 3, 2, 1)).reshape(2, 128, 120)
    scb = np.ascontiguousarray(inp["c_conv_b"].reshape(2, 24, 128).transpose(0, 2, 1))

    def rep(a, n):
        return np.ascontiguousarray(np.broadcast_to(a.reshape(2, 1, n), (2, 128, n))).astype(np.float32)
    extra = {"c_w_in": np.ascontiguousarray(inp["c_w_in"]), "c_w_out": np.ascontiguousarray(inp["c_w_out"]),
             "scw": scw.astype(np.float32), "scb": scb.astype(np.float32),
             "sdtb": rep(inp["c_dt_bias"], 64), "salog": rep(inp["c_a_log"], 64),
             "sdsk": rep(inp["c_d_skip"], 32), "sng": rep(inp["c_norm_g"], 2048), "cmats": cmats}
    extra.update(host_even(inp))
    return {**extra, "gains": gains, "ffn_w_up": np.ascontiguousarray(inp["ffn_w_up"]),
            "ffn_w_down": np.ascontiguousarray(inp["ffn_w_down"]),
            "fcw": fcw.astype(np.float32), "fcb": fcb.astype(np.float32)}


_CACHE = {}


def kernel(**inputs):
    x = np.asarray(inputs["x"], dtype=np.float32)
    B, S, _ = x.shape
    n_cores = 8
    cpb = n_cores // B
    if "prog" not in _CACHE:
        _CACHE["prog"] = Prog(S, [0, 1, 2, 3], do_mix=True, do_ffn=True)
    P = _CACHE["prog"]
    hp = host_prep({k: np.asarray(v, dtype=np.float32) for k, v in inputs.items()})
    hp["rope"] = host_rope(np.arange(S))
    in_maps = []
    for c in range(n_cores):
        m = dict(hp)
        m["x"] = np.ascontiguousarray(x[c // cpb])
        in_maps.append(m)
    res = run_bass_kernel_spmd(P.nc, in_maps, core_ids=list(range(n_cores)))
    out = np.stack([res.results[b * cpb]["out"] for b in range(B)], axis=0)
    return out.astype(np.float32)
```
